# Optimizing a Trainium2 kernel written in Bass

```python
import jax, jax.numpy as jnp
from jax import lax
import numpy as np

D_MODEL = 2048
BATCH = 4
SEQ = 2048
DEPTH = 2
DEC_BATCH = 8
DEC_SEQ = 8
PAST_LEN = 16384
PAGE_SIZE = 128

HEAD_DIM = 128
NSA_WIDTH = D_MODEL // 2
GM_WIDTH = D_MODEL // 4
CONV_WIDTH = D_MODEL - NSA_WIDTH - GM_WIDTH
NSA_HEADS = NSA_WIDTH // HEAD_DIM
NSA_KV_HEADS = 2
GQA_R = NSA_HEADS // NSA_KV_HEADS
KV_WIDTH = NSA_KV_HEADS * HEAD_DIM
GM_GROUPS = 4
CONV_GROUPS = 4
CMP_BLOCK = 32
SEL_BLOCK = 64
N_SELECT = 16
WINDOW = 512
FORCED_SCORE = 1.0e4
SEL_QBLOCK = 64
WIN_QBLOCK = 128
GM_CHUNK = 128
CONV_K = 31
PEER_HEADS = 8
N_KEYS = 128
N_EXPERTS = N_KEYS * N_KEYS
PEER_TOPK = 16
SUBKEY_DIM = 128
PEER_QDIM = 2 * SUBKEY_DIM
PEER_TOKEN_BLOCK = 128
IN_WIDTH = 2 * GM_WIDTH + NSA_WIDTH + 6 * KV_WIDTH + 3 * NSA_HEADS + 2 * CONV_WIDTH
ALPHA = (2 * DEPTH) ** 0.25
BETA = (8 * DEPTH) ** -0.25
ROPE_THETA = 10000.0
LN_EPS = 1e-5

kernel_name = "hymba_gmlp_nsa_conformer_peer_step"


def _ln(x, g, b):
    xf = x.astype(jnp.float32)
    mu = xf.mean(-1, keepdims=True)
    var = jnp.square(xf - mu).mean(-1, keepdims=True)
    return ((xf - mu) * lax.rsqrt(var + LN_EPS)).astype(x.dtype) * g + b


def _rope(x, pos):
    half = HEAD_DIM // 2
    inv = ROPE_THETA ** (-jnp.arange(half, dtype=jnp.float32) / half)
    ang = pos.astype(jnp.float32)[:, None] * inv[None, :]
    cos, sin = jnp.cos(ang)[:, None, :], jnp.sin(ang)[:, None, :]
    x1, x2 = x[..., :half].astype(jnp.float32), x[..., half:].astype(jnp.float32)
    return jnp.concatenate([x1 * cos - x2 * sin, x2 * cos + x1 * sin], -1).astype(x.dtype)


def _masked_softmax(s, mask):
    s = jnp.where(mask, s.astype(jnp.float32), -jnp.inf)
    m = jnp.max(s, -1, keepdims=True)
    m = jnp.where(jnp.isfinite(m), m, 0.0)
    e = jnp.exp(s - m)
    return e / jnp.maximum(e.sum(-1, keepdims=True), 1e-30)


def _attend(q, k, v, mask):
    s = jnp.einsum("...tgrd,...lgd->...tgrl", q, k).astype(jnp.float32) * HEAD_DIM ** -0.5
    p = _masked_softmax(s, mask[..., :, None, None, :])
    return jnp.einsum("...tgrl,...lgd->...tgrd", p.astype(v.dtype), v), p


def _sel_attend(q, kb, vb, sel, qpos):
    B, T = q.shape[0], q.shape[1]
    bi = jnp.arange(B)[:, None, None, None]
    gi = jnp.arange(NSA_KV_HEADS)[None, None, :, None]
    kg, vg = kb[bi, gi, sel], vb[bi, gi, sel]
    nk = sel.shape[-1] * SEL_BLOCK
    s = jnp.einsum("btgrd,btgksd->btgrks", q, kg).reshape(B, T, NSA_KV_HEADS, GQA_R, nk)
    s = s.astype(jnp.float32) * HEAD_DIM ** -0.5
    kpos = sel[..., None] * SEL_BLOCK + jnp.arange(SEL_BLOCK)
    mask = (kpos <= qpos[None, :, None, None, None]).reshape(B, T, NSA_KV_HEADS, 1, nk)
    p = _masked_softmax(s, mask)
    return jnp.einsum("btgrn,btgnd->btgrd", p.astype(vg.dtype), vg.reshape(B, T, NSA_KV_HEADS, nk, HEAD_DIM))


def _nsa_cmp_slc(q, kc, vc, ks, vs, qpos, cmp_wk, cmp_wv):
    B, T, L = q.shape[0], q.shape[1], kc.shape[1]
    n_cmp = L // CMP_BLOCK

    def compress(t, w):
        tb = t[:, : n_cmp * CMP_BLOCK].reshape(B, n_cmp, CMP_BLOCK, NSA_KV_HEADS, HEAD_DIM)
        return jnp.einsum("bnjgd,gj->bngd", tb, w)

    cmp_last = (jnp.arange(n_cmp) + 1) * CMP_BLOCK - 1
    o_cmp, p_cmp = _attend(q, compress(kc, cmp_wk), compress(vc, cmp_wv), cmp_last[None, :] <= qpos[:, None])
    ratio = SEL_BLOCK // CMP_BLOCK
    n_slc = -(-L // SEL_BLOCK)
    imp = p_cmp.sum(axis=3)
    imp = jnp.pad(imp, ((0, 0), (0, 0), (0, 0), (0, n_slc * ratio - n_cmp)))
    imp = imp.reshape(B, T, NSA_KV_HEADS, n_slc, ratio).sum(-1)
    blk = jnp.arange(n_slc)[None, :]
    cur = (qpos // SEL_BLOCK)[:, None]
    valid = blk * SEL_BLOCK <= qpos[:, None]
    forced = (blk == 0) | (blk == cur) | (blk == cur - 1)
    score = jnp.where(valid[None, :, None, :], jnp.where(forced[None, :, None, :], FORCED_SCORE, imp), -jnp.inf)
    _, sel = lax.top_k(score, min(N_SELECT, n_slc))
    pad_len = n_slc * SEL_BLOCK - L

    def blocks(t):
        t = jnp.pad(t, ((0, 0), (0, pad_len), (0, 0), (0, 0)))
        return t.reshape(B, n_slc, SEL_BLOCK, NSA_KV_HEADS, HEAD_DIM).transpose(0, 3, 1, 2, 4)

    kb, vb = blocks(ks), blocks(vs)
    if T > SEL_QBLOCK and T % SEL_QBLOCK == 0:
        nq = T // SEL_QBLOCK
        qb = q.reshape(B, nq, SEL_QBLOCK, NSA_KV_HEADS, GQA_R, HEAD_DIM).swapaxes(0, 1)
        sb = sel.reshape(B, nq, SEL_QBLOCK, NSA_KV_HEADS, sel.shape[-1]).swapaxes(0, 1)
        pb = qpos.reshape(nq, SEL_QBLOCK)
        o = lax.map(lambda a: _sel_attend(a[0], kb, vb, a[1], a[2]), (qb, sb, pb))
        o_slc = o.swapaxes(0, 1).reshape(B, T, NSA_KV_HEADS, GQA_R, HEAD_DIM)
    else:
        o_slc = _sel_attend(q, kb, vb, sel, qpos)
    return o_cmp, o_slc


def _window_prompt(q, k, v):
    B, S = q.shape[0], q.shape[1]
    nb = S // WIN_QBLOCK
    span = WINDOW + WIN_QBLOCK
    idx = jnp.arange(nb)[:, None] * WIN_QBLOCK + jnp.arange(span)[None, :]
    pad = ((0, 0), (WINDOW, 0), (0, 0), (0, 0))
    kb, vb = jnp.pad(k, pad)[:, idx], jnp.pad(v, pad)[:, idx]
    qb = q.reshape(B, nb, WIN_QBLOCK, NSA_KV_HEADS, GQA_R, HEAD_DIM)
    kpos = (idx - WINDOW)[:, None, :]
    qp = (jnp.arange(nb)[:, None] * WIN_QBLOCK + jnp.arange(WIN_QBLOCK)[None, :])[:, :, None]
    mask = (kpos <= qp) & (kpos >= qp - WINDOW) & (kpos >= 0)
    o, _ = _attend(qb, kb, vb, mask)
    return o.reshape(B, S, NSA_KV_HEADS, GQA_R, HEAD_DIM)


def _gmlp(u, v, g, b, ws, bs):
    B, T, _ = u.shape
    gw = GM_WIDTH // GM_GROUPS
    vn = _ln(v.reshape(B, T, GM_GROUPS, gw), g.reshape(GM_GROUPS, gw), b.reshape(GM_GROUPS, gw))
    C = min(T, GM_CHUNK)
    w = jnp.tril(ws[:, :C, :C])
    s = jnp.einsum("hij,bnjhc->bnihc", w, vn.reshape(B, T // C, C, GM_GROUPS, gw)) + bs[:, :C].T[:, :, None]
    vn = vn.reshape(B, T, GM_WIDTH)
    start = ((T - 1) // GM_CHUNK) * GM_CHUNK
    return u * s.reshape(B, T, GM_WIDTH), vn[:, start:]


def _token_mix(x, qpos, past, w_in, gm_g, gm_b, gm_ws, gm_bs, cmp_wk, cmp_wv,
               conv_dw, conv_db, conv_g, conv_b, conv_pw, w_out):
    B, T, _ = x.shape
    sizes = [GM_WIDTH, GM_WIDTH, NSA_WIDTH] + [KV_WIDTH] * 6 + [3 * NSA_HEADS, CONV_WIDTH, CONV_WIDTH]
    cuts = np.cumsum(sizes)[:-1].tolist()
    gu, gv, q, kc, vc, ks, vs, kw, vw, gate, ca, cb = jnp.split(x @ w_in, cuts, axis=-1)
    gm_out, gm_state = _gmlp(jax.nn.gelu(gu), jax.nn.gelu(gv), gm_g, gm_b, gm_ws, gm_bs)
    kvh = lambda t: t.reshape(B, T, NSA_KV_HEADS, HEAD_DIM)
    q = _rope(q.reshape(B, T, NSA_HEADS, HEAD_DIM), qpos).reshape(B, T, NSA_KV_HEADS, GQA_R, HEAD_DIM)
    kc, ks, kw = _rope(kvh(kc), qpos), _rope(kvh(ks), qpos), _rope(kvh(kw), qpos)
    vc, vs, vw = kvh(vc), kvh(vs), kvh(vw)
    if past is None:
        o_cmp, o_slc = _nsa_cmp_slc(q, kc, vc, ks, vs, qpos, cmp_wk, cmp_wv)
        o_win = _window_prompt(q, kw, vw)
        keep = min(WINDOW, T)
        win_k, win_v = kw[:, -keep:], vw[:, -keep:]
        hist = jnp.zeros((B, CONV_K - 1, CONV_WIDTH), x.dtype)
    else:
        kc_p, vc_p, ks_p, vs_p, wk_buf, wv_buf, hist = past
        P, WB = kc_p.shape[1], wk_buf.shape[1]
        cat = lambda a, b_: jnp.concatenate([a, b_], axis=1)
        o_cmp, o_slc = _nsa_cmp_slc(q, cat(kc_p, kc), cat(vc_p, vc), cat(ks_p, ks), cat(vs_p, vs), qpos, cmp_wk, cmp_wv)
        wk_all, wv_all = cat(wk_buf, kw), cat(wv_buf, vw)
        kpos = P - WB + jnp.arange(WB + T)
        mask = (kpos[None, :] <= qpos[:, None]) & (kpos[None, :] >= qpos[:, None] - WINDOW)
        o_win, _ = _attend(q, wk_all, wv_all, mask)
        keep = min(WINDOW, P + T)
        win_k, win_v = wk_all[:, -keep:], wv_all[:, -keep:]
    g = jax.nn.sigmoid(gate.astype(jnp.float32)).astype(x.dtype).reshape(B, T, NSA_KV_HEADS, GQA_R, 3, 1)
    o_nsa = (g[..., 0, :] * o_cmp + g[..., 1, :] * o_slc + g[..., 2, :] * o_win).reshape(B, T, NSA_WIDTH)
    c = ca * jax.nn.sigmoid(cb)
    cseq = jnp.concatenate([hist, c], axis=1)
    y = lax.conv_general_dilated(cseq, conv_dw[:, None, :], window_strides=(1,), padding="VALID",
                                 dimension_numbers=("NWC", "WIO", "NWC"), feature_group_count=CONV_WIDTH) + conv_db
    cw = CONV_WIDTH // CONV_GROUPS
    y = _ln(y.reshape(B, T, CONV_GROUPS, cw), conv_g.reshape(CONV_GROUPS, cw), conv_b.reshape(CONV_GROUPS, cw))
    y = jax.nn.silu(y.reshape(B, T, CONV_WIDTH)) @ conv_pw
    conv_state = cseq[:, -(CONV_K - 1):]
    mix = jnp.concatenate([gm_out, o_nsa, y], axis=-1) @ w_out
    return mix, (kc, vc, ks, vs, win_k, win_v, conv_state, gm_state)


def _peer_tokens(x, wq, subkeys, u_tab, v_tab):
    n = x.shape[0]
    q = (x @ wq).reshape(n, PEER_HEADS, 2, SUBKEY_DIM)
    s = jnp.einsum("nhpd,hpkd->nhpk", q, subkeys).astype(jnp.float32)
    s1, i1 = lax.top_k(s[:, :, 0], PEER_TOPK)
    s2, i2 = lax.top_k(s[:, :, 1], PEER_TOPK)
    ncand = PEER_TOPK * PEER_TOPK
    cand = (s1[..., :, None] + s2[..., None, :]).reshape(n, PEER_HEADS, ncand)
    cidx = (i1[..., :, None] * N_KEYS + i2[..., None, :]).reshape(n, PEER_HEADS, ncand)
    top, pos = lax.top_k(cand, PEER_TOPK)
    experts = jnp.take_along_axis(cidx, pos, axis=-1)
    gate = jax.nn.softmax(top, axis=-1)
    act = jax.nn.gelu(jnp.einsum("nhkd,nd->nhk", u_tab[experts], x))
    w = (gate * act.astype(jnp.float32)).astype(x.dtype)
    return jnp.einsum("nhk,nhkd->nd", w, v_tab[experts])


def _peer(x, wq, subkeys, u_tab, v_tab):
    B, T, D = x.shape
    n = B * T
    xf = x.reshape(n, D)
    if n > PEER_TOKEN_BLOCK and n % PEER_TOKEN_BLOCK == 0:
        y = lax.map(lambda xb: _peer_tokens(xb, wq, subkeys, u_tab, v_tab),
                    xf.reshape(n // PEER_TOKEN_BLOCK, PEER_TOKEN_BLOCK, D)).reshape(n, D)
    else:
        y = _peer_tokens(xf, wq, subkeys, u_tab, v_tab)
    return y.reshape(B, T, D)


def _gather_pages(cache_l, page_table):
    g = cache_l[page_table]
    return g.reshape(g.shape[0], g.shape[1] * g.shape[2], g.shape[3], g.shape[4])


def setup_inputs(seed: int = 0) -> dict:
    key = jax.random.key(seed)
    ks = iter(jax.random.split(key, 48))
    nrm = lambda shape, scale: jax.random.normal(next(ks), shape, jnp.float32) * scale
    gain = lambda shape: 1.0 + nrm(shape, 0.01)
    n_pages = PAST_LEN // PAGE_SIZE
    used = DEC_BATCH * n_pages
    n_pool = used + max(used // 4, 1)
    wbuf = min(WINDOW, PAST_LEN)
    kvshape = (DEPTH, n_pool, PAGE_SIZE, NSA_KV_HEADS, HEAD_DIM)
    wshape = (DEPTH, DEC_BATCH, wbuf, NSA_KV_HEADS, HEAD_DIM)
    page_table = jax.random.permutation(next(ks), n_pool)[:used].reshape(DEC_BATCH, n_pages).astype(jnp.int32)
    return {
        "x_prompt": nrm((BATCH, SEQ, D_MODEL), 1.0),
        "x_sample": nrm((DEC_BATCH, DEC_SEQ, D_MODEL), 1.0),
        "cache_cmp_k": nrm(kvshape, 1.0),
        "cache_cmp_v": nrm(kvshape, 1.0),
        "cache_slc_k": nrm(kvshape, 1.0),
        "cache_slc_v": nrm(kvshape, 1.0),
        "cache_win_k": nrm(wshape, 1.0),
        "cache_win_v": nrm(wshape, 1.0),
        "state_conv": nrm((DEPTH, DEC_BATCH, CONV_K - 1, CONV_WIDTH), 1.0),
        "page_table": page_table,
        "w_in": nrm((DEPTH, D_MODEL, IN_WIDTH), D_MODEL ** -0.5),
        "gm_ln_g": gain((DEPTH, GM_WIDTH)),
        "gm_ln_b": nrm((DEPTH, GM_WIDTH), 0.01),
        "gm_ws": nrm((DEPTH, GM_GROUPS, GM_CHUNK, GM_CHUNK), 0.5 * GM_CHUNK ** -0.5),
        "gm_bs": gain((DEPTH, GM_GROUPS, GM_CHUNK)),
        "cmp_wk": nrm((DEPTH, NSA_KV_HEADS, CMP_BLOCK), CMP_BLOCK ** -0.5),
        "cmp_wv": nrm((DEPTH, NSA_KV_HEADS, CMP_BLOCK), CMP_BLOCK ** -0.5),
        "conv_dw": nrm((DEPTH, CONV_K, CONV_WIDTH), CONV_K ** -0.5),
        "conv_db": nrm((DEPTH, CONV_WIDTH), 0.01),
        "conv_ln_g": gain((DEPTH, CONV_WIDTH)),
        "conv_ln_b": nrm((DEPTH, CONV_WIDTH), 0.01),
        "conv_pw": nrm((DEPTH, CONV_WIDTH, CONV_WIDTH), CONV_WIDTH ** -0.5),
        "w_out": nrm((DEPTH, D_MODEL, D_MODEL), BETA * D_MODEL ** -0.5),
        "ln1_g": gain((DEPTH, D_MODEL)),
        "ln1_b": nrm((DEPTH, D_MODEL), 0.01),
        "peer_wq": nrm((DEPTH, D_MODEL, PEER_HEADS * PEER_QDIM), D_MODEL ** -0.5),
        "peer_subkeys": nrm((DEPTH, PEER_HEADS, 2, N_KEYS, SUBKEY_DIM), SUBKEY_DIM ** -0.5),
        "peer_u": nrm((DEPTH, N_EXPERTS, D_MODEL), D_MODEL ** -0.5),
        "peer_v": nrm((DEPTH, N_EXPERTS, D_MODEL), BETA * PEER_HEADS ** -0.5),
        "ln2_g": gain((DEPTH, D_MODEL)),
        "ln2_b": nrm((DEPTH, D_MODEL), 0.01),
    }


def reference(x_prompt, x_sample, cache_cmp_k, cache_cmp_v, cache_slc_k, cache_slc_v,
              cache_win_k, cache_win_v, state_conv, page_table, w_in, gm_ln_g, gm_ln_b,
              gm_ws, gm_bs, cmp_wk, cmp_wv, conv_dw, conv_db, conv_ln_g, conv_ln_b, conv_pw,
              w_out, ln1_g, ln1_b, peer_wq, peer_subkeys, peer_u, peer_v, ln2_g, ln2_b):
    S, T = x_prompt.shape[1], x_sample.shape[1]
    P = page_table.shape[1] * cache_cmp_k.shape[2]
    pos_p = jnp.arange(S, dtype=jnp.int32)
    pos_s = P + jnp.arange(T, dtype=jnp.int32)
    xp, xs = x_prompt, x_sample
    p_states, s_states = [], []
    for l in range(DEPTH):
        lw = (w_in[l], gm_ln_g[l], gm_ln_b[l], gm_ws[l], gm_bs[l], cmp_wk[l], cmp_wv[l],
              conv_dw[l], conv_db[l], conv_ln_g[l], conv_ln_b[l], conv_pw[l], w_out[l])
        mp, sp = _token_mix(xp, pos_p, None, *lw)
        past = (_gather_pages(cache_cmp_k[l], page_table), _gather_pages(cache_cmp_v[l], page_table),
                _gather_pages(cache_slc_k[l], page_table), _gather_pages(cache_slc_v[l], page_table),
                cache_win_k[l], cache_win_v[l], state_conv[l])
        ms, ss = _token_mix(xs, pos_s, past, *lw)
        xp = _ln(ALPHA * xp + mp, ln1_g[l], ln1_b[l])
        xs = _ln(ALPHA * xs + ms, ln1_g[l], ln1_b[l])
        xp = _ln(ALPHA * xp + _peer(xp, peer_wq[l], peer_subkeys[l], peer_u[l], peer_v[l]), ln2_g[l], ln2_b[l])
        xs = _ln(ALPHA * xs + _peer(xs, peer_wq[l], peer_subkeys[l], peer_u[l], peer_v[l]), ln2_g[l], ln2_b[l])
        p_states.append(sp)
        s_states.append(ss)
    stack = lambda states, i: jnp.stack([st[i] for st in states], axis=0)
    p_cmp_k, p_cmp_v, p_slc_k, p_slc_v = stack(p_states, 0), stack(p_states, 1), stack(p_states, 2), stack(p_states, 3)
    p_win_k, p_win_v, p_conv, p_gm_v = stack(p_states, 4), stack(p_states, 5), stack(p_states, 6), stack(p_states, 7)
    s_cmp_k, s_cmp_v, s_slc_k, s_slc_v = stack(s_states, 0), stack(s_states, 1), stack(s_states, 2), stack(s_states, 3)
    s_win_k, s_win_v, s_conv, s_gm_v = stack(s_states, 4), stack(s_states, 5), stack(s_states, 6), stack(s_states, 7)
    return (xp, xs, p_cmp_k, p_cmp_v, p_slc_k, p_slc_v, p_win_k, p_win_v, p_conv, p_gm_v,
            s_cmp_k, s_cmp_v, s_slc_k, s_slc_v, s_win_k, s_win_v, s_conv, s_gm_v)
```

```python
import numpy as np
from contextlib import ExitStack
import concourse.bass as bass
import concourse.mybir as mybir
from concourse.bass_utils import run_bass_kernel_spmd

F32 = mybir.dt.float32
BF16 = mybir.dt.bfloat16
I32 = mybir.dt.int32
U32 = mybir.dt.uint32
AF = mybir.ActivationFunctionType
ALU = mybir.AluOpType
AX = mybir.AxisListType


class Buf:
    def __init__(self, t, name):
        self.t = t
        self.name = name
        self.w = {}
        self.r = {}

    def __getitem__(self, idx):
        return self.t[idx]


class Eng:
    def __init__(self, fw, name, h, pe=False, ndma=0):
        self.fw = fw
        self.name = name
        self.h = h
        self.pe = pe
        self.sem = fw.new_sem(name + "_prog")
        self.n = 0
        self.seen = {}
        self.dma_sems = [fw.new_sem(f"{name}_d{i}") for i in range(ndma)]
        self.dma_n = 0

    def wait(self, tok):
        key, sem, val = tok
        if self.seen.get(key, 0) >= val:
            return
        self.h.wait_ge(sem, val)
        self.seen[key] = val


class FW:
    def __init__(self, nc):
        self.nc = nc
        self.es0 = ExitStack()
        self.es = self.es0
        self.es_stack = []
        self.sems = []
        self.pe = Eng(self, "pe", nc.tensor, pe=True)
        self.act = Eng(self, "act", nc.scalar, ndma=8)
        self.dve = Eng(self, "dve", nc.vector)
        self.pool = Eng(self, "pool", nc.gpsimd, ndma=24)
        self.sp = Eng(self, "sp", nc.sync, ndma=24)
        self.engs = [self.pe, self.act, self.dve, self.pool, self.sp]
        self.bufs = []

    def new_sem(self, name):
        s = self.es0.enter_context(self.nc.semaphore(name))
        self.sems.append(s)
        return s

    def push(self):
        self.es_stack.append((self.es, len(self.bufs)))
        self.es = ExitStack()

    def pop(self):
        self.barrier()
        self.es.close()
        self.es, nb = self.es_stack.pop()
        del self.bufs[nb:]

    def sbuf(self, name, shape, dtype=F32):
        self.uid = getattr(self, "uid", 0) + 1
        name = f"{name}_{self.uid}"
        t = self.es.enter_context(self.nc.sbuf_tensor(name, list(shape), dtype))
        b = Buf(t, name)
        self.bufs.append(b)
        return b

    def psum(self, name, shape, dtype=F32):
        t = self.es.enter_context(self.nc.psum_tensor(name, list(shape), dtype))
        b = Buf(t, name)
        self.bufs.append(b)
        return b

    def dram(self, name, shape, dtype=F32, kind="Internal"):
        t = self.nc.dram_tensor(name, list(shape), dtype, kind=kind).ap()
        b = Buf(t, name)
        self.bufs.append(b)
        return b

    def view(self, ap, name="v"):
        b = Buf(ap, name)
        self.bufs.append(b)
        return b

    def _waits(self, eng, reads, writes, join, is_dma):
        for b in reads:
            for tok in b.w.values():
                if tok[0] == eng.name and not is_dma and eng.pe:
                    continue
                eng.wait(tok)
        for b in writes:
            if not join:
                for tok in b.w.values():
                    if tok[0] == eng.name and not is_dma:
                        continue
                    eng.wait(tok)
            for tok in b.r.values():
                if tok[0] == eng.name and not is_dma:
                    continue
                eng.wait(tok)

    def _record(self, tok, reads, writes, join):
        for b in writes:
            if join:
                b.w[tok[0]] = tok
            else:
                b.w = {tok[0]: tok}
            b.r = {}
        for b in reads:
            if b not in writes:
                b.r[tok[0]] = tok

    def op(self, eng, fn, reads=(), writes=(), join=False):
        self._waits(eng, reads, writes, join, False)
        inst = fn(eng.h)
        eng.n += 1
        inst.then_inc(eng.sem, 1)
        tok = (eng.name, eng.sem, eng.n)
        self._record(tok, reads, writes, join)
        return inst

    def dma(self, q, fn, reads=(), writes=(), join=False):
        self._waits(q, reads, writes, join, True)
        k = len(q.dma_sems)
        i = q.dma_n % k
        gen = q.dma_n // k
        sem = q.dma_sems[i]
        key = f"{q.name}_d{i}"
        if gen > 0:
            q.wait((key, sem, 16 * gen))
        inst = fn(q.h)
        inst.then_inc(sem, 16)
        q.dma_n += 1
        tok = (key, sem, 16 * (gen + 1))
        self._record(tok, reads, writes, join)
        return inst

    def barrier(self):
        toks = []
        for e in self.engs:
            if e.n:
                toks.append((e.name, e.sem, e.n))
            k = len(e.dma_sems)
            for i in range(min(k, e.dma_n)):
                cnt = (e.dma_n - 1 - i) // k + 1
                toks.append((f"{e.name}_d{i}", e.dma_sems[i], 16 * cnt))
        for e in self.engs:
            for tok in toks:
                if tok[0] == e.name:
                    continue
                e.wait(tok)
        for b in self.bufs:
            b.w = {}
            b.r = {}

    def finish(self):
        self.barrier()


D = 2048
S = 2048
NT = S // 128
DEPTH = 2
TS = 8
IN_W = 4632
TOK0, TOK1 = 512, 3608
NTOKC = TOK1 - TOK0
ALPHA = float((2 * DEPTH) ** 0.25)
LN_EPS = 1e-5
GELU_C = 0.7978845608028654
NEG = -30000.0


class Ctx:
    pass


def emit_gelu(fw, eng_dve, eng_act, out_ap, in_ap, tmp_ap, bufs_r, bufs_w, tmpbuf):
    fw.op(eng_act, lambda e: e.activation(out=tmp_ap, in_=in_ap, func=AF.Square), reads=bufs_r, writes=[tmpbuf])
    fw.op(eng_dve, lambda e: e.tensor_scalar(out=tmp_ap, in0=tmp_ap, scalar1=0.044715, scalar2=1.0, op0=ALU.mult, op1=ALU.add),
          reads=[tmpbuf], writes=[tmpbuf])
    fw.op(eng_dve, lambda e: e.tensor_tensor(out=tmp_ap, in0=tmp_ap, in1=in_ap, op=ALU.mult), reads=bufs_r + [tmpbuf], writes=[tmpbuf])
    fw.op(eng_act, lambda e: e.activation(out=tmp_ap, in_=tmp_ap, func=AF.Sigmoid, scale=2.0 * GELU_C), reads=[tmpbuf], writes=[tmpbuf])
    fw.op(eng_dve, lambda e: e.tensor_tensor(out=out_ap, in0=tmp_ap, in1=in_ap, op=ALU.mult), reads=bufs_r + [tmpbuf], writes=bufs_w)


def phase_proj(fw, C, l, x_src, nt, Htok, HfT, xT, tagp):
    nc = fw.nc
    ntok = nt * 128 if nt > 0 else TS
    P = 128 if nt > 0 else TS
    ntile = max(nt, 1)
    for t in range(ntile):
        xt = C.xin[t % 2]
        fw.dma(fw.sp, lambda h: h.dma_start(out=xt[0:P, :], in_=x_src[t * 128:t * 128 + P, :]), reads=[x_src], writes=[xt])
        for cb in range(4):
            ps = C.ps[(t * 4 + cb) % 4]
            for k in range(4):
                c = cb * 4 + k
                fw.op(fw.pe, lambda e: e.transpose(out=ps[:, k * 128:k * 128 + P], in_=xt[0:P, c * 128:(c + 1) * 128], identity=C.ident[0:P, 0:P]),
                      reads=[xt, C.ident], writes=[ps], join=(k > 0))
            eng = fw.dve if cb % 2 == 0 else fw.act
            src = ps[:, :].rearrange("p (k q) -> p k q", k=4)[:, :, 0:P]
            dst = xT[:, cb * 4:(cb + 1) * 4, t * 128:t * 128 + P]
            if eng is fw.dve:
                fw.op(eng, lambda e: e.tensor_copy(out=dst, in_=src), reads=[ps], writes=[xT], join=True)
            else:
                fw.op(eng, lambda e: e.copy(out=dst, in_=src), reads=[ps], writes=[xT], join=True)
    w_in = C.w_in
    ncol = [(TOK0 + j * 512, min(512, TOK1 - (TOK0 + j * 512))) for j in range((NTOKC + 511) // 512)]
    it = 0
    for j, (c0, cw) in enumerate(ncol):
        wb = C.wbuf[j % 2]
        fw.dma(fw.pool, lambda h: h.dma_start(out=wb[:, :, 0:cw], in_=w_in[l, :, c0:c0 + cw].rearrange("(c p) n -> p c n", p=128)),
               reads=[w_in], writes=[wb])
        for t in range(ntile):
            ps = C.ps[it % 4]
            for c in range(16):
                fw.op(fw.pe, lambda e: e.matmul(ps[0:P, 0:cw], lhsT=xT[:, c, t * 128:t * 128 + P], rhs=wb[:, c, 0:cw], start=(c == 0), stop=(c == 15)),
                      reads=[xT, wb], writes=[ps], join=(c > 0))
            hb = C.hbuf[it % 4]
            if it % 2 == 0:
                fw.op(fw.dve, lambda e: e.tensor_copy(out=hb[0:P, 0:cw], in_=ps[0:P, 0:cw]), reads=[ps], writes=[hb])
            else:
                fw.op(fw.act, lambda e: e.copy(out=hb[0:P, 0:cw], in_=ps[0:P, 0:cw]), reads=[ps], writes=[hb])
            fw.dma(fw.sp, lambda h: h.dma_start(out=Htok[t * 128:t * 128 + P, c0 - TOK0:c0 - TOK0 + cw], in_=hb[0:P, 0:cw]),
                   reads=[hb], writes=[Htok], join=True)
            it += 1
    fcols = [(0, 0), (128, 128), (256, 256), (384, 384)] + [(3608 + i * 128, 512 + i * 128) for i in range(8)]
    TB = 512 if nt > 0 else TS
    ntb = max(ntok // 512, 1)
    for j, (c0, r0) in enumerate(fcols):
        wb = C.wbuf[j % 2]
        fw.dma(fw.pool, lambda h: h.dma_start(out=wb[:, :, 0:128], in_=w_in[l, :, c0:c0 + 128].rearrange("(c p) n -> p c n", p=128)),
               reads=[w_in], writes=[wb])
        for tb in range(ntb):
            ps = C.ps[it % 4]
            for c in range(16):
                fw.op(fw.pe, lambda e: e.matmul(ps[:, 0:TB], lhsT=wb[:, c, 0:128], rhs=xT[:, c, tb * 512:tb * 512 + TB], start=(c == 0), stop=(c == 15)),
                      reads=[xT, wb], writes=[ps], join=(c > 0))
            hb = C.hbuf[it % 4]
            if r0 < 512:
                tmp = C.htmp
                emit_gelu(fw, fw.dve, fw.act, hb[:, 0:TB], ps[:, 0:TB], tmp[:, 0:TB], [ps], [hb], tmp)
            elif it % 2 == 0:
                fw.op(fw.dve, lambda e: e.tensor_copy(out=hb[:, 0:TB], in_=ps[:, 0:TB]), reads=[ps], writes=[hb])
            else:
                fw.op(fw.act, lambda e: e.copy(out=hb[:, 0:TB], in_=ps[:, 0:TB]), reads=[ps], writes=[hb])
            fw.dma(fw.sp, lambda h: h.dma_start(out=HfT[r0:r0 + 128, tb * 512:tb * 512 + TB], in_=hb[:, 0:TB]),
                   reads=[hb], writes=[HfT], join=True)
            it += 1


def host_consts():
    c = {}
    c["ident"] = np.eye(128, dtype=np.float32)
    half = 64
    inv = (np.float32(10000.0) ** (-np.arange(half, dtype=np.float32) / np.float32(half))).astype(np.float32)
    def rope_tab(pos):
        ang = pos.astype(np.float32)[:, None] * inv[None, :]
        cs, sn = np.cos(ang).astype(np.float32), np.sin(ang).astype(np.float32)
        sc = np.float32(128 ** -0.5)
        return np.stack([cs, sn, cs * sc, sn * sc], axis=1).astype(np.float32)
    c["rope_p"] = rope_tab(np.arange(S))
    c["rope_s"] = rope_tab(16384 + np.arange(TS))
    q = np.arange(128)[:, None, None]
    t = np.arange(NT)[None, :, None]
    n = np.arange(64)[None, None, :]
    qpos = 128 * t + q
    c["cmpbias"] = np.where(32 * n + 31 <= qpos, 0.0, NEG).astype(np.float32)
    c["cmpvalid"] = (qpos[:, :, 0] >= 31).astype(np.float32)
    m = np.arange(32)[None, None, :]
    cur = qpos // 64
    valid = 64 * m <= qpos
    forced = (m == 0) | (m == cur) | (m == cur - 1)
    c["selA"] = (valid & ~forced).astype(np.float32)
    c["selB"] = np.where(valid, np.where(forced, 1.0e4, 0.0), -1.0e9).astype(np.float32)
    qq = np.arange(128)[:, None]; kk = np.arange(128)[None, :]
    c["causal"] = np.where(kk <= qq, 0.0, NEG).astype(np.float32)
    c["acausal"] = np.where(kk >= qq, 0.0, NEG).astype(np.float32)
    tok = np.arange(128)[:, None]
    c["mask4"] = (tok // 32 == np.arange(4)[None, :]).astype(np.float32)
    c["maskpad"] = (np.arange(64)[None, None, :] == 4 * np.arange(NT)[None, :, None] + (tok // 32)[:, :, None]).astype(np.float32)
    c["triu"] = (qq <= kk).astype(np.float32)
    c["onesdiv"] = np.full((128, 128), 1.0 / 128, np.float32)
    c["iota16"] = np.tile(np.arange(16, dtype=np.float32)[None, :], (128, 1))
    c.update(sample_consts())
    return c


CONST_SHAPES = {"ident": [128, 128], "rope_p": [S, 4, 64], "rope_s": [TS, 4, 64], "cmpbias": [128, NT, 64], "cmpvalid": [128, NT],
                "selA": [128, NT, 32], "selB": [128, NT, 32], "causal": [128, 128], "acausal": [128, 128], "mask4": [128, 4],
                "maskpad": [128, NT, 64], "triu": [128, 128], "onesdiv": [128, 128], "iota16": [128, 16]}


def load_const(fw, C, name, dtype=F32):
    shp = CONST_SHAPES[name]
    b = fw.sbuf("c_" + name, shp, F32)
    src = C.cd[name]
    idx = tuple(slice(None) for _ in shp)
    fw.dma(fw.sp, lambda h: h.dma_start(out=b[idx], in_=src[idx]), reads=[src], writes=[b])
    return b


def load_colvec(fw, C, vec_ap, n, name, srcbuf):
    rows = fw.sbuf(name + "_r", [n, 128], F32)
    fw.dma(fw.sp, lambda h: h.dma_start(out=rows[:, :], in_=vec_ap.rearrange("(j p) -> j p", p=128)), reads=[srcbuf], writes=[rows])
    ps = C.ps[0]
    fw.op(fw.pe, lambda e: e.transpose(out=ps[:, 0:n], in_=rows[0:n, :], identity=C.ident[0:n, 0:n]), reads=[rows, C.ident], writes=[ps])
    col = fw.sbuf(name, [128, n], F32)
    fw.op(fw.dve, lambda e: e.tensor_copy(out=col[:, :], in_=ps[:, 0:n]), reads=[ps], writes=[col])
    return col


def emit_rope(fw, x1, x2, cs, sn, tmp, shape, rbufs, xbuf, tmpbuf):
    nd = len(shape)
    def bc(a):
        v = a
        for _ in range(nd - 2):
            v = v.unsqueeze(1)
        return v.to_broadcast(list(shape))
    t1, t2, t3, t4 = tmp
    fw.op(fw.dve, lambda e: e.tensor_tensor(out=t1, in0=x1, in1=bc(cs), op=ALU.mult), reads=[xbuf] + rbufs, writes=[tmpbuf[0]])
    fw.op(fw.dve, lambda e: e.tensor_tensor(out=t2, in0=x2, in1=bc(sn), op=ALU.mult), reads=[xbuf] + rbufs, writes=[tmpbuf[1]])
    fw.op(fw.pool, lambda e: e.tensor_tensor(out=t3, in0=x2, in1=bc(cs), op=ALU.mult), reads=[xbuf] + rbufs, writes=[tmpbuf[2]])
    fw.op(fw.pool, lambda e: e.tensor_tensor(out=t4, in0=x1, in1=bc(sn), op=ALU.mult), reads=[xbuf] + rbufs, writes=[tmpbuf[3]])
    fw.op(fw.dve, lambda e: e.tensor_tensor(out=x1, in0=t1, in1=t2, op=ALU.subtract), reads=[tmpbuf[0], tmpbuf[1], tmpbuf[2], tmpbuf[3]], writes=[xbuf])
    fw.op(fw.dve, lambda e: e.tensor_tensor(out=x2, in0=t3, in1=t4, op=ALU.add), reads=[tmpbuf[2], tmpbuf[3]], writes=[xbuf])


def phase_kv_prompt(fw, C, l):
    P = 128
    C.maskpad = load_const(fw, C, "maskpad")
    wcol = fw.sbuf("wcol", [128, 4], F32)
    for wi, wsrc in enumerate([C.cmp_wk, C.cmp_wv]):
        for g in range(2):
            for blk in range(4):
                fw.dma(fw.sp, lambda h: h.dma_start(out=wcol[blk * 32:(blk + 1) * 32, wi * 2 + g:wi * 2 + g + 1],
                                                    in_=wsrc[l, g, :].rearrange("(j o) -> j o", o=1)), reads=[wsrc], writes=[wcol], join=True)
    Wck = fw.sbuf("Wck", [128, 2, 4], BF16)
    WcvPad = fw.sbuf("WcvPad", [128, 2, NT * 64], BF16)
    for g in range(2):
        fw.op(fw.dve, lambda e: e.tensor_scalar(out=Wck[:, g, :], in0=C.mask4[:, :], scalar1=wcol[:, g:g + 1], scalar2=None, op0=ALU.mult),
              reads=[C.mask4, wcol], writes=[Wck], join=True)
        fw.op(fw.dve, lambda e: e.tensor_scalar(out=WcvPad[:, g, :], in0=C.maskpad[:, :, :].rearrange("p t n -> p (t n)"), scalar1=wcol[:, 2 + g:3 + g], scalar2=None, op0=ALU.mult),
              reads=[C.maskpad, wcol], writes=[WcvPad], join=True)
    vacc = fw.sbuf("vacc", [64, 256], F32)
    kvin = [fw.sbuf(f"kvin{i}", [128, 1536], F32) for i in range(2)]
    kvb = [fw.sbuf(f"kvb{i}", [128, 1536], BF16) for i in range(2)]
    rtab = [fw.sbuf(f"rtab{i}", [128, 4, 64], F32) for i in range(2)]
    rtmp = [fw.sbuf(f"rtmp{i}", [128, 384], F32) for i in range(4)]
    for t in range(NT):
        kv = kvin[t % 2]
        kb = kvb[t % 2]
        rt = rtab[t % 2]
        fw.dma(fw.sp, lambda h: h.dma_start(out=kv[:, :], in_=C.Htok[t * 128:(t + 1) * 128, 1536:3072]), reads=[C.Htok], writes=[kv])
        fw.dma(fw.sp, lambda h: h.dma_start(out=rt[:, :, :], in_=C.cd["rope_p"][t * 128:(t + 1) * 128, :, :]), reads=[C.cd["rope_p"]], writes=[rt])
        kview = kv[:, :].rearrange("p (a v g h d) -> p a v g h d", a=3, v=2, g=2, h=2, d=64)
        x1 = kview[:, :, 0, :, 0, :]
        x2 = kview[:, :, 0, :, 1, :]
        tv = [r[:, :].rearrange("p (a g d) -> p a g d", a=3, g=2) for r in rtmp]
        emit_rope(fw, x1, x2, rt[:, 0, :], rt[:, 1, :], tv, [128, 3, 2, 64], [rt], kv, rtmp)
        for a in range(6):
            fw.dma(fw.sp, lambda h: h.dma_start(out=C.out_kv[l, a, t * 128:(t + 1) * 128, :], in_=kv[:, a * 256:(a + 1) * 256]),
                   reads=[kv], writes=[C.out_kv], join=True)
        fw.op(fw.act, lambda e: e.copy(out=kb[:, :], in_=kv[:, :]), reads=[kv], writes=[kb])
        psb = C.psb[t % 2]
        for i, (a, g) in enumerate([(1, 0), (1, 1), (2, 0), (2, 1)]):
            c0 = a * 512 + g * 128
            fw.op(fw.pe, lambda e: e.transpose(out=psb[:, i * 128:(i + 1) * 128], in_=kb[:, c0:c0 + 128], identity=C.identb[:, :]),
                  reads=[kb, C.identb], writes=[psb], join=(i > 0))
        fw.op(fw.dve, lambda e: e.tensor_copy(out=C.kT[:, :, t * 128:(t + 1) * 128], in_=psb[:, 0:512].rearrange("p (i q) -> p i q", i=4)),
              reads=[psb], writes=[C.kT], join=True)
        ps = C.ps[t % 4]
        for g in range(2):
            fw.op(fw.pe, lambda e: e.matmul(ps[:, g * 4:(g + 1) * 4], lhsT=kb[:, g * 128:(g + 1) * 128], rhs=Wck[:, g, :], start=True, stop=True),
                  reads=[kb, Wck], writes=[ps], join=(g > 0))
        for g in range(2):
            fw.op(fw.pe, lambda e: e.matmul(ps[0:64, 128 + g * 128:256 + g * 128], lhsT=WcvPad[:, g, t * 64:(t + 1) * 64], rhs=kb[:, 256 + g * 128:384 + g * 128], start=True, stop=True),
                  reads=[kb, WcvPad], writes=[ps], join=True)
        fw.op(fw.dve, lambda e: e.tensor_copy(out=C.kcmpT[:, :, t * 4:(t + 1) * 4], in_=ps[:, 0:8].rearrange("p (g n) -> p g n", g=2)),
              reads=[ps], writes=[C.kcmpT], join=True)
        if t == 0:
            fw.op(fw.dve, lambda e: e.tensor_copy(out=vacc[:, :], in_=ps[0:64, 128:384]), reads=[ps], writes=[vacc])
        else:
            fw.op(fw.dve, lambda e: e.tensor_tensor(out=vacc[:, :], in0=vacc[:, :], in1=ps[0:64, 128:384], op=ALU.add), reads=[ps, vacc], writes=[vacc])
        vsrc = kb[:, 512:1536].rearrange("p (a v c) -> p a v c", a=2, v=2)[:, :, 1, :]
        fw.op(fw.pool, lambda e: e.tensor_copy(out=C.V[:, t, :, :], in_=vsrc), reads=[kb], writes=[C.V], join=True)
    fw.op(fw.dve, lambda e: e.tensor_copy(out=C.vcmp[:, :], in_=vacc[:, :]), reads=[vacc], writes=[C.vcmp])


def emit_ln_partition(fw, C, y, blk_cols, g_ap, b_ap, gb_bufs, out_ap_fn, tmpA, tmpB, func=None):
    ncols = blk_cols
    for c0 in range(0, ncols, 512):
        cw = min(512, ncols - c0)
        ps1 = C.ps[0]
        ps2 = C.ps[1]
        ysl = y[:, c0:c0 + cw]
        fw.op(fw.pe, lambda e: e.matmul(ps1[:, 0:cw], lhsT=C.onesdiv[:, :], rhs=ysl, start=True, stop=True), reads=[C.onesdiv, y], writes=[ps1])
        fw.op(fw.dve, lambda e: e.tensor_tensor(out=ysl, in0=ysl, in1=ps1[:, 0:cw], op=ALU.subtract), reads=[ps1, y], writes=[y])
        fw.op(fw.act, lambda e: e.activation(out=tmpA[:, 0:cw], in_=ysl, func=AF.Square), reads=[y], writes=[tmpA])
        fw.op(fw.pe, lambda e: e.matmul(ps2[:, 0:cw], lhsT=C.onesdiv[:, :], rhs=tmpA[:, 0:cw], start=True, stop=True), reads=[C.onesdiv, tmpA], writes=[ps2])
        fw.op(fw.act, lambda e: e.activation(out=tmpB[:, 0:cw], in_=ps2[:, 0:cw], func=AF.Sqrt, bias=C.epsc[:, 0:1], scale=1.0), reads=[ps2, C.epsc], writes=[tmpB])
        fw.op(fw.dve, lambda e: e.reciprocal(out=tmpB[:, 0:cw], in_=tmpB[:, 0:cw]), reads=[tmpB], writes=[tmpB])
        fw.op(fw.dve, lambda e: e.tensor_tensor(out=tmpA[:, 0:cw], in0=ysl, in1=tmpB[:, 0:cw], op=ALU.mult), reads=[y, tmpB], writes=[tmpA])
        oap, obuf = out_ap_fn(c0, cw)
        if func is None:
            fw.op(fw.dve, lambda e: e.tensor_scalar(out=oap, in0=tmpA[:, 0:cw], scalar1=g_ap, scalar2=b_ap, op0=ALU.mult, op1=ALU.add),
                  reads=[tmpA] + gb_bufs, writes=[obuf], join=True)
        else:
            fw.op(fw.dve, lambda e: e.tensor_scalar(out=tmpA[:, 0:cw], in0=tmpA[:, 0:cw], scalar1=g_ap, scalar2=b_ap, op0=ALU.mult, op1=ALU.add),
                  reads=[tmpA] + gb_bufs, writes=[tmpA])
            fw.op(fw.act, lambda e: e.activation(out=oap, in_=tmpA[:, 0:cw], func=func), reads=[tmpA], writes=[obuf], join=True)


def phase_conv(fw, C, l, HfT, T, hist_src, out_conv, convT):
    fw.push()
    dwr = fw.sbuf("dwr", [31, 512], F32)
    fw.dma(fw.sp, lambda h: h.dma_start(out=dwr[:, :], in_=C.conv_dw[l, :, :]), reads=[C.conv_dw], writes=[dwr])
    dwT = fw.sbuf("dwT", [128, 4, 31], F32)
    ps = C.ps[2]
    for cc in range(4):
        fw.op(fw.pe, lambda e: e.transpose(out=ps[:, cc * 32:cc * 32 + 31], in_=dwr[0:31, cc * 128:(cc + 1) * 128], identity=C.ident[0:31, 0:31]),
              reads=[dwr, C.ident], writes=[ps], join=(cc > 0))
    fw.op(fw.dve, lambda e: e.tensor_copy(out=dwT[:, :, :], in_=ps[:, 0:128].rearrange("p (c k) -> p c k", c=4)[:, :, 0:31]), reads=[ps], writes=[dwT])
    dbc = load_colvec(fw, C, C.conv_db[l, :], 4, "dbc", C.conv_db)
    lng = load_colvec(fw, C, C.conv_ln_g[l, :], 4, "clng", C.conv_ln_g)
    lnb = load_colvec(fw, C, C.conv_ln_b[l, :], 4, "clnb", C.conv_ln_b)
    pw = fw.sbuf("pw", [128, 4, 512], BF16)
    fw.dma(fw.pool, lambda h: h.dma_start(out=pw[:, :, :], in_=C.conv_pw[l, :, :].rearrange("(c p) n -> p c n", p=128)), reads=[C.conv_pw], writes=[pw])
    zT = fw.sbuf("zT", [128, 4, T], BF16)
    pcs = fw.sbuf("pcs", [30, 512], F32)
    psc = C.ps[3]
    sets = []
    for i in range(2):
        sets.append(dict(ca=fw.sbuf(f"cca{i}", [128, T], F32), cb=fw.sbuf(f"ccb{i}", [128, T], F32),
                         cseq=fw.sbuf(f"cseq{i}", [128, 30 + T], F32), y=fw.sbuf(f"cy{i}", [128, T], F32)))
    tmpA = fw.sbuf("ctmpA", [128, 512], F32)
    tmpB = fw.sbuf("ctmpB", [128, 512], F32)
    hs = None
    if hist_src is not None:
        hs = fw.sbuf("hist_r", [30, 512], F32)
        fw.dma(fw.sp, lambda h: h.dma_start(out=hs[:, :], in_=hist_src), reads=[C.state_conv], writes=[hs])
    for cc in range(4):
        s_ = sets[cc % 2]
        ca, cb, cseq, y = s_["ca"], s_["cb"], s_["cseq"], s_["y"]
        fw.dma(fw.sp, lambda h: h.dma_start(out=ca[:, :], in_=HfT[512 + cc * 128:640 + cc * 128, 0:T]), reads=[HfT], writes=[ca])
        fw.dma(fw.sp, lambda h: h.dma_start(out=cb[:, :], in_=HfT[1024 + cc * 128:1152 + cc * 128, 0:T]), reads=[HfT], writes=[cb])
        fw.op(fw.act, lambda e: e.activation(out=cb[:, :], in_=cb[:, :], func=AF.Sigmoid), reads=[cb], writes=[cb])
        if hs is None:
            fw.op(fw.pool, lambda e: e.memset(cseq[:, 0:30], 0.0), writes=[cseq])
        else:
            pst = C.ps[0]
            fw.op(fw.pe, lambda e: e.transpose(out=pst[:, 0:30], in_=hs[0:30, cc * 128:(cc + 1) * 128], identity=C.ident[0:30, 0:30]), reads=[hs, C.ident], writes=[pst])
            fw.op(fw.dve, lambda e: e.tensor_copy(out=cseq[:, 0:30], in_=pst[:, 0:30]), reads=[pst], writes=[cseq])
        fw.op(fw.dve, lambda e: e.tensor_tensor(out=cseq[:, 30:30 + T], in0=ca[:, :], in1=cb[:, :], op=ALU.mult), reads=[ca, cb], writes=[cseq], join=True)
        fw.op(fw.pe, lambda e: e.transpose(out=psc[0:30, cc * 128:(cc + 1) * 128], in_=cseq[:, T:T + 30], identity=C.ident[:, :]),
              reads=[cseq, C.ident], writes=[psc], join=(cc > 0))
        eng = fw.dve
        fw.op(eng, lambda e: e.tensor_scalar(out=y[:, :], in0=cseq[:, 0:T], scalar1=dwT[:, cc, 0:1], scalar2=dbc[:, cc:cc + 1], op0=ALU.mult, op1=ALU.add),
              reads=[cseq, dwT, dbc], writes=[y])
        for k in range(1, 31):
            fw.op(eng, lambda e: e.scalar_tensor_tensor(out=y[:, :], in0=cseq[:, k:k + T], scalar=dwT[:, cc, k:k + 1], in1=y[:, :], op0=ALU.mult, op1=ALU.add),
                  reads=[cseq, dwT, y], writes=[y])
        emit_ln_partition(fw, C, y, T, lng[:, cc:cc + 1], lnb[:, cc:cc + 1], [lng, lnb],
                          lambda c0, cw: (zT[:, cc, c0:c0 + cw], zT), tmpA, tmpB, func=AF.Silu)
    fw.op(fw.dve, lambda e: e.tensor_copy(out=pcs[:, :], in_=psc[0:30, 0:512]), reads=[psc], writes=[pcs])
    fw.dma(fw.sp, lambda h: h.dma_start(out=out_conv, in_=pcs[:, :]), reads=[pcs], writes=[C.out_conv_buf])
    it = 0
    for co in range(4):
        for c0 in range(0, T, 512):
            cw = min(512, T - c0)
            ps = C.ps[it % 4]
            for cc in range(4):
                fw.op(fw.pe, lambda e: e.matmul(ps[:, 0:cw], lhsT=pw[:, cc, co * 128:(co + 1) * 128], rhs=zT[:, cc, c0:c0 + cw], start=(cc == 0), stop=(cc == 3)),
                      reads=[pw, zT], writes=[ps], join=(cc > 0))
            if it % 2 == 0:
                fw.op(fw.dve, lambda e: e.tensor_copy(out=convT[:, co, c0:c0 + cw], in_=ps[:, 0:cw]), reads=[ps], writes=[convT], join=True)
            else:
                fw.op(fw.act, lambda e: e.copy(out=convT[:, co, c0:c0 + cw], in_=ps[:, 0:cw]), reads=[ps], writes=[convT], join=True)
            it += 1
    fw.pop()


def emit_ln_free(fw, C, x, P, g_bc, b_bc, junk, stat):
    xs = x[0:P, :]
    fw.op(fw.dve, lambda e: e.tensor_reduce(out=stat[0:P, 0:1], in_=xs, axis=AX.X, op=ALU.add), reads=[x], writes=[stat])
    fw.op(fw.dve, lambda e: e.tensor_scalar(out=stat[0:P, 1:2], in0=stat[0:P, 0:1], scalar1=1.0 / D, scalar2=None, op0=ALU.mult), reads=[stat], writes=[stat])
    fw.op(fw.dve, lambda e: e.tensor_scalar(out=xs, in0=xs, scalar1=stat[0:P, 1:2], scalar2=None, op0=ALU.subtract), reads=[x, stat], writes=[x])
    fw.op(fw.act, lambda e: e.activation(out=junk[0:P, :], in_=xs, func=AF.Square, accum_out=stat[0:P, 2:3]), reads=[x], writes=[junk, stat])
    fw.op(fw.act, lambda e: e.activation(out=stat[0:P, 3:4], in_=stat[0:P, 2:3], func=AF.Sqrt, bias=C.epsc[0:P, 0:1], scale=1.0 / D), reads=[stat, C.epsc], writes=[stat])
    fw.op(fw.dve, lambda e: e.reciprocal(out=stat[0:P, 4:5], in_=stat[0:P, 3:4]), reads=[stat], writes=[stat])
    fw.op(fw.dve, lambda e: e.scalar_tensor_tensor(out=xs, in0=xs, scalar=stat[0:P, 4:5], in1=g_bc[0:P, :], op0=ALU.mult, op1=ALU.mult), reads=[x, stat, g_bc], writes=[x])
    fw.op(fw.pool, lambda e: e.tensor_tensor(out=xs, in0=xs, in1=b_bc[0:P, :], op=ALU.add), reads=[x, b_bc], writes=[x])


def load_bcast(fw, name, vec_ap, n, srcbuf, P=128):
    b = fw.sbuf(name, [128, n], F32)
    fw.dma(fw.sp, lambda h: h.dma_start(out=b[0:P, :], in_=vec_ap.partition_broadcast(P)), reads=[srcbuf], writes=[b])
    return b


def emit_gmlp(fw, C, G, l, t, P, Htok, HfT, mixg, out_gm_ap):
    gv = G["gvin"][t % 2]
    gt = G["gtmp"]
    st = G["gstat"]
    fw.dma(fw.sp, lambda h: h.dma_start(out=gv[0:P, :], in_=Htok[t * 128:t * 128 + P, 0:512]), reads=[Htok], writes=[gv])
    emit_gelu(fw, fw.dve, fw.act, gv[0:P, :], gv[0:P, :], gt[0:P, :], [gv], [gv], gt)
    g3 = gv[0:P, :].rearrange("p (h c) -> p h c", h=4)
    t3 = gt[0:P, :].rearrange("p (h c) -> p h c", h=4)
    fw.op(fw.dve, lambda e: e.tensor_reduce(out=st[0:P, 0:4], in_=g3, axis=AX.X, op=ALU.add), reads=[gv], writes=[st])
    fw.op(fw.dve, lambda e: e.tensor_scalar(out=st[0:P, 4:8], in0=st[0:P, 0:4], scalar1=1.0 / 128, scalar2=None, op0=ALU.mult), reads=[st], writes=[st])
    fw.op(fw.dve, lambda e: e.tensor_tensor(out=g3, in0=g3, in1=st[0:P, 4:8].unsqueeze(2).to_broadcast([P, 4, 128]), op=ALU.subtract), reads=[gv, st], writes=[gv])
    fw.op(fw.dve, lambda e: e.tensor_tensor(out=t3, in0=g3, in1=g3, op=ALU.mult), reads=[gv], writes=[gt])
    fw.op(fw.dve, lambda e: e.tensor_reduce(out=st[0:P, 8:12], in_=t3, axis=AX.X, op=ALU.add), reads=[gt], writes=[st])
    fw.op(fw.act, lambda e: e.activation(out=st[0:P, 12:16], in_=st[0:P, 8:12], func=AF.Sqrt, bias=C.epsc[0:P, 0:1], scale=1.0 / 128), reads=[st, C.epsc], writes=[st])
    fw.op(fw.dve, lambda e: e.reciprocal(out=st[0:P, 16:20], in_=st[0:P, 12:16]), reads=[st], writes=[st])
    fw.op(fw.dve, lambda e: e.tensor_tensor(out=g3, in0=g3, in1=st[0:P, 16:20].unsqueeze(2).to_broadcast([P, 4, 128]), op=ALU.mult), reads=[gv, st], writes=[gv])
    fw.op(fw.dve, lambda e: e.tensor_tensor(out=gv[0:P, :], in0=gv[0:P, :], in1=G["gmg"][0:P, :], op=ALU.mult), reads=[gv, G["gmg"]], writes=[gv])
    fw.op(fw.pool, lambda e: e.tensor_tensor(out=gv[0:P, :], in0=gv[0:P, :], in1=G["gmb"][0:P, :], op=ALU.add), reads=[gv, G["gmb"]], writes=[gv])
    if out_gm_ap is not None:
        fw.dma(fw.sp, lambda h: h.dma_start(out=out_gm_ap, in_=gv[0:P, :]), reads=[gv], writes=[C.out_gm_buf], join=True)
    vnb = G["vnb"]
    fw.op(fw.act, lambda e: e.copy(out=vnb[0:P, :], in_=gv[0:P, :]), reads=[gv], writes=[vnb])
    ps = C.ps[0]
    for h in range(4):
        fw.op(fw.pe, lambda e: e.matmul(ps[:, h * 128:h * 128 + P], lhsT=vnb[0:P, h * 128:(h + 1) * 128], rhs=G["trilWT"][0:P, h, 0:P], start=True, stop=True),
              reads=[vnb, G["trilWT"]], writes=[ps], join=(h > 0))
    ut = G["ut"][t % 2]
    fw.dma(fw.sp, lambda h_: h_.dma_start(out=ut[:, :, 0:P], in_=HfT[0:512, t * 128:t * 128 + P].rearrange("(h c) q -> c h q", h=4)), reads=[HfT], writes=[ut])
    sv = gt[:, :].rearrange("p (h c) -> p h c", h=4)[:, :, 0:P]
    fw.op(fw.dve, lambda e: e.tensor_tensor(out=sv, in0=ps[:, 0:512].rearrange("p (h c) -> p h c", h=4)[:, :, 0:P], in1=G["bsb"][:, :, 0:P], op=ALU.add),
          reads=[ps, G["bsb"]], writes=[gt])
    fw.op(fw.dve, lambda e: e.tensor_tensor(out=mixg[:, :, 0:P], in0=sv, in1=ut[:, :, 0:P], op=ALU.mult), reads=[gt, ut], writes=[mixg])


def gmlp_consts(fw, C, l, P):
    G = {}
    G["gvin"] = [fw.sbuf("gvin0", [128, 512], F32)] * 2
    G["gtmp"] = fw.sbuf("gtmp", [128, 512], F32)
    G["gstat"] = fw.sbuf("gstat", [128, 20], F32)
    G["vnb"] = fw.sbuf("vnb", [128, 512], BF16)
    G["ut"] = [fw.sbuf("ut0", [128, 4, 128], F32)] * 2
    G["gmg"] = load_bcast(fw, "gmg", C.gm_ln_g[l, :], 512, C.gm_ln_g, P)
    G["gmb"] = load_bcast(fw, "gmb", C.gm_ln_b[l, :], 512, C.gm_ln_b, P)
    bsb = fw.sbuf("bsb", [128, 4, 128], F32)
    fw.dma(fw.sp, lambda h: h.dma_start(out=bsb[:, :, :].rearrange("p h i -> p (h i)"), in_=C.gm_bs[l, :, :].rearrange("h i -> (h i)").partition_broadcast(128)),
           reads=[C.gm_bs], writes=[bsb])
    G["bsb"] = bsb
    wsr = fw.sbuf("wsr", [128, 4, 128], F32)
    fw.dma(fw.sp, lambda h: h.dma_start(out=wsr[:, :, :], in_=C.gm_ws[l, :, :, :].rearrange("h i j -> i h j")), reads=[C.gm_ws], writes=[wsr])
    ps = C.ps[1]
    for h in range(4):
        fw.op(fw.pe, lambda e: e.transpose(out=ps[:, h * 128:(h + 1) * 128], in_=wsr[:, h, :], identity=C.ident[:, :]), reads=[wsr, C.ident], writes=[ps], join=(h > 0))
    tw = fw.sbuf("trilWT", [128, 4, 128], BF16)
    fw.op(fw.dve, lambda e: e.tensor_tensor(out=tw[:, :, :], in0=ps[:, 0:512].rearrange("p (h i) -> p h i", h=4), in1=C.triu[:, :].unsqueeze(1).to_broadcast([128, 4, 128]), op=ALU.mult),
          reads=[ps, C.triu], writes=[tw])
    G["trilWT"] = tw
    return G


def emit_softmax_pv(fw, C, A, sbuf_s, nk, gate_ap, gate_buf, acc, v_fn, first, last, tagi):
    st = A["st"]
    eb = A["eb"]
    pT = A["pT"]
    fw.op(fw.dve, lambda e: e.tensor_reduce(out=st[:, 0:1], in_=sbuf_s[:, 0:nk], axis=AX.X, op=ALU.max, negate=True), reads=[sbuf_s], writes=[st])
    fw.op(fw.act, lambda e: e.activation(out=eb[:, 0:nk], in_=sbuf_s[:, 0:nk], func=AF.Exp, bias=st[:, 0:1], scale=1.0, accum_out=st[:, 1:2]),
          reads=[sbuf_s, st], writes=[eb, st])
    fw.op(fw.dve, lambda e: e.reciprocal(out=st[:, 2:3], in_=st[:, 1:2]), reads=[st], writes=[st])
    fw.op(fw.dve, lambda e: e.tensor_tensor(out=st[:, 3:4], in0=st[:, 2:3], in1=gate_ap, op=ALU.mult), reads=[st, gate_buf], writes=[st])
    fw.op(fw.pool, lambda e: e.tensor_scalar(out=eb[:, 0:nk], in0=eb[:, 0:nk], scalar1=st[:, 3:4], scalar2=None, op0=ALU.mult), reads=[eb, st], writes=[eb])
    nkt = nk // 128
    for k0 in range(0, nkt, 8):
        kn = min(8, nkt - k0)
        psb = C.psb[A["psbi"] % 2]
        A["psbi"] += 1
        for j in range(kn):
            kt = k0 + j
            fw.op(fw.pe, lambda e: e.transpose(out=psb[:, j * 128:(j + 1) * 128], in_=eb[:, kt * 128:(kt + 1) * 128], identity=C.identb[:, :]),
                  reads=[eb, C.identb], writes=[psb], join=(j > 0))
        if (k0 // 8) % 2 == 0:
            fw.op(fw.dve, lambda e: e.tensor_copy(out=pT[:, k0 * 128:(k0 + kn) * 128], in_=psb[:, 0:kn * 128]), reads=[psb], writes=[pT], join=True)
        else:
            fw.op(fw.act, lambda e: e.copy(out=pT[:, k0 * 128:(k0 + kn) * 128], in_=psb[:, 0:kn * 128]), reads=[psb], writes=[pT], join=True)
    for kt in range(nkt):
        fw.op(fw.pe, lambda e: e.matmul(acc[:, 0:128], lhsT=v_fn(kt), rhs=pT[:, kt * 128:(kt + 1) * 128], start=(first and kt == 0), stop=(last and kt == nkt - 1)),
              reads=[C.V, pT], writes=[acc], join=not (first and kt == 0))


def emit_attn_prompt(fw, C, A, l, t, mixn):
    qt = A["qt"][t % 2]
    gt = A["gate"][t % 2]
    rt = A["rt"][t % 2]
    fw.dma(fw.sp, lambda h: h.dma_start(out=qt[:, :], in_=C.Htok[t * 128:(t + 1) * 128, 512:1536]), reads=[C.Htok], writes=[qt])
    fw.dma(fw.sp, lambda h: h.dma_start(out=gt[:, :], in_=C.Htok[t * 128:(t + 1) * 128, 3072:3096]), reads=[C.Htok], writes=[gt])
    fw.dma(fw.sp, lambda h: h.dma_start(out=rt[:, :, :], in_=C.cd["rope_p"][t * 128:(t + 1) * 128, :, :]), reads=[C.cd["rope_p"]], writes=[rt])
    qv = qt[:, :].rearrange("p (h x d) -> p h x d", h=8, x=2)
    tv = [A["ssb"][:, i * 512:(i + 1) * 512].rearrange("p (h d) -> p h d", h=8) for i in range(4)]
    emit_rope(fw, qv[:, :, 0, :], qv[:, :, 1, :], rt[:, 2, :], rt[:, 3, :], tv, [128, 8, 64], [rt], qt, A["rtmp"])
    qb = A["qb"]
    fw.op(fw.act, lambda e: e.copy(out=qb[:, :], in_=qt[:, :]), reads=[qt], writes=[qb])
    psb = C.psb[A["psbi"] % 2]
    A["psbi"] += 1
    for h in range(8):
        fw.op(fw.pe, lambda e: e.transpose(out=psb[:, h * 128:(h + 1) * 128], in_=qb[:, h * 128:(h + 1) * 128], identity=C.identb[:, :]),
              reads=[qb, C.identb], writes=[psb], join=(h > 0))
    qT = A["qT"]
    fw.op(fw.dve, lambda e: e.tensor_copy(out=qT[:, :], in_=psb[:, :]), reads=[psb], writes=[qT])
    sig = A["sig"]
    fw.op(fw.act, lambda e: e.activation(out=sig[:, :], in_=gt[:, :], func=AF.Sigmoid), reads=[gt], writes=[sig])
    sig3 = sig[:, :].rearrange("p (h b) -> p h b", b=3)
    sc, ec, st4 = A["sc"], A["ec"], A["st4"]
    for g in range(2):
        psc = C.ps[0]
        for r in range(4):
            fw.op(fw.pe, lambda e: e.matmul(psc[:, r * 64:(r + 1) * 64], lhsT=qT[:, (g * 4 + r) * 128:(g * 4 + r + 1) * 128], rhs=C.kcmpT[:, g, :], start=True, stop=True),
                  reads=[qT, C.kcmpT], writes=[psc], join=(r > 0))
        sc3 = sc[:, :].rearrange("p (r n) -> p r n", r=4)
        ec3 = ec[:, :].rearrange("p (r n) -> p r n", r=4)
        fw.op(fw.dve, lambda e: e.tensor_tensor(out=sc3, in0=psc[:, 0:256].rearrange("p (r n) -> p r n", r=4),
                                                in1=C.cmpbias[:, t, :].unsqueeze(1).to_broadcast([128, 4, 64]), op=ALU.add), reads=[psc, C.cmpbias], writes=[sc])
        fw.op(fw.dve, lambda e: e.tensor_reduce(out=st4[:, 0:4], in_=sc3, axis=AX.X, op=ALU.max), reads=[sc], writes=[st4])
        fw.op(fw.dve, lambda e: e.tensor_tensor(out=sc3, in0=sc3, in1=st4[:, 0:4].unsqueeze(2).to_broadcast([128, 4, 64]), op=ALU.subtract), reads=[sc, st4], writes=[sc])
        fw.op(fw.act, lambda e: e.activation(out=ec[:, :], in_=sc[:, :], func=AF.Exp), reads=[sc], writes=[ec])
        fw.op(fw.dve, lambda e: e.tensor_reduce(out=st4[:, 4:8], in_=ec3, axis=AX.X, op=ALU.add), reads=[ec], writes=[st4])
        fw.op(fw.dve, lambda e: e.tensor_scalar(out=st4[:, 4:8], in0=st4[:, 4:8], scalar1=1e-30, scalar2=None, op0=ALU.max), reads=[st4], writes=[st4])
        fw.op(fw.dve, lambda e: e.reciprocal(out=st4[:, 8:12], in_=st4[:, 4:8]), reads=[st4], writes=[st4])
        fw.op(fw.dve, lambda e: e.tensor_scalar(out=st4[:, 8:12], in0=st4[:, 8:12], scalar1=C.cmpvalid[:, t:t + 1], scalar2=None, op0=ALU.mult), reads=[st4, C.cmpvalid], writes=[st4])
        fw.op(fw.dve, lambda e: e.tensor_tensor(out=ec3, in0=ec3, in1=st4[:, 8:12].unsqueeze(2).to_broadcast([128, 4, 64]), op=ALU.mult), reads=[ec, st4], writes=[ec])
        imp = A["imp"]
        fw.op(fw.dve, lambda e: e.tensor_reduce(out=imp[:, 0:64], in_=ec[:, :].rearrange("p (r n) -> p n r", r=4), axis=AX.X, op=ALU.add), reads=[ec], writes=[imp])
        fw.op(fw.dve, lambda e: e.tensor_reduce(out=imp[:, 64:96], in_=imp[:, 0:64].rearrange("p (m two) -> p m two", two=2), axis=AX.X, op=ALU.add), reads=[imp], writes=[imp])
        fw.op(fw.dve, lambda e: e.tensor_tensor(out=imp[:, 64:96], in0=imp[:, 64:96], in1=C.selA[:, t, :], op=ALU.mult), reads=[imp, C.selA], writes=[imp])
        fw.op(fw.dve, lambda e: e.tensor_tensor(out=imp[:, 64:96], in0=imp[:, 64:96], in1=C.selB[:, t, :], op=ALU.add), reads=[imp, C.selB], writes=[imp])
        m8 = A["m8"]
        fw.op(fw.dve, lambda e: e.max(out=m8[:, 0:8], in_=imp[:, 64:96]), reads=[imp], writes=[m8])
        fw.op(fw.dve, lambda e: e.match_replace(out=imp[:, 96:128], in_to_replace=m8[:, 0:8], in_values=imp[:, 64:96], imm_value=-1e30), reads=[imp, m8], writes=[imp])
        fw.op(fw.dve, lambda e: e.max(out=m8[:, 8:16], in_=imp[:, 96:128]), reads=[imp], writes=[m8])
        bsel = A["bsel"]
        fw.op(fw.dve, lambda e: e.tensor_scalar(out=bsel[:, :], in0=imp[:, 64:96], scalar1=m8[:, 15:16], scalar2=None, op0=ALU.is_ge), reads=[imp, m8], writes=[bsel])
        fw.op(fw.dve, lambda e: e.tensor_scalar(out=bsel[:, :], in0=bsel[:, :], scalar1=-NEG, scalar2=NEG, op0=ALU.mult, op1=ALU.add), reads=[bsel], writes=[bsel])
        pcb = A["pcb"]
        fw.op(fw.dve, lambda e: e.tensor_tensor(out=pcb[:, :].rearrange("p (r n) -> p r n", r=4), in0=ec3,
                                                in1=sig3[:, g * 4:(g + 1) * 4, 0:1].to_broadcast([128, 4, 64]), op=ALU.mult), reads=[ec, sig], writes=[pcb])
        psb = C.psb[A["psbi"] % 2]
        A["psbi"] += 1
        for r in range(4):
            fw.op(fw.pe, lambda e: e.transpose(out=psb[0:64, r * 128:(r + 1) * 128], in_=pcb[:, r * 64:(r + 1) * 64], identity=C.identb[:, :]),
                  reads=[pcb, C.identb], writes=[psb], join=(r > 0))
        pTc = A["pTc"]
        fw.op(fw.act, lambda e: e.copy(out=pTc[0:64, :], in_=psb[0:64, 0:512]), reads=[psb], writes=[pTc])
        for r in range(4):
            h = g * 4 + r
            acc = C.pacc[h % 2]
            fw.op(fw.pe, lambda e: e.matmul(acc[:, 0:128], lhsT=C.vcmp[0:64, g * 128:(g + 1) * 128], rhs=pTc[0:64, r * 128:(r + 1) * 128], start=True, stop=False),
                  reads=[C.vcmp, pTc], writes=[acc])
            nk = (t + 1) * 128
            ssb = A["ssb"]
            for ci, c0 in enumerate(range(0, nk, 512)):
                cw = min(512, nk - c0)
                ps = C.ps[1 + (ci % 3)]
                fw.op(fw.pe, lambda e: e.matmul(ps[:, 0:cw], lhsT=qT[:, h * 128:(h + 1) * 128], rhs=C.kT[:, g, c0:c0 + cw], start=True, stop=True),
                      reads=[qT, C.kT], writes=[ps])
                nb = cw // 64
                fw.op(fw.dve, lambda e: e.tensor_tensor(out=ssb[:, c0:c0 + cw].rearrange("p (m j) -> p m j", j=64), in0=ps[:, 0:cw].rearrange("p (m j) -> p m j", j=64),
                                                        in1=bsel[:, c0 // 64:c0 // 64 + nb].unsqueeze(2).to_broadcast([128, nb, 64]), op=ALU.add),
                      reads=[ps, bsel], writes=[ssb], join=True)
            fw.op(fw.pool, lambda e: e.tensor_tensor(out=ssb[:, t * 128:(t + 1) * 128], in0=ssb[:, t * 128:(t + 1) * 128], in1=C.causal[:, :], op=ALU.add),
                  reads=[ssb, C.causal], writes=[ssb])
            emit_softmax_pv(fw, C, A, ssb, nk, sig3[:, h, 1:2], sig, acc, lambda kt: C.V[:, kt, 0, g * 128:(g + 1) * 128], False, False, 0)
            kt0 = max(0, t - 4)
            nkw = (t - kt0 + 1) * 128
            swb = A["swb"]
            for ci, c0 in enumerate(range(0, nkw, 512)):
                cw = min(512, nkw - c0)
                ps = C.ps[1 + (ci % 3)]
                fw.op(fw.pe, lambda e: e.matmul(ps[:, 0:cw], lhsT=qT[:, h * 128:(h + 1) * 128], rhs=C.kT[:, 2 + g, kt0 * 128 + c0:kt0 * 128 + c0 + cw], start=True, stop=True),
                      reads=[qT, C.kT], writes=[ps])
                fw.op(fw.act, lambda e: e.copy(out=swb[:, c0:c0 + cw], in_=ps[:, 0:cw]), reads=[ps], writes=[swb], join=True)
            fw.op(fw.pool, lambda e: e.tensor_tensor(out=swb[:, nkw - 128:nkw], in0=swb[:, nkw - 128:nkw], in1=C.causal[:, :], op=ALU.add), reads=[swb, C.causal], writes=[swb])
            if t >= 4:
                fw.op(fw.pool, lambda e: e.tensor_tensor(out=swb[:, 0:128], in0=swb[:, 0:128], in1=C.acausal[:, :], op=ALU.add), reads=[swb, C.acausal], writes=[swb])
            emit_softmax_pv(fw, C, A, swb, nkw, sig3[:, h, 2:3], sig, acc, lambda kt: C.V[:, kt0 + kt, 1, g * 128:(g + 1) * 128], False, True, 1)
            if h % 2 == 0:
                fw.op(fw.dve, lambda e: e.tensor_copy(out=mixn[:, h, :], in_=acc[:, 0:128]), reads=[acc], writes=[mixn], join=True)
            else:
                fw.op(fw.act, lambda e: e.copy(out=mixn[:, h, :], in_=acc[:, 0:128]), reads=[acc], writes=[mixn], join=True)


def attn_bufs(fw):
    A = {"psbi": 0}
    A["qt"] = [fw.sbuf("qt0", [128, 1024], F32)] * 2
    A["gate"] = [fw.sbuf(f"gate{i}", [128, 24], F32) for i in range(2)]
    A["rt"] = [fw.sbuf(f"art{i}", [128, 4, 64], F32) for i in range(2)]

    A["qb"] = fw.sbuf("qb", [128, 1024], BF16)
    A["qT"] = fw.sbuf("qT", [128, 1024], BF16)
    A["sig"] = fw.sbuf("sig", [128, 24], F32)
    A["sc"] = fw.sbuf("sc", [128, 256], F32)
    A["ec"] = fw.sbuf("ec", [128, 256], F32)
    A["st4"] = fw.sbuf("st4", [128, 12], F32)
    A["st"] = fw.sbuf("ast", [128, 4], F32)
    A["imp"] = fw.sbuf("imp", [128, 128], F32)
    A["m8"] = fw.sbuf("m8", [128, 16], F32)
    A["bsel"] = fw.sbuf("bsel", [128, 32], F32)
    A["pcb"] = fw.sbuf("pcb", [128, 256], BF16)
    A["pTc"] = fw.sbuf("pTc", [128, 512], BF16)
    A["ssb"] = fw.sbuf("ssb", [128, S], F32)
    A["rtmp"] = [A["ssb"]] * 4
    A["swb"] = fw.sbuf("swb", [128, 640], F32)
    A["eb"] = fw.sbuf("eb", [128, S], BF16)
    A["pT"] = fw.sbuf("pT", [128, S], BF16)
    return A


def emit_mix_ln(fw, C, M, l, t, P, x_src, mixg, mixn, convT, X1):
    xt = M["xres"][0]
    fw.dma(fw.sp, lambda h: h.dma_start(out=xt[0:P, :], in_=x_src[t * 128:t * 128 + P, :]), reads=[x_src], writes=[xt])
    x1 = xt
    for n4 in range(4):
        ps = C.ps[n4]
        for c in range(16):
            if c < 4:
                lt, lb = mixg[:, c, 0:P], mixg
            elif c < 12:
                lt, lb = mixn[:, c - 4, 0:P], mixn
            else:
                lt, lb = convT[:, c - 12, t * 128:t * 128 + P], convT
            fw.op(fw.pe, lambda e: e.matmul(ps[0:P, :], lhsT=lt, rhs=M["wout"][:, c, n4 * 512:(n4 + 1) * 512], start=(c == 0), stop=(c == 15)),
                  reads=[lb, M["wout"]], writes=[ps], join=(c > 0))
        fw.op(fw.dve, lambda e: e.scalar_tensor_tensor(out=x1[0:P, n4 * 512:(n4 + 1) * 512], in0=xt[0:P, n4 * 512:(n4 + 1) * 512], scalar=ALPHA, in1=ps[0:P, :], op0=ALU.mult, op1=ALU.add),
              reads=[xt, ps], writes=[x1])
    emit_ln_free(fw, C, x1, P, M["ln1g"], M["ln1b"], M["junk"], M["lnst"])
    fw.dma(fw.sp, lambda h: h.dma_start(out=X1[t * 128:t * 128 + P, :], in_=x1[0:P, :]), reads=[x1], writes=[X1], join=True)


def mix_bufs(fw, C, l, P):
    M = {}
    wout = fw.sbuf("wout", [128, 16, D], BF16)
    for c4 in range(4):
        fw.dma(fw.pool, lambda h: h.dma_start(out=wout[:, c4 * 4:(c4 + 1) * 4, :], in_=C.w_out[l, c4 * 512:(c4 + 1) * 512, :].rearrange("(c p) n -> p c n", p=128)),
               reads=[C.w_out], writes=[wout], join=True)
    M["wout"] = wout
    M["xres"] = [fw.sbuf("xres0", [128, D], F32)]
    M["x1"] = M["xres"]
    M["junk"] = None
    M["lnst"] = fw.sbuf("lnst", [128, 8], F32)
    M["ln1g"] = load_bcast(fw, "ln1g", C.ln1_g[l, :], D, C.ln1_g, P)
    M["ln1b"] = load_bcast(fw, "ln1b", C.ln1_b[l, :], D, C.ln1_b, P)
    return M


def phase_mix_prompt(fw, C, l, x_src, X1, samp=None):
    fw.push()
    for nm in ["cmpbias", "cmpvalid", "selA", "selB"]:
        setattr(C, nm, load_const(fw, C, nm))
    M = mix_bufs(fw, C, l, 128)
    G = gmlp_consts(fw, C, l, 128)
    A = attn_bufs(fw)
    M["junk"] = A["eb"]
    mixg = [fw.sbuf(f"mixg{i}", [128, 4, 128], BF16) for i in range(2)]
    mixn = [fw.sbuf(f"mixn{i}", [128, 8, 128], BF16) for i in range(2)]
    for t in range(NT):
        out_gm = C.out_gm[l, :, :] if t == NT - 1 else None
        emit_gmlp(fw, C, G, l, t, 128, C.Htok, C.HfT, mixg[t % 2], out_gm)
        emit_attn_prompt(fw, C, A, l, t, mixn[t % 2])
        emit_mix_ln(fw, C, M, l, t, 128, x_src, mixg[t % 2], mixn[t % 2], C.convT, X1)
    if samp is not None:
        xs_src, XS1, mixn_s, convTs = samp
        emit_gmlp(fw, C, G, l, 0, TS, C.HtokS, C.HfTS, mixg[0], C.out_sgm[l, :, :])
        emit_mix_ln(fw, C, M, l, 0, TS, xs_src, mixg[0], mixn_s, convTs, XS1)
    fw.pop()


INPUT_SPECS = [
    ("w_in", [DEPTH, D, IN_W]), ("gm_ln_g", [DEPTH, 512]), ("gm_ln_b", [DEPTH, 512]), ("gm_ws", [DEPTH, 4, 128, 128]), ("gm_bs", [DEPTH, 4, 128]),
    ("cmp_wk", [DEPTH, 2, 32]), ("cmp_wv", [DEPTH, 2, 32]), ("conv_dw", [DEPTH, 31, 512]), ("conv_db", [DEPTH, 512]),
    ("conv_ln_g", [DEPTH, 512]), ("conv_ln_b", [DEPTH, 512]), ("conv_pw", [DEPTH, 512, 512]), ("w_out", [DEPTH, D, D]),
    ("ln1_g", [DEPTH, D]), ("ln1_b", [DEPTH, D]), ("peer_wq", [DEPTH, D, D]), ("peer_subkeys", [DEPTH, 8, 2, 128, 128]),
    ("ln2_g", [DEPTH, D]), ("ln2_b", [DEPTH, D]), ("peer_u", [DEPTH, 16384, D]), ("peer_v", [DEPTH, 16384, D]),
]


def phase_peer(fw, C, l, jobs, NB=6):
    fw.push()
    wq = fw.sbuf("wq", [128, 16, D], BF16)
    for c4 in range(4):
        fw.dma(fw.pool, lambda h: h.dma_start(out=wq[:, c4 * 4:(c4 + 1) * 4, :], in_=C.peer_wq[l, c4 * 512:(c4 + 1) * 512, :].rearrange("(c p) n -> p c n", p=128)),
               reads=[C.peer_wq], writes=[wq], join=True)
    skr = fw.sbuf("skr", [128, 16, 128], F32)
    fw.dma(fw.sp, lambda h: h.dma_start(out=skr[:, :, :], in_=C.peer_subkeys[l, :, :, :, :].rearrange("h p k d -> k (h p) d")), reads=[C.peer_subkeys], writes=[skr])
    skT = fw.sbuf("skT", [128, 16, 128], BF16)
    for q4 in range(4):
        ps = C.ps[q4]
        for j in range(4):
            fw.op(fw.pe, lambda e: e.transpose(out=ps[:, j * 128:(j + 1) * 128], in_=skr[:, q4 * 4 + j, :], identity=C.ident[:, :]), reads=[skr, C.ident], writes=[ps], join=(j > 0))
        fw.op(fw.dve, lambda e: e.tensor_copy(out=skT[:, q4 * 4:(q4 + 1) * 4, :], in_=ps[:, :].rearrange("p (j k) -> p j k", j=4)), reads=[ps], writes=[skT], join=True)
    ln2g = load_bcast(fw, "ln2g", C.ln2_g[l, :], D, C.ln2_g, 128)
    ln2b = load_bcast(fw, "ln2b", C.ln2_b[l, :], D, C.ln2_b, 128)
    iota = load_const(fw, C, "iota16")
    x1 = fw.sbuf("px1", [128, D], F32)
    x1T = fw.sbuf("px1T", [128, 16, 128], BF16)
    qTb = fw.sbuf("pqT", [128, 16, 128], BF16)
    sb = fw.sbuf("psc", [128, 16, 128], F32)
    m = fw.sbuf("pm", [128, 16, 16], F32)
    ix = fw.sbuf("pix", [128, 16, 16], U32)
    ixf = fw.sbuf("pixf", [128, 16, 16], F32)
    tmp = fw.sbuf("ptmp", [128, 256], F32)
    cand = fw.sbuf("pcand", [128, 8, 256], F32)
    oh = fw.sbuf("poh", [128, 8, 256], F32)
    tm = fw.sbuf("ptm", [128, 8, 16], F32)
    pos = fw.sbuf("ppos", [128, 8, 16], U32)
    pa = fw.sbuf("ppa", [128, 8, 16], U32)
    paf = fw.sbuf("ppaf", [128, 2, 128], F32)
    isel = fw.sbuf("pisel", [128, 2, 128], F32)
    eid = fw.sbuf("peid", [128, 128], I32)
    gate = fw.sbuf("pgate", [128, 8, 16], F32)
    gst = fw.sbuf("pgst", [128, 16], F32)
    actp = fw.sbuf("pactp", [128, 128], F32)
    wgt = fw.sbuf("pwgt", [128, 128], F32)
    gtmp = fw.sbuf("pgtmp", [128, 128], F32)
    y = fw.sbuf("py", [128, D], F32)
    junk = fw.sbuf("pjunk", [128, D], BF16)
    lnst = fw.sbuf("plnst", [128, 8], F32)
    ring = [fw.sbuf(f"pring{i}", [128, D], F32) for i in range(NB)]
    u_rows = C.peer_u[:, :, :].rearrange("l n d -> (l n) d")
    v_rows = C.peer_v[:, :, :].rearrange("l n d -> (l n) d")
    gi = 0
    for (X1, P, X2, t) in [(j[0], j[2], j[3], t_) for j in jobs for t_ in range(j[1])]:
        fw.dma(fw.sp, lambda h: h.dma_start(out=x1[0:P, :], in_=X1[t * 128:t * 128 + P, :]), reads=[X1], writes=[x1])
        for cb in range(4):
            ps = C.ps[cb]
            for k in range(4):
                c = cb * 4 + k
                fw.op(fw.pe, lambda e: e.transpose(out=ps[:, k * 128:k * 128 + P], in_=x1[0:P, c * 128:(c + 1) * 128], identity=C.ident[0:P, 0:P]),
                      reads=[x1, C.ident], writes=[ps], join=(k > 0))
            src = ps[:, :].rearrange("p (k q) -> p k q", k=4)[:, :, 0:P]
            fw.op(fw.act if cb % 2 else fw.dve, (lambda e: e.copy(out=x1T[:, cb * 4:(cb + 1) * 4, 0:P], in_=src)) if cb % 2 else (lambda e: e.tensor_copy(out=x1T[:, cb * 4:(cb + 1) * 4, 0:P], in_=src)),
                  reads=[ps], writes=[x1T], join=True)
        for q4 in range(4):
            ps = C.ps[q4]
            for j in range(4):
                hp = q4 * 4 + j
                for c in range(16):
                    fw.op(fw.pe, lambda e: e.matmul(ps[:, j * 128:j * 128 + P], lhsT=wq[:, c, hp * 128:(hp + 1) * 128], rhs=x1T[:, c, 0:P], start=(c == 0), stop=(c == 15)),
                          reads=[wq, x1T], writes=[ps], join=not (j == 0 and c == 0))
            src = ps[:, :].rearrange("p (j q) -> p j q", j=4)[:, :, 0:P]
            fw.op(fw.act if q4 % 2 else fw.dve, (lambda e: e.copy(out=qTb[:, q4 * 4:(q4 + 1) * 4, 0:P], in_=src)) if q4 % 2 else (lambda e: e.tensor_copy(out=qTb[:, q4 * 4:(q4 + 1) * 4, 0:P], in_=src)),
                  reads=[ps], writes=[qTb], join=True)
        for q4 in range(4):
            ps = C.ps[q4]
            for j in range(4):
                hp = q4 * 4 + j
                fw.op(fw.pe, lambda e: e.matmul(ps[0:P, j * 128:(j + 1) * 128], lhsT=qTb[:, hp, 0:P], rhs=skT[:, hp, :], start=True, stop=True),
                      reads=[qTb, skT], writes=[ps], join=(j > 0))
            fw.op(fw.act if q4 % 2 else fw.dve, (lambda e: e.copy(out=sb[0:P, q4 * 4:(q4 + 1) * 4, :], in_=ps[0:P, :].rearrange("p (j k) -> p j k", j=4))) if q4 % 2 else
                  (lambda e: e.tensor_copy(out=sb[0:P, q4 * 4:(q4 + 1) * 4, :], in_=ps[0:P, :].rearrange("p (j k) -> p j k", j=4))), reads=[ps], writes=[sb], join=True)
        for hp in range(16):
            fw.op(fw.dve, lambda e: e.max(out=m[0:P, hp, 0:8], in_=sb[0:P, hp, :]), reads=[sb], writes=[m])
            fw.op(fw.dve, lambda e: e.match_replace(out=tmp[0:P, 0:128], in_to_replace=m[0:P, hp, 0:8], in_values=sb[0:P, hp, :], imm_value=-1e30), reads=[sb, m], writes=[tmp])
            fw.op(fw.dve, lambda e: e.max(out=m[0:P, hp, 8:16], in_=tmp[0:P, 0:128]), reads=[tmp], writes=[m])
            fw.op(fw.dve, lambda e: e.max_index(out=ix[0:P, hp, 0:8], in_max=m[0:P, hp, 0:8], in_values=sb[0:P, hp, :]), reads=[sb, m], writes=[ix])
            fw.op(fw.dve, lambda e: e.max_index(out=ix[0:P, hp, 8:16], in_max=m[0:P, hp, 8:16], in_values=tmp[0:P, 0:128]), reads=[tmp, m], writes=[ix])
        fw.op(fw.dve, lambda e: e.tensor_copy(out=ixf[0:P, :, :], in_=ix[0:P, :, :]), reads=[ix], writes=[ixf])
        m4 = m[0:P, :, :].rearrange("p (h two) k -> p h two k", two=2)
        i4 = ixf[0:P, :, :].rearrange("p (h two) k -> p h two k", two=2)
        c4v = cand[0:P, :, :].rearrange("p h (a b) -> p h a b", a=16)
        fw.op(fw.dve, lambda e: e.tensor_tensor(out=c4v, in0=m4[:, :, 0, :].unsqueeze(3).to_broadcast([P, 8, 16, 16]),
                                                in1=m4[:, :, 1, :].unsqueeze(2).to_broadcast([P, 8, 16, 16]), op=ALU.add), reads=[m], writes=[cand])
        for h in range(8):
            fw.op(fw.dve, lambda e: e.max(out=tm[0:P, h, 0:8], in_=cand[0:P, h, :]), reads=[cand], writes=[tm])
            fw.op(fw.dve, lambda e: e.match_replace(out=tmp[0:P, :], in_to_replace=tm[0:P, h, 0:8], in_values=cand[0:P, h, :], imm_value=-1e30), reads=[cand, tm], writes=[tmp])
            fw.op(fw.dve, lambda e: e.max(out=tm[0:P, h, 8:16], in_=tmp[0:P, :]), reads=[tmp], writes=[tm])
            fw.op(fw.dve, lambda e: e.max_index(out=pos[0:P, h, 0:8], in_max=tm[0:P, h, 0:8], in_values=cand[0:P, h, :]), reads=[cand, tm], writes=[pos])
            fw.op(fw.dve, lambda e: e.max_index(out=pos[0:P, h, 8:16], in_max=tm[0:P, h, 8:16], in_values=tmp[0:P, :]), reads=[tmp, tm], writes=[pos])
        fw.op(fw.dve, lambda e: e.tensor_single_scalar(out=pa[0:P, :, :], in_=pos[0:P, :, :], scalar=4, op=ALU.logical_shift_right), reads=[pos], writes=[pa])
        fw.op(fw.dve, lambda e: e.tensor_copy(out=paf[0:P, 0, :], in_=pa[0:P, :, :].rearrange("p h k -> p (h k)")), reads=[pa], writes=[paf])
        fw.op(fw.dve, lambda e: e.tensor_single_scalar(out=pa[0:P, :, :], in_=pos[0:P, :, :], scalar=15, op=ALU.bitwise_and), reads=[pos, paf], writes=[pa])
        fw.op(fw.dve, lambda e: e.tensor_copy(out=paf[0:P, 1, :], in_=pa[0:P, :, :].rearrange("p h k -> p (h k)")), reads=[pa], writes=[paf])
        for w_ in range(2):
            o4 = oh[0:P, :, :].rearrange("p h (k a) -> p h k a", a=16)
            sel = paf[0:P, w_, :].rearrange("p (h k) -> p h k", h=8)
            fw.op(fw.dve, lambda e: e.tensor_tensor(out=o4, in0=sel.unsqueeze(3).to_broadcast([P, 8, 16, 16]),
                                                    in1=iota[0:P, :].unsqueeze(1).unsqueeze(1).to_broadcast([P, 8, 16, 16]), op=ALU.is_equal), reads=[paf, iota], writes=[oh])
            fw.op(fw.dve, lambda e: e.tensor_tensor(out=o4, in0=o4, in1=i4[:, :, w_, :].unsqueeze(2).to_broadcast([P, 8, 16, 16]), op=ALU.mult), reads=[oh, ixf], writes=[oh])
            fw.op(fw.dve, lambda e: e.tensor_reduce(out=isel[0:P, w_, :], in_=oh[0:P, :, :].rearrange("p h (k a) -> p (h k) a", a=16), axis=AX.X, op=ALU.add), reads=[oh], writes=[isel])
        fw.op(fw.dve, lambda e: e.scalar_tensor_tensor(out=eid[0:P, :], in0=isel[0:P, 0, :], scalar=128.0, in1=isel[0:P, 1, :], op0=ALU.mult, op1=ALU.add), reads=[isel], writes=[eid])
        fw.op(fw.dve, lambda e: e.tensor_tensor(out=gate[0:P, :, :], in0=tm[0:P, :, :], in1=tm[0:P, :, 0:1].to_broadcast([P, 8, 16]), op=ALU.subtract), reads=[tm], writes=[gate])
        fw.op(fw.act, lambda e: e.activation(out=gate[0:P, :, :], in_=gate[0:P, :, :], func=AF.Exp), reads=[gate], writes=[gate])
        fw.op(fw.dve, lambda e: e.tensor_reduce(out=gst[0:P, 0:8], in_=gate[0:P, :, :], axis=AX.X, op=ALU.add), reads=[gate], writes=[gst])
        fw.op(fw.dve, lambda e: e.reciprocal(out=gst[0:P, 8:16], in_=gst[0:P, 0:8]), reads=[gst], writes=[gst])
        fw.op(fw.dve, lambda e: e.tensor_tensor(out=gate[0:P, :, :], in0=gate[0:P, :, :], in1=gst[0:P, 8:16].unsqueeze(2).to_broadcast([P, 8, 16]), op=ALU.mult), reads=[gate, gst], writes=[gate])
        fw.op(fw.dve, lambda e: e.memset(actp[0:P, :], 0.0), writes=[actp])
        for s_ in range(128):
            rb = ring[gi % NB]
            gi += 1
            fw.dma(fw.pool, lambda h: h.indirect_dma_start(out=rb[0:P, :], out_offset=None, in_=u_rows, in_offset=bass.IndirectOffsetOnAxis(ap=eid[0:P, s_:s_ + 1], axis=0),
                                                         element_offset=l * 16384 * D), reads=[eid, C.peer_u], writes=[rb])
            fw.op(fw.dve, lambda e: e.scalar_tensor_tensor(out=junk[0:P, :], in0=rb[0:P, :], scalar=1.0, in1=x1[0:P, :], op0=ALU.mult, op1=ALU.mult, accum_out=actp[0:P, s_:s_ + 1]),
                  reads=[rb, x1], writes=[junk, actp], join=True)
        emit_gelu(fw, fw.dve, fw.act, wgt[0:P, :], actp[0:P, :], gtmp[0:P, :], [actp], [wgt], gtmp)
        fw.op(fw.dve, lambda e: e.tensor_tensor(out=wgt[0:P, :], in0=wgt[0:P, :], in1=gate[0:P, :, :].rearrange("p h k -> p (h k)"), op=ALU.mult), reads=[wgt, gate], writes=[wgt])
        for s_ in range(128):
            rb = ring[gi % NB]
            gi += 1
            fw.dma(fw.pool, lambda h: h.indirect_dma_start(out=rb[0:P, :], out_offset=None, in_=v_rows, in_offset=bass.IndirectOffsetOnAxis(ap=eid[0:P, s_:s_ + 1], axis=0),
                                                         element_offset=l * 16384 * D), reads=[eid, C.peer_v], writes=[rb])
            if s_ == 0:
                fw.op(fw.dve, lambda e: e.tensor_scalar(out=y[0:P, :], in0=rb[0:P, :], scalar1=wgt[0:P, 0:1], scalar2=None, op0=ALU.mult), reads=[rb, wgt], writes=[y])
            else:
                fw.op(fw.dve, lambda e: e.scalar_tensor_tensor(out=y[0:P, :], in0=rb[0:P, :], scalar=wgt[0:P, s_:s_ + 1], in1=y[0:P, :], op0=ALU.mult, op1=ALU.add),
                      reads=[rb, wgt, y], writes=[y])
        fw.op(fw.dve, lambda e: e.scalar_tensor_tensor(out=y[0:P, :], in0=x1[0:P, :], scalar=ALPHA, in1=y[0:P, :], op0=ALU.mult, op1=ALU.add), reads=[x1, y], writes=[y])
        emit_ln_free(fw, C, y, P, ln2g, ln2b, junk, lnst)
        fw.dma(fw.sp, lambda h: h.dma_start(out=X2[t * 128:t * 128 + P, :], in_=y[0:P, :]), reads=[y], writes=[X2], join=True)
    fw.pop()


def sample_consts():
    c = {}
    rows = np.arange(64)
    tok = rows % 8
    g = rows // 32
    c["msum"] = ((g[:, None] == g[None, :]) & (tok[:, None] == tok[None, :])).astype(np.float32)
    sA = np.ones((64, 257), np.float32); sB = np.zeros((64, 257), np.float32)
    for col in (0, 255, 256):
        sA[:, col] = 0.0; sB[:, col] = 1.0e4
    c["ssA"] = sA; c["ssB"] = sB
    c["caus8"] = np.where(np.arange(8)[None, :] <= tok[:, None], 0.0, NEG).astype(np.float32)
    idx = np.arange(520)[None, :]
    ok = np.where(idx < 512, idx >= tok[:, None], (idx - 512) <= tok[:, None])
    c["wbias"] = np.where(ok, 0.0, NEG).astype(np.float32)
    return c


SCONST_SHAPES = {"msum": [64, 64], "ssA": [64, 257], "ssB": [64, 257], "caus8": [64, 8], "wbias": [64, 520]}
CONST_SHAPES.update(SCONST_SHAPES)


def emit_rows_softmax(fw, sbuf_s, n, st, gate_ap, gate_buf, R=64):
    fw.op(fw.dve, lambda e: e.tensor_reduce(out=st[0:R, 0:1], in_=sbuf_s[0:R, 0:n], axis=AX.X, op=ALU.max, negate=True), reads=[sbuf_s], writes=[st])
    fw.op(fw.act, lambda e: e.activation(out=sbuf_s[0:R, 0:n], in_=sbuf_s[0:R, 0:n], func=AF.Exp, bias=st[0:R, 0:1], scale=1.0, accum_out=st[0:R, 1:2]),
          reads=[sbuf_s, st], writes=[sbuf_s, st])
    fw.op(fw.dve, lambda e: e.reciprocal(out=st[0:R, 2:3], in_=st[0:R, 1:2]), reads=[st], writes=[st])
    if gate_ap is not None:
        fw.op(fw.dve, lambda e: e.tensor_tensor(out=st[0:R, 2:3], in0=st[0:R, 2:3], in1=gate_ap, op=ALU.mult), reads=[st, gate_buf], writes=[st])
    fw.op(fw.dve, lambda e: e.tensor_scalar(out=sbuf_s[0:R, 0:n], in0=sbuf_s[0:R, 0:n], scalar1=st[0:R, 2:3], scalar2=None, op0=ALU.mult), reads=[sbuf_s, st], writes=[sbuf_s])


def phase_nsa_sample(fw, C, l, HtokS, mixn_s):
    fw.push()
    R = 64
    cst = {nm: load_const(fw, C, nm) for nm in SCONST_SHAPES}
    pti = fw.sbuf("pti", [128, 1], I32)
    fw.dma(fw.sp, lambda h: h.dma_start(out=pti[:, :], in_=C.pt[:, :]), reads=[C.pt], writes=[pti])
    idx8 = fw.sbuf("idx8", [128, 1], I32)
    fw.op(fw.dve, lambda e: e.tensor_scalar(out=idx8[:, :], in0=pti[:, :], scalar1=8.0, scalar2=None, op0=ALU.mult), reads=[pti], writes=[idx8])
    kv = fw.sbuf("skv", [TS, 1536], F32)
    rt = fw.sbuf("srt", [TS, 4, 64], F32)
    fw.dma(fw.sp, lambda h: h.dma_start(out=kv[:, :], in_=HtokS[0:TS, 1536:3072]), reads=[HtokS], writes=[kv])
    fw.dma(fw.sp, lambda h: h.dma_start(out=rt[:, :, :], in_=C.cd["rope_s"][:, :, :]), reads=[C.cd["rope_s"]], writes=[rt])
    rtmp = [fw.sbuf(f"srtmp{i}", [TS, 512], F32) for i in range(4)]
    kview = kv[:, :].rearrange("p (a v g h d) -> p a v g h d", a=3, v=2, g=2, h=2, d=64)
    tv = [r[:, 0:384].rearrange("p (a g d) -> p a g d", a=3, g=2) for r in rtmp]
    emit_rope(fw, kview[:, :, 0, :, 0, :], kview[:, :, 0, :, 1, :], rt[:, 0, :], rt[:, 1, :], tv, [TS, 3, 2, 64], [rt], kv, rtmp)
    for a in range(6):
        fw.dma(fw.sp, lambda h: h.dma_start(out=C.out_skv[l, a, :, :], in_=kv[:, a * 256:(a + 1) * 256]), reads=[kv], writes=[C.out_skv], join=True)
    for wi in range(2):
        fw.dma(fw.sp, lambda h: h.dma_start(out=C.out_swin[l, wi, 0:504, :], in_=C.cwin[wi][l, 8:512, :]), reads=[C.cwin[wi]], writes=[C.out_swin], join=True)
        fw.dma(fw.sp, lambda h: h.dma_start(out=C.out_swin[l, wi, 504:512, :], in_=kv[:, 1024 + wi * 256:1280 + wi * 256]), reads=[kv], writes=[C.out_swin], join=True)
    kvb = fw.sbuf("skvb", [TS, 1536], BF16)
    fw.op(fw.act, lambda e: e.copy(out=kvb[:, :], in_=kv[:, :]), reads=[kv], writes=[kvb])
    psb = C.psb[0]
    for i, (a, g) in enumerate([(1, 0), (1, 1), (2, 0), (2, 1)]):
        c0 = a * 512 + g * 128
        fw.op(fw.pe, lambda e: e.transpose(out=psb[:, i * 8:(i + 1) * 8], in_=kvb[:, c0:c0 + 128], identity=C.identb[0:TS, 0:TS]), reads=[kvb, C.identb], writes=[psb], join=(i > 0))
    knT = fw.sbuf("knT", [128, 32], BF16)
    fw.op(fw.dve, lambda e: e.tensor_copy(out=knT[:, :], in_=psb[:, 0:32]), reads=[psb], writes=[knT])
    qt = fw.sbuf("sqt", [TS, 1024], F32)
    gt = fw.sbuf("sgt", [TS, 24], F32)
    fw.dma(fw.sp, lambda h: h.dma_start(out=qt[:, :], in_=HtokS[0:TS, 512:1536]), reads=[HtokS], writes=[qt])
    fw.dma(fw.sp, lambda h: h.dma_start(out=gt[:, :], in_=HtokS[0:TS, 3072:3096]), reads=[HtokS], writes=[gt])
    qv = qt[:, :].rearrange("p (h x d) -> p h x d", h=8, x=2)
    tv2 = [r[:, :].rearrange("p (h d) -> p h d", h=8) for r in rtmp]
    emit_rope(fw, qv[:, :, 0, :], qv[:, :, 1, :], rt[:, 2, :], rt[:, 3, :], tv2, [TS, 8, 64], [rt], qt, rtmp)
    qb = fw.sbuf("sqb", [TS, 1024], BF16)
    fw.op(fw.act, lambda e: e.copy(out=qb[:, :], in_=qt[:, :]), reads=[qt], writes=[qb])
    psb = C.psb[1]
    for h in range(8):
        fw.op(fw.pe, lambda e: e.transpose(out=psb[:, h * 8:(h + 1) * 8], in_=qb[:, h * 128:(h + 1) * 128], identity=C.identb[0:TS, 0:TS]), reads=[qb, C.identb], writes=[psb], join=(h > 0))
    Q = [fw.sbuf(f"sQ{g}", [128, 64], BF16) for g in range(2)]
    for g in range(2):
        fw.op(fw.dve, lambda e: e.memset(Q[g][:, :], 0.0), writes=[Q[g]])
        fw.op(fw.dve, lambda e: e.tensor_copy(out=Q[g][:, g * 32:(g + 1) * 32], in_=psb[:, g * 32:(g + 1) * 32]), reads=[psb], writes=[Q[g]])
    fw.op(fw.act, lambda e: e.activation(out=gt[:, :], in_=gt[:, :], func=AF.Sigmoid), reads=[gt], writes=[gt])
    fw.dma(fw.sp, lambda h: h.dma_start(out=C.gscr[:, :], in_=gt[:, :]), reads=[gt], writes=[C.gscr])
    g64 = fw.sbuf("g64", [64, 3], F32)
    for h in range(8):
        fw.dma(fw.sp, lambda h_: h_.dma_start(out=g64[h * 8:(h + 1) * 8, :], in_=C.gscr[:, h * 3:(h + 1) * 3]), reads=[C.gscr], writes=[g64], join=True)
    st = fw.sbuf("sst", [64, 4], F32)
    wsm = []
    for wi, wsrc in enumerate([C.cmp_wk, C.cmp_wv]):
        w_ = fw.sbuf(f"wsm{wi}", [128, 64], F32)
        fw.dma(fw.sp, lambda h: h.dma_start(out=w_[:, :], in_=wsrc[l, :, :].rearrange("g j -> (g j)").partition_broadcast(128)), reads=[wsrc], writes=[w_])
        wsm.append(w_)
    chunk = [fw.sbuf(f"chunk{i}", [128, 4096], F32) for i in range(2)]
    cacc = [fw.sbuf(f"cacc{i}", [128, 4, 256], F32) for i in range(2)]
    red = fw.sbuf("sred", [128, 256], F32)
    ci = 0
    for wi, cache in enumerate([C.c_cmp_k, C.c_cmp_v]):
        for jb8 in range(8):
            ch = chunk[ci % 2]
            ci += 1
            fw.dma(fw.pool, lambda h: h.indirect_dma_start(out=ch[:, :], out_offset=None, in_=cache[:, :], in_offset=bass.IndirectOffsetOnAxis(ap=idx8[:, 0:1], axis=0),
                                                         element_offset=(l * 1280 * 8 + jb8) * 4096), reads=[idx8, cache], writes=[ch])
            jb, j0 = jb8 // 2, (jb8 % 2) * 16
            wv_ = wsm[wi][:, :].rearrange("p (g j) -> p j g", g=2)[:, j0:j0 + 16, :].unsqueeze(3).to_broadcast([128, 16, 2, 128])
            chv4 = ch[:, :].rearrange("p (j g d) -> p j g d", j=16, g=2)
            fw.op(fw.dve, lambda e: e.tensor_tensor(out=chv4, in0=chv4, in1=wv_, op=ALU.mult), reads=[ch, wsm[wi]], writes=[ch])
            if j0 == 0:
                fw.op(fw.dve, lambda e: e.tensor_reduce(out=cacc[wi][:, jb, :], in_=ch[:, :].rearrange("p (j c) -> p c j", j=16), axis=AX.X, op=ALU.add), reads=[ch], writes=[cacc[wi]], join=True)
            else:
                fw.op(fw.dve, lambda e: e.tensor_reduce(out=red[:, :], in_=ch[:, :].rearrange("p (j c) -> p c j", j=16), axis=AX.X, op=ALU.add), reads=[ch], writes=[red])
                fw.op(fw.dve, lambda e: e.tensor_tensor(out=cacc[wi][:, jb, :], in0=cacc[wi][:, jb, :], in1=red[:, :], op=ALU.add), reads=[cacc[wi], red], writes=[cacc[wi]])
    kcT = fw.sbuf("skcT", [128, 2, 512], BF16)
    for g in range(2):
        ps = C.ps[g]
        for jb in range(4):
            fw.op(fw.pe, lambda e: e.transpose(out=ps[:, jb * 128:(jb + 1) * 128], in_=cacc[0][:, jb, g * 128:(g + 1) * 128], identity=C.ident[:, :]), reads=[cacc[0], C.ident], writes=[ps], join=(jb > 0))
        fw.op(fw.dve, lambda e: e.tensor_copy(out=kcT[:, g, :], in_=ps[:, :]), reads=[ps], writes=[kcT], join=True)
    vcb = fw.sbuf("svcb", [128, 4, 256], BF16)
    fw.op(fw.act, lambda e: e.copy(out=vcb[:, :, :], in_=cacc[1][:, :, :]), reads=[cacc[1]], writes=[vcb])
    ps = C.ps[2]
    for g in range(2):
        fw.op(fw.pe, lambda e: e.matmul(ps[0:R, :], lhsT=Q[g][:, :], rhs=kcT[:, g, :], start=(g == 0), stop=(g == 1)), reads=[Q[g], kcT], writes=[ps], join=(g > 0))
    pc = fw.sbuf("spc", [64, 512], F32)
    fw.op(fw.act, lambda e: e.copy(out=pc[:, :], in_=ps[0:R, :]), reads=[ps], writes=[pc])
    emit_rows_softmax(fw, pc, 512, st, None, None)
    ps = C.ps[3]
    fw.op(fw.pe, lambda e: e.matmul(ps[0:R, :], lhsT=cst["msum"][:, :], rhs=pc[:, :], start=True, stop=True), reads=[cst["msum"], pc], writes=[ps])
    impr = fw.sbuf("simpr", [64, 512], F32)
    fw.op(fw.act, lambda e: e.copy(out=impr[:, :], in_=ps[0:R, :]), reads=[ps], writes=[impr])
    sco = fw.sbuf("ssco", [64, 264], F32)
    sco2 = fw.sbuf("ssco2", [64, 264], F32)
    iv = impr[:, :].rearrange("r (hb two p) -> r hb two p", hb=2, two=2)
    fw.op(fw.dve, lambda e: e.memset(sco[:, 256:264], 0.0), writes=[sco])
    fw.op(fw.dve, lambda e: e.tensor_tensor(out=sco[:, 0:256].rearrange("r (hb p) -> r hb p", hb=2), in0=iv[:, :, 0, :], in1=iv[:, :, 1, :], op=ALU.add), reads=[impr], writes=[sco], join=True)
    fw.op(fw.dve, lambda e: e.tensor_tensor(out=sco[:, 0:257], in0=sco[:, 0:257], in1=cst["ssA"][:, :], op=ALU.mult), reads=[sco, cst["ssA"]], writes=[sco])
    fw.op(fw.dve, lambda e: e.tensor_tensor(out=sco[:, 0:257], in0=sco[:, 0:257], in1=cst["ssB"][:, :], op=ALU.add), reads=[sco, cst["ssB"]], writes=[sco])
    m8 = fw.sbuf("sm8", [64, 16], F32)
    fw.op(fw.dve, lambda e: e.max(out=m8[:, 0:8], in_=sco[:, 0:257]), reads=[sco], writes=[m8])
    fw.op(fw.dve, lambda e: e.match_replace(out=sco2[:, 0:257], in_to_replace=m8[:, 0:8], in_values=sco[:, 0:257], imm_value=-1e30), reads=[sco, m8], writes=[sco2])
    fw.op(fw.dve, lambda e: e.max(out=m8[:, 8:16], in_=sco2[:, 0:257]), reads=[sco2], writes=[m8])
    bsel = fw.sbuf("sbsel", [64, 257], F32)
    fw.op(fw.dve, lambda e: e.tensor_scalar(out=bsel[:, :], in0=sco[:, 0:257], scalar1=m8[:, 15:16], scalar2=None, op0=ALU.is_ge), reads=[sco, m8], writes=[bsel])
    fw.op(fw.dve, lambda e: e.tensor_scalar(out=bsel[:, :], in0=bsel[:, :], scalar1=-NEG, scalar2=NEG, op0=ALU.mult, op1=ALU.add), reads=[bsel], writes=[bsel])
    fw.op(fw.dve, lambda e: e.tensor_scalar(out=pc[:, :], in0=pc[:, :], scalar1=g64[:, 0:1], scalar2=None, op0=ALU.mult), reads=[pc, g64], writes=[pc])
    pcT = fw.sbuf("spcT", [128, 4, 64], BF16)
    ps = C.ps[0]
    for jb in range(4):
        fw.op(fw.pe, lambda e: e.transpose(out=ps[:, jb * 64:(jb + 1) * 64], in_=pc[:, jb * 128:(jb + 1) * 128], identity=C.ident[0:R, 0:R]), reads=[pc, C.ident], writes=[ps], join=(jb > 0))
    fw.op(fw.dve, lambda e: e.tensor_copy(out=pcT[:, :, :], in_=ps[:, 0:256].rearrange("p (j r) -> p j r", j=4)), reads=[ps], writes=[pcT])
    Ss = fw.sbuf("sS", [64, 16392], F32)
    vt = [fw.sbuf(f"svt{i}", [128, 16, 256], BF16) for i in range(2)]
    ktile = [fw.sbuf(f"sktile{i}", [128, 512], BF16) for i in range(4)]
    ki = 0
    for jb8 in range(8):
        ch = chunk[ci % 2]
        ci += 1
        fw.dma(fw.pool, lambda h: h.indirect_dma_start(out=ch[:, :], out_offset=None, in_=C.c_slc_k[:, :], in_offset=bass.IndirectOffsetOnAxis(ap=idx8[:, 0:1], axis=0),
                                                     element_offset=(l * 1280 * 8 + jb8) * 4096), reads=[idx8, C.c_slc_k], writes=[ch])
        chv = ch[:, :].rearrange("p (j g d) -> p j g d", j=16, g=2)
        for j4 in range(4):
            kts = []
            for g in range(2):
                pst = C.ps[(ki % 2) * 2 + g]
                for jj in range(4):
                    fw.op(fw.pe, lambda e: e.transpose(out=pst[:, jj * 128:(jj + 1) * 128], in_=chv[:, j4 * 4 + jj, g, :], identity=C.ident[:, :]), reads=[ch, C.ident], writes=[pst], join=(jj > 0))
                kt_ = ktile[(ki % 2) * 2 + g]
                if g == 0:
                    fw.op(fw.dve, lambda e: e.tensor_copy(out=kt_[:, :], in_=pst[:, :]), reads=[pst], writes=[kt_])
                else:
                    fw.op(fw.act, lambda e: e.copy(out=kt_[:, :], in_=pst[:, :]), reads=[pst], writes=[kt_])
                kts.append(kt_)
            pss = C.pacc[ki % 2]
            for g in range(2):
                fw.op(fw.pe, lambda e: e.matmul(pss[0:R, :], lhsT=Q[g][:, :], rhs=kts[g][:, :], start=(g == 0), stop=(g == 1)), reads=[Q[g], kts[g]], writes=[pss], join=(g > 0))
            jabs = jb8 * 16 + j4 * 4
            hb = 1 if jabs >= 64 else 0
            fw.op(fw.dve, lambda e: e.tensor_tensor(out=Ss[:, jabs * 128:(jabs + 4) * 128].rearrange("r (j p) -> r j p", j=4), in0=pss[0:R, :].rearrange("r (j p) -> r j p", j=4),
                                                    in1=bsel[:, hb * 128:(hb + 1) * 128].unsqueeze(1).to_broadcast([R, 4, 128]), op=ALU.add), reads=[pss, bsel], writes=[Ss], join=True)
            ki += 1
    pss = C.pacc[0]
    for g in range(2):
        fw.op(fw.pe, lambda e: e.matmul(pss[0:R, 0:8], lhsT=Q[g][:, :], rhs=knT[:, g * 8:(g + 1) * 8], start=(g == 0), stop=(g == 1)), reads=[Q[g], knT], writes=[pss], join=(g > 0))
    fw.op(fw.dve, lambda e: e.scalar_tensor_tensor(out=Ss[:, 16384:16392], in0=pss[0:R, 0:8], scalar=bsel[:, 256:257], in1=cst["caus8"][:, :], op0=ALU.add, op1=ALU.add),
          reads=[pss, bsel, cst["caus8"]], writes=[Ss], join=True)
    emit_rows_softmax(fw, Ss, 16392, st, g64[:, 1:2], g64)
    pT = fw.sbuf("spT", [128, 128, 64], BF16)
    for j8 in range(16):
        ps = C.ps[j8 % 4]
        for jj in range(8):
            j = j8 * 8 + jj
            fw.op(fw.pe, lambda e: e.transpose(out=ps[:, jj * 64:(jj + 1) * 64], in_=Ss[:, j * 128:(j + 1) * 128], identity=C.ident[0:R, 0:R]), reads=[Ss, C.ident], writes=[ps], join=(jj > 0))
        fw.op(fw.act if j8 % 2 else fw.dve, (lambda e: e.copy(out=pT[:, j8 * 8:(j8 + 1) * 8, :], in_=ps[:, :].rearrange("p (j r) -> p j r", j=8))) if j8 % 2 else
              (lambda e: e.tensor_copy(out=pT[:, j8 * 8:(j8 + 1) * 8, :], in_=ps[:, :].rearrange("p (j r) -> p j r", j=8))), reads=[ps], writes=[pT], join=True)
    ps = C.ps[0]
    fw.op(fw.pe, lambda e: e.transpose(out=ps[0:8, 0:64], in_=Ss[:, 16384:16392], identity=C.ident[0:R, 0:R]), reads=[Ss, C.ident], writes=[ps])
    pTn = fw.sbuf("spTn", [8, 64], BF16)
    fw.op(fw.dve, lambda e: e.tensor_copy(out=pTn[:, :], in_=ps[0:8, 0:64]), reads=[ps], writes=[pTn])
    wk = fw.sbuf("swk", [128, 4, 256], F32)
    wv = fw.sbuf("swv", [128, 4, 256], F32)
    fw.dma(fw.sp, lambda h: h.dma_start(out=wk[:, :, :], in_=C.cwin[0][l, :, :].rearrange("(a p) c -> p a c", p=128)), reads=[C.cwin[0]], writes=[wk])
    fw.dma(fw.sp, lambda h: h.dma_start(out=wv[:, :, :], in_=C.cwin[1][l, :, :].rearrange("(a p) c -> p a c", p=128)), reads=[C.cwin[1]], writes=[wv])
    wvb = fw.sbuf("swvb", [128, 4, 256], BF16)
    fw.op(fw.act, lambda e: e.copy(out=wvb[:, :, :], in_=wv[:, :, :]), reads=[wv], writes=[wvb])
    wkT = fw.sbuf("swkT", [128, 2, 512], BF16)
    for g in range(2):
        ps = C.ps[1 + g]
        for a in range(4):
            fw.op(fw.pe, lambda e: e.transpose(out=ps[:, a * 128:(a + 1) * 128], in_=wk[:, a, g * 128:(g + 1) * 128], identity=C.ident[:, :]), reads=[wk, C.ident], writes=[ps], join=(a > 0))
        fw.op(fw.dve, lambda e: e.tensor_copy(out=wkT[:, g, :], in_=ps[:, :]), reads=[ps], writes=[wkT], join=True)
    Sw = fw.sbuf("sSw", [64, 520], F32)
    pss = C.pacc[1]
    for g in range(2):
        fw.op(fw.pe, lambda e: e.matmul(pss[0:R, :], lhsT=Q[g][:, :], rhs=wkT[:, g, :], start=(g == 0), stop=(g == 1)), reads=[Q[g], wkT], writes=[pss], join=(g > 0))
    fw.op(fw.dve, lambda e: e.tensor_tensor(out=Sw[:, 0:512], in0=pss[0:R, :], in1=cst["wbias"][:, 0:512], op=ALU.add), reads=[pss, cst["wbias"]], writes=[Sw], join=True)
    pss = C.pacc[0]
    for g in range(2):
        fw.op(fw.pe, lambda e: e.matmul(pss[0:R, 0:8], lhsT=Q[g][:, :], rhs=knT[:, 16 + g * 8:16 + (g + 1) * 8], start=(g == 0), stop=(g == 1)), reads=[Q[g], knT], writes=[pss], join=(g > 0))
    fw.op(fw.dve, lambda e: e.tensor_tensor(out=Sw[:, 512:520], in0=pss[0:R, 0:8], in1=cst["wbias"][:, 512:520], op=ALU.add), reads=[pss, cst["wbias"]], writes=[Sw], join=True)
    emit_rows_softmax(fw, Sw, 520, st, g64[:, 2:3], g64)
    pwT = fw.sbuf("spwT", [128, 4, 64], BF16)
    ps = C.ps[3]
    for a in range(4):
        fw.op(fw.pe, lambda e: e.transpose(out=ps[:, a * 64:(a + 1) * 64], in_=Sw[:, a * 128:(a + 1) * 128], identity=C.ident[0:R, 0:R]), reads=[Sw, C.ident], writes=[ps], join=(a > 0))
    fw.op(fw.pe, lambda e: e.transpose(out=ps[0:8, 256:320], in_=Sw[:, 512:520], identity=C.ident[0:R, 0:R]), reads=[Sw, C.ident], writes=[ps], join=True)
    fw.op(fw.dve, lambda e: e.tensor_copy(out=pwT[:, :, :], in_=ps[:, 0:256].rearrange("p (a r) -> p a r", a=4)), reads=[ps], writes=[pwT])
    pwTn = fw.sbuf("spwTn", [8, 64], BF16)
    fw.op(fw.dve, lambda e: e.tensor_copy(out=pwTn[:, :], in_=ps[0:8, 256:320]), reads=[ps], writes=[pwTn])
    accs = [C.pacc[0], C.pacc[1]]
    for g in range(2):
        gs = slice(g * 32, (g + 1) * 32)
        for jb in range(4):
            fw.op(fw.pe, lambda e: e.matmul(accs[g][:, 0:32], lhsT=vcb[:, jb, g * 128:(g + 1) * 128], rhs=pcT[:, jb, gs], start=(jb == 0), stop=False), reads=[vcb, pcT], writes=[accs[g]], join=(jb > 0))
    for jb8 in range(8):
        ch = chunk[ci % 2]
        v_ = vt[ci % 2]
        ci += 1
        fw.dma(fw.pool, lambda h: h.indirect_dma_start(out=ch[:, :], out_offset=None, in_=C.c_slc_v[:, :], in_offset=bass.IndirectOffsetOnAxis(ap=idx8[:, 0:1], axis=0),
                                                     element_offset=(l * 1280 * 8 + jb8) * 4096), reads=[idx8, C.c_slc_v], writes=[ch])
        fw.op(fw.act if jb8 % 2 else fw.dve, (lambda e: e.copy(out=v_[:, :, :], in_=ch[:, :].rearrange("p (j c) -> p j c", j=16))) if jb8 % 2 else
              (lambda e: e.tensor_copy(out=v_[:, :, :], in_=ch[:, :].rearrange("p (j c) -> p j c", j=16))), reads=[ch], writes=[v_])
        for g in range(2):
            gs = slice(g * 32, (g + 1) * 32)
            for jj in range(16):
                j = jb8 * 16 + jj
                fw.op(fw.pe, lambda e: e.matmul(accs[g][:, 0:32], lhsT=v_[:, jj, g * 128:(g + 1) * 128], rhs=pT[:, j, gs], start=False, stop=False), reads=[v_, pT], writes=[accs[g]], join=True)
    for g in range(2):
        acc = accs[g]
        gs = slice(g * 32, (g + 1) * 32)
        fw.op(fw.pe, lambda e: e.matmul(acc[:, 0:32], lhsT=kvb[:, 768 + g * 128:896 + g * 128], rhs=pTn[:, gs], start=False, stop=False), reads=[kvb, pTn], writes=[acc], join=True)
        for a in range(4):
            fw.op(fw.pe, lambda e: e.matmul(acc[:, 0:32], lhsT=wvb[:, a, g * 128:(g + 1) * 128], rhs=pwT[:, a, gs], start=False, stop=False), reads=[wvb, pwT], writes=[acc], join=True)
        fw.op(fw.pe, lambda e: e.matmul(acc[:, 0:32], lhsT=kvb[:, 1280 + g * 128:1408 + g * 128], rhs=pwTn[:, gs], start=False, stop=True), reads=[kvb, pwTn], writes=[acc], join=True)
        fw.op(fw.dve, lambda e: e.tensor_copy(out=mixn_s[:, g * 4:(g + 1) * 4, 0:TS], in_=acc[:, 0:32].rearrange("p (r t) -> p r t", r=4)), reads=[acc], writes=[mixn_s], join=True)
    fw.pop()


def build(mode="full"):
    nc = bass.Bass("TRN2", target_bir_lowering=False)
    fw = FW(nc)
    C = Ctx()
    dbg = mode != "full"
    nlayers = 1 if dbg else DEPTH
    def inp(name, shape, dtype=F32):
        return fw.dram(name, shape, dtype, kind="ExternalInput")
    C.xp = inp("xp", [S, D])
    C.xs = inp("xs", [TS, D])
    C.pt = inp("pt", [128, 1], I32)
    C.c_cmp_k = inp("c_cmp_k", [DEPTH * 1280 * 8, 4096])
    C.c_cmp_v = inp("c_cmp_v", [DEPTH * 1280 * 8, 4096])
    C.c_slc_k = inp("c_slc_k", [DEPTH * 1280 * 8, 4096])
    C.c_slc_v = inp("c_slc_v", [DEPTH * 1280 * 8, 4096])
    C.cwin = [inp("c_win_k", [DEPTH, 512, 256]), inp("c_win_v", [DEPTH, 512, 256])]
    C.state_conv = inp("state_conv", [DEPTH, 30, 512])
    for nm, shp in INPUT_SPECS:
        setattr(C, nm, inp(nm, shp))
    C.cd = {nm: inp(nm, shp) for nm, shp in CONST_SHAPES.items()}
    C.out_kv = fw.dram("o_pkv", [DEPTH, 6, S, 256], F32, kind="ExternalOutput")
    C.out_conv_buf = fw.dram("o_pconv", [DEPTH, 30, 512], F32, kind="ExternalOutput")
    C.out_gm = fw.dram("o_pgm", [DEPTH, 128, 512], F32, kind="ExternalOutput")
    C.out_gm_buf = C.out_gm
    C.out_skv = fw.dram("o_skv", [DEPTH, 6, TS, 256], F32, kind="ExternalOutput")
    C.out_swin = fw.dram("o_swin", [DEPTH, 2, 512, 256], F32, kind="ExternalOutput")
    C.out_sconv = fw.dram("o_sconv", [DEPTH, 30, 512], F32, kind="ExternalOutput")
    C.out_sgm = fw.dram("o_sgm", [DEPTH, TS, 512], F32, kind="ExternalOutput")
    C.out_y = fw.dram("o_y", [S, D], F32, kind="ExternalOutput")
    C.out_ys = fw.dram("o_ys", [TS, D], F32, kind="ExternalOutput")
    C.ident = load_const(fw, C, "ident")
    C.identb = fw.sbuf("identb", [128, 128], BF16)
    fw.op(fw.dve, lambda e: e.tensor_copy(out=C.identb[:, :], in_=C.ident[:, :]), reads=[C.ident], writes=[C.identb])
    C.epsc = fw.sbuf("epsc", [128, 1], F32)
    fw.op(fw.dve, lambda e: e.memset(C.epsc[:, :], LN_EPS), writes=[C.epsc])
    for nm in ["causal", "acausal", "mask4", "triu", "onesdiv"]:
        setattr(C, nm, load_const(fw, C, nm))
    C.ps = [fw.psum(f"ps{i}", [128, 512], F32) for i in range(4)]
    C.psb = [fw.psum(f"psb{i}", [128, 1024], BF16) for i in range(2)]
    C.pacc = [fw.psum(f"pacc{i}", [128, 512], F32) for i in range(2)]
    okind = "ExternalOutput" if dbg else "Internal"
    C.Htok = fw.dram("Htok", [S, NTOKC], F32)
    C.HfT = fw.dram("HfT", [1536, S], F32)
    C.HtokS = fw.dram("HtokS", [TS, NTOKC], F32)
    C.HfTS = fw.dram("HfTS", [1536, TS], F32)
    C.gscr = fw.dram("gscr", [TS, 24], F32)
    C.X1 = fw.dram("X1", [S, D], F32, kind=okind)
    C.XS1 = fw.dram("XS1", [TS, D], F32, kind=okind)
    X2 = [fw.dram("X2a", [S, D], F32), C.out_y] if not dbg else [C.out_y]
    XS2 = [fw.dram("XS2a", [TS, D], F32), C.out_ys] if not dbg else [C.out_ys]
    mixn_s = fw.sbuf("mixn_s", [128, 8, TS], BF16)
    convTs = fw.sbuf("convTs", [128, 4, TS], BF16)
    x_src, xs_src = C.xp, C.xs
    for l in range(nlayers):
        fw.push()
        C.xin = [fw.sbuf(f"xin{i}", [128, D], F32) for i in range(2)]
        C.wbuf = [fw.sbuf(f"wbuf{i}", [128, 16, 512], BF16) for i in range(2)]
        C.hbuf = [fw.sbuf(f"hbuf{i}", [128, 512], F32) for i in range(4)]
        C.htmp = fw.sbuf("htmp", [128, 512], F32)
        xT = fw.sbuf("xT", [128, 16, S], BF16)
        phase_proj(fw, C, l, x_src, NT, C.Htok, C.HfT, xT, "p")
        phase_proj(fw, C, l, xs_src, 0, C.HtokS, C.HfTS, xT, "s")
        fw.pop()
        phase_nsa_sample(fw, C, l, C.HtokS, mixn_s)
        fw.push()
        C.kT = fw.sbuf("kT", [128, 4, S], BF16)
        C.V = fw.sbuf("V", [128, NT, 2, 256], BF16)
        C.kcmpT = fw.sbuf("kcmpT", [128, 2, 64], BF16)
        C.vcmp = fw.sbuf("vcmp", [64, 256], BF16)
        C.convT = fw.sbuf("convT", [128, 4, S], BF16)
        fw.push()
        phase_kv_prompt(fw, C, l)
        fw.pop()
        phase_conv(fw, C, l, C.HfT, S, None, C.out_conv_buf[l, :, :], C.convT)
        phase_conv(fw, C, l, C.HfTS, TS, C.state_conv[l, :, :], C.out_sconv[l, :, :], convTs)
        phase_mix_prompt(fw, C, l, x_src, C.X1, (xs_src, C.XS1, mixn_s, convTs))
        fw.pop()
        ntp = NT if not dbg else 1
        phase_peer(fw, C, l, [(C.X1, ntp, 128, X2[l]), (C.XS1, 1, TS, XS2[l])])
        x_src, xs_src = X2[l], XS2[l]
    fw.finish()
    return nc


def core_inputs(inputs, c):
    b = c // 2
    m = {"xp": np.ascontiguousarray(inputs["x_prompt"][b]), "xs": np.ascontiguousarray(inputs["x_sample"][c]),
         "pt": np.ascontiguousarray(inputs["page_table"][c].reshape(128, 1)).astype(np.int32)}
    for nm, key in [("c_cmp_k", "cache_cmp_k"), ("c_cmp_v", "cache_cmp_v"), ("c_slc_k", "cache_slc_k"), ("c_slc_v", "cache_slc_v")]:
        m[nm] = np.asarray(inputs[key]).reshape(DEPTH * 1280 * 8, 4096)
    m["c_win_k"] = np.ascontiguousarray(np.asarray(inputs["cache_win_k"])[:, c].reshape(DEPTH, 512, 256))
    m["c_win_v"] = np.ascontiguousarray(np.asarray(inputs["cache_win_v"])[:, c].reshape(DEPTH, 512, 256))
    m["state_conv"] = np.ascontiguousarray(np.asarray(inputs["state_conv"])[:, c])
    for nm, shp in INPUT_SPECS:
        m[nm] = np.asarray(inputs[nm])
    m.update(host_consts())
    return m


_NC_CACHE = {}


def kernel(**inputs):
    n = 8
    if "nc" not in _NC_CACHE:
        _NC_CACHE["nc"] = build("full")
    nc = _NC_CACHE["nc"]
    in_maps = [core_inputs(inputs, c) for c in range(n)]
    res = run_bass_kernel_spmd(nc, in_maps, core_ids=list(range(n))).results
    B = 4
    ev = [res[2 * b] for b in range(B)]
    y_prompt = np.stack([r["o_y"] for r in ev], 0).astype(np.float32)
    y_sample = np.stack([res[c]["o_ys"] for c in range(n)], 0).astype(np.float32)
    pkv = np.stack([r["o_pkv"] for r in ev], 0)
    def pk(a, rows=None):
        t = pkv[:, :, a]
        if rows is not None:
            t = t[:, :, rows:]
        t = np.transpose(t, (1, 0, 2, 3))
        return np.ascontiguousarray(t.reshape(t.shape[0], t.shape[1], t.shape[2], 2, 128)).astype(np.float32)
    p_conv = np.ascontiguousarray(np.stack([r["o_pconv"] for r in ev], 1)).astype(np.float32)
    p_gm = np.ascontiguousarray(np.stack([r["o_pgm"] for r in ev], 1)).astype(np.float32)
    skv = np.stack([res[c]["o_skv"] for c in range(n)], 0)
    def sk(a):
        t = np.transpose(skv[:, :, a], (1, 0, 2, 3))
        return np.ascontiguousarray(t.reshape(DEPTH, n, TS, 2, 128)).astype(np.float32)
    swin = np.stack([res[c]["o_swin"] for c in range(n)], 0)
    def sw(a):
        t = np.transpose(swin[:, :, a], (1, 0, 2, 3))
        return np.ascontiguousarray(t.reshape(DEPTH, n, 512, 2, 128)).astype(np.float32)
    s_conv = np.ascontiguousarray(np.stack([res[c]["o_sconv"] for c in range(n)], 1)).astype(np.float32)
    s_gm = np.ascontiguousarray(np.stack([res[c]["o_sgm"] for c in range(n)], 1)).astype(np.float32)
    return (y_prompt, y_sample, pk(0), pk(1), pk(2), pk(3), pk(4, S - 512), pk(5, S - 512), p_conv, p_gm,
            sk(0), sk(1), sk(2), sk(3), sw(0), sw(1), s_conv, s_gm)
```

```python
import numpy as np
from contextlib import ExitStack
import concourse.bass as bass
import concourse.mybir as mybir
from concourse.bass_utils import run_bass_kernel_spmd

F32 = mybir.dt.float32
BF16 = mybir.dt.bfloat16
I32 = mybir.dt.int32
U32 = mybir.dt.uint32
AF = mybir.ActivationFunctionType
ALU = mybir.AluOpType
AX = mybir.AxisListType


class Buf:
    def __init__(self, t, name):
        self.t = t
        self.name = name
        self.w = {}
        self.r = {}

    def __getitem__(self, idx):
        return self.t[idx]


class Eng:
    def __init__(self, fw, name, h, pe=False, ndma=0):
        self.fw = fw
        self.name = name
        self.h = h
        self.pe = pe
        self.sem = fw.new_sem(name + "_prog")
        self.n = 0
        self.seen = {}
        self.dma_sems = [fw.new_sem(f"{name}_d{i}") for i in range(ndma)]
        self.dma_n = 0

    def wait(self, tok):
        key, sem, val = tok
        if self.seen.get(key, 0) >= val:
            return
        self.h.wait_ge(sem, val)
        self.seen[key] = val


class FW:
    def __init__(self, nc):
        self.nc = nc
        self.es0 = ExitStack()
        self.es = self.es0
        self.es_stack = []
        self.sems = []
        self.pe = Eng(self, "pe", nc.tensor, pe=True)
        self.act = Eng(self, "act", nc.scalar, ndma=8)
        self.dve = Eng(self, "dve", nc.vector)
        self.pool = Eng(self, "pool", nc.gpsimd, ndma=24)
        self.sp = Eng(self, "sp", nc.sync, ndma=24)
        self.engs = [self.pe, self.act, self.dve, self.pool, self.sp]
        self.bufs = []

    def new_sem(self, name):
        s = self.es0.enter_context(self.nc.semaphore(name))
        self.sems.append(s)
        return s

    def push(self):
        self.es_stack.append((self.es, len(self.bufs)))
        self.es = ExitStack()

    def pop(self):
        self.barrier()
        self.es.close()
        self.es, nb = self.es_stack.pop()
        del self.bufs[nb:]

    def sbuf(self, name, shape, dtype=F32):
        self.uid = getattr(self, "uid", 0) + 1
        name = f"{name}_{self.uid}"
        t = self.es.enter_context(self.nc.sbuf_tensor(name, list(shape), dtype))
        b = Buf(t, name)
        self.bufs.append(b)
        return b

    def psum(self, name, shape, dtype=F32):
        t = self.es.enter_context(self.nc.psum_tensor(name, list(shape), dtype))
        b = Buf(t, name)
        self.bufs.append(b)
        return b

    def dram(self, name, shape, dtype=F32, kind="Internal"):
        t = self.nc.dram_tensor(name, list(shape), dtype, kind=kind).ap()
        b = Buf(t, name)
        self.bufs.append(b)
        return b

    def view(self, ap, name="v"):
        b = Buf(ap, name)
        self.bufs.append(b)
        return b

    def _waits(self, eng, reads, writes, join, is_dma):
        for b in reads:
            for tok in b.w.values():
                if tok[0] == eng.name and not is_dma and eng.pe:
                    continue
                eng.wait(tok)
        for b in writes:
            if not join:
                for tok in b.w.values():
                    if tok[0] == eng.name and not is_dma:
                        continue
                    eng.wait(tok)
            for tok in b.r.values():
                if tok[0] == eng.name and not is_dma:
                    continue
                eng.wait(tok)

    def _record(self, tok, reads, writes, join):
        for b in writes:
            if join:
                b.w[tok[0]] = tok
            else:
                b.w = {tok[0]: tok}
            b.r = {}
        for b in reads:
            if b not in writes:
                b.r[tok[0]] = tok

    def op(self, eng, fn, reads=(), writes=(), join=False):
        self._waits(eng, reads, writes, join, False)
        inst = fn(eng.h)
        eng.n += 1
        inst.then_inc(eng.sem, 1)
        tok = (eng.name, eng.sem, eng.n)
        self._record(tok, reads, writes, join)
        return inst

    def dma(self, q, fn, reads=(), writes=(), join=False):
        self._waits(q, reads, writes, join, True)
        k = len(q.dma_sems)
        i = q.dma_n % k
        gen = q.dma_n // k
        sem = q.dma_sems[i]
        key = f"{q.name}_d{i}"
        if gen > 0:
            q.wait((key, sem, 16 * gen))
        inst = fn(q.h)
        inst.then_inc(sem, 16)
        q.dma_n += 1
        tok = (key, sem, 16 * (gen + 1))
        self._record(tok, reads, writes, join)
        return inst

    def barrier(self):
        toks = []
        for e in self.engs:
            if e.n:
                toks.append((e.name, e.sem, e.n))
            k = len(e.dma_sems)
            for i in range(min(k, e.dma_n)):
                cnt = (e.dma_n - 1 - i) // k + 1
                toks.append((f"{e.name}_d{i}", e.dma_sems[i], 16 * cnt))
        for e in self.engs:
            for tok in toks:
                if tok[0] == e.name:
                    continue
                e.wait(tok)
        for b in self.bufs:
            b.w = {}
            b.r = {}

    def finish(self):
        self.barrier()


D = 2048
S = 2048
NT = S // 128
DEPTH = 2
TS = 8
IN_W = 4632
TOK0, TOK1 = 512, 3608
NTOKC = TOK1 - TOK0
ALPHA = float((2 * DEPTH) ** 0.25)
LN_EPS = 1e-5
GELU_C = 0.7978845608028654
NEG = -30000.0


class Ctx:
    pass


def emit_gelu(fw, eng_dve, eng_act, out_ap, in_ap, tmp_ap, bufs_r, bufs_w, tmpbuf):
    fw.op(eng_act, lambda e: e.activation(out=tmp_ap, in_=in_ap, func=AF.Square), reads=bufs_r, writes=[tmpbuf])
    fw.op(eng_dve, lambda e: e.tensor_scalar(out=tmp_ap, in0=tmp_ap, scalar1=0.044715, scalar2=1.0, op0=ALU.mult, op1=ALU.add),
          reads=[tmpbuf], writes=[tmpbuf])
    fw.op(eng_dve, lambda e: e.tensor_tensor(out=tmp_ap, in0=tmp_ap, in1=in_ap, op=ALU.mult), reads=bufs_r + [tmpbuf], writes=[tmpbuf])
    fw.op(eng_act, lambda e: e.activation(out=tmp_ap, in_=tmp_ap, func=AF.Sigmoid, scale=2.0 * GELU_C), reads=[tmpbuf], writes=[tmpbuf])
    fw.op(eng_dve, lambda e: e.tensor_tensor(out=out_ap, in0=tmp_ap, in1=in_ap, op=ALU.mult), reads=bufs_r + [tmpbuf], writes=bufs_w)


def phase_proj(fw, C, l, x_src, nt, Htok, HfT, xT, tagp):
    nc = fw.nc
    ntok = nt * 128 if nt > 0 else TS
    P = 128 if nt > 0 else TS
    ntile = max(nt, 1)
    for t in range(ntile):
        xt = C.xin[t % 2]
        fw.dma(fw.sp, lambda h: h.dma_start(out=xt[0:P, :], in_=x_src[t * 128:t * 128 + P, :]), reads=[x_src], writes=[xt])
        for cb in range(4):
            ps = C.ps[(t * 4 + cb) % 4]
            for k in range(4):
                c = cb * 4 + k
                fw.op(fw.pe, lambda e: e.transpose(out=ps[:, k * 128:k * 128 + P], in_=xt[0:P, c * 128:(c + 1) * 128], identity=C.ident[0:P, 0:P]),
                      reads=[xt, C.ident], writes=[ps], join=(k > 0))
            eng = fw.dve if cb % 2 == 0 else fw.act
            src = ps[:, :].rearrange("p (k q) -> p k q", k=4)[:, :, 0:P]
            dst = xT[:, cb * 4:(cb + 1) * 4, t * 128:t * 128 + P]
            if eng is fw.dve:
                fw.op(eng, lambda e: e.tensor_copy(out=dst, in_=src), reads=[ps], writes=[xT], join=True)
            else:
                fw.op(eng, lambda e: e.copy(out=dst, in_=src), reads=[ps], writes=[xT], join=True)
    w_in = C.w_in
    ncol = [(TOK0 + j * 512, min(512, TOK1 - (TOK0 + j * 512))) for j in range((NTOKC + 511) // 512)]
    it = 0
    for j, (c0, cw) in enumerate(ncol):
        wb = C.wbuf[j % 2]
        fw.dma(fw.pool, lambda h: h.dma_start(out=wb[:, :, 0:cw], in_=w_in[l, :, c0:c0 + cw].rearrange("(c p) n -> p c n", p=128)),
               reads=[w_in], writes=[wb])
        for t in range(ntile):
            ps = C.ps[it % 4]
            for c in range(16):
                fw.op(fw.pe, lambda e: e.matmul(ps[0:P, 0:cw], lhsT=xT[:, c, t * 128:t * 128 + P], rhs=wb[:, c, 0:cw], start=(c == 0), stop=(c == 15)),
                      reads=[xT, wb], writes=[ps], join=(c > 0))
            hb = C.hbuf[it % 4]
            if it % 2 == 0:
                fw.op(fw.dve, lambda e: e.tensor_copy(out=hb[0:P, 0:cw], in_=ps[0:P, 0:cw]), reads=[ps], writes=[hb])
            else:
                fw.op(fw.act, lambda e: e.copy(out=hb[0:P, 0:cw], in_=ps[0:P, 0:cw]), reads=[ps], writes=[hb])
            fw.dma(fw.sp, lambda h: h.dma_start(out=Htok[t * 128:t * 128 + P, c0 - TOK0:c0 - TOK0 + cw], in_=hb[0:P, 0:cw]),
                   reads=[hb], writes=[Htok], join=True)
            it += 1
    fcols = [(0, 0), (128, 128), (256, 256), (384, 384)] + [(3608 + i * 128, 512 + i * 128) for i in range(8)]
    TB = 512 if nt > 0 else TS
    ntb = max(ntok // 512, 1)
    for j, (c0, r0) in enumerate(fcols):
        wb = C.wbuf[j % 2]
        fw.dma(fw.pool, lambda h: h.dma_start(out=wb[:, :, 0:128], in_=w_in[l, :, c0:c0 + 128].rearrange("(c p) n -> p c n", p=128)),
               reads=[w_in], writes=[wb])
        for tb in range(ntb):
            ps = C.ps[it % 4]
            for c in range(16):
                fw.op(fw.pe, lambda e: e.matmul(ps[:, 0:TB], lhsT=wb[:, c, 0:128], rhs=xT[:, c, tb * 512:tb * 512 + TB], start=(c == 0), stop=(c == 15)),
                      reads=[xT, wb], writes=[ps], join=(c > 0))
            hb = C.hbuf[it % 4]
            if r0 < 512:
                tmp = C.htmp
                emit_gelu(fw, fw.dve, fw.act, hb[:, 0:TB], ps[:, 0:TB], tmp[:, 0:TB], [ps], [hb], tmp)
            elif it % 2 == 0:
                fw.op(fw.dve, lambda e: e.tensor_copy(out=hb[:, 0:TB], in_=ps[:, 0:TB]), reads=[ps], writes=[hb])
            else:
                fw.op(fw.act, lambda e: e.copy(out=hb[:, 0:TB], in_=ps[:, 0:TB]), reads=[ps], writes=[hb])
            fw.dma(fw.sp, lambda h: h.dma_start(out=HfT[r0:r0 + 128, tb * 512:tb * 512 + TB], in_=hb[:, 0:TB]),
                   reads=[hb], writes=[HfT], join=True)
            it += 1


def host_consts():
    c = {}
    c["ident"] = np.eye(128, dtype=np.float32)
    half = 64
    inv = (np.float32(10000.0) ** (-np.arange(half, dtype=np.float32) / np.float32(half))).astype(np.float32)
    def rope_tab(pos):
        ang = pos.astype(np.float32)[:, None] * inv[None, :]
        cs, sn = np.cos(ang).astype(np.float32), np.sin(ang).astype(np.float32)
        sc = np.float32(128 ** -0.5)
        return np.stack([cs, sn, cs * sc, sn * sc], axis=1).astype(np.float32)
    c["rope_p"] = rope_tab(np.arange(S))
    c["rope_s"] = rope_tab(16384 + np.arange(TS))
    q = np.arange(128)[:, None, None]
    t = np.arange(NT)[None, :, None]
    n = np.arange(64)[None, None, :]
    qpos = 128 * t + q
    c["cmpbias"] = np.where(32 * n + 31 <= qpos, 0.0, NEG).astype(np.float32)
    c["cmpvalid"] = (qpos[:, :, 0] >= 31).astype(np.float32)
    m = np.arange(32)[None, None, :]
    cur = qpos // 64
    valid = 64 * m <= qpos
    forced = (m == 0) | (m == cur) | (m == cur - 1)
    c["selA"] = (valid & ~forced).astype(np.float32)
    c["selB"] = np.where(valid, np.where(forced, 1.0e4, 0.0), -1.0e9).astype(np.float32)
    qq = np.arange(128)[:, None]; kk = np.arange(128)[None, :]
    c["causal"] = np.where(kk <= qq, 0.0, NEG).astype(np.float32)
    c["acausal"] = np.where(kk >= qq, 0.0, NEG).astype(np.float32)
    tok = np.arange(128)[:, None]
    c["mask4"] = (tok // 32 == np.arange(4)[None, :]).astype(np.float32)
    c["maskpad"] = (np.arange(64)[None, None, :] == 4 * np.arange(NT)[None, :, None] + (tok // 32)[:, :, None]).astype(np.float32)
    c["triu"] = (qq <= kk).astype(np.float32)
    c["onesdiv"] = np.full((128, 128), 1.0 / 128, np.float32)
    c["iota16"] = np.tile(np.arange(16, dtype=np.float32)[None, :], (128, 1))
    c.update(sample_consts())
    return c


CONST_SHAPES = {"ident": [128, 128], "rope_p": [S, 4, 64], "rope_s": [TS, 4, 64], "cmpbias": [128, NT, 64], "cmpvalid": [128, NT],
                "selA": [128, NT, 32], "selB": [128, NT, 32], "causal": [128, 128], "acausal": [128, 128], "mask4": [128, 4],
                "maskpad": [128, NT, 64], "triu": [128, 128], "onesdiv": [128, 128], "iota16": [128, 16]}


def load_const(fw, C, name, dtype=F32):
    shp = CONST_SHAPES[name]
    b = fw.sbuf("c_" + name, shp, F32)
    src = C.cd[name]
    idx = tuple(slice(None) for _ in shp)
    fw.dma(fw.sp, lambda h: h.dma_start(out=b[idx], in_=src[idx]), reads=[src], writes=[b])
    return b


def load_colvec(fw, C, vec_ap, n, name, srcbuf):
    rows = fw.sbuf(name + "_r", [n, 128], F32)
    fw.dma(fw.sp, lambda h: h.dma_start(out=rows[:, :], in_=vec_ap.rearrange("(j p) -> j p", p=128)), reads=[srcbuf], writes=[rows])
    ps = C.ps[0]
    fw.op(fw.pe, lambda e: e.transpose(out=ps[:, 0:n], in_=rows[0:n, :], identity=C.ident[0:n, 0:n]), reads=[rows, C.ident], writes=[ps])
    col = fw.sbuf(name, [128, n], F32)
    fw.op(fw.dve, lambda e: e.tensor_copy(out=col[:, :], in_=ps[:, 0:n]), reads=[ps], writes=[col])
    return col


def emit_rope(fw, x1, x2, cs, sn, tmp, shape, rbufs, xbuf, tmpbuf):
    nd = len(shape)
    def bc(a):
        v = a
        for _ in range(nd - 2):
            v = v.unsqueeze(1)
        return v.to_broadcast(list(shape))
    t1, t2, t3, t4 = tmp
    fw.op(fw.dve, lambda e: e.tensor_tensor(out=t1, in0=x1, in1=bc(cs), op=ALU.mult), reads=[xbuf] + rbufs, writes=[tmpbuf[0]])
    fw.op(fw.dve, lambda e: e.tensor_tensor(out=t2, in0=x2, in1=bc(sn), op=ALU.mult), reads=[xbuf] + rbufs, writes=[tmpbuf[1]])
    fw.op(fw.pool, lambda e: e.tensor_tensor(out=t3, in0=x2, in1=bc(cs), op=ALU.mult), reads=[xbuf] + rbufs, writes=[tmpbuf[2]])
    fw.op(fw.pool, lambda e: e.tensor_tensor(out=t4, in0=x1, in1=bc(sn), op=ALU.mult), reads=[xbuf] + rbufs, writes=[tmpbuf[3]])
    fw.op(fw.dve, lambda e: e.tensor_tensor(out=x1, in0=t1, in1=t2, op=ALU.subtract), reads=[tmpbuf[0], tmpbuf[1], tmpbuf[2], tmpbuf[3]], writes=[xbuf])
    fw.op(fw.dve, lambda e: e.tensor_tensor(out=x2, in0=t3, in1=t4, op=ALU.add), reads=[tmpbuf[2], tmpbuf[3]], writes=[xbuf])


def phase_kv_prompt(fw, C, l):
    P = 128
    C.maskpad = load_const(fw, C, "maskpad")
    wcol = fw.sbuf("wcol", [128, 4], F32)
    for wi, wsrc in enumerate([C.cmp_wk, C.cmp_wv]):
        for g in range(2):
            for blk in range(4):
                fw.dma(fw.sp, lambda h: h.dma_start(out=wcol[blk * 32:(blk + 1) * 32, wi * 2 + g:wi * 2 + g + 1],
                                                    in_=wsrc[l, g, :].rearrange("(j o) -> j o", o=1)), reads=[wsrc], writes=[wcol], join=True)
    Wck = fw.sbuf("Wck", [128, 2, 4], BF16)
    WcvPad = fw.sbuf("WcvPad", [128, 2, NT * 64], BF16)
    for g in range(2):
        fw.op(fw.dve, lambda e: e.tensor_scalar(out=Wck[:, g, :], in0=C.mask4[:, :], scalar1=wcol[:, g:g + 1], scalar2=None, op0=ALU.mult),
              reads=[C.mask4, wcol], writes=[Wck], join=True)
        fw.op(fw.dve, lambda e: e.tensor_scalar(out=WcvPad[:, g, :], in0=C.maskpad[:, :, :].rearrange("p t n -> p (t n)"), scalar1=wcol[:, 2 + g:3 + g], scalar2=None, op0=ALU.mult),
              reads=[C.maskpad, wcol], writes=[WcvPad], join=True)
    vacc = fw.sbuf("vacc", [64, 256], F32)
    kvin = [fw.sbuf(f"kvin{i}", [128, 1536], F32) for i in range(2)]
    kvb = [fw.sbuf(f"kvb{i}", [128, 1536], BF16) for i in range(2)]
    rtab = [fw.sbuf(f"rtab{i}", [128, 4, 64], F32) for i in range(2)]
    rtmp = [fw.sbuf(f"rtmp{i}", [128, 384], F32) for i in range(4)]
    for t in range(NT):
        kv = kvin[t % 2]
        kb = kvb[t % 2]
        rt = rtab[t % 2]
        fw.dma(fw.sp, lambda h: h.dma_start(out=kv[:, :], in_=C.Htok[t * 128:(t + 1) * 128, 1536:3072]), reads=[C.Htok], writes=[kv])
        fw.dma(fw.sp, lambda h: h.dma_start(out=rt[:, :, :], in_=C.cd["rope_p"][t * 128:(t + 1) * 128, :, :]), reads=[C.cd["rope_p"]], writes=[rt])
        kview = kv[:, :].rearrange("p (a v g h d) -> p a v g h d", a=3, v=2, g=2, h=2, d=64)
        x1 = kview[:, :, 0, :, 0, :]
        x2 = kview[:, :, 0, :, 1, :]
        tv = [r[:, :].rearrange("p (a g d) -> p a g d", a=3, g=2) for r in rtmp]
        emit_rope(fw, x1, x2, rt[:, 0, :], rt[:, 1, :], tv, [128, 3, 2, 64], [rt], kv, rtmp)
        for a in range(6):
            fw.dma(fw.sp, lambda h: h.dma_start(out=C.out_kv[l, a, t * 128:(t + 1) * 128, :], in_=kv[:, a * 256:(a + 1) * 256]),
                   reads=[kv], writes=[C.out_kv], join=True)
        fw.op(fw.act, lambda e: e.copy(out=kb[:, :], in_=kv[:, :]), reads=[kv], writes=[kb])
        psb = C.psb[t % 2]
        for i, (a, g) in enumerate([(1, 0), (1, 1), (2, 0), (2, 1)]):
            c0 = a * 512 + g * 128
            fw.op(fw.pe, lambda e: e.transpose(out=psb[:, i * 128:(i + 1) * 128], in_=kb[:, c0:c0 + 128], identity=C.identb[:, :]),
                  reads=[kb, C.identb], writes=[psb], join=(i > 0))
        fw.op(fw.dve, lambda e: e.tensor_copy(out=C.kT[:, :, t * 128:(t + 1) * 128], in_=psb[:, 0:512].rearrange("p (i q) -> p i q", i=4)),
              reads=[psb], writes=[C.kT], join=True)
        ps = C.ps[t % 4]
        for g in range(2):
            fw.op(fw.pe, lambda e: e.matmul(ps[:, g * 4:(g + 1) * 4], lhsT=kb[:, g * 128:(g + 1) * 128], rhs=Wck[:, g, :], start=True, stop=True),
                  reads=[kb, Wck], writes=[ps], join=(g > 0))
        for g in range(2):
            fw.op(fw.pe, lambda e: e.matmul(ps[0:64, 128 + g * 128:256 + g * 128], lhsT=WcvPad[:, g, t * 64:(t + 1) * 64], rhs=kb[:, 256 + g * 128:384 + g * 128], start=True, stop=True),
                  reads=[kb, WcvPad], writes=[ps], join=True)
        fw.op(fw.dve, lambda e: e.tensor_copy(out=C.kcmpT[:, :, t * 4:(t + 1) * 4], in_=ps[:, 0:8].rearrange("p (g n) -> p g n", g=2)),
              reads=[ps], writes=[C.kcmpT], join=True)
        if t == 0:
            fw.op(fw.dve, lambda e: e.tensor_copy(out=vacc[:, :], in_=ps[0:64, 128:384]), reads=[ps], writes=[vacc])
        else:
            fw.op(fw.dve, lambda e: e.tensor_tensor(out=vacc[:, :], in0=vacc[:, :], in1=ps[0:64, 128:384], op=ALU.add), reads=[ps, vacc], writes=[vacc])
        vsrc = kb[:, 512:1536].rearrange("p (a v c) -> p a v c", a=2, v=2)[:, :, 1, :]
        fw.op(fw.pool, lambda e: e.tensor_copy(out=C.V[:, t, :, :], in_=vsrc), reads=[kb], writes=[C.V], join=True)
    fw.op(fw.dve, lambda e: e.tensor_copy(out=C.vcmp[:, :], in_=vacc[:, :]), reads=[vacc], writes=[C.vcmp])


def emit_ln_partition(fw, C, y, blk_cols, g_ap, b_ap, gb_bufs, out_ap_fn, tmpA, tmpB, func=None):
    ncols = blk_cols
    for c0 in range(0, ncols, 512):
        cw = min(512, ncols - c0)
        ps1 = C.ps[0]
        ps2 = C.ps[1]
        ysl = y[:, c0:c0 + cw]
        fw.op(fw.pe, lambda e: e.matmul(ps1[:, 0:cw], lhsT=C.onesdiv[:, :], rhs=ysl, start=True, stop=True), reads=[C.onesdiv, y], writes=[ps1])
        fw.op(fw.dve, lambda e: e.tensor_tensor(out=ysl, in0=ysl, in1=ps1[:, 0:cw], op=ALU.subtract), reads=[ps1, y], writes=[y])
        fw.op(fw.act, lambda e: e.activation(out=tmpA[:, 0:cw], in_=ysl, func=AF.Square), reads=[y], writes=[tmpA])
        fw.op(fw.pe, lambda e: e.matmul(ps2[:, 0:cw], lhsT=C.onesdiv[:, :], rhs=tmpA[:, 0:cw], start=True, stop=True), reads=[C.onesdiv, tmpA], writes=[ps2])
        fw.op(fw.act, lambda e: e.activation(out=tmpB[:, 0:cw], in_=ps2[:, 0:cw], func=AF.Sqrt, bias=C.epsc[:, 0:1], scale=1.0), reads=[ps2, C.epsc], writes=[tmpB])
        fw.op(fw.dve, lambda e: e.reciprocal(out=tmpB[:, 0:cw], in_=tmpB[:, 0:cw]), reads=[tmpB], writes=[tmpB])
        fw.op(fw.dve, lambda e: e.tensor_tensor(out=tmpA[:, 0:cw], in0=ysl, in1=tmpB[:, 0:cw], op=ALU.mult), reads=[y, tmpB], writes=[tmpA])
        oap, obuf = out_ap_fn(c0, cw)
        if func is None:
            fw.op(fw.dve, lambda e: e.tensor_scalar(out=oap, in0=tmpA[:, 0:cw], scalar1=g_ap, scalar2=b_ap, op0=ALU.mult, op1=ALU.add),
                  reads=[tmpA] + gb_bufs, writes=[obuf], join=True)
        else:
            fw.op(fw.dve, lambda e: e.tensor_scalar(out=tmpA[:, 0:cw], in0=tmpA[:, 0:cw], scalar1=g_ap, scalar2=b_ap, op0=ALU.mult, op1=ALU.add),
                  reads=[tmpA] + gb_bufs, writes=[tmpA])
            fw.op(fw.act, lambda e: e.activation(out=oap, in_=tmpA[:, 0:cw], func=func), reads=[tmpA], writes=[obuf], join=True)


def phase_conv(fw, C, l, HfT, T, hist_src, out_conv, convT):
    fw.push()
    dwr = fw.sbuf("dwr", [31, 512], F32)
    fw.dma(fw.sp, lambda h: h.dma_start(out=dwr[:, :], in_=C.conv_dw[l, :, :]), reads=[C.conv_dw], writes=[dwr])
    dwT = fw.sbuf("dwT", [128, 4, 31], F32)
    ps = C.ps[2]
    for cc in range(4):
        fw.op(fw.pe, lambda e: e.transpose(out=ps[:, cc * 32:cc * 32 + 31], in_=dwr[0:31, cc * 128:(cc + 1) * 128], identity=C.ident[0:31, 0:31]),
              reads=[dwr, C.ident], writes=[ps], join=(cc > 0))
    fw.op(fw.dve, lambda e: e.tensor_copy(out=dwT[:, :, :], in_=ps[:, 0:128].rearrange("p (c k) -> p c k", c=4)[:, :, 0:31]), reads=[ps], writes=[dwT])
    dbc = load_colvec(fw, C, C.conv_db[l, :], 4, "dbc", C.conv_db)
    lng = load_colvec(fw, C, C.conv_ln_g[l, :], 4, "clng", C.conv_ln_g)
    lnb = load_colvec(fw, C, C.conv_ln_b[l, :], 4, "clnb", C.conv_ln_b)
    pw = fw.sbuf("pw", [128, 4, 512], BF16)
    fw.dma(fw.pool, lambda h: h.dma_start(out=pw[:, :, :], in_=C.conv_pw[l, :, :].rearrange("(c p) n -> p c n", p=128)), reads=[C.conv_pw], writes=[pw])
    zT = fw.sbuf("zT", [128, 4, T], BF16)
    pcs = fw.sbuf("pcs", [30, 512], F32)
    psc = C.ps[3]
    sets = []
    for i in range(2):
        sets.append(dict(ca=fw.sbuf(f"cca{i}", [128, T], F32), cb=fw.sbuf(f"ccb{i}", [128, T], F32),
                         cseq=fw.sbuf(f"cseq{i}", [128, 30 + T], F32), y=fw.sbuf(f"cy{i}", [128, T], F32)))
    tmpA = fw.sbuf("ctmpA", [128, 512], F32)
    tmpB = fw.sbuf("ctmpB", [128, 512], F32)
    hs = None
    if hist_src is not None:
        hs = fw.sbuf("hist_r", [30, 512], F32)
        fw.dma(fw.sp, lambda h: h.dma_start(out=hs[:, :], in_=hist_src), reads=[C.state_conv], writes=[hs])
    for cc in range(4):
        s_ = sets[cc % 2]
        ca, cb, cseq, y = s_["ca"], s_["cb"], s_["cseq"], s_["y"]
        fw.dma(fw.sp, lambda h: h.dma_start(out=ca[:, :], in_=HfT[512 + cc * 128:640 + cc * 128, 0:T]), reads=[HfT], writes=[ca])
        fw.dma(fw.sp, lambda h: h.dma_start(out=cb[:, :], in_=HfT[1024 + cc * 128:1152 + cc * 128, 0:T]), reads=[HfT], writes=[cb])
        fw.op(fw.act, lambda e: e.activation(out=cb[:, :], in_=cb[:, :], func=AF.Sigmoid), reads=[cb], writes=[cb])
        if hs is None:
            fw.op(fw.pool, lambda e: e.memset(cseq[:, 0:30], 0.0), writes=[cseq])
        else:
            pst = C.ps[0]
            fw.op(fw.pe, lambda e: e.transpose(out=pst[:, 0:30], in_=hs[0:30, cc * 128:(cc + 1) * 128], identity=C.ident[0:30, 0:30]), reads=[hs, C.ident], writes=[pst])
            fw.op(fw.dve, lambda e: e.tensor_copy(out=cseq[:, 0:30], in_=pst[:, 0:30]), reads=[pst], writes=[cseq])
        fw.op(fw.dve, lambda e: e.tensor_tensor(out=cseq[:, 30:30 + T], in0=ca[:, :], in1=cb[:, :], op=ALU.mult), reads=[ca, cb], writes=[cseq], join=True)
        fw.op(fw.pe, lambda e: e.transpose(out=psc[0:30, cc * 128:(cc + 1) * 128], in_=cseq[:, T:T + 30], identity=C.ident[:, :]),
              reads=[cseq, C.ident], writes=[psc], join=(cc > 0))
        eng = fw.dve
        fw.op(eng, lambda e: e.tensor_scalar(out=y[:, :], in0=cseq[:, 0:T], scalar1=dwT[:, cc, 0:1], scalar2=dbc[:, cc:cc + 1], op0=ALU.mult, op1=ALU.add),
              reads=[cseq, dwT, dbc], writes=[y])
        for k in range(1, 31):
            fw.op(eng, lambda e: e.scalar_tensor_tensor(out=y[:, :], in0=cseq[:, k:k + T], scalar=dwT[:, cc, k:k + 1], in1=y[:, :], op0=ALU.mult, op1=ALU.add),
                  reads=[cseq, dwT, y], writes=[y])
        emit_ln_partition(fw, C, y, T, lng[:, cc:cc + 1], lnb[:, cc:cc + 1], [lng, lnb],
                          lambda c0, cw: (zT[:, cc, c0:c0 + cw], zT), tmpA, tmpB, func=AF.Silu)
    fw.op(fw.dve, lambda e: e.tensor_copy(out=pcs[:, :], in_=psc[0:30, 0:512]), reads=[psc], writes=[pcs])
    fw.dma(fw.sp, lambda h: h.dma_start(out=out_conv, in_=pcs[:, :]), reads=[pcs], writes=[C.out_conv_buf])
    it = 0
    for co in range(4):
        for c0 in range(0, T, 512):
            cw = min(512, T - c0)
            ps = C.ps[it % 4]
            for cc in range(4):
                fw.op(fw.pe, lambda e: e.matmul(ps[:, 0:cw], lhsT=pw[:, cc, co * 128:(co + 1) * 128], rhs=zT[:, cc, c0:c0 + cw], start=(cc == 0), stop=(cc == 3)),
                      reads=[pw, zT], writes=[ps], join=(cc > 0))
            if it % 2 == 0:
                fw.op(fw.dve, lambda e: e.tensor_copy(out=convT[:, co, c0:c0 + cw], in_=ps[:, 0:cw]), reads=[ps], writes=[convT], join=True)
            else:
                fw.op(fw.act, lambda e: e.copy(out=convT[:, co, c0:c0 + cw], in_=ps[:, 0:cw]), reads=[ps], writes=[convT], join=True)
            it += 1
    fw.pop()


def emit_ln_free(fw, C, x, P, g_bc, b_bc, junk, stat):
    xs = x[0:P, :]
    fw.op(fw.dve, lambda e: e.tensor_reduce(out=stat[0:P, 0:1], in_=xs, axis=AX.X, op=ALU.add), reads=[x], writes=[stat])
    fw.op(fw.dve, lambda e: e.tensor_scalar(out=stat[0:P, 1:2], in0=stat[0:P, 0:1], scalar1=1.0 / D, scalar2=None, op0=ALU.mult), reads=[stat], writes=[stat])
    fw.op(fw.dve, lambda e: e.tensor_scalar(out=xs, in0=xs, scalar1=stat[0:P, 1:2], scalar2=None, op0=ALU.subtract), reads=[x, stat], writes=[x])
    fw.op(fw.act, lambda e: e.activation(out=junk[0:P, :], in_=xs, func=AF.Square, accum_out=stat[0:P, 2:3]), reads=[x], writes=[junk, stat])
    fw.op(fw.act, lambda e: e.activation(out=stat[0:P, 3:4], in_=stat[0:P, 2:3], func=AF.Sqrt, bias=C.epsc[0:P, 0:1], scale=1.0 / D), reads=[stat, C.epsc], writes=[stat])
    fw.op(fw.dve, lambda e: e.reciprocal(out=stat[0:P, 4:5], in_=stat[0:P, 3:4]), reads=[stat], writes=[stat])
    fw.op(fw.dve, lambda e: e.scalar_tensor_tensor(out=xs, in0=xs, scalar=stat[0:P, 4:5], in1=g_bc[0:P, :], op0=ALU.mult, op1=ALU.mult), reads=[x, stat, g_bc], writes=[x])
    fw.op(fw.pool, lambda e: e.tensor_tensor(out=xs, in0=xs, in1=b_bc[0:P, :], op=ALU.add), reads=[x, b_bc], writes=[x])


def load_bcast(fw, name, vec_ap, n, srcbuf, P=128):
    b = fw.sbuf(name, [128, n], F32)
    fw.dma(fw.sp, lambda h: h.dma_start(out=b[0:P, :], in_=vec_ap.partition_broadcast(P)), reads=[srcbuf], writes=[b])
    return b


def emit_gmlp(fw, C, G, l, t, P, Htok, HfT, mixg, out_gm_ap):
    gv = G["gvin"][t % 2]
    gt = G["gtmp"]
    st = G["gstat"]
    fw.dma(fw.sp, lambda h: h.dma_start(out=gv[0:P, :], in_=Htok[t * 128:t * 128 + P, 0:512]), reads=[Htok], writes=[gv])
    emit_gelu(fw, fw.dve, fw.act, gv[0:P, :], gv[0:P, :], gt[0:P, :], [gv], [gv], gt)
    g3 = gv[0:P, :].rearrange("p (h c) -> p h c", h=4)
    t3 = gt[0:P, :].rearrange("p (h c) -> p h c", h=4)
    fw.op(fw.dve, lambda e: e.tensor_reduce(out=st[0:P, 0:4], in_=g3, axis=AX.X, op=ALU.add), reads=[gv], writes=[st])
    fw.op(fw.dve, lambda e: e.tensor_scalar(out=st[0:P, 4:8], in0=st[0:P, 0:4], scalar1=1.0 / 128, scalar2=None, op0=ALU.mult), reads=[st], writes=[st])
    fw.op(fw.dve, lambda e: e.tensor_tensor(out=g3, in0=g3, in1=st[0:P, 4:8].unsqueeze(2).to_broadcast([P, 4, 128]), op=ALU.subtract), reads=[gv, st], writes=[gv])
    fw.op(fw.dve, lambda e: e.tensor_tensor(out=t3, in0=g3, in1=g3, op=ALU.mult), reads=[gv], writes=[gt])
    fw.op(fw.dve, lambda e: e.tensor_reduce(out=st[0:P, 8:12], in_=t3, axis=AX.X, op=ALU.add), reads=[gt], writes=[st])
    fw.op(fw.act, lambda e: e.activation(out=st[0:P, 12:16], in_=st[0:P, 8:12], func=AF.Sqrt, bias=C.epsc[0:P, 0:1], scale=1.0 / 128), reads=[st, C.epsc], writes=[st])
    fw.op(fw.dve, lambda e: e.reciprocal(out=st[0:P, 16:20], in_=st[0:P, 12:16]), reads=[st], writes=[st])
    fw.op(fw.dve, lambda e: e.tensor_tensor(out=g3, in0=g3, in1=st[0:P, 16:20].unsqueeze(2).to_broadcast([P, 4, 128]), op=ALU.mult), reads=[gv, st], writes=[gv])
    fw.op(fw.dve, lambda e: e.tensor_tensor(out=gv[0:P, :], in0=gv[0:P, :], in1=G["gmg"][0:P, :], op=ALU.mult), reads=[gv, G["gmg"]], writes=[gv])
    fw.op(fw.pool, lambda e: e.tensor_tensor(out=gv[0:P, :], in0=gv[0:P, :], in1=G["gmb"][0:P, :], op=ALU.add), reads=[gv, G["gmb"]], writes=[gv])
    if out_gm_ap is not None:
        fw.dma(fw.sp, lambda h: h.dma_start(out=out_gm_ap, in_=gv[0:P, :]), reads=[gv], writes=[C.out_gm_buf], join=True)
    vnb = G["vnb"]
    fw.op(fw.act, lambda e: e.copy(out=vnb[0:P, :], in_=gv[0:P, :]), reads=[gv], writes=[vnb])
    ps = C.ps[0]
    for h in range(4):
        fw.op(fw.pe, lambda e: e.matmul(ps[:, h * 128:h * 128 + P], lhsT=vnb[0:P, h * 128:(h + 1) * 128], rhs=G["trilWT"][0:P, h, 0:P], start=True, stop=True),
              reads=[vnb, G["trilWT"]], writes=[ps], join=(h > 0))
    ut = G["ut"][t % 2]
    fw.dma(fw.sp, lambda h_: h_.dma_start(out=ut[:, :, 0:P], in_=HfT[0:512, t * 128:t * 128 + P].rearrange("(h c) q -> c h q", h=4)), reads=[HfT], writes=[ut])
    sv = gt[:, :].rearrange("p (h c) -> p h c", h=4)[:, :, 0:P]
    fw.op(fw.dve, lambda e: e.tensor_tensor(out=sv, in0=ps[:, 0:512].rearrange("p (h c) -> p h c", h=4)[:, :, 0:P], in1=G["bsb"][:, :, 0:P], op=ALU.add),
          reads=[ps, G["bsb"]], writes=[gt])
    fw.op(fw.dve, lambda e: e.tensor_tensor(out=mixg[:, :, 0:P], in0=sv, in1=ut[:, :, 0:P], op=ALU.mult), reads=[gt, ut], writes=[mixg])


def gmlp_consts(fw, C, l, P):
    G = {}
    G["gvin"] = [fw.sbuf("gvin0", [128, 512], F32)] * 2
    G["gtmp"] = fw.sbuf("gtmp", [128, 512], F32)
    G["gstat"] = fw.sbuf("gstat", [128, 20], F32)
    G["vnb"] = fw.sbuf("vnb", [128, 512], BF16)
    G["ut"] = [fw.sbuf("ut0", [128, 4, 128], F32)] * 2
    G["gmg"] = load_bcast(fw, "gmg", C.gm_ln_g[l, :], 512, C.gm_ln_g, P)
    G["gmb"] = load_bcast(fw, "gmb", C.gm_ln_b[l, :], 512, C.gm_ln_b, P)
    bsb = fw.sbuf("bsb", [128, 4, 128], F32)
    fw.dma(fw.sp, lambda h: h.dma_start(out=bsb[:, :, :].rearrange("p h i -> p (h i)"), in_=C.gm_bs[l, :, :].rearrange("h i -> (h i)").partition_broadcast(128)),
           reads=[C.gm_bs], writes=[bsb])
    G["bsb"] = bsb
    wsr = fw.sbuf("wsr", [128, 4, 128], F32)
    fw.dma(fw.sp, lambda h: h.dma_start(out=wsr[:, :, :], in_=C.gm_ws[l, :, :, :].rearrange("h i j -> i h j")), reads=[C.gm_ws], writes=[wsr])
    ps = C.ps[1]
    for h in range(4):
        fw.op(fw.pe, lambda e: e.transpose(out=ps[:, h * 128:(h + 1) * 128], in_=wsr[:, h, :], identity=C.ident[:, :]), reads=[wsr, C.ident], writes=[ps], join=(h > 0))
    tw = fw.sbuf("trilWT", [128, 4, 128], BF16)
    fw.op(fw.dve, lambda e: e.tensor_tensor(out=tw[:, :, :], in0=ps[:, 0:512].rearrange("p (h i) -> p h i", h=4), in1=C.triu[:, :].unsqueeze(1).to_broadcast([128, 4, 128]), op=ALU.mult),
          reads=[ps, C.triu], writes=[tw])
    G["trilWT"] = tw
    return G


def emit_softmax_pv(fw, C, A, sbuf_s, nk, gate_ap, gate_buf, acc, v_fn, first, last, tagi, part="all"):
    st = A["st"][tagi]
    eb = A["eb"][tagi]
    pT = A["pT"][tagi]
    if part in ("all", "sm"):
      fw.op(fw.dve, lambda e: e.tensor_reduce(out=st[:, 0:1], in_=sbuf_s[:, 0:nk], axis=AX.X, op=ALU.max, negate=True), reads=[sbuf_s], writes=[st])
      fw.op(fw.act, lambda e: e.activation(out=eb[:, 0:nk], in_=sbuf_s[:, 0:nk], func=AF.Exp, bias=st[:, 0:1], scale=1.0, accum_out=st[:, 1:2]),
          reads=[sbuf_s, st], writes=[eb, st])
      fw.op(fw.dve, lambda e: e.reciprocal(out=st[:, 2:3], in_=st[:, 1:2]), reads=[st], writes=[st])
      fw.op(fw.dve, lambda e: e.tensor_tensor(out=st[:, 3:4], in0=st[:, 2:3], in1=gate_ap, op=ALU.mult), reads=[st, gate_buf], writes=[st])
      fw.op(fw.pool, lambda e: e.tensor_scalar(out=eb[:, 0:nk], in0=eb[:, 0:nk], scalar1=st[:, 3:4], scalar2=None, op0=ALU.mult), reads=[eb, st], writes=[eb])
    if part == "sm":
        return
    nkt = nk // 128
    for k0 in range(0, nkt, 8):
        kn = min(8, nkt - k0)
        psb = C.psb[A["psbi"] % 2]
        A["psbi"] += 1
        for j in range(kn):
            kt = k0 + j
            fw.op(fw.pe, lambda e: e.transpose(out=psb[:, j * 128:(j + 1) * 128], in_=eb[:, kt * 128:(kt + 1) * 128], identity=C.identb[:, :]),
                  reads=[eb, C.identb], writes=[psb], join=(j > 0))
        if (k0 // 8) % 2 == 0:
            fw.op(fw.dve, lambda e: e.tensor_copy(out=pT[:, k0 * 128:(k0 + kn) * 128], in_=psb[:, 0:kn * 128]), reads=[psb], writes=[pT], join=True)
        else:
            fw.op(fw.act, lambda e: e.copy(out=pT[:, k0 * 128:(k0 + kn) * 128], in_=psb[:, 0:kn * 128]), reads=[psb], writes=[pT], join=True)
    for kt in range(nkt):
        fw.op(fw.pe, lambda e: e.matmul(acc[:, 0:128], lhsT=v_fn(kt), rhs=pT[:, kt * 128:(kt + 1) * 128], start=(first and kt == 0), stop=(last and kt == nkt - 1)),
              reads=[C.V, pT], writes=[acc], join=not (first and kt == 0))


def emit_attn_prompt(fw, C, A, l, t, mixn):
    qt = A["qt"][t % 2]
    gt = A["gate"][t % 2]
    rt = A["rt"][t % 2]
    fw.dma(fw.sp, lambda h: h.dma_start(out=qt[:, :], in_=C.Htok[t * 128:(t + 1) * 128, 512:1536]), reads=[C.Htok], writes=[qt])
    fw.dma(fw.sp, lambda h: h.dma_start(out=gt[:, :], in_=C.Htok[t * 128:(t + 1) * 128, 3072:3096]), reads=[C.Htok], writes=[gt])
    fw.dma(fw.sp, lambda h: h.dma_start(out=rt[:, :, :], in_=C.cd["rope_p"][t * 128:(t + 1) * 128, :, :]), reads=[C.cd["rope_p"]], writes=[rt])
    qv = qt[:, :].rearrange("p (h x d) -> p h x d", h=8, x=2)
    tv = [A["ssb"][:, i * 512:(i + 1) * 512].rearrange("p (h d) -> p h d", h=8) for i in range(4)]
    emit_rope(fw, qv[:, :, 0, :], qv[:, :, 1, :], rt[:, 2, :], rt[:, 3, :], tv, [128, 8, 64], [rt], qt, A["rtmp"])
    qb = A["qb"]
    fw.op(fw.act, lambda e: e.copy(out=qb[:, :], in_=qt[:, :]), reads=[qt], writes=[qb])
    psb = C.psb[A["psbi"] % 2]
    A["psbi"] += 1
    for h in range(8):
        fw.op(fw.pe, lambda e: e.transpose(out=psb[:, h * 128:(h + 1) * 128], in_=qb[:, h * 128:(h + 1) * 128], identity=C.identb[:, :]),
              reads=[qb, C.identb], writes=[psb], join=(h > 0))
    qT = A["qT"]
    fw.op(fw.dve, lambda e: e.tensor_copy(out=qT[:, :], in_=psb[:, :]), reads=[psb], writes=[qT])
    sig = A["sig"]
    fw.op(fw.act, lambda e: e.activation(out=sig[:, :], in_=gt[:, :], func=AF.Sigmoid), reads=[gt], writes=[sig])
    sig3 = sig[:, :].rearrange("p (h b) -> p h b", b=3)
    sc, ec, st4 = A["sc"], A["ec"], A["st4"]
    for g in range(2):
        psc = C.ps[0]
        for r in range(4):
            fw.op(fw.pe, lambda e: e.matmul(psc[:, r * 64:(r + 1) * 64], lhsT=qT[:, (g * 4 + r) * 128:(g * 4 + r + 1) * 128], rhs=C.kcmpT[:, g, :], start=True, stop=True),
                  reads=[qT, C.kcmpT], writes=[psc], join=(r > 0))
        sc3 = sc[:, :].rearrange("p (r n) -> p r n", r=4)
        ec3 = ec[:, :].rearrange("p (r n) -> p r n", r=4)
        fw.op(fw.dve, lambda e: e.tensor_tensor(out=sc3, in0=psc[:, 0:256].rearrange("p (r n) -> p r n", r=4),
                                                in1=C.cmpbias[:, t, :].unsqueeze(1).to_broadcast([128, 4, 64]), op=ALU.add), reads=[psc, C.cmpbias], writes=[sc])
        fw.op(fw.dve, lambda e: e.tensor_reduce(out=st4[:, 0:4], in_=sc3, axis=AX.X, op=ALU.max), reads=[sc], writes=[st4])
        fw.op(fw.dve, lambda e: e.tensor_tensor(out=sc3, in0=sc3, in1=st4[:, 0:4].unsqueeze(2).to_broadcast([128, 4, 64]), op=ALU.subtract), reads=[sc, st4], writes=[sc])
        fw.op(fw.act, lambda e: e.activation(out=ec[:, :], in_=sc[:, :], func=AF.Exp), reads=[sc], writes=[ec])
        fw.op(fw.dve, lambda e: e.tensor_reduce(out=st4[:, 4:8], in_=ec3, axis=AX.X, op=ALU.add), reads=[ec], writes=[st4])
        fw.op(fw.dve, lambda e: e.tensor_scalar(out=st4[:, 4:8], in0=st4[:, 4:8], scalar1=1e-30, scalar2=None, op0=ALU.max), reads=[st4], writes=[st4])
        fw.op(fw.dve, lambda e: e.reciprocal(out=st4[:, 8:12], in_=st4[:, 4:8]), reads=[st4], writes=[st4])
        fw.op(fw.dve, lambda e: e.tensor_scalar(out=st4[:, 8:12], in0=st4[:, 8:12], scalar1=C.cmpvalid[:, t:t + 1], scalar2=None, op0=ALU.mult), reads=[st4, C.cmpvalid], writes=[st4])
        fw.op(fw.dve, lambda e: e.tensor_tensor(out=ec3, in0=ec3, in1=st4[:, 8:12].unsqueeze(2).to_broadcast([128, 4, 64]), op=ALU.mult), reads=[ec, st4], writes=[ec])
        imp = A["imp"]
        fw.op(fw.dve, lambda e: e.tensor_reduce(out=imp[:, 0:64], in_=ec[:, :].rearrange("p (r n) -> p n r", r=4), axis=AX.X, op=ALU.add), reads=[ec], writes=[imp])
        fw.op(fw.dve, lambda e: e.tensor_reduce(out=imp[:, 64:96], in_=imp[:, 0:64].rearrange("p (m two) -> p m two", two=2), axis=AX.X, op=ALU.add), reads=[imp], writes=[imp])
        fw.op(fw.dve, lambda e: e.tensor_tensor(out=imp[:, 64:96], in0=imp[:, 64:96], in1=C.selA[:, t, :], op=ALU.mult), reads=[imp, C.selA], writes=[imp])
        fw.op(fw.dve, lambda e: e.tensor_tensor(out=imp[:, 64:96], in0=imp[:, 64:96], in1=C.selB[:, t, :], op=ALU.add), reads=[imp, C.selB], writes=[imp])
        m8 = A["m8"]
        fw.op(fw.dve, lambda e: e.max(out=m8[:, 0:8], in_=imp[:, 64:96]), reads=[imp], writes=[m8])
        fw.op(fw.dve, lambda e: e.match_replace(out=imp[:, 96:128], in_to_replace=m8[:, 0:8], in_values=imp[:, 64:96], imm_value=-1e30), reads=[imp, m8], writes=[imp])
        fw.op(fw.dve, lambda e: e.max(out=m8[:, 8:16], in_=imp[:, 96:128]), reads=[imp], writes=[m8])
        bsel = A["bsel"]
        fw.op(fw.dve, lambda e: e.tensor_scalar(out=bsel[:, :], in0=imp[:, 64:96], scalar1=m8[:, 15:16], scalar2=None, op0=ALU.is_ge), reads=[imp, m8], writes=[bsel])
        fw.op(fw.dve, lambda e: e.tensor_scalar(out=bsel[:, :], in0=bsel[:, :], scalar1=-NEG, scalar2=NEG, op0=ALU.mult, op1=ALU.add), reads=[bsel], writes=[bsel])
        pcb = A["pcb"]
        fw.op(fw.dve, lambda e: e.tensor_tensor(out=pcb[:, :].rearrange("p (r n) -> p r n", r=4), in0=ec3,
                                                in1=sig3[:, g * 4:(g + 1) * 4, 0:1].to_broadcast([128, 4, 64]), op=ALU.mult), reads=[ec, sig], writes=[pcb])
        psb = C.psb[A["psbi"] % 2]
        A["psbi"] += 1
        for r in range(4):
            fw.op(fw.pe, lambda e: e.transpose(out=psb[0:64, r * 128:(r + 1) * 128], in_=pcb[:, r * 64:(r + 1) * 64], identity=C.identb[:, :]),
                  reads=[pcb, C.identb], writes=[psb], join=(r > 0))
        pTc = A["pTc"]
        fw.op(fw.act, lambda e: e.copy(out=pTc[0:64, :], in_=psb[0:64, 0:512]), reads=[psb], writes=[pTc])
        for r in range(4):
            h = g * 4 + r
            acc = C.pacc[h % 2]
            fw.op(fw.pe, lambda e: e.matmul(acc[:, 0:128], lhsT=C.vcmp[0:64, g * 128:(g + 1) * 128], rhs=pTc[0:64, r * 128:(r + 1) * 128], start=True, stop=False),
                  reads=[C.vcmp, pTc], writes=[acc])
            nk = (t + 1) * 128
            ssb = A["ssb"]
            for ci, c0 in enumerate(range(0, nk, 512)):
                cw = min(512, nk - c0)
                ps = C.ps[1 + (ci % 3)]
                fw.op(fw.pe, lambda e: e.matmul(ps[:, 0:cw], lhsT=qT[:, h * 128:(h + 1) * 128], rhs=C.kT[:, g, c0:c0 + cw], start=True, stop=True),
                      reads=[qT, C.kT], writes=[ps])
                nb = cw // 64
                fw.op(fw.dve, lambda e: e.tensor_tensor(out=ssb[:, c0:c0 + cw].rearrange("p (m j) -> p m j", j=64), in0=ps[:, 0:cw].rearrange("p (m j) -> p m j", j=64),
                                                        in1=bsel[:, c0 // 64:c0 // 64 + nb].unsqueeze(2).to_broadcast([128, nb, 64]), op=ALU.add),
                      reads=[ps, bsel], writes=[ssb], join=True)
            fw.op(fw.pool, lambda e: e.tensor_tensor(out=ssb[:, t * 128:(t + 1) * 128], in0=ssb[:, t * 128:(t + 1) * 128], in1=C.causal[:, :], op=ALU.add),
                  reads=[ssb, C.causal], writes=[ssb])
            kt0 = max(0, t - 4)
            nkw = (t - kt0 + 1) * 128
            swb = A["swb"]
            for ci, c0 in enumerate(range(0, nkw, 512)):
                cw = min(512, nkw - c0)
                ps = C.ps[1 + (ci % 3)]
                fw.op(fw.pe, lambda e: e.matmul(ps[:, 0:cw], lhsT=qT[:, h * 128:(h + 1) * 128], rhs=C.kT[:, 2 + g, kt0 * 128 + c0:kt0 * 128 + c0 + cw], start=True, stop=True),
                      reads=[qT, C.kT], writes=[ps])
                fw.op(fw.act, lambda e: e.copy(out=swb[:, c0:c0 + cw], in_=ps[:, 0:cw]), reads=[ps], writes=[swb], join=True)
            fw.op(fw.pool, lambda e: e.tensor_tensor(out=swb[:, nkw - 128:nkw], in0=swb[:, nkw - 128:nkw], in1=C.causal[:, :], op=ALU.add), reads=[swb, C.causal], writes=[swb])
            if t >= 4:
                fw.op(fw.pool, lambda e: e.tensor_tensor(out=swb[:, 0:128], in0=swb[:, 0:128], in1=C.acausal[:, :], op=ALU.add), reads=[swb, C.acausal], writes=[swb])
            emit_softmax_pv(fw, C, A, ssb, nk, sig3[:, h, 1:2], sig, acc, None, False, False, 0, part="sm")
            emit_softmax_pv(fw, C, A, swb, nkw, sig3[:, h, 2:3], sig, acc, None, False, True, 1, part="sm")
            emit_softmax_pv(fw, C, A, ssb, nk, sig3[:, h, 1:2], sig, acc, lambda kt: C.V[:, kt, 0, g * 128:(g + 1) * 128], False, False, 0, part="pv")
            emit_softmax_pv(fw, C, A, swb, nkw, sig3[:, h, 2:3], sig, acc, lambda kt: C.V[:, kt0 + kt, 1, g * 128:(g + 1) * 128], False, True, 1, part="pv")
            if h % 2 == 0:
                fw.op(fw.dve, lambda e: e.tensor_copy(out=mixn[:, h, :], in_=acc[:, 0:128]), reads=[acc], writes=[mixn], join=True)
            else:
                fw.op(fw.act, lambda e: e.copy(out=mixn[:, h, :], in_=acc[:, 0:128]), reads=[acc], writes=[mixn], join=True)


def attn_bufs(fw):
    A = {"psbi": 0}
    A["qt"] = [fw.sbuf("qt0", [128, 1024], F32)] * 2
    A["gate"] = [fw.sbuf(f"gate{i}", [128, 24], F32) for i in range(2)]
    A["rt"] = [fw.sbuf(f"art{i}", [128, 4, 64], F32) for i in range(2)]

    A["qb"] = fw.sbuf("qb", [128, 1024], BF16)
    A["qT"] = fw.sbuf("qT", [128, 1024], BF16)
    A["sig"] = fw.sbuf("sig", [128, 24], F32)
    A["sc"] = fw.sbuf("sc", [128, 256], F32)
    A["ec"] = fw.sbuf("ec", [128, 256], F32)
    A["st4"] = fw.sbuf("st4", [128, 12], F32)
    A["st"] = [fw.sbuf("ast0", [128, 4], F32), fw.sbuf("ast1", [128, 4], F32)]
    A["imp"] = fw.sbuf("imp", [128, 128], F32)
    A["m8"] = fw.sbuf("m8", [128, 16], F32)
    A["bsel"] = fw.sbuf("bsel", [128, 32], F32)
    A["pcb"] = fw.sbuf("pcb", [128, 256], BF16)
    A["pTc"] = fw.sbuf("pTc", [128, 512], BF16)
    A["ssb"] = fw.sbuf("ssb", [128, S], F32)
    A["rtmp"] = [A["ssb"]] * 4
    A["swb"] = fw.sbuf("swb", [128, 640], F32)
    A["eb"] = [fw.sbuf("eb", [128, S], BF16), fw.sbuf("ebw", [128, 640], BF16)]
    A["pT"] = [fw.sbuf("pT", [128, S], BF16), fw.sbuf("pTw", [128, 640], BF16)]
    return A


def emit_mix_ln(fw, C, M, l, t, P, x_src, mixg, mixn, convT, X1):
    xt = M["xres"][0]
    fw.dma(fw.sp, lambda h: h.dma_start(out=xt[0:P, :], in_=x_src[t * 128:t * 128 + P, :]), reads=[x_src], writes=[xt])
    x1 = xt
    for n4 in range(4):
        ps = C.ps[n4]
        for c in range(16):
            if c < 4:
                lt, lb = mixg[:, c, 0:P], mixg
            elif c < 12:
                lt, lb = mixn[:, c - 4, 0:P], mixn
            else:
                lt, lb = convT[:, c - 12, t * 128:t * 128 + P], convT
            fw.op(fw.pe, lambda e: e.matmul(ps[0:P, :], lhsT=lt, rhs=M["wout"][:, c, n4 * 512:(n4 + 1) * 512], start=(c == 0), stop=(c == 15)),
                  reads=[lb, M["wout"]], writes=[ps], join=(c > 0))
        fw.op(fw.dve, lambda e: e.scalar_tensor_tensor(out=x1[0:P, n4 * 512:(n4 + 1) * 512], in0=xt[0:P, n4 * 512:(n4 + 1) * 512], scalar=ALPHA, in1=ps[0:P, :], op0=ALU.mult, op1=ALU.add),
              reads=[xt, ps], writes=[x1])
    emit_ln_free(fw, C, x1, P, M["ln1g"], M["ln1b"], M["junk"], M["lnst"])
    fw.dma(fw.sp, lambda h: h.dma_start(out=X1[t * 128:t * 128 + P, :], in_=x1[0:P, :]), reads=[x1], writes=[X1], join=True)


def mix_bufs(fw, C, l, P):
    M = {}
    wout = fw.sbuf("wout", [128, 16, D], BF16)
    for c4 in range(4):
        fw.dma(fw.pool, lambda h: h.dma_start(out=wout[:, c4 * 4:(c4 + 1) * 4, :], in_=C.w_out[l, c4 * 512:(c4 + 1) * 512, :].rearrange("(c p) n -> p c n", p=128)),
               reads=[C.w_out], writes=[wout], join=True)
    M["wout"] = wout
    M["xres"] = [fw.sbuf("xres0", [128, D], F32)]
    M["x1"] = M["xres"]
    M["junk"] = None
    M["lnst"] = fw.sbuf("lnst", [128, 8], F32)
    M["ln1g"] = load_bcast(fw, "ln1g", C.ln1_g[l, :], D, C.ln1_g, P)
    M["ln1b"] = load_bcast(fw, "ln1b", C.ln1_b[l, :], D, C.ln1_b, P)
    return M


def phase_mix_prompt(fw, C, l, x_src, X1, samp=None):
    fw.push()
    for nm in ["cmpbias", "cmpvalid", "selA", "selB"]:
        setattr(C, nm, load_const(fw, C, nm))
    M = mix_bufs(fw, C, l, 128)
    G = gmlp_consts(fw, C, l, 128)
    A = attn_bufs(fw)
    M["junk"] = A["eb"][0]
    mixg = [fw.sbuf(f"mixg{i}", [128, 4, 128], BF16) for i in range(2)]
    mixn = [fw.sbuf(f"mixn{i}", [128, 8, 128], BF16) for i in range(2)]
    for t in range(NT):
        out_gm = C.out_gm[l, :, :] if t == NT - 1 else None
        emit_gmlp(fw, C, G, l, t, 128, C.Htok, C.HfT, mixg[t % 2], out_gm)
        emit_attn_prompt(fw, C, A, l, t, mixn[t % 2])
        emit_mix_ln(fw, C, M, l, t, 128, x_src, mixg[t % 2], mixn[t % 2], C.convT, X1)
    if samp is not None:
        xs_src, XS1, mixn_s, convTs = samp
        emit_gmlp(fw, C, G, l, 0, TS, C.HtokS, C.HfTS, mixg[0], C.out_sgm[l, :, :])
        emit_mix_ln(fw, C, M, l, 0, TS, xs_src, mixg[0], mixn_s, convTs, XS1)
    fw.pop()


INPUT_SPECS = [
    ("w_in", [DEPTH, D, IN_W]), ("gm_ln_g", [DEPTH, 512]), ("gm_ln_b", [DEPTH, 512]), ("gm_ws", [DEPTH, 4, 128, 128]), ("gm_bs", [DEPTH, 4, 128]),
    ("cmp_wk", [DEPTH, 2, 32]), ("cmp_wv", [DEPTH, 2, 32]), ("conv_dw", [DEPTH, 31, 512]), ("conv_db", [DEPTH, 512]),
    ("conv_ln_g", [DEPTH, 512]), ("conv_ln_b", [DEPTH, 512]), ("conv_pw", [DEPTH, 512, 512]), ("w_out", [DEPTH, D, D]),
    ("ln1_g", [DEPTH, D]), ("ln1_b", [DEPTH, D]), ("peer_wq", [DEPTH, D, D]), ("peer_subkeys", [DEPTH, 8, 2, 128, 128]),
    ("ln2_g", [DEPTH, D]), ("ln2_b", [DEPTH, D]), ("peer_u", [DEPTH, 16384, D]), ("peer_v", [DEPTH, 16384, D]),
]


def phase_peer(fw, C, l, jobs, NB=6):
    fw.push()
    wq = fw.sbuf("wq", [128, 16, D], BF16)
    for c4 in range(4):
        fw.dma(fw.pool, lambda h: h.dma_start(out=wq[:, c4 * 4:(c4 + 1) * 4, :], in_=C.peer_wq[l, c4 * 512:(c4 + 1) * 512, :].rearrange("(c p) n -> p c n", p=128)),
               reads=[C.peer_wq], writes=[wq], join=True)
    skr = fw.sbuf("skr", [128, 16, 128], F32)
    fw.dma(fw.sp, lambda h: h.dma_start(out=skr[:, :, :], in_=C.peer_subkeys[l, :, :, :, :].rearrange("h p k d -> k (h p) d")), reads=[C.peer_subkeys], writes=[skr])
    skT = fw.sbuf("skT", [128, 16, 128], BF16)
    for q4 in range(4):
        ps = C.ps[q4]
        for j in range(4):
            fw.op(fw.pe, lambda e: e.transpose(out=ps[:, j * 128:(j + 1) * 128], in_=skr[:, q4 * 4 + j, :], identity=C.ident[:, :]), reads=[skr, C.ident], writes=[ps], join=(j > 0))
        fw.op(fw.dve, lambda e: e.tensor_copy(out=skT[:, q4 * 4:(q4 + 1) * 4, :], in_=ps[:, :].rearrange("p (j k) -> p j k", j=4)), reads=[ps], writes=[skT], join=True)
    ln2g = load_bcast(fw, "ln2g", C.ln2_g[l, :], D, C.ln2_g, 128)
    ln2b = load_bcast(fw, "ln2b", C.ln2_b[l, :], D, C.ln2_b, 128)
    iota = load_const(fw, C, "iota16")
    x1 = fw.sbuf("px1", [128, D], F32)
    x1T = fw.sbuf("px1T", [128, 16, 128], BF16)
    qTb = fw.sbuf("pqT", [128, 16, 128], BF16)
    sb = fw.sbuf("psc", [128, 16, 128], F32)
    m = fw.sbuf("pm", [128, 16, 16], F32)
    ix = fw.sbuf("pix", [128, 16, 16], U32)
    ixf = fw.sbuf("pixf", [128, 16, 16], F32)
    tmp = fw.sbuf("ptmp", [128, 256], F32)
    cand = fw.sbuf("pcand", [128, 8, 256], F32)
    oh = fw.sbuf("poh", [128, 8, 256], F32)
    tm = fw.sbuf("ptm", [128, 8, 16], F32)
    pos = fw.sbuf("ppos", [128, 8, 16], U32)
    pa = fw.sbuf("ppa", [128, 8, 16], U32)
    paf = fw.sbuf("ppaf", [128, 2, 128], F32)
    isel = fw.sbuf("pisel", [128, 2, 128], F32)
    eid = fw.sbuf("peid", [128, 128], I32)
    gate = fw.sbuf("pgate", [128, 8, 16], F32)
    gst = fw.sbuf("pgst", [128, 16], F32)
    actp = fw.sbuf("pactp", [128, 128], F32)
    wgt = fw.sbuf("pwgt", [128, 128], F32)
    gtmp = fw.sbuf("pgtmp", [128, 128], F32)
    y = fw.sbuf("py", [128, D], F32)
    junk = fw.sbuf("pjunk", [128, D], BF16)
    lnst = fw.sbuf("plnst", [128, 8], F32)
    ring = [fw.sbuf(f"pring{i}", [128, D], F32) for i in range(NB)]
    u_rows = C.peer_u[:, :, :].rearrange("l n d -> (l n) d")
    v_rows = C.peer_v[:, :, :].rearrange("l n d -> (l n) d")
    gi = 0
    for (X1, P, X2, t, ridx) in [(j[0], j[2], j[3], t_, j[4] if len(j) > 4 else None) for j in jobs for t_ in range(j[1])]:
        if ridx is None:
            fw.dma(fw.sp, lambda h: h.dma_start(out=x1[0:P, :], in_=X1[t * 128:t * 128 + P, :]), reads=[X1], writes=[x1])
        else:
            fw.dma(fw.pool, lambda h: h.indirect_dma_start(out=x1[0:P, :], out_offset=None, in_=X1[:, :], in_offset=bass.IndirectOffsetOnAxis(ap=ridx[0:P, t:t + 1], axis=0)),
                   reads=[X1, ridx], writes=[x1])
        for cb in range(4):
            ps = C.ps[cb]
            for k in range(4):
                c = cb * 4 + k
                fw.op(fw.pe, lambda e: e.transpose(out=ps[:, k * 128:k * 128 + P], in_=x1[0:P, c * 128:(c + 1) * 128], identity=C.ident[0:P, 0:P]),
                      reads=[x1, C.ident], writes=[ps], join=(k > 0))
            src = ps[:, :].rearrange("p (k q) -> p k q", k=4)[:, :, 0:P]
            fw.op(fw.act if cb % 2 else fw.dve, (lambda e: e.copy(out=x1T[:, cb * 4:(cb + 1) * 4, 0:P], in_=src)) if cb % 2 else (lambda e: e.tensor_copy(out=x1T[:, cb * 4:(cb + 1) * 4, 0:P], in_=src)),
                  reads=[ps], writes=[x1T], join=True)
        for q4 in range(4):
            ps = C.ps[q4]
            for j in range(4):
                hp = q4 * 4 + j
                for c in range(16):
                    fw.op(fw.pe, lambda e: e.matmul(ps[:, j * 128:j * 128 + P], lhsT=wq[:, c, hp * 128:(hp + 1) * 128], rhs=x1T[:, c, 0:P], start=(c == 0), stop=(c == 15)),
                          reads=[wq, x1T], writes=[ps], join=not (j == 0 and c == 0))
            src = ps[:, :].rearrange("p (j q) -> p j q", j=4)[:, :, 0:P]
            fw.op(fw.act if q4 % 2 else fw.dve, (lambda e: e.copy(out=qTb[:, q4 * 4:(q4 + 1) * 4, 0:P], in_=src)) if q4 % 2 else (lambda e: e.tensor_copy(out=qTb[:, q4 * 4:(q4 + 1) * 4, 0:P], in_=src)),
                  reads=[ps], writes=[qTb], join=True)
        for q4 in range(4):
            ps = C.ps[q4]
            for j in range(4):
                hp = q4 * 4 + j
                fw.op(fw.pe, lambda e: e.matmul(ps[0:P, j * 128:(j + 1) * 128], lhsT=qTb[:, hp, 0:P], rhs=skT[:, hp, :], start=True, stop=True),
                      reads=[qTb, skT], writes=[ps], join=(j > 0))
            fw.op(fw.act if q4 % 2 else fw.dve, (lambda e: e.copy(out=sb[0:P, q4 * 4:(q4 + 1) * 4, :], in_=ps[0:P, :].rearrange("p (j k) -> p j k", j=4))) if q4 % 2 else
                  (lambda e: e.tensor_copy(out=sb[0:P, q4 * 4:(q4 + 1) * 4, :], in_=ps[0:P, :].rearrange("p (j k) -> p j k", j=4))), reads=[ps], writes=[sb], join=True)
        for hp in range(16):
            fw.op(fw.dve, lambda e: e.max(out=m[0:P, hp, 0:8], in_=sb[0:P, hp, :]), reads=[sb], writes=[m])
            fw.op(fw.dve, lambda e: e.match_replace(out=tmp[0:P, 0:128], in_to_replace=m[0:P, hp, 0:8], in_values=sb[0:P, hp, :], imm_value=-1e30), reads=[sb, m], writes=[tmp])
            fw.op(fw.dve, lambda e: e.max(out=m[0:P, hp, 8:16], in_=tmp[0:P, 0:128]), reads=[tmp], writes=[m])
            fw.op(fw.dve, lambda e: e.max_index(out=ix[0:P, hp, 0:8], in_max=m[0:P, hp, 0:8], in_values=sb[0:P, hp, :]), reads=[sb, m], writes=[ix])
            fw.op(fw.dve, lambda e: e.max_index(out=ix[0:P, hp, 8:16], in_max=m[0:P, hp, 8:16], in_values=tmp[0:P, 0:128]), reads=[tmp, m], writes=[ix])
        fw.op(fw.dve, lambda e: e.tensor_copy(out=ixf[0:P, :, :], in_=ix[0:P, :, :]), reads=[ix], writes=[ixf])
        m4 = m[0:P, :, :].rearrange("p (h two) k -> p h two k", two=2)
        i4 = ixf[0:P, :, :].rearrange("p (h two) k -> p h two k", two=2)
        c4v = cand[0:P, :, :].rearrange("p h (a b) -> p h a b", a=16)
        fw.op(fw.dve, lambda e: e.tensor_tensor(out=c4v, in0=m4[:, :, 0, :].unsqueeze(3).to_broadcast([P, 8, 16, 16]),
                                                in1=m4[:, :, 1, :].unsqueeze(2).to_broadcast([P, 8, 16, 16]), op=ALU.add), reads=[m], writes=[cand])
        for h in range(8):
            fw.op(fw.dve, lambda e: e.max(out=tm[0:P, h, 0:8], in_=cand[0:P, h, :]), reads=[cand], writes=[tm])
            fw.op(fw.dve, lambda e: e.match_replace(out=tmp[0:P, :], in_to_replace=tm[0:P, h, 0:8], in_values=cand[0:P, h, :], imm_value=-1e30), reads=[cand, tm], writes=[tmp])
            fw.op(fw.dve, lambda e: e.max(out=tm[0:P, h, 8:16], in_=tmp[0:P, :]), reads=[tmp], writes=[tm])
            fw.op(fw.dve, lambda e: e.max_index(out=pos[0:P, h, 0:8], in_max=tm[0:P, h, 0:8], in_values=cand[0:P, h, :]), reads=[cand, tm], writes=[pos])
            fw.op(fw.dve, lambda e: e.max_index(out=pos[0:P, h, 8:16], in_max=tm[0:P, h, 8:16], in_values=tmp[0:P, :]), reads=[tmp, tm], writes=[pos])
        fw.op(fw.dve, lambda e: e.tensor_single_scalar(out=pa[0:P, :, :], in_=pos[0:P, :, :], scalar=4, op=ALU.logical_shift_right), reads=[pos], writes=[pa])
        fw.op(fw.dve, lambda e: e.tensor_copy(out=paf[0:P, 0, :], in_=pa[0:P, :, :].rearrange("p h k -> p (h k)")), reads=[pa], writes=[paf])
        fw.op(fw.dve, lambda e: e.tensor_single_scalar(out=pa[0:P, :, :], in_=pos[0:P, :, :], scalar=15, op=ALU.bitwise_and), reads=[pos, paf], writes=[pa])
        fw.op(fw.dve, lambda e: e.tensor_copy(out=paf[0:P, 1, :], in_=pa[0:P, :, :].rearrange("p h k -> p (h k)")), reads=[pa], writes=[paf])
        for w_ in range(2):
            o4 = oh[0:P, :, :].rearrange("p h (k a) -> p h k a", a=16)
            sel = paf[0:P, w_, :].rearrange("p (h k) -> p h k", h=8)
            fw.op(fw.dve, lambda e: e.tensor_tensor(out=o4, in0=sel.unsqueeze(3).to_broadcast([P, 8, 16, 16]),
                                                    in1=iota[0:P, :].unsqueeze(1).unsqueeze(1).to_broadcast([P, 8, 16, 16]), op=ALU.is_equal), reads=[paf, iota], writes=[oh])
            fw.op(fw.dve, lambda e: e.tensor_tensor(out=o4, in0=o4, in1=i4[:, :, w_, :].unsqueeze(2).to_broadcast([P, 8, 16, 16]), op=ALU.mult), reads=[oh, ixf], writes=[oh])
            fw.op(fw.dve, lambda e: e.tensor_reduce(out=isel[0:P, w_, :], in_=oh[0:P, :, :].rearrange("p h (k a) -> p (h k) a", a=16), axis=AX.X, op=ALU.add), reads=[oh], writes=[isel])
        fw.op(fw.dve, lambda e: e.scalar_tensor_tensor(out=eid[0:P, :], in0=isel[0:P, 0, :], scalar=128.0, in1=isel[0:P, 1, :], op0=ALU.mult, op1=ALU.add), reads=[isel], writes=[eid])
        fw.op(fw.dve, lambda e: e.tensor_tensor(out=gate[0:P, :, :], in0=tm[0:P, :, :], in1=tm[0:P, :, 0:1].to_broadcast([P, 8, 16]), op=ALU.subtract), reads=[tm], writes=[gate])
        fw.op(fw.act, lambda e: e.activation(out=gate[0:P, :, :], in_=gate[0:P, :, :], func=AF.Exp), reads=[gate], writes=[gate])
        fw.op(fw.dve, lambda e: e.tensor_reduce(out=gst[0:P, 0:8], in_=gate[0:P, :, :], axis=AX.X, op=ALU.add), reads=[gate], writes=[gst])
        fw.op(fw.dve, lambda e: e.reciprocal(out=gst[0:P, 8:16], in_=gst[0:P, 0:8]), reads=[gst], writes=[gst])
        fw.op(fw.dve, lambda e: e.tensor_tensor(out=gate[0:P, :, :], in0=gate[0:P, :, :], in1=gst[0:P, 8:16].unsqueeze(2).to_broadcast([P, 8, 16]), op=ALU.mult), reads=[gate, gst], writes=[gate])
        fw.op(fw.dve, lambda e: e.memset(actp[0:P, :], 0.0), writes=[actp])
        for s_ in range(128):
            rb = ring[gi % NB]
            gi += 1
            fw.dma(fw.pool, lambda h: h.indirect_dma_start(out=rb[0:P, :], out_offset=None, in_=u_rows, in_offset=bass.IndirectOffsetOnAxis(ap=eid[0:P, s_:s_ + 1], axis=0),
                                                         element_offset=l * 16384 * D), reads=[eid, C.peer_u], writes=[rb])
            fw.op(fw.dve, lambda e: e.scalar_tensor_tensor(out=junk[0:P, :], in0=rb[0:P, :], scalar=1.0, in1=x1[0:P, :], op0=ALU.mult, op1=ALU.mult, accum_out=actp[0:P, s_:s_ + 1]),
                  reads=[rb, x1], writes=[junk, actp], join=True)
        emit_gelu(fw, fw.dve, fw.act, wgt[0:P, :], actp[0:P, :], gtmp[0:P, :], [actp], [wgt], gtmp)
        fw.op(fw.dve, lambda e: e.tensor_tensor(out=wgt[0:P, :], in0=wgt[0:P, :], in1=gate[0:P, :, :].rearrange("p h k -> p (h k)"), op=ALU.mult), reads=[wgt, gate], writes=[wgt])
        for s_ in range(128):
            rb = ring[gi % NB]
            gi += 1
            fw.dma(fw.pool, lambda h: h.indirect_dma_start(out=rb[0:P, :], out_offset=None, in_=v_rows, in_offset=bass.IndirectOffsetOnAxis(ap=eid[0:P, s_:s_ + 1], axis=0),
                                                         element_offset=l * 16384 * D), reads=[eid, C.peer_v], writes=[rb])
            if s_ == 0:
                fw.op(fw.dve, lambda e: e.tensor_scalar(out=y[0:P, :], in0=rb[0:P, :], scalar1=wgt[0:P, 0:1], scalar2=None, op0=ALU.mult), reads=[rb, wgt], writes=[y])
            else:
                fw.op(fw.dve, lambda e: e.scalar_tensor_tensor(out=y[0:P, :], in0=rb[0:P, :], scalar=wgt[0:P, s_:s_ + 1], in1=y[0:P, :], op0=ALU.mult, op1=ALU.add),
                      reads=[rb, wgt, y], writes=[y])
        fw.op(fw.dve, lambda e: e.scalar_tensor_tensor(out=y[0:P, :], in0=x1[0:P, :], scalar=ALPHA, in1=y[0:P, :], op0=ALU.mult, op1=ALU.add), reads=[x1, y], writes=[y])
        emit_ln_free(fw, C, y, P, ln2g, ln2b, junk, lnst)
        fw.dma(fw.sp, lambda h: h.dma_start(out=X2[t * 128:t * 128 + P, :], in_=y[0:P, :]), reads=[y], writes=[X2], join=True)
    fw.pop()


def sample_consts():
    c = {}
    rows = np.arange(64)
    tok = rows % 8
    g = rows // 32
    c["msum"] = ((g[:, None] == g[None, :]) & (tok[:, None] == tok[None, :])).astype(np.float32)
    sA = np.ones((64, 257), np.float32); sB = np.zeros((64, 257), np.float32)
    for col in (0, 255, 256):
        sA[:, col] = 0.0; sB[:, col] = 1.0e4
    c["ssA"] = sA; c["ssB"] = sB
    c["caus8"] = np.where(np.arange(8)[None, :] <= tok[:, None], 0.0, NEG).astype(np.float32)
    idx = np.arange(520)[None, :]
    ok = np.where(idx < 512, idx >= tok[:, None], (idx - 512) <= tok[:, None])
    c["wbias"] = np.where(ok, 0.0, NEG).astype(np.float32)
    return c


SCONST_SHAPES = {"msum": [64, 64], "ssA": [64, 257], "ssB": [64, 257], "caus8": [64, 8], "wbias": [64, 520]}
CONST_SHAPES.update(SCONST_SHAPES)


def emit_rows_softmax(fw, sbuf_s, n, st, gate_ap, gate_buf, R=64):
    fw.op(fw.dve, lambda e: e.tensor_reduce(out=st[0:R, 0:1], in_=sbuf_s[0:R, 0:n], axis=AX.X, op=ALU.max, negate=True), reads=[sbuf_s], writes=[st])
    fw.op(fw.act, lambda e: e.activation(out=sbuf_s[0:R, 0:n], in_=sbuf_s[0:R, 0:n], func=AF.Exp, bias=st[0:R, 0:1], scale=1.0, accum_out=st[0:R, 1:2]),
          reads=[sbuf_s, st], writes=[sbuf_s, st])
    fw.op(fw.dve, lambda e: e.reciprocal(out=st[0:R, 2:3], in_=st[0:R, 1:2]), reads=[st], writes=[st])
    if gate_ap is not None:
        fw.op(fw.dve, lambda e: e.tensor_tensor(out=st[0:R, 2:3], in0=st[0:R, 2:3], in1=gate_ap, op=ALU.mult), reads=[st, gate_buf], writes=[st])
    fw.op(fw.dve, lambda e: e.tensor_scalar(out=sbuf_s[0:R, 0:n], in0=sbuf_s[0:R, 0:n], scalar1=st[0:R, 2:3], scalar2=None, op0=ALU.mult), reads=[sbuf_s, st], writes=[sbuf_s])


def phase_nsa_sample(fw, C, l, HtokS, mixn_s):
    fw.push()
    R = 64
    cst = {nm: load_const(fw, C, nm) for nm in SCONST_SHAPES}
    pti = fw.sbuf("pti", [128, 1], I32)
    fw.dma(fw.sp, lambda h: h.dma_start(out=pti[:, :], in_=C.pt[:, :]), reads=[C.pt], writes=[pti])
    idx8 = fw.sbuf("idx8", [128, 1], I32)
    fw.op(fw.dve, lambda e: e.tensor_scalar(out=idx8[:, :], in0=pti[:, :], scalar1=8.0, scalar2=None, op0=ALU.mult), reads=[pti], writes=[idx8])
    kv = fw.sbuf("skv", [TS, 1536], F32)
    rt = fw.sbuf("srt", [TS, 4, 64], F32)
    fw.dma(fw.sp, lambda h: h.dma_start(out=kv[:, :], in_=HtokS[0:TS, 1536:3072]), reads=[HtokS], writes=[kv])
    fw.dma(fw.sp, lambda h: h.dma_start(out=rt[:, :, :], in_=C.cd["rope_s"][:, :, :]), reads=[C.cd["rope_s"]], writes=[rt])
    rtmp = [fw.sbuf(f"srtmp{i}", [TS, 512], F32) for i in range(4)]
    kview = kv[:, :].rearrange("p (a v g h d) -> p a v g h d", a=3, v=2, g=2, h=2, d=64)
    tv = [r[:, 0:384].rearrange("p (a g d) -> p a g d", a=3, g=2) for r in rtmp]
    emit_rope(fw, kview[:, :, 0, :, 0, :], kview[:, :, 0, :, 1, :], rt[:, 0, :], rt[:, 1, :], tv, [TS, 3, 2, 64], [rt], kv, rtmp)
    for a in range(6):
        fw.dma(fw.sp, lambda h: h.dma_start(out=C.out_skv[l, a, :, :], in_=kv[:, a * 256:(a + 1) * 256]), reads=[kv], writes=[C.out_skv], join=True)
    for wi in range(2):
        fw.dma(fw.sp, lambda h: h.dma_start(out=C.out_swin[l, wi, 0:504, :], in_=C.cwin[wi][l, 8:512, :]), reads=[C.cwin[wi]], writes=[C.out_swin], join=True)
        fw.dma(fw.sp, lambda h: h.dma_start(out=C.out_swin[l, wi, 504:512, :], in_=kv[:, 1024 + wi * 256:1280 + wi * 256]), reads=[kv], writes=[C.out_swin], join=True)
    kvb = fw.sbuf("skvb", [TS, 1536], BF16)
    fw.op(fw.act, lambda e: e.copy(out=kvb[:, :], in_=kv[:, :]), reads=[kv], writes=[kvb])
    psb = C.psb[0]
    for i, (a, g) in enumerate([(1, 0), (1, 1), (2, 0), (2, 1)]):
        c0 = a * 512 + g * 128
        fw.op(fw.pe, lambda e: e.transpose(out=psb[:, i * 8:(i + 1) * 8], in_=kvb[:, c0:c0 + 128], identity=C.identb[0:TS, 0:TS]), reads=[kvb, C.identb], writes=[psb], join=(i > 0))
    knT = fw.sbuf("knT", [128, 32], BF16)
    fw.op(fw.dve, lambda e: e.tensor_copy(out=knT[:, :], in_=psb[:, 0:32]), reads=[psb], writes=[knT])
    qt = fw.sbuf("sqt", [TS, 1024], F32)
    gt = fw.sbuf("sgt", [TS, 24], F32)
    fw.dma(fw.sp, lambda h: h.dma_start(out=qt[:, :], in_=HtokS[0:TS, 512:1536]), reads=[HtokS], writes=[qt])
    fw.dma(fw.sp, lambda h: h.dma_start(out=gt[:, :], in_=HtokS[0:TS, 3072:3096]), reads=[HtokS], writes=[gt])
    qv = qt[:, :].rearrange("p (h x d) -> p h x d", h=8, x=2)
    tv2 = [r[:, :].rearrange("p (h d) -> p h d", h=8) for r in rtmp]
    emit_rope(fw, qv[:, :, 0, :], qv[:, :, 1, :], rt[:, 2, :], rt[:, 3, :], tv2, [TS, 8, 64], [rt], qt, rtmp)
    qb = fw.sbuf("sqb", [TS, 1024], BF16)
    fw.op(fw.act, lambda e: e.copy(out=qb[:, :], in_=qt[:, :]), reads=[qt], writes=[qb])
    psb = C.psb[1]
    for h in range(8):
        fw.op(fw.pe, lambda e: e.transpose(out=psb[:, h * 8:(h + 1) * 8], in_=qb[:, h * 128:(h + 1) * 128], identity=C.identb[0:TS, 0:TS]), reads=[qb, C.identb], writes=[psb], join=(h > 0))
    Q = [fw.sbuf(f"sQ{g}", [128, 64], BF16) for g in range(2)]
    for g in range(2):
        fw.op(fw.dve, lambda e: e.memset(Q[g][:, :], 0.0), writes=[Q[g]])
        fw.op(fw.dve, lambda e: e.tensor_copy(out=Q[g][:, g * 32:(g + 1) * 32], in_=psb[:, g * 32:(g + 1) * 32]), reads=[psb], writes=[Q[g]])
    fw.op(fw.act, lambda e: e.activation(out=gt[:, :], in_=gt[:, :], func=AF.Sigmoid), reads=[gt], writes=[gt])
    fw.dma(fw.sp, lambda h: h.dma_start(out=C.gscr[:, :], in_=gt[:, :]), reads=[gt], writes=[C.gscr])
    g64 = fw.sbuf("g64", [64, 3], F32)
    for h in range(8):
        fw.dma(fw.sp, lambda h_: h_.dma_start(out=g64[h * 8:(h + 1) * 8, :], in_=C.gscr[:, h * 3:(h + 1) * 3]), reads=[C.gscr], writes=[g64], join=True)
    st = fw.sbuf("sst", [64, 4], F32)
    wsm = []
    for wi, wsrc in enumerate([C.cmp_wk, C.cmp_wv]):
        w_ = fw.sbuf(f"wsm{wi}", [128, 64], F32)
        fw.dma(fw.sp, lambda h: h.dma_start(out=w_[:, :], in_=wsrc[l, :, :].rearrange("g j -> (g j)").partition_broadcast(128)), reads=[wsrc], writes=[w_])
        wsm.append(w_)
    chunk = [fw.sbuf(f"chunk{i}", [128, 4096], F32) for i in range(2)]
    cacc = [fw.sbuf(f"cacc{i}", [128, 4, 256], F32) for i in range(2)]
    red = fw.sbuf("sred", [128, 256], F32)
    ci = 0
    for wi, cache in enumerate([C.c_cmp_k, C.c_cmp_v]):
        for jb8 in range(8):
            ch = chunk[ci % 2]
            ci += 1
            fw.dma(fw.pool, lambda h: h.indirect_dma_start(out=ch[:, :], out_offset=None, in_=cache[:, :], in_offset=bass.IndirectOffsetOnAxis(ap=idx8[:, 0:1], axis=0),
                                                         element_offset=(l * 1280 * 8 + jb8) * 4096), reads=[idx8, cache], writes=[ch])
            jb, j0 = jb8 // 2, (jb8 % 2) * 16
            wv_ = wsm[wi][:, :].rearrange("p (g j) -> p j g", g=2)[:, j0:j0 + 16, :].unsqueeze(3).to_broadcast([128, 16, 2, 128])
            chv4 = ch[:, :].rearrange("p (j g d) -> p j g d", j=16, g=2)
            fw.op(fw.dve, lambda e: e.tensor_tensor(out=chv4, in0=chv4, in1=wv_, op=ALU.mult), reads=[ch, wsm[wi]], writes=[ch])
            if j0 == 0:
                fw.op(fw.dve, lambda e: e.tensor_reduce(out=cacc[wi][:, jb, :], in_=ch[:, :].rearrange("p (j c) -> p c j", j=16), axis=AX.X, op=ALU.add), reads=[ch], writes=[cacc[wi]], join=True)
            else:
                fw.op(fw.dve, lambda e: e.tensor_reduce(out=red[:, :], in_=ch[:, :].rearrange("p (j c) -> p c j", j=16), axis=AX.X, op=ALU.add), reads=[ch], writes=[red])
                fw.op(fw.dve, lambda e: e.tensor_tensor(out=cacc[wi][:, jb, :], in0=cacc[wi][:, jb, :], in1=red[:, :], op=ALU.add), reads=[cacc[wi], red], writes=[cacc[wi]])
    kcT = fw.sbuf("skcT", [128, 2, 512], BF16)
    for g in range(2):
        ps = C.ps[g]
        for jb in range(4):
            fw.op(fw.pe, lambda e: e.transpose(out=ps[:, jb * 128:(jb + 1) * 128], in_=cacc[0][:, jb, g * 128:(g + 1) * 128], identity=C.ident[:, :]), reads=[cacc[0], C.ident], writes=[ps], join=(jb > 0))
        fw.op(fw.dve, lambda e: e.tensor_copy(out=kcT[:, g, :], in_=ps[:, :]), reads=[ps], writes=[kcT], join=True)
    vcb = fw.sbuf("svcb", [128, 4, 256], BF16)
    fw.op(fw.act, lambda e: e.copy(out=vcb[:, :, :], in_=cacc[1][:, :, :]), reads=[cacc[1]], writes=[vcb])
    ps = C.ps[2]
    for g in range(2):
        fw.op(fw.pe, lambda e: e.matmul(ps[0:R, :], lhsT=Q[g][:, :], rhs=kcT[:, g, :], start=(g == 0), stop=(g == 1)), reads=[Q[g], kcT], writes=[ps], join=(g > 0))
    pc = fw.sbuf("spc", [64, 512], F32)
    fw.op(fw.act, lambda e: e.copy(out=pc[:, :], in_=ps[0:R, :]), reads=[ps], writes=[pc])
    emit_rows_softmax(fw, pc, 512, st, None, None)
    ps = C.ps[3]
    fw.op(fw.pe, lambda e: e.matmul(ps[0:R, :], lhsT=cst["msum"][:, :], rhs=pc[:, :], start=True, stop=True), reads=[cst["msum"], pc], writes=[ps])
    impr = fw.sbuf("simpr", [64, 512], F32)
    fw.op(fw.act, lambda e: e.copy(out=impr[:, :], in_=ps[0:R, :]), reads=[ps], writes=[impr])
    sco = fw.sbuf("ssco", [64, 264], F32)
    sco2 = fw.sbuf("ssco2", [64, 264], F32)
    iv = impr[:, :].rearrange("r (hb two p) -> r hb two p", hb=2, two=2)
    fw.op(fw.dve, lambda e: e.memset(sco[:, 256:264], 0.0), writes=[sco])
    fw.op(fw.dve, lambda e: e.tensor_tensor(out=sco[:, 0:256].rearrange("r (hb p) -> r hb p", hb=2), in0=iv[:, :, 0, :], in1=iv[:, :, 1, :], op=ALU.add), reads=[impr], writes=[sco], join=True)
    fw.op(fw.dve, lambda e: e.tensor_tensor(out=sco[:, 0:257], in0=sco[:, 0:257], in1=cst["ssA"][:, :], op=ALU.mult), reads=[sco, cst["ssA"]], writes=[sco])
    fw.op(fw.dve, lambda e: e.tensor_tensor(out=sco[:, 0:257], in0=sco[:, 0:257], in1=cst["ssB"][:, :], op=ALU.add), reads=[sco, cst["ssB"]], writes=[sco])
    m8 = fw.sbuf("sm8", [64, 16], F32)
    fw.op(fw.dve, lambda e: e.max(out=m8[:, 0:8], in_=sco[:, 0:257]), reads=[sco], writes=[m8])
    fw.op(fw.dve, lambda e: e.match_replace(out=sco2[:, 0:257], in_to_replace=m8[:, 0:8], in_values=sco[:, 0:257], imm_value=-1e30), reads=[sco, m8], writes=[sco2])
    fw.op(fw.dve, lambda e: e.max(out=m8[:, 8:16], in_=sco2[:, 0:257]), reads=[sco2], writes=[m8])
    bsel = fw.sbuf("sbsel", [64, 257], F32)
    fw.op(fw.dve, lambda e: e.tensor_scalar(out=bsel[:, :], in0=sco[:, 0:257], scalar1=m8[:, 15:16], scalar2=None, op0=ALU.is_ge), reads=[sco, m8], writes=[bsel])
    fw.op(fw.dve, lambda e: e.tensor_scalar(out=bsel[:, :], in0=bsel[:, :], scalar1=-NEG, scalar2=NEG, op0=ALU.mult, op1=ALU.add), reads=[bsel], writes=[bsel])
    fw.op(fw.dve, lambda e: e.tensor_scalar(out=pc[:, :], in0=pc[:, :], scalar1=g64[:, 0:1], scalar2=None, op0=ALU.mult), reads=[pc, g64], writes=[pc])
    pcT = fw.sbuf("spcT", [128, 4, 64], BF16)
    ps = C.ps[0]
    for jb in range(4):
        fw.op(fw.pe, lambda e: e.transpose(out=ps[:, jb * 64:(jb + 1) * 64], in_=pc[:, jb * 128:(jb + 1) * 128], identity=C.ident[0:R, 0:R]), reads=[pc, C.ident], writes=[ps], join=(jb > 0))
    fw.op(fw.dve, lambda e: e.tensor_copy(out=pcT[:, :, :], in_=ps[:, 0:256].rearrange("p (j r) -> p j r", j=4)), reads=[ps], writes=[pcT])
    Ss = fw.sbuf("sS", [64, 16392], F32)
    vt = [fw.sbuf(f"svt{i}", [128, 16, 256], BF16) for i in range(2)]
    ktile = [fw.sbuf(f"sktile{i}", [128, 512], BF16) for i in range(4)]
    ki = 0
    for jb8 in range(8):
        ch = chunk[ci % 2]
        ci += 1
        fw.dma(fw.pool, lambda h: h.indirect_dma_start(out=ch[:, :], out_offset=None, in_=C.c_slc_k[:, :], in_offset=bass.IndirectOffsetOnAxis(ap=idx8[:, 0:1], axis=0),
                                                     element_offset=(l * 1280 * 8 + jb8) * 4096), reads=[idx8, C.c_slc_k], writes=[ch])
        chv = ch[:, :].rearrange("p (j g d) -> p j g d", j=16, g=2)
        for j4 in range(4):
            kts = []
            for g in range(2):
                pst = C.ps[(ki % 2) * 2 + g]
                for jj in range(4):
                    fw.op(fw.pe, lambda e: e.transpose(out=pst[:, jj * 128:(jj + 1) * 128], in_=chv[:, j4 * 4 + jj, g, :], identity=C.ident[:, :]), reads=[ch, C.ident], writes=[pst], join=(jj > 0))
                kt_ = ktile[(ki % 2) * 2 + g]
                if g == 0:
                    fw.op(fw.dve, lambda e: e.tensor_copy(out=kt_[:, :], in_=pst[:, :]), reads=[pst], writes=[kt_])
                else:
                    fw.op(fw.act, lambda e: e.copy(out=kt_[:, :], in_=pst[:, :]), reads=[pst], writes=[kt_])
                kts.append(kt_)
            pss = C.pacc[ki % 2]
            for g in range(2):
                fw.op(fw.pe, lambda e: e.matmul(pss[0:R, :], lhsT=Q[g][:, :], rhs=kts[g][:, :], start=(g == 0), stop=(g == 1)), reads=[Q[g], kts[g]], writes=[pss], join=(g > 0))
            jabs = jb8 * 16 + j4 * 4
            hb = 1 if jabs >= 64 else 0
            fw.op(fw.dve, lambda e: e.tensor_tensor(out=Ss[:, jabs * 128:(jabs + 4) * 128].rearrange("r (j p) -> r j p", j=4), in0=pss[0:R, :].rearrange("r (j p) -> r j p", j=4),
                                                    in1=bsel[:, hb * 128:(hb + 1) * 128].unsqueeze(1).to_broadcast([R, 4, 128]), op=ALU.add), reads=[pss, bsel], writes=[Ss], join=True)
            ki += 1
    pss = C.pacc[0]
    for g in range(2):
        fw.op(fw.pe, lambda e: e.matmul(pss[0:R, 0:8], lhsT=Q[g][:, :], rhs=knT[:, g * 8:(g + 1) * 8], start=(g == 0), stop=(g == 1)), reads=[Q[g], knT], writes=[pss], join=(g > 0))
    fw.op(fw.dve, lambda e: e.scalar_tensor_tensor(out=Ss[:, 16384:16392], in0=pss[0:R, 0:8], scalar=bsel[:, 256:257], in1=cst["caus8"][:, :], op0=ALU.add, op1=ALU.add),
          reads=[pss, bsel, cst["caus8"]], writes=[Ss], join=True)
    emit_rows_softmax(fw, Ss, 16392, st, g64[:, 1:2], g64)
    pT = fw.sbuf("spT", [128, 128, 64], BF16)
    for j8 in range(16):
        ps = C.ps[j8 % 4]
        for jj in range(8):
            j = j8 * 8 + jj
            fw.op(fw.pe, lambda e: e.transpose(out=ps[:, jj * 64:(jj + 1) * 64], in_=Ss[:, j * 128:(j + 1) * 128], identity=C.ident[0:R, 0:R]), reads=[Ss, C.ident], writes=[ps], join=(jj > 0))
        fw.op(fw.act if j8 % 2 else fw.dve, (lambda e: e.copy(out=pT[:, j8 * 8:(j8 + 1) * 8, :], in_=ps[:, :].rearrange("p (j r) -> p j r", j=8))) if j8 % 2 else
              (lambda e: e.tensor_copy(out=pT[:, j8 * 8:(j8 + 1) * 8, :], in_=ps[:, :].rearrange("p (j r) -> p j r", j=8))), reads=[ps], writes=[pT], join=True)
    ps = C.ps[0]
    fw.op(fw.pe, lambda e: e.transpose(out=ps[0:8, 0:64], in_=Ss[:, 16384:16392], identity=C.ident[0:R, 0:R]), reads=[Ss, C.ident], writes=[ps])
    pTn = fw.sbuf("spTn", [8, 64], BF16)
    fw.op(fw.dve, lambda e: e.tensor_copy(out=pTn[:, :], in_=ps[0:8, 0:64]), reads=[ps], writes=[pTn])
    wk = fw.sbuf("swk", [128, 4, 256], F32)
    wv = fw.sbuf("swv", [128, 4, 256], F32)
    fw.dma(fw.sp, lambda h: h.dma_start(out=wk[:, :, :], in_=C.cwin[0][l, :, :].rearrange("(a p) c -> p a c", p=128)), reads=[C.cwin[0]], writes=[wk])
    fw.dma(fw.sp, lambda h: h.dma_start(out=wv[:, :, :], in_=C.cwin[1][l, :, :].rearrange("(a p) c -> p a c", p=128)), reads=[C.cwin[1]], writes=[wv])
    wvb = fw.sbuf("swvb", [128, 4, 256], BF16)
    fw.op(fw.act, lambda e: e.copy(out=wvb[:, :, :], in_=wv[:, :, :]), reads=[wv], writes=[wvb])
    wkT = fw.sbuf("swkT", [128, 2, 512], BF16)
    for g in range(2):
        ps = C.ps[1 + g]
        for a in range(4):
            fw.op(fw.pe, lambda e: e.transpose(out=ps[:, a * 128:(a + 1) * 128], in_=wk[:, a, g * 128:(g + 1) * 128], identity=C.ident[:, :]), reads=[wk, C.ident], writes=[ps], join=(a > 0))
        fw.op(fw.dve, lambda e: e.tensor_copy(out=wkT[:, g, :], in_=ps[:, :]), reads=[ps], writes=[wkT], join=True)
    Sw = fw.sbuf("sSw", [64, 520], F32)
    pss = C.pacc[1]
    for g in range(2):
        fw.op(fw.pe, lambda e: e.matmul(pss[0:R, :], lhsT=Q[g][:, :], rhs=wkT[:, g, :], start=(g == 0), stop=(g == 1)), reads=[Q[g], wkT], writes=[pss], join=(g > 0))
    fw.op(fw.dve, lambda e: e.tensor_tensor(out=Sw[:, 0:512], in0=pss[0:R, :], in1=cst["wbias"][:, 0:512], op=ALU.add), reads=[pss, cst["wbias"]], writes=[Sw], join=True)
    pss = C.pacc[0]
    for g in range(2):
        fw.op(fw.pe, lambda e: e.matmul(pss[0:R, 0:8], lhsT=Q[g][:, :], rhs=knT[:, 16 + g * 8:16 + (g + 1) * 8], start=(g == 0), stop=(g == 1)), reads=[Q[g], knT], writes=[pss], join=(g > 0))
    fw.op(fw.dve, lambda e: e.tensor_tensor(out=Sw[:, 512:520], in0=pss[0:R, 0:8], in1=cst["wbias"][:, 512:520], op=ALU.add), reads=[pss, cst["wbias"]], writes=[Sw], join=True)
    emit_rows_softmax(fw, Sw, 520, st, g64[:, 2:3], g64)
    pwT = fw.sbuf("spwT", [128, 4, 64], BF16)
    ps = C.ps[3]
    for a in range(4):
        fw.op(fw.pe, lambda e: e.transpose(out=ps[:, a * 64:(a + 1) * 64], in_=Sw[:, a * 128:(a + 1) * 128], identity=C.ident[0:R, 0:R]), reads=[Sw, C.ident], writes=[ps], join=(a > 0))
    fw.op(fw.pe, lambda e: e.transpose(out=ps[0:8, 256:320], in_=Sw[:, 512:520], identity=C.ident[0:R, 0:R]), reads=[Sw, C.ident], writes=[ps], join=True)
    fw.op(fw.dve, lambda e: e.tensor_copy(out=pwT[:, :, :], in_=ps[:, 0:256].rearrange("p (a r) -> p a r", a=4)), reads=[ps], writes=[pwT])
    pwTn = fw.sbuf("spwTn", [8, 64], BF16)
    fw.op(fw.dve, lambda e: e.tensor_copy(out=pwTn[:, :], in_=ps[0:8, 256:320]), reads=[ps], writes=[pwTn])
    accs = [C.pacc[0], C.pacc[1]]
    for g in range(2):
        gs = slice(g * 32, (g + 1) * 32)
        for jb in range(4):
            fw.op(fw.pe, lambda e: e.matmul(accs[g][:, 0:32], lhsT=vcb[:, jb, g * 128:(g + 1) * 128], rhs=pcT[:, jb, gs], start=(jb == 0), stop=False), reads=[vcb, pcT], writes=[accs[g]], join=(jb > 0))
    for jb8 in range(8):
        ch = chunk[ci % 2]
        v_ = vt[ci % 2]
        ci += 1
        fw.dma(fw.pool, lambda h: h.indirect_dma_start(out=ch[:, :], out_offset=None, in_=C.c_slc_v[:, :], in_offset=bass.IndirectOffsetOnAxis(ap=idx8[:, 0:1], axis=0),
                                                     element_offset=(l * 1280 * 8 + jb8) * 4096), reads=[idx8, C.c_slc_v], writes=[ch])
        fw.op(fw.act if jb8 % 2 else fw.dve, (lambda e: e.copy(out=v_[:, :, :], in_=ch[:, :].rearrange("p (j c) -> p j c", j=16))) if jb8 % 2 else
              (lambda e: e.tensor_copy(out=v_[:, :, :], in_=ch[:, :].rearrange("p (j c) -> p j c", j=16))), reads=[ch], writes=[v_])
        for g in range(2):
            gs = slice(g * 32, (g + 1) * 32)
            for jj in range(16):
                j = jb8 * 16 + jj
                fw.op(fw.pe, lambda e: e.matmul(accs[g][:, 0:32], lhsT=v_[:, jj, g * 128:(g + 1) * 128], rhs=pT[:, j, gs], start=False, stop=False), reads=[v_, pT], writes=[accs[g]], join=True)
    for g in range(2):
        acc = accs[g]
        gs = slice(g * 32, (g + 1) * 32)
        fw.op(fw.pe, lambda e: e.matmul(acc[:, 0:32], lhsT=kvb[:, 768 + g * 128:896 + g * 128], rhs=pTn[:, gs], start=False, stop=False), reads=[kvb, pTn], writes=[acc], join=True)
        for a in range(4):
            fw.op(fw.pe, lambda e: e.matmul(acc[:, 0:32], lhsT=wvb[:, a, g * 128:(g + 1) * 128], rhs=pwT[:, a, gs], start=False, stop=False), reads=[wvb, pwT], writes=[acc], join=True)
        fw.op(fw.pe, lambda e: e.matmul(acc[:, 0:32], lhsT=kvb[:, 1280 + g * 128:1408 + g * 128], rhs=pwTn[:, gs], start=False, stop=True), reads=[kvb, pwTn], writes=[acc], join=True)
        fw.op(fw.dve, lambda e: e.tensor_copy(out=mixn_s[:, g * 4:(g + 1) * 4, 0:TS], in_=acc[:, 0:32].rearrange("p (r t) -> p r t", r=4)), reads=[acc], writes=[mixn_s], join=True)
    fw.pop()


def build(mode="full"):
    nc = bass.Bass("TRN2", target_bir_lowering=False)
    fw = FW(nc)
    C = Ctx()
    dbg = mode != "full"
    nlayers = 1 if dbg else DEPTH
    def inp(name, shape, dtype=F32):
        return fw.dram(name, shape, dtype, kind="ExternalInput")
    C.xp = inp("xp", [S, D])
    C.xs = inp("xs", [TS, D])
    C.pt = inp("pt", [128, 1], I32)
    C.c_cmp_k = inp("c_cmp_k", [DEPTH * 1280 * 8, 4096])
    C.c_cmp_v = inp("c_cmp_v", [DEPTH * 1280 * 8, 4096])
    C.c_slc_k = inp("c_slc_k", [DEPTH * 1280 * 8, 4096])
    C.c_slc_v = inp("c_slc_v", [DEPTH * 1280 * 8, 4096])
    C.cwin = [inp("c_win_k", [DEPTH, 512, 256]), inp("c_win_v", [DEPTH, 512, 256])]
    C.state_conv = inp("state_conv", [DEPTH, 30, 512])
    for nm, shp in INPUT_SPECS:
        setattr(C, nm, inp(nm, shp))
    C.cd = {nm: inp(nm, shp) for nm, shp in CONST_SHAPES.items()}
    C.out_kv = fw.dram("o_pkv", [DEPTH, 6, S, 256], F32, kind="ExternalOutput")
    C.out_conv_buf = fw.dram("o_pconv", [DEPTH, 30, 512], F32, kind="ExternalOutput")
    C.out_gm = fw.dram("o_pgm", [DEPTH, 128, 512], F32, kind="ExternalOutput")
    C.out_gm_buf = C.out_gm
    C.out_skv = fw.dram("o_skv", [DEPTH, 6, TS, 256], F32, kind="ExternalOutput")
    C.out_swin = fw.dram("o_swin", [DEPTH, 2, 512, 256], F32, kind="ExternalOutput")
    C.out_sconv = fw.dram("o_sconv", [DEPTH, 30, 512], F32, kind="ExternalOutput")
    C.out_sgm = fw.dram("o_sgm", [DEPTH, TS, 512], F32, kind="ExternalOutput")
    C.out_y = fw.dram("o_y", [S // 2, D], F32, kind="ExternalOutput")
    C.prow_d = inp("prow", [128, NT // 2], I32)
    C.prow = fw.sbuf("prow_sb", [128, NT // 2], I32)
    fw.dma(fw.sp, lambda h: h.dma_start(out=C.prow[:, :], in_=C.prow_d[:, :]), reads=[C.prow_d], writes=[C.prow])
    C.out_ys = fw.dram("o_ys", [TS, D], F32, kind="ExternalOutput")
    C.ident = load_const(fw, C, "ident")
    C.identb = fw.sbuf("identb", [128, 128], BF16)
    fw.op(fw.dve, lambda e: e.tensor_copy(out=C.identb[:, :], in_=C.ident[:, :]), reads=[C.ident], writes=[C.identb])
    C.epsc = fw.sbuf("epsc", [128, 1], F32)
    fw.op(fw.dve, lambda e: e.memset(C.epsc[:, :], LN_EPS), writes=[C.epsc])
    for nm in ["causal", "acausal", "mask4", "triu", "onesdiv"]:
        setattr(C, nm, load_const(fw, C, nm))
    C.ps = [fw.psum(f"ps{i}", [128, 512], F32) for i in range(4)]
    C.psb = [fw.psum(f"psb{i}", [128, 1024], BF16) for i in range(2)]
    C.pacc = [fw.psum(f"pacc{i}", [128, 512], F32) for i in range(2)]
    okind = "ExternalOutput" if dbg else "Internal"
    C.Htok = fw.dram("Htok", [S, NTOKC], F32)
    C.HfT = fw.dram("HfT", [1536, S], F32)
    C.HtokS = fw.dram("HtokS", [TS, NTOKC], F32)
    C.HfTS = fw.dram("HfTS", [1536, TS], F32)
    C.gscr = fw.dram("gscr", [TS, 24], F32)
    C.X1 = fw.dram("X1", [S, D], F32, kind=okind)
    C.XS1 = fw.dram("XS1", [TS, D], F32, kind=okind)
    X2 = [fw.dram("X2a", [S, D], F32), C.out_y] if not dbg else [C.out_y]
    XS2 = [fw.dram("XS2a", [TS, D], F32), C.out_ys] if not dbg else [C.out_ys]
    mixn_s = fw.sbuf("mixn_s", [128, 8, TS], BF16)
    convTs = fw.sbuf("convTs", [128, 4, TS], BF16)
    x_src, xs_src = C.xp, C.xs
    for l in range(nlayers):
        fw.push()
        C.xin = [fw.sbuf(f"xin{i}", [128, D], F32) for i in range(2)]
        C.wbuf = [fw.sbuf(f"wbuf{i}", [128, 16, 512], BF16) for i in range(2)]
        C.hbuf = [fw.sbuf(f"hbuf{i}", [128, 512], F32) for i in range(4)]
        C.htmp = fw.sbuf("htmp", [128, 512], F32)
        xT = fw.sbuf("xT", [128, 16, S], BF16)
        phase_proj(fw, C, l, x_src, NT, C.Htok, C.HfT, xT, "p")
        phase_proj(fw, C, l, xs_src, 0, C.HtokS, C.HfTS, xT, "s")
        fw.pop()
        phase_nsa_sample(fw, C, l, C.HtokS, mixn_s)
        fw.push()
        C.kT = fw.sbuf("kT", [128, 4, S], BF16)
        C.V = fw.sbuf("V", [128, NT, 2, 256], BF16)
        C.kcmpT = fw.sbuf("kcmpT", [128, 2, 64], BF16)
        C.vcmp = fw.sbuf("vcmp", [64, 256], BF16)
        C.convT = fw.sbuf("convT", [128, 4, S], BF16)
        fw.push()
        phase_kv_prompt(fw, C, l)
        fw.pop()
        phase_conv(fw, C, l, C.HfT, S, None, C.out_conv_buf[l, :, :], C.convT)
        phase_conv(fw, C, l, C.HfTS, TS, C.state_conv[l, :, :], C.out_sconv[l, :, :], convTs)
        phase_mix_prompt(fw, C, l, x_src, C.X1, (xs_src, C.XS1, mixn_s, convTs))
        fw.pop()
        if l == nlayers - 1:
            phase_peer(fw, C, l, [(C.X1, NT // 2 if not dbg else 1, 128, X2[l], C.prow), (C.XS1, 1, TS, XS2[l])])
        else:
            phase_peer(fw, C, l, [(C.X1, NT, 128, X2[l]), (C.XS1, 1, TS, XS2[l])])
        x_src, xs_src = X2[l], XS2[l]
    fw.finish()
    return nc


def core_inputs(inputs, c):
    b = c // 2
    m = {"xp": np.ascontiguousarray(inputs["x_prompt"][b]), "xs": np.ascontiguousarray(inputs["x_sample"][c]),
         "pt": np.ascontiguousarray(inputs["page_table"][c].reshape(128, 1)).astype(np.int32)}
    for nm, key in [("c_cmp_k", "cache_cmp_k"), ("c_cmp_v", "cache_cmp_v"), ("c_slc_k", "cache_slc_k"), ("c_slc_v", "cache_slc_v")]:
        m[nm] = np.asarray(inputs[key]).reshape(DEPTH * 1280 * 8, 4096)
    m["c_win_k"] = np.ascontiguousarray(np.asarray(inputs["cache_win_k"])[:, c].reshape(DEPTH, 512, 256))
    m["c_win_v"] = np.ascontiguousarray(np.asarray(inputs["cache_win_v"])[:, c].reshape(DEPTH, 512, 256))
    m["state_conv"] = np.ascontiguousarray(np.asarray(inputs["state_conv"])[:, c])
    hh = c % 2
    m["prow"] = ((hh * (NT // 2) + np.arange(NT // 2)[None, :]) * 128 + np.arange(128)[:, None]).astype(np.int32)
    for nm, shp in INPUT_SPECS:
        m[nm] = np.asarray(inputs[nm])
    m.update(host_consts())
    return m


_NC_CACHE = {}


def kernel(**inputs):
    n = 8
    if "nc" not in _NC_CACHE:
        _NC_CACHE["nc"] = build("full")
    nc = _NC_CACHE["nc"]
    in_maps = [core_inputs(inputs, c) for c in range(n)]
    res = run_bass_kernel_spmd(nc, in_maps, core_ids=list(range(n))).results
    B = 4
    ev = [res[2 * b] for b in range(B)]
    y_prompt = np.stack([np.concatenate([res[2 * b]["o_y"], res[2 * b + 1]["o_y"]], 0) for b in range(B)], 0).astype(np.float32)
    y_sample = np.stack([res[c]["o_ys"] for c in range(n)], 0).astype(np.float32)
    pkv = np.stack([r["o_pkv"] for r in ev], 0)
    def pk(a, rows=None):
        t = pkv[:, :, a]
        if rows is not None:
            t = t[:, :, rows:]
        t = np.transpose(t, (1, 0, 2, 3))
        return np.ascontiguousarray(t.reshape(t.shape[0], t.shape[1], t.shape[2], 2, 128)).astype(np.float32)
    p_conv = np.ascontiguousarray(np.stack([r["o_pconv"] for r in ev], 1)).astype(np.float32)
    p_gm = np.ascontiguousarray(np.stack([r["o_pgm"] for r in ev], 1)).astype(np.float32)
    skv = np.stack([res[c]["o_skv"] for c in range(n)], 0)
    def sk(a):
        t = np.transpose(skv[:, :, a], (1, 0, 2, 3))
        return np.ascontiguousarray(t.reshape(DEPTH, n, TS, 2, 128)).astype(np.float32)
    swin = np.stack([res[c]["o_swin"] for c in range(n)], 0)
    def sw(a):
        t = np.transpose(swin[:, :, a], (1, 0, 2, 3))
        return np.ascontiguousarray(t.reshape(DEPTH, n, 512, 2, 128)).astype(np.float32)
    s_conv = np.ascontiguousarray(np.stack([res[c]["o_sconv"] for c in range(n)], 1)).astype(np.float32)
    s_gm = np.ascontiguousarray(np.stack([res[c]["o_sgm"] for c in range(n)], 1)).astype(np.float32)
    return (y_prompt, y_sample, pk(0), pk(1), pk(2), pk(3), pk(4, S - 512), pk(5, S - 512), p_conv, p_gm,
            sk(0), sk(1), sk(2), sk(3), sw(0), sw(1), s_conv, s_gm)
```

```python
import numpy as np
from contextlib import ExitStack
import concourse.bass as bass
import concourse.mybir as mybir
from concourse.bass_utils import run_bass_kernel_spmd

F32 = mybir.dt.float32
BF16 = mybir.dt.bfloat16
I32 = mybir.dt.int32
U32 = mybir.dt.uint32
AF = mybir.ActivationFunctionType
ALU = mybir.AluOpType
AX = mybir.AxisListType


class Buf:
    def __init__(self, t, name):
        self.t = t
        self.name = name
        self.w = {}
        self.r = {}

    def __getitem__(self, idx):
        return self.t[idx]


class Eng:
    def __init__(self, fw, name, h, pe=False, ndma=0):
        self.fw = fw
        self.name = name
        self.h = h
        self.pe = pe
        self.sem = fw.new_sem(name + "_prog")
        self.n = 0
        self.seen = {}
        self.dma_sems = [fw.new_sem(f"{name}_d{i}") for i in range(ndma)]
        self.dma_n = 0

    def wait(self, tok):
        key, sem, val = tok
        if self.seen.get(key, 0) >= val:
            return
        self.h.wait_ge(sem, val)
        self.seen[key] = val


class FW:
    def __init__(self, nc):
        self.nc = nc
        self.es0 = ExitStack()
        self.es = self.es0
        self.es_stack = []
        self.sems = []
        self.pe = Eng(self, "pe", nc.tensor, pe=True)
        self.act = Eng(self, "act", nc.scalar, ndma=8)
        self.dve = Eng(self, "dve", nc.vector)
        self.pool = Eng(self, "pool", nc.gpsimd, ndma=24)
        self.sp = Eng(self, "sp", nc.sync, ndma=24)
        self.engs = [self.pe, self.act, self.dve, self.pool, self.sp]
        self.bufs = []

    def new_sem(self, name):
        s = self.es0.enter_context(self.nc.semaphore(name))
        self.sems.append(s)
        return s

    def push(self):
        self.es_stack.append((self.es, len(self.bufs)))
        self.es = ExitStack()

    def pop(self):
        self.barrier()
        self.es.close()
        self.es, nb = self.es_stack.pop()
        del self.bufs[nb:]

    def sbuf(self, name, shape, dtype=F32):
        self.uid = getattr(self, "uid", 0) + 1
        name = f"{name}_{self.uid}"
        t = self.es.enter_context(self.nc.sbuf_tensor(name, list(shape), dtype))
        b = Buf(t, name)
        self.bufs.append(b)
        return b

    def psum(self, name, shape, dtype=F32):
        t = self.es.enter_context(self.nc.psum_tensor(name, list(shape), dtype))
        b = Buf(t, name)
        self.bufs.append(b)
        return b

    def dram(self, name, shape, dtype=F32, kind="Internal"):
        t = self.nc.dram_tensor(name, list(shape), dtype, kind=kind).ap()
        b = Buf(t, name)
        self.bufs.append(b)
        return b

    def view(self, ap, name="v"):
        b = Buf(ap, name)
        self.bufs.append(b)
        return b

    def _waits(self, eng, reads, writes, join, is_dma):
        for b in reads:
            for tok in b.w.values():
                if tok[0] == eng.name and not is_dma and eng.pe:
                    continue
                eng.wait(tok)
        for b in writes:
            if not join:
                for tok in b.w.values():
                    if tok[0] == eng.name and not is_dma:
                        continue
                    eng.wait(tok)
            for tok in b.r.values():
                if tok[0] == eng.name and not is_dma:
                    continue
                eng.wait(tok)

    def _record(self, tok, reads, writes, join):
        for b in writes:
            if join:
                b.w[tok[0]] = tok
            else:
                b.w = {tok[0]: tok}
            b.r = {}
        for b in reads:
            if b not in writes:
                b.r[tok[0]] = tok

    def op(self, eng, fn, reads=(), writes=(), join=False):
        self._waits(eng, reads, writes, join, False)
        inst = fn(eng.h)
        eng.n += 1
        inst.then_inc(eng.sem, 1)
        tok = (eng.name, eng.sem, eng.n)
        self._record(tok, reads, writes, join)
        return inst

    def dma(self, q, fn, reads=(), writes=(), join=False):
        self._waits(q, reads, writes, join, True)
        k = len(q.dma_sems)
        i = q.dma_n % k
        gen = q.dma_n // k
        sem = q.dma_sems[i]
        key = f"{q.name}_d{i}"
        if gen > 0:
            q.wait((key, sem, 16 * gen))
        inst = fn(q.h)
        inst.then_inc(sem, 16)
        q.dma_n += 1
        tok = (key, sem, 16 * (gen + 1))
        self._record(tok, reads, writes, join)
        return inst

    def barrier(self):
        toks = []
        for e in self.engs:
            if e.n:
                toks.append((e.name, e.sem, e.n))
            k = len(e.dma_sems)
            for i in range(min(k, e.dma_n)):
                cnt = (e.dma_n - 1 - i) // k + 1
                toks.append((f"{e.name}_d{i}", e.dma_sems[i], 16 * cnt))
        for e in self.engs:
            for tok in toks:
                if tok[0] == e.name:
                    continue
                e.wait(tok)
        for b in self.bufs:
            b.w = {}
            b.r = {}

    def finish(self):
        self.barrier()


D = 2048
S = 2048
NT = S // 128
DEPTH = 2
TS = 8
IN_W = 4632
TOK0, TOK1 = 512, 3608
NTOKC = TOK1 - TOK0
ALPHA = float((2 * DEPTH) ** 0.25)
LN_EPS = 1e-5
GELU_C = 0.7978845608028654
NEG = -30000.0


class Ctx:
    pass


def emit_gelu(fw, eng_dve, eng_act, out_ap, in_ap, tmp_ap, bufs_r, bufs_w, tmpbuf):
    fw.op(eng_act, lambda e: e.activation(out=tmp_ap, in_=in_ap, func=AF.Square), reads=bufs_r, writes=[tmpbuf])
    fw.op(eng_dve, lambda e: e.tensor_scalar(out=tmp_ap, in0=tmp_ap, scalar1=0.044715, scalar2=1.0, op0=ALU.mult, op1=ALU.add),
          reads=[tmpbuf], writes=[tmpbuf])
    fw.op(eng_dve, lambda e: e.tensor_tensor(out=tmp_ap, in0=tmp_ap, in1=in_ap, op=ALU.mult), reads=bufs_r + [tmpbuf], writes=[tmpbuf])
    fw.op(eng_act, lambda e: e.activation(out=tmp_ap, in_=tmp_ap, func=AF.Sigmoid, scale=2.0 * GELU_C), reads=[tmpbuf], writes=[tmpbuf])
    fw.op(eng_dve, lambda e: e.tensor_tensor(out=out_ap, in0=tmp_ap, in1=in_ap, op=ALU.mult), reads=bufs_r + [tmpbuf], writes=bufs_w)


def phase_proj(fw, C, l, x_src, nt, Htok, HfT, xT, tagp):
    nc = fw.nc
    ntok = nt * 128 if nt > 0 else TS
    P = 128 if nt > 0 else TS
    ntile = max(nt, 1)
    for t in range(ntile):
        xt = C.xin[t % 2]
        fw.dma(fw.sp, lambda h: h.dma_start(out=xt[0:P, :], in_=x_src[t * 128:t * 128 + P, :]), reads=[x_src], writes=[xt])
        for cb in range(4):
            ps = C.ps[(t * 4 + cb) % 4]
            for k in range(4):
                c = cb * 4 + k
                fw.op(fw.pe, lambda e: e.transpose(out=ps[:, k * 128:k * 128 + P], in_=xt[0:P, c * 128:(c + 1) * 128], identity=C.ident[0:P, 0:P]),
                      reads=[xt, C.ident], writes=[ps], join=(k > 0))
            eng = fw.dve if cb % 2 == 0 else fw.act
            src = ps[:, :].rearrange("p (k q) -> p k q", k=4)[:, :, 0:P]
            dst = xT[:, cb * 4:(cb + 1) * 4, t * 128:t * 128 + P]
            if eng is fw.dve:
                fw.op(eng, lambda e: e.tensor_copy(out=dst, in_=src), reads=[ps], writes=[xT], join=True)
            else:
                fw.op(eng, lambda e: e.copy(out=dst, in_=src), reads=[ps], writes=[xT], join=True)
    w_in = C.w_in
    ncol = [(TOK0 + j * 512, min(512, TOK1 - (TOK0 + j * 512))) for j in range((NTOKC + 511) // 512)]
    it = 0
    for j, (c0, cw) in enumerate(ncol):
        wb = C.wbuf[j % 2]
        fw.dma(fw.pool, lambda h: h.dma_start(out=wb[:, :, 0:cw], in_=w_in[l, :, c0:c0 + cw].rearrange("(c p) n -> p c n", p=128)),
               reads=[w_in], writes=[wb])
        for t in range(ntile):
            ps = C.ps[it % 4]
            for c in range(16):
                fw.op(fw.pe, lambda e: e.matmul(ps[0:P, 0:cw], lhsT=xT[:, c, t * 128:t * 128 + P], rhs=wb[:, c, 0:cw], start=(c == 0), stop=(c == 15)),
                      reads=[xT, wb], writes=[ps], join=(c > 0))
            hb = C.hbuf[it % 4]
            if it % 2 == 0:
                fw.op(fw.dve, lambda e: e.tensor_copy(out=hb[0:P, 0:cw], in_=ps[0:P, 0:cw]), reads=[ps], writes=[hb])
            else:
                fw.op(fw.act, lambda e: e.copy(out=hb[0:P, 0:cw], in_=ps[0:P, 0:cw]), reads=[ps], writes=[hb])
            fw.dma(fw.sp, lambda h: h.dma_start(out=Htok[t * 128:t * 128 + P, c0 - TOK0:c0 - TOK0 + cw], in_=hb[0:P, 0:cw]),
                   reads=[hb], writes=[Htok], join=True)
            it += 1
    fcols = [(0, 0), (128, 128), (256, 256), (384, 384)] + [(3608 + i * 128, 512 + i * 128) for i in range(8)]
    TB = 512 if nt > 0 else TS
    ntb = max(ntok // 512, 1)
    for j, (c0, r0) in enumerate(fcols):
        wb = C.wbuf[j % 2]
        fw.dma(fw.pool, lambda h: h.dma_start(out=wb[:, :, 0:128], in_=w_in[l, :, c0:c0 + 128].rearrange("(c p) n -> p c n", p=128)),
               reads=[w_in], writes=[wb])
        for tb in range(ntb):
            ps = C.ps[it % 4]
            for c in range(16):
                fw.op(fw.pe, lambda e: e.matmul(ps[:, 0:TB], lhsT=wb[:, c, 0:128], rhs=xT[:, c, tb * 512:tb * 512 + TB], start=(c == 0), stop=(c == 15)),
                      reads=[xT, wb], writes=[ps], join=(c > 0))
            hb = C.hbuf[it % 4]
            if r0 < 512:
                tmp = C.htmp
                emit_gelu(fw, fw.dve, fw.act, hb[:, 0:TB], ps[:, 0:TB], tmp[:, 0:TB], [ps], [hb], tmp)
            elif it % 2 == 0:
                fw.op(fw.dve, lambda e: e.tensor_copy(out=hb[:, 0:TB], in_=ps[:, 0:TB]), reads=[ps], writes=[hb])
            else:
                fw.op(fw.act, lambda e: e.copy(out=hb[:, 0:TB], in_=ps[:, 0:TB]), reads=[ps], writes=[hb])
            fw.dma(fw.sp, lambda h: h.dma_start(out=HfT[r0:r0 + 128, tb * 512:tb * 512 + TB], in_=hb[:, 0:TB]),
                   reads=[hb], writes=[HfT], join=True)
            it += 1


def host_consts():
    c = {}
    c["ident"] = np.eye(128, dtype=np.float32)
    half = 64
    inv = (np.float32(10000.0) ** (-np.arange(half, dtype=np.float32) / np.float32(half))).astype(np.float32)
    def rope_tab(pos):
        ang = pos.astype(np.float32)[:, None] * inv[None, :]
        cs, sn = np.cos(ang).astype(np.float32), np.sin(ang).astype(np.float32)
        sc = np.float32(128 ** -0.5)
        return np.stack([cs, sn, cs * sc, sn * sc], axis=1).astype(np.float32)
    c["rope_p"] = rope_tab(np.arange(S))
    c["rope_s"] = rope_tab(16384 + np.arange(TS))
    q = np.arange(128)[:, None, None]
    t = np.arange(NT)[None, :, None]
    n = np.arange(64)[None, None, :]
    qpos = 128 * t + q
    c["cmpbias"] = np.where(32 * n + 31 <= qpos, 0.0, NEG).astype(np.float32)
    c["cmpvalid"] = (qpos[:, :, 0] >= 31).astype(np.float32)
    m = np.arange(32)[None, None, :]
    cur = qpos // 64
    valid = 64 * m <= qpos
    forced = (m == 0) | (m == cur) | (m == cur - 1)
    c["selA"] = (valid & ~forced).astype(np.float32)
    c["selB"] = np.where(valid, np.where(forced, 1.0e4, 0.0), -1.0e9).astype(np.float32)
    qq = np.arange(128)[:, None]; kk = np.arange(128)[None, :]
    c["causal"] = np.where(kk <= qq, 0.0, NEG).astype(np.float32)
    c["acausal"] = np.where(kk >= qq, 0.0, NEG).astype(np.float32)
    tok = np.arange(128)[:, None]
    c["mask4"] = (tok // 32 == np.arange(4)[None, :]).astype(np.float32)
    c["maskpad"] = (np.arange(64)[None, None, :] == 4 * np.arange(NT)[None, :, None] + (tok // 32)[:, :, None]).astype(np.float32)
    c["triu"] = (qq <= kk).astype(np.float32)
    c["onesdiv"] = np.full((128, 128), 1.0 / 128, np.float32)
    c["iota16"] = np.tile(np.arange(16, dtype=np.float32)[None, :], (128, 1))
    c.update(sample_consts())
    return c


CONST_SHAPES = {"ident": [128, 128], "rope_p": [S, 4, 64], "rope_s": [TS, 4, 64], "cmpbias": [128, NT, 64], "cmpvalid": [128, NT],
                "selA": [128, NT, 32], "selB": [128, NT, 32], "causal": [128, 128], "acausal": [128, 128], "mask4": [128, 4],
                "maskpad": [128, NT, 64], "triu": [128, 128], "onesdiv": [128, 128], "iota16": [128, 16]}


def load_const(fw, C, name, dtype=F32):
    shp = CONST_SHAPES[name]
    b = fw.sbuf("c_" + name, shp, F32)
    src = C.cd[name]
    idx = tuple(slice(None) for _ in shp)
    fw.dma(fw.sp, lambda h: h.dma_start(out=b[idx], in_=src[idx]), reads=[src], writes=[b])
    return b


def load_colvec(fw, C, vec_ap, n, name, srcbuf):
    rows = fw.sbuf(name + "_r", [n, 128], F32)
    fw.dma(fw.sp, lambda h: h.dma_start(out=rows[:, :], in_=vec_ap.rearrange("(j p) -> j p", p=128)), reads=[srcbuf], writes=[rows])
    ps = C.ps[0]
    fw.op(fw.pe, lambda e: e.transpose(out=ps[:, 0:n], in_=rows[0:n, :], identity=C.ident[0:n, 0:n]), reads=[rows, C.ident], writes=[ps])
    col = fw.sbuf(name, [128, n], F32)
    fw.op(fw.dve, lambda e: e.tensor_copy(out=col[:, :], in_=ps[:, 0:n]), reads=[ps], writes=[col])
    return col


def emit_rope(fw, x1, x2, cs, sn, tmp, shape, rbufs, xbuf, tmpbuf):
    nd = len(shape)
    def bc(a):
        v = a
        for _ in range(nd - 2):
            v = v.unsqueeze(1)
        return v.to_broadcast(list(shape))
    t1, t2, t3, t4 = tmp
    fw.op(fw.dve, lambda e: e.tensor_tensor(out=t1, in0=x1, in1=bc(cs), op=ALU.mult), reads=[xbuf] + rbufs, writes=[tmpbuf[0]])
    fw.op(fw.dve, lambda e: e.tensor_tensor(out=t2, in0=x2, in1=bc(sn), op=ALU.mult), reads=[xbuf] + rbufs, writes=[tmpbuf[1]])
    fw.op(fw.pool, lambda e: e.tensor_tensor(out=t3, in0=x2, in1=bc(cs), op=ALU.mult), reads=[xbuf] + rbufs, writes=[tmpbuf[2]])
    fw.op(fw.pool, lambda e: e.tensor_tensor(out=t4, in0=x1, in1=bc(sn), op=ALU.mult), reads=[xbuf] + rbufs, writes=[tmpbuf[3]])
    fw.op(fw.dve, lambda e: e.tensor_tensor(out=x1, in0=t1, in1=t2, op=ALU.subtract), reads=[tmpbuf[0], tmpbuf[1], tmpbuf[2], tmpbuf[3]], writes=[xbuf])
    fw.op(fw.dve, lambda e: e.tensor_tensor(out=x2, in0=t3, in1=t4, op=ALU.add), reads=[tmpbuf[2], tmpbuf[3]], writes=[xbuf])


def phase_kv_prompt(fw, C, l):
    P = 128
    C.maskpad = load_const(fw, C, "maskpad")
    wcol = fw.sbuf("wcol", [128, 4], F32)
    for wi, wsrc in enumerate([C.cmp_wk, C.cmp_wv]):
        for g in range(2):
            for blk in range(4):
                fw.dma(fw.sp, lambda h: h.dma_start(out=wcol[blk * 32:(blk + 1) * 32, wi * 2 + g:wi * 2 + g + 1],
                                                    in_=wsrc[l, g, :].rearrange("(j o) -> j o", o=1)), reads=[wsrc], writes=[wcol], join=True)
    Wck = fw.sbuf("Wck", [128, 2, 4], BF16)
    WcvPad = fw.sbuf("WcvPad", [128, 2, NT * 64], BF16)
    for g in range(2):
        fw.op(fw.dve, lambda e: e.tensor_scalar(out=Wck[:, g, :], in0=C.mask4[:, :], scalar1=wcol[:, g:g + 1], scalar2=None, op0=ALU.mult),
              reads=[C.mask4, wcol], writes=[Wck], join=True)
        fw.op(fw.dve, lambda e: e.tensor_scalar(out=WcvPad[:, g, :], in0=C.maskpad[:, :, :].rearrange("p t n -> p (t n)"), scalar1=wcol[:, 2 + g:3 + g], scalar2=None, op0=ALU.mult),
              reads=[C.maskpad, wcol], writes=[WcvPad], join=True)
    vacc = fw.sbuf("vacc", [64, 256], F32)
    kvin = [fw.sbuf(f"kvin{i}", [128, 1536], F32) for i in range(2)]
    kvb = [fw.sbuf(f"kvb{i}", [128, 1536], BF16) for i in range(2)]
    rtab = [fw.sbuf(f"rtab{i}", [128, 4, 64], F32) for i in range(2)]
    rtmp = [fw.sbuf(f"rtmp{i}", [128, 384], F32) for i in range(4)]
    for t in range(NT):
        kv = kvin[t % 2]
        kb = kvb[t % 2]
        rt = rtab[t % 2]
        fw.dma(fw.sp, lambda h: h.dma_start(out=kv[:, :], in_=C.Htok[t * 128:(t + 1) * 128, 1536:3072]), reads=[C.Htok], writes=[kv])
        fw.dma(fw.sp, lambda h: h.dma_start(out=rt[:, :, :], in_=C.cd["rope_p"][t * 128:(t + 1) * 128, :, :]), reads=[C.cd["rope_p"]], writes=[rt])
        kview = kv[:, :].rearrange("p (a v g h d) -> p a v g h d", a=3, v=2, g=2, h=2, d=64)
        x1 = kview[:, :, 0, :, 0, :]
        x2 = kview[:, :, 0, :, 1, :]
        tv = [r[:, :].rearrange("p (a g d) -> p a g d", a=3, g=2) for r in rtmp]
        emit_rope(fw, x1, x2, rt[:, 0, :], rt[:, 1, :], tv, [128, 3, 2, 64], [rt], kv, rtmp)
        for a in range(6):
            fw.dma(fw.sp, lambda h: h.dma_start(out=C.out_kv[l, a, t * 128:(t + 1) * 128, :], in_=kv[:, a * 256:(a + 1) * 256]),
                   reads=[kv], writes=[C.out_kv], join=True)
        fw.op(fw.act, lambda e: e.copy(out=kb[:, :], in_=kv[:, :]), reads=[kv], writes=[kb])
        psb = C.psb[t % 2]
        for i, (a, g) in enumerate([(1, 0), (1, 1), (2, 0), (2, 1)]):
            c0 = a * 512 + g * 128
            fw.op(fw.pe, lambda e: e.transpose(out=psb[:, i * 128:(i + 1) * 128], in_=kb[:, c0:c0 + 128], identity=C.identb[:, :]),
                  reads=[kb, C.identb], writes=[psb], join=(i > 0))
        fw.op(fw.dve, lambda e: e.tensor_copy(out=C.kT[:, :, t * 128:(t + 1) * 128], in_=psb[:, 0:512].rearrange("p (i q) -> p i q", i=4)),
              reads=[psb], writes=[C.kT], join=True)
        ps = C.ps[t % 4]
        for g in range(2):
            fw.op(fw.pe, lambda e: e.matmul(ps[:, g * 4:(g + 1) * 4], lhsT=kb[:, g * 128:(g + 1) * 128], rhs=Wck[:, g, :], start=True, stop=True),
                  reads=[kb, Wck], writes=[ps], join=(g > 0))
        for g in range(2):
            fw.op(fw.pe, lambda e: e.matmul(ps[0:64, 128 + g * 128:256 + g * 128], lhsT=WcvPad[:, g, t * 64:(t + 1) * 64], rhs=kb[:, 256 + g * 128:384 + g * 128], start=True, stop=True),
                  reads=[kb, WcvPad], writes=[ps], join=True)
        fw.op(fw.dve, lambda e: e.tensor_copy(out=C.kcmpT[:, :, t * 4:(t + 1) * 4], in_=ps[:, 0:8].rearrange("p (g n) -> p g n", g=2)),
              reads=[ps], writes=[C.kcmpT], join=True)
        if t == 0:
            fw.op(fw.dve, lambda e: e.tensor_copy(out=vacc[:, :], in_=ps[0:64, 128:384]), reads=[ps], writes=[vacc])
        else:
            fw.op(fw.dve, lambda e: e.tensor_tensor(out=vacc[:, :], in0=vacc[:, :], in1=ps[0:64, 128:384], op=ALU.add), reads=[ps, vacc], writes=[vacc])
        vsrc = kb[:, 512:1536].rearrange("p (a v c) -> p a v c", a=2, v=2)[:, :, 1, :]
        fw.op(fw.pool, lambda e: e.tensor_copy(out=C.V[:, t, :, :], in_=vsrc), reads=[kb], writes=[C.V], join=True)
    fw.op(fw.dve, lambda e: e.tensor_copy(out=C.vcmp[:, :], in_=vacc[:, :]), reads=[vacc], writes=[C.vcmp])


def emit_ln_partition(fw, C, y, blk_cols, g_ap, b_ap, gb_bufs, out_ap_fn, tmpA, tmpB, func=None):
    ncols = blk_cols
    for c0 in range(0, ncols, 512):
        cw = min(512, ncols - c0)
        ps1 = C.ps[0]
        ps2 = C.ps[1]
        ysl = y[:, c0:c0 + cw]
        fw.op(fw.pe, lambda e: e.matmul(ps1[:, 0:cw], lhsT=C.onesdiv[:, :], rhs=ysl, start=True, stop=True), reads=[C.onesdiv, y], writes=[ps1])
        fw.op(fw.dve, lambda e: e.tensor_tensor(out=ysl, in0=ysl, in1=ps1[:, 0:cw], op=ALU.subtract), reads=[ps1, y], writes=[y])
        fw.op(fw.act, lambda e: e.activation(out=tmpA[:, 0:cw], in_=ysl, func=AF.Square), reads=[y], writes=[tmpA])
        fw.op(fw.pe, lambda e: e.matmul(ps2[:, 0:cw], lhsT=C.onesdiv[:, :], rhs=tmpA[:, 0:cw], start=True, stop=True), reads=[C.onesdiv, tmpA], writes=[ps2])
        fw.op(fw.act, lambda e: e.activation(out=tmpB[:, 0:cw], in_=ps2[:, 0:cw], func=AF.Sqrt, bias=C.epsc[:, 0:1], scale=1.0), reads=[ps2, C.epsc], writes=[tmpB])
        fw.op(fw.dve, lambda e: e.reciprocal(out=tmpB[:, 0:cw], in_=tmpB[:, 0:cw]), reads=[tmpB], writes=[tmpB])
        fw.op(fw.dve, lambda e: e.tensor_tensor(out=tmpA[:, 0:cw], in0=ysl, in1=tmpB[:, 0:cw], op=ALU.mult), reads=[y, tmpB], writes=[tmpA])
        oap, obuf = out_ap_fn(c0, cw)
        if func is None:
            fw.op(fw.dve, lambda e: e.tensor_scalar(out=oap, in0=tmpA[:, 0:cw], scalar1=g_ap, scalar2=b_ap, op0=ALU.mult, op1=ALU.add),
                  reads=[tmpA] + gb_bufs, writes=[obuf], join=True)
        else:
            fw.op(fw.dve, lambda e: e.tensor_scalar(out=tmpA[:, 0:cw], in0=tmpA[:, 0:cw], scalar1=g_ap, scalar2=b_ap, op0=ALU.mult, op1=ALU.add),
                  reads=[tmpA] + gb_bufs, writes=[tmpA])
            fw.op(fw.act, lambda e: e.activation(out=oap, in_=tmpA[:, 0:cw], func=func), reads=[tmpA], writes=[obuf], join=True)


def phase_conv(fw, C, l, HfT, T, hist_src, out_conv, convT):
    fw.push()
    dwr = fw.sbuf("dwr", [31, 512], F32)
    fw.dma(fw.sp, lambda h: h.dma_start(out=dwr[:, :], in_=C.conv_dw[l, :, :]), reads=[C.conv_dw], writes=[dwr])
    dwT = fw.sbuf("dwT", [128, 4, 31], F32)
    ps = C.ps[2]
    for cc in range(4):
        fw.op(fw.pe, lambda e: e.transpose(out=ps[:, cc * 32:cc * 32 + 31], in_=dwr[0:31, cc * 128:(cc + 1) * 128], identity=C.ident[0:31, 0:31]),
              reads=[dwr, C.ident], writes=[ps], join=(cc > 0))
    fw.op(fw.dve, lambda e: e.tensor_copy(out=dwT[:, :, :], in_=ps[:, 0:128].rearrange("p (c k) -> p c k", c=4)[:, :, 0:31]), reads=[ps], writes=[dwT])
    dbc = load_colvec(fw, C, C.conv_db[l, :], 4, "dbc", C.conv_db)
    lng = load_colvec(fw, C, C.conv_ln_g[l, :], 4, "clng", C.conv_ln_g)
    lnb = load_colvec(fw, C, C.conv_ln_b[l, :], 4, "clnb", C.conv_ln_b)
    pw = fw.sbuf("pw", [128, 4, 512], BF16)
    fw.dma(fw.pool, lambda h: h.dma_start(out=pw[:, :, :], in_=C.conv_pw[l, :, :].rearrange("(c p) n -> p c n", p=128)), reads=[C.conv_pw], writes=[pw])
    zT = fw.sbuf("zT", [128, 4, T], BF16)
    pcs = fw.sbuf("pcs", [30, 512], F32)
    psc = C.ps[3]
    sets = []
    for i in range(2):
        sets.append(dict(ca=fw.sbuf(f"cca{i}", [128, T], F32), cb=fw.sbuf(f"ccb{i}", [128, T], F32),
                         cseq=fw.sbuf(f"cseq{i}", [128, 30 + T], F32), y=fw.sbuf(f"cy{i}", [128, T], F32)))
    tmpA = fw.sbuf("ctmpA", [128, 512], F32)
    tmpB = fw.sbuf("ctmpB", [128, 512], F32)
    hs = None
    if hist_src is not None:
        hs = fw.sbuf("hist_r", [30, 512], F32)
        fw.dma(fw.sp, lambda h: h.dma_start(out=hs[:, :], in_=hist_src), reads=[C.state_conv], writes=[hs])
    for cc in range(4):
        s_ = sets[cc % 2]
        ca, cb, cseq, y = s_["ca"], s_["cb"], s_["cseq"], s_["y"]
        fw.dma(fw.sp, lambda h: h.dma_start(out=ca[:, :], in_=HfT[512 + cc * 128:640 + cc * 128, 0:T]), reads=[HfT], writes=[ca])
        fw.dma(fw.sp, lambda h: h.dma_start(out=cb[:, :], in_=HfT[1024 + cc * 128:1152 + cc * 128, 0:T]), reads=[HfT], writes=[cb])
        fw.op(fw.act, lambda e: e.activation(out=cb[:, :], in_=cb[:, :], func=AF.Sigmoid), reads=[cb], writes=[cb])
        if hs is None:
            fw.op(fw.pool, lambda e: e.memset(cseq[:, 0:30], 0.0), writes=[cseq])
        else:
            pst = C.ps[0]
            fw.op(fw.pe, lambda e: e.transpose(out=pst[:, 0:30], in_=hs[0:30, cc * 128:(cc + 1) * 128], identity=C.ident[0:30, 0:30]), reads=[hs, C.ident], writes=[pst])
            fw.op(fw.dve, lambda e: e.tensor_copy(out=cseq[:, 0:30], in_=pst[:, 0:30]), reads=[pst], writes=[cseq])
        fw.op(fw.dve, lambda e: e.tensor_tensor(out=cseq[:, 30:30 + T], in0=ca[:, :], in1=cb[:, :], op=ALU.mult), reads=[ca, cb], writes=[cseq], join=True)
        fw.op(fw.pe, lambda e: e.transpose(out=psc[0:30, cc * 128:(cc + 1) * 128], in_=cseq[:, T:T + 30], identity=C.ident[:, :]),
              reads=[cseq, C.ident], writes=[psc], join=(cc > 0))
        eng = fw.dve
        fw.op(eng, lambda e: e.tensor_scalar(out=y[:, :], in0=cseq[:, 0:T], scalar1=dwT[:, cc, 0:1], scalar2=dbc[:, cc:cc + 1], op0=ALU.mult, op1=ALU.add),
              reads=[cseq, dwT, dbc], writes=[y])
        for k in range(1, 31):
            fw.op(eng, lambda e: e.scalar_tensor_tensor(out=y[:, :], in0=cseq[:, k:k + T], scalar=dwT[:, cc, k:k + 1], in1=y[:, :], op0=ALU.mult, op1=ALU.add),
                  reads=[cseq, dwT, y], writes=[y])
        emit_ln_partition(fw, C, y, T, lng[:, cc:cc + 1], lnb[:, cc:cc + 1], [lng, lnb],
                          lambda c0, cw: (zT[:, cc, c0:c0 + cw], zT), tmpA, tmpB, func=AF.Silu)
    fw.op(fw.dve, lambda e: e.tensor_copy(out=pcs[:, :], in_=psc[0:30, 0:512]), reads=[psc], writes=[pcs])
    fw.dma(fw.sp, lambda h: h.dma_start(out=out_conv, in_=pcs[:, :]), reads=[pcs], writes=[C.out_conv_buf])
    it = 0
    for co in range(4):
        for c0 in range(0, T, 512):
            cw = min(512, T - c0)
            ps = C.ps[it % 4]
            for cc in range(4):
                fw.op(fw.pe, lambda e: e.matmul(ps[:, 0:cw], lhsT=pw[:, cc, co * 128:(co + 1) * 128], rhs=zT[:, cc, c0:c0 + cw], start=(cc == 0), stop=(cc == 3)),
                      reads=[pw, zT], writes=[ps], join=(cc > 0))
            if it % 2 == 0:
                fw.op(fw.dve, lambda e: e.tensor_copy(out=convT[:, co, c0:c0 + cw], in_=ps[:, 0:cw]), reads=[ps], writes=[convT], join=True)
            else:
                fw.op(fw.act, lambda e: e.copy(out=convT[:, co, c0:c0 + cw], in_=ps[:, 0:cw]), reads=[ps], writes=[convT], join=True)
            it += 1
    fw.pop()


def emit_ln_free(fw, C, x, P, g_bc, b_bc, junk, stat):
    xs = x[0:P, :]
    fw.op(fw.dve, lambda e: e.tensor_reduce(out=stat[0:P, 0:1], in_=xs, axis=AX.X, op=ALU.add), reads=[x], writes=[stat])
    fw.op(fw.dve, lambda e: e.tensor_scalar(out=stat[0:P, 1:2], in0=stat[0:P, 0:1], scalar1=1.0 / D, scalar2=None, op0=ALU.mult), reads=[stat], writes=[stat])
    fw.op(fw.dve, lambda e: e.tensor_scalar(out=xs, in0=xs, scalar1=stat[0:P, 1:2], scalar2=None, op0=ALU.subtract), reads=[x, stat], writes=[x])
    fw.op(fw.act, lambda e: e.activation(out=junk[0:P, :], in_=xs, func=AF.Square, accum_out=stat[0:P, 2:3]), reads=[x], writes=[junk, stat])
    fw.op(fw.act, lambda e: e.activation(out=stat[0:P, 3:4], in_=stat[0:P, 2:3], func=AF.Sqrt, bias=C.epsc[0:P, 0:1], scale=1.0 / D), reads=[stat, C.epsc], writes=[stat])
    fw.op(fw.dve, lambda e: e.reciprocal(out=stat[0:P, 4:5], in_=stat[0:P, 3:4]), reads=[stat], writes=[stat])
    fw.op(fw.dve, lambda e: e.scalar_tensor_tensor(out=xs, in0=xs, scalar=stat[0:P, 4:5], in1=g_bc[0:P, :], op0=ALU.mult, op1=ALU.mult), reads=[x, stat, g_bc], writes=[x])
    fw.op(fw.dve, lambda e: e.tensor_tensor(out=xs, in0=xs, in1=b_bc[0:P, :], op=ALU.add), reads=[x, b_bc], writes=[x])


def load_bcast(fw, name, vec_ap, n, srcbuf, P=128):
    b = fw.sbuf(name, [128, n], F32)
    fw.dma(fw.sp, lambda h: h.dma_start(out=b[0:P, :], in_=vec_ap.partition_broadcast(P)), reads=[srcbuf], writes=[b])
    return b


def emit_gmlp(fw, C, G, l, t, P, Htok, HfT, mixg, out_gm_ap):
    gv = G["gvin"][t % 2]
    gt = G["gtmp"]
    st = G["gstat"]
    fw.dma(fw.sp, lambda h: h.dma_start(out=gv[0:P, :], in_=Htok[t * 128:t * 128 + P, 0:512]), reads=[Htok], writes=[gv])
    emit_gelu(fw, fw.dve, fw.act, gv[0:P, :], gv[0:P, :], gt[0:P, :], [gv], [gv], gt)
    g3 = gv[0:P, :].rearrange("p (h c) -> p h c", h=4)
    t3 = gt[0:P, :].rearrange("p (h c) -> p h c", h=4)
    fw.op(fw.dve, lambda e: e.tensor_reduce(out=st[0:P, 0:4], in_=g3, axis=AX.X, op=ALU.add), reads=[gv], writes=[st])
    fw.op(fw.dve, lambda e: e.tensor_scalar(out=st[0:P, 4:8], in0=st[0:P, 0:4], scalar1=1.0 / 128, scalar2=None, op0=ALU.mult), reads=[st], writes=[st])
    fw.op(fw.dve, lambda e: e.tensor_tensor(out=g3, in0=g3, in1=st[0:P, 4:8].unsqueeze(2).to_broadcast([P, 4, 128]), op=ALU.subtract), reads=[gv, st], writes=[gv])
    fw.op(fw.dve, lambda e: e.tensor_tensor(out=t3, in0=g3, in1=g3, op=ALU.mult), reads=[gv], writes=[gt])
    fw.op(fw.dve, lambda e: e.tensor_reduce(out=st[0:P, 8:12], in_=t3, axis=AX.X, op=ALU.add), reads=[gt], writes=[st])
    fw.op(fw.act, lambda e: e.activation(out=st[0:P, 12:16], in_=st[0:P, 8:12], func=AF.Sqrt, bias=C.epsc[0:P, 0:1], scale=1.0 / 128), reads=[st, C.epsc], writes=[st])
    fw.op(fw.dve, lambda e: e.reciprocal(out=st[0:P, 16:20], in_=st[0:P, 12:16]), reads=[st], writes=[st])
    fw.op(fw.dve, lambda e: e.tensor_tensor(out=g3, in0=g3, in1=st[0:P, 16:20].unsqueeze(2).to_broadcast([P, 4, 128]), op=ALU.mult), reads=[gv, st], writes=[gv])
    fw.op(fw.dve, lambda e: e.tensor_tensor(out=gv[0:P, :], in0=gv[0:P, :], in1=G["gmg"][0:P, :], op=ALU.mult), reads=[gv, G["gmg"]], writes=[gv])
    fw.op(fw.pool, lambda e: e.tensor_tensor(out=gv[0:P, :], in0=gv[0:P, :], in1=G["gmb"][0:P, :], op=ALU.add), reads=[gv, G["gmb"]], writes=[gv])
    if out_gm_ap is not None:
        fw.dma(fw.sp, lambda h: h.dma_start(out=out_gm_ap, in_=gv[0:P, :]), reads=[gv], writes=[C.out_gm_buf], join=True)
    vnb = G["vnb"]
    fw.op(fw.act, lambda e: e.copy(out=vnb[0:P, :], in_=gv[0:P, :]), reads=[gv], writes=[vnb])
    ps = C.ps[0]
    for h in range(4):
        fw.op(fw.pe, lambda e: e.matmul(ps[:, h * 128:h * 128 + P], lhsT=vnb[0:P, h * 128:(h + 1) * 128], rhs=G["trilWT"][0:P, h, 0:P], start=True, stop=True),
              reads=[vnb, G["trilWT"]], writes=[ps], join=(h > 0))
    ut = G["ut"][t % 2]
    fw.dma(fw.sp, lambda h_: h_.dma_start(out=ut[:, :, 0:P], in_=HfT[0:512, t * 128:t * 128 + P].rearrange("(h c) q -> c h q", h=4)), reads=[HfT], writes=[ut])
    sv = gt[:, :].rearrange("p (h c) -> p h c", h=4)[:, :, 0:P]
    fw.op(fw.dve, lambda e: e.tensor_tensor(out=sv, in0=ps[:, 0:512].rearrange("p (h c) -> p h c", h=4)[:, :, 0:P], in1=G["bsb"][:, :, 0:P], op=ALU.add),
          reads=[ps, G["bsb"]], writes=[gt])
    fw.op(fw.dve, lambda e: e.tensor_tensor(out=mixg[:, :, 0:P], in0=sv, in1=ut[:, :, 0:P], op=ALU.mult), reads=[gt, ut], writes=[mixg])


def gmlp_consts(fw, C, l, P):
    G = {}
    G["gvin"] = [fw.sbuf("gvin0", [128, 512], F32)] * 2
    G["gtmp"] = fw.sbuf("gtmp", [128, 512], F32)
    G["gstat"] = fw.sbuf("gstat", [128, 20], F32)
    G["vnb"] = fw.sbuf("vnb", [128, 512], BF16)
    G["ut"] = [fw.sbuf("ut0", [128, 4, 128], F32)] * 2
    G["gmg"] = load_bcast(fw, "gmg", C.gm_ln_g[l, :], 512, C.gm_ln_g, P)
    G["gmb"] = load_bcast(fw, "gmb", C.gm_ln_b[l, :], 512, C.gm_ln_b, P)
    bsb = fw.sbuf("bsb", [128, 4, 128], F32)
    fw.dma(fw.sp, lambda h: h.dma_start(out=bsb[:, :, :].rearrange("p h i -> p (h i)"), in_=C.gm_bs[l, :, :].rearrange("h i -> (h i)").partition_broadcast(128)),
           reads=[C.gm_bs], writes=[bsb])
    G["bsb"] = bsb
    wsr = fw.sbuf("wsr", [128, 4, 128], F32)
    fw.dma(fw.sp, lambda h: h.dma_start(out=wsr[:, :, :], in_=C.gm_ws[l, :, :, :].rearrange("h i j -> i h j")), reads=[C.gm_ws], writes=[wsr])
    ps = C.ps[1]
    for h in range(4):
        fw.op(fw.pe, lambda e: e.transpose(out=ps[:, h * 128:(h + 1) * 128], in_=wsr[:, h, :], identity=C.ident[:, :]), reads=[wsr, C.ident], writes=[ps], join=(h > 0))
    tw = fw.sbuf("trilWT", [128, 4, 128], BF16)
    fw.op(fw.dve, lambda e: e.tensor_tensor(out=tw[:, :, :], in0=ps[:, 0:512].rearrange("p (h i) -> p h i", h=4), in1=C.triu[:, :].unsqueeze(1).to_broadcast([128, 4, 128]), op=ALU.mult),
          reads=[ps, C.triu], writes=[tw])
    G["trilWT"] = tw
    return G


def emit_softmax_pv(fw, C, A, sbuf_s, nk, gate_ap, gate_buf, acc, v_fn, first, last, tagi, part="all"):
    st = A["st"][tagi]
    eb = A["eb"][tagi]
    pT = A["pT"][tagi]
    if part in ("all", "sm"):
      fw.op(fw.dve, lambda e: e.tensor_reduce(out=st[:, 0:1], in_=sbuf_s[:, 0:nk], axis=AX.X, op=ALU.max, negate=True), reads=[sbuf_s], writes=[st])
      fw.op(fw.act, lambda e: e.activation(out=eb[:, 0:nk], in_=sbuf_s[:, 0:nk], func=AF.Exp, bias=st[:, 0:1], scale=1.0, accum_out=st[:, 1:2]),
          reads=[sbuf_s, st], writes=[eb, st])
      fw.op(fw.dve, lambda e: e.reciprocal(out=st[:, 2:3], in_=st[:, 1:2]), reads=[st], writes=[st])
      fw.op(fw.dve, lambda e: e.tensor_tensor(out=st[:, 3:4], in0=st[:, 2:3], in1=gate_ap, op=ALU.mult), reads=[st, gate_buf], writes=[st])
      fw.op(fw.act, lambda e: e.mul(out=eb[:, 0:nk], in_=eb[:, 0:nk], mul=st[:, 3:4]), reads=[eb, st], writes=[eb])
    if part == "sm":
        return
    nkt = nk // 128
    for k0 in range(0, nkt, 8):
        kn = min(8, nkt - k0)
        psb = C.psb[A["psbi"] % 2]
        A["psbi"] += 1
        for j in range(kn):
            kt = k0 + j
            fw.op(fw.pe, lambda e: e.transpose(out=psb[:, j * 128:(j + 1) * 128], in_=eb[:, kt * 128:(kt + 1) * 128], identity=C.identb[:, :]),
                  reads=[eb, C.identb], writes=[psb], join=(j > 0))
        if (k0 // 8) % 2 == 0:
            fw.op(fw.dve, lambda e: e.tensor_copy(out=pT[:, k0 * 128:(k0 + kn) * 128], in_=psb[:, 0:kn * 128]), reads=[psb], writes=[pT], join=True)
        else:
            fw.op(fw.act, lambda e: e.copy(out=pT[:, k0 * 128:(k0 + kn) * 128], in_=psb[:, 0:kn * 128]), reads=[psb], writes=[pT], join=True)
    for kt in range(nkt):
        fw.op(fw.pe, lambda e: e.matmul(acc[:, 0:128], lhsT=v_fn(kt), rhs=pT[:, kt * 128:(kt + 1) * 128], start=(first and kt == 0), stop=(last and kt == nkt - 1)),
              reads=[C.V, pT], writes=[acc], join=not (first and kt == 0))


def emit_attn_prompt(fw, C, A, l, t, mixn):
    qt = A["qt"][t % 2]
    gt = A["gate"][t % 2]
    rt = A["rt"][t % 2]
    fw.dma(fw.sp, lambda h: h.dma_start(out=qt[:, :], in_=C.Htok[t * 128:(t + 1) * 128, 512:1536]), reads=[C.Htok], writes=[qt])
    fw.dma(fw.sp, lambda h: h.dma_start(out=gt[:, :], in_=C.Htok[t * 128:(t + 1) * 128, 3072:3096]), reads=[C.Htok], writes=[gt])
    fw.dma(fw.sp, lambda h: h.dma_start(out=rt[:, :, :], in_=C.cd["rope_p"][t * 128:(t + 1) * 128, :, :]), reads=[C.cd["rope_p"]], writes=[rt])
    qv = qt[:, :].rearrange("p (h x d) -> p h x d", h=8, x=2)
    tv = [A["ssb"][:, i * 512:(i + 1) * 512].rearrange("p (h d) -> p h d", h=8) for i in range(4)]
    emit_rope(fw, qv[:, :, 0, :], qv[:, :, 1, :], rt[:, 2, :], rt[:, 3, :], tv, [128, 8, 64], [rt], qt, A["rtmp"])
    qb = A["qb"]
    fw.op(fw.act, lambda e: e.copy(out=qb[:, :], in_=qt[:, :]), reads=[qt], writes=[qb])
    psb = C.psb[A["psbi"] % 2]
    A["psbi"] += 1
    for h in range(8):
        fw.op(fw.pe, lambda e: e.transpose(out=psb[:, h * 128:(h + 1) * 128], in_=qb[:, h * 128:(h + 1) * 128], identity=C.identb[:, :]),
              reads=[qb, C.identb], writes=[psb], join=(h > 0))
    qT = A["qT"]
    fw.op(fw.dve, lambda e: e.tensor_copy(out=qT[:, :], in_=psb[:, :]), reads=[psb], writes=[qT])
    sig = A["sig"]
    fw.op(fw.act, lambda e: e.activation(out=sig[:, :], in_=gt[:, :], func=AF.Sigmoid), reads=[gt], writes=[sig])
    sig3 = sig[:, :].rearrange("p (h b) -> p h b", b=3)
    sc, ec, st4 = A["sc"], A["ec"], A["st4"]
    for g in range(2):
        psc = C.ps[0]
        for r in range(4):
            fw.op(fw.pe, lambda e: e.matmul(psc[:, r * 64:(r + 1) * 64], lhsT=qT[:, (g * 4 + r) * 128:(g * 4 + r + 1) * 128], rhs=C.kcmpT[:, g, :], start=True, stop=True),
                  reads=[qT, C.kcmpT], writes=[psc], join=(r > 0))
        sc3 = sc[:, :].rearrange("p (r n) -> p r n", r=4)
        ec3 = ec[:, :].rearrange("p (r n) -> p r n", r=4)
        fw.op(fw.dve, lambda e: e.tensor_tensor(out=sc3, in0=psc[:, 0:256].rearrange("p (r n) -> p r n", r=4),
                                                in1=C.cmpbias[:, t, :].unsqueeze(1).to_broadcast([128, 4, 64]), op=ALU.add), reads=[psc, C.cmpbias], writes=[sc])
        fw.op(fw.dve, lambda e: e.tensor_reduce(out=st4[:, 0:4], in_=sc3, axis=AX.X, op=ALU.max), reads=[sc], writes=[st4])
        fw.op(fw.dve, lambda e: e.tensor_tensor(out=sc3, in0=sc3, in1=st4[:, 0:4].unsqueeze(2).to_broadcast([128, 4, 64]), op=ALU.subtract), reads=[sc, st4], writes=[sc])
        fw.op(fw.act, lambda e: e.activation(out=ec[:, :], in_=sc[:, :], func=AF.Exp), reads=[sc], writes=[ec])
        fw.op(fw.dve, lambda e: e.tensor_reduce(out=st4[:, 4:8], in_=ec3, axis=AX.X, op=ALU.add), reads=[ec], writes=[st4])
        fw.op(fw.dve, lambda e: e.tensor_scalar(out=st4[:, 4:8], in0=st4[:, 4:8], scalar1=1e-30, scalar2=None, op0=ALU.max), reads=[st4], writes=[st4])
        fw.op(fw.dve, lambda e: e.reciprocal(out=st4[:, 8:12], in_=st4[:, 4:8]), reads=[st4], writes=[st4])
        fw.op(fw.dve, lambda e: e.tensor_scalar(out=st4[:, 8:12], in0=st4[:, 8:12], scalar1=C.cmpvalid[:, t:t + 1], scalar2=None, op0=ALU.mult), reads=[st4, C.cmpvalid], writes=[st4])
        fw.op(fw.dve, lambda e: e.tensor_tensor(out=ec3, in0=ec3, in1=st4[:, 8:12].unsqueeze(2).to_broadcast([128, 4, 64]), op=ALU.mult), reads=[ec, st4], writes=[ec])
        imp = A["imp"]
        fw.op(fw.dve, lambda e: e.tensor_reduce(out=imp[:, 0:64], in_=ec[:, :].rearrange("p (r n) -> p n r", r=4), axis=AX.X, op=ALU.add), reads=[ec], writes=[imp])
        fw.op(fw.dve, lambda e: e.tensor_reduce(out=imp[:, 64:96], in_=imp[:, 0:64].rearrange("p (m two) -> p m two", two=2), axis=AX.X, op=ALU.add), reads=[imp], writes=[imp])
        fw.op(fw.dve, lambda e: e.tensor_tensor(out=imp[:, 64:96], in0=imp[:, 64:96], in1=C.selA[:, t, :], op=ALU.mult), reads=[imp, C.selA], writes=[imp])
        fw.op(fw.dve, lambda e: e.tensor_tensor(out=imp[:, 64:96], in0=imp[:, 64:96], in1=C.selB[:, t, :], op=ALU.add), reads=[imp, C.selB], writes=[imp])
        m8 = A["m8"]
        fw.op(fw.dve, lambda e: e.max(out=m8[:, 0:8], in_=imp[:, 64:96]), reads=[imp], writes=[m8])
        fw.op(fw.dve, lambda e: e.match_replace(out=imp[:, 96:128], in_to_replace=m8[:, 0:8], in_values=imp[:, 64:96], imm_value=-1e30), reads=[imp, m8], writes=[imp])
        fw.op(fw.dve, lambda e: e.max(out=m8[:, 8:16], in_=imp[:, 96:128]), reads=[imp], writes=[m8])
        bsel = A["bsel"]
        fw.op(fw.dve, lambda e: e.tensor_scalar(out=bsel[:, :], in0=imp[:, 64:96], scalar1=m8[:, 15:16], scalar2=None, op0=ALU.is_ge), reads=[imp, m8], writes=[bsel])
        fw.op(fw.dve, lambda e: e.tensor_scalar(out=bsel[:, :], in0=bsel[:, :], scalar1=-NEG, scalar2=NEG, op0=ALU.mult, op1=ALU.add), reads=[bsel], writes=[bsel])
        pcb = A["pcb"]
        fw.op(fw.dve, lambda e: e.tensor_tensor(out=pcb[:, :].rearrange("p (r n) -> p r n", r=4), in0=ec3,
                                                in1=sig3[:, g * 4:(g + 1) * 4, 0:1].to_broadcast([128, 4, 64]), op=ALU.mult), reads=[ec, sig], writes=[pcb])
        psb = C.psb[A["psbi"] % 2]
        A["psbi"] += 1
        for r in range(4):
            fw.op(fw.pe, lambda e: e.transpose(out=psb[0:64, r * 128:(r + 1) * 128], in_=pcb[:, r * 64:(r + 1) * 64], identity=C.identb[:, :]),
                  reads=[pcb, C.identb], writes=[psb], join=(r > 0))
        pTc = A["pTc"]
        fw.op(fw.act, lambda e: e.copy(out=pTc[0:64, :], in_=psb[0:64, 0:512]), reads=[psb], writes=[pTc])
        for r in range(4):
            h = g * 4 + r
            acc = C.pacc[h % 2]
            fw.op(fw.pe, lambda e: e.matmul(acc[:, 0:128], lhsT=C.vcmp[0:64, g * 128:(g + 1) * 128], rhs=pTc[0:64, r * 128:(r + 1) * 128], start=True, stop=False),
                  reads=[C.vcmp, pTc], writes=[acc])
            nk = (t + 1) * 128
            ssb = A["ssb"]
            for ci, c0 in enumerate(range(0, nk, 512)):
                cw = min(512, nk - c0)
                ps = C.ps[1 + (ci % 3)]
                fw.op(fw.pe, lambda e: e.matmul(ps[:, 0:cw], lhsT=qT[:, h * 128:(h + 1) * 128], rhs=C.kT[:, g, c0:c0 + cw], start=True, stop=True),
                      reads=[qT, C.kT], writes=[ps])
                nb = cw // 64
                fw.op(fw.dve, lambda e: e.tensor_tensor(out=ssb[:, c0:c0 + cw].rearrange("p (m j) -> p m j", j=64), in0=ps[:, 0:cw].rearrange("p (m j) -> p m j", j=64),
                                                        in1=bsel[:, c0 // 64:c0 // 64 + nb].unsqueeze(2).to_broadcast([128, nb, 64]), op=ALU.add),
                      reads=[ps, bsel], writes=[ssb], join=True)
            fw.op(fw.pool, lambda e: e.tensor_tensor(out=ssb[:, t * 128:(t + 1) * 128], in0=ssb[:, t * 128:(t + 1) * 128], in1=C.causal[:, :], op=ALU.add),
                  reads=[ssb, C.causal], writes=[ssb])
            kt0 = max(0, t - 4)
            nkw = (t - kt0 + 1) * 128
            swb = A["swb"]
            for ci, c0 in enumerate(range(0, nkw, 512)):
                cw = min(512, nkw - c0)
                ps = C.ps[1 + (ci % 3)]
                fw.op(fw.pe, lambda e: e.matmul(ps[:, 0:cw], lhsT=qT[:, h * 128:(h + 1) * 128], rhs=C.kT[:, 2 + g, kt0 * 128 + c0:kt0 * 128 + c0 + cw], start=True, stop=True),
                      reads=[qT, C.kT], writes=[ps])
                fw.op(fw.act, lambda e: e.copy(out=swb[:, c0:c0 + cw], in_=ps[:, 0:cw]), reads=[ps], writes=[swb], join=True)
            fw.op(fw.pool, lambda e: e.tensor_tensor(out=swb[:, nkw - 128:nkw], in0=swb[:, nkw - 128:nkw], in1=C.causal[:, :], op=ALU.add), reads=[swb, C.causal], writes=[swb])
            if t >= 4:
                fw.op(fw.pool, lambda e: e.tensor_tensor(out=swb[:, 0:128], in0=swb[:, 0:128], in1=C.acausal[:, :], op=ALU.add), reads=[swb, C.acausal], writes=[swb])
            emit_softmax_pv(fw, C, A, ssb, nk, sig3[:, h, 1:2], sig, acc, None, False, False, 0, part="sm")
            emit_softmax_pv(fw, C, A, swb, nkw, sig3[:, h, 2:3], sig, acc, None, False, True, 1, part="sm")
            emit_softmax_pv(fw, C, A, ssb, nk, sig3[:, h, 1:2], sig, acc, lambda kt: C.V[:, kt, 0, g * 128:(g + 1) * 128], False, False, 0, part="pv")
            emit_softmax_pv(fw, C, A, swb, nkw, sig3[:, h, 2:3], sig, acc, lambda kt: C.V[:, kt0 + kt, 1, g * 128:(g + 1) * 128], False, True, 1, part="pv")
            if h % 2 == 0:
                fw.op(fw.dve, lambda e: e.tensor_copy(out=mixn[:, h, :], in_=acc[:, 0:128]), reads=[acc], writes=[mixn], join=True)
            else:
                fw.op(fw.act, lambda e: e.copy(out=mixn[:, h, :], in_=acc[:, 0:128]), reads=[acc], writes=[mixn], join=True)


def attn_bufs(fw):
    A = {"psbi": 0}
    A["qt"] = [fw.sbuf("qt0", [128, 1024], F32)] * 2
    A["gate"] = [fw.sbuf(f"gate{i}", [128, 24], F32) for i in range(2)]
    A["rt"] = [fw.sbuf(f"art{i}", [128, 4, 64], F32) for i in range(2)]

    A["qb"] = fw.sbuf("qb", [128, 1024], BF16)
    A["qT"] = fw.sbuf("qT", [128, 1024], BF16)
    A["sig"] = fw.sbuf("sig", [128, 24], F32)
    A["sc"] = fw.sbuf("sc", [128, 256], F32)
    A["ec"] = fw.sbuf("ec", [128, 256], F32)
    A["st4"] = fw.sbuf("st4", [128, 12], F32)
    A["st"] = [fw.sbuf("ast0", [128, 4], F32), fw.sbuf("ast1", [128, 4], F32)]
    A["imp"] = fw.sbuf("imp", [128, 128], F32)
    A["m8"] = fw.sbuf("m8", [128, 16], F32)
    A["bsel"] = fw.sbuf("bsel", [128, 32], F32)
    A["pcb"] = fw.sbuf("pcb", [128, 256], BF16)
    A["pTc"] = fw.sbuf("pTc", [128, 512], BF16)
    A["ssb"] = fw.sbuf("ssb", [128, S], F32)
    A["rtmp"] = [A["ssb"]] * 4
    A["swb"] = fw.sbuf("swb", [128, 640], F32)
    A["eb"] = [fw.sbuf("eb", [128, S], BF16), fw.sbuf("ebw", [128, 640], BF16)]
    A["pT"] = [fw.sbuf("pT", [128, S], BF16), fw.sbuf("pTw", [128, 640], BF16)]
    return A


def emit_mix_ln(fw, C, M, l, t, P, x_src, mixg, mixn, convT, X1):
    xt = M["xres"][0]
    fw.dma(fw.sp, lambda h: h.dma_start(out=xt[0:P, :], in_=x_src[t * 128:t * 128 + P, :]), reads=[x_src], writes=[xt])
    x1 = xt
    for n4 in range(4):
        ps = C.ps[n4]
        for c in range(16):
            if c < 4:
                lt, lb = mixg[:, c, 0:P], mixg
            elif c < 12:
                lt, lb = mixn[:, c - 4, 0:P], mixn
            else:
                lt, lb = convT[:, c - 12, t * 128:t * 128 + P], convT
            fw.op(fw.pe, lambda e: e.matmul(ps[0:P, :], lhsT=lt, rhs=M["wout"][:, c, n4 * 512:(n4 + 1) * 512], start=(c == 0), stop=(c == 15)),
                  reads=[lb, M["wout"]], writes=[ps], join=(c > 0))
        fw.op(fw.dve, lambda e: e.scalar_tensor_tensor(out=x1[0:P, n4 * 512:(n4 + 1) * 512], in0=xt[0:P, n4 * 512:(n4 + 1) * 512], scalar=ALPHA, in1=ps[0:P, :], op0=ALU.mult, op1=ALU.add),
              reads=[xt, ps], writes=[x1])
    emit_ln_free(fw, C, x1, P, M["ln1g"], M["ln1b"], M["junk"], M["lnst"])
    fw.dma(fw.sp, lambda h: h.dma_start(out=X1[t * 128:t * 128 + P, :], in_=x1[0:P, :]), reads=[x1], writes=[X1], join=True)


def mix_bufs(fw, C, l, P):
    M = {}
    wout = fw.sbuf("wout", [128, 16, D], BF16)
    for c4 in range(4):
        fw.dma(fw.pool, lambda h: h.dma_start(out=wout[:, c4 * 4:(c4 + 1) * 4, :], in_=C.w_out[l, c4 * 512:(c4 + 1) * 512, :].rearrange("(c p) n -> p c n", p=128)),
               reads=[C.w_out], writes=[wout], join=True)
    M["wout"] = wout
    M["xres"] = [fw.sbuf("xres0", [128, D], F32)]
    M["x1"] = M["xres"]
    M["junk"] = None
    M["lnst"] = fw.sbuf("lnst", [128, 8], F32)
    M["ln1g"] = load_bcast(fw, "ln1g", C.ln1_g[l, :], D, C.ln1_g, P)
    M["ln1b"] = load_bcast(fw, "ln1b", C.ln1_b[l, :], D, C.ln1_b, P)
    return M


def phase_mix_prompt(fw, C, l, x_src, X1, samp=None):
    fw.push()
    for nm in ["cmpbias", "cmpvalid", "selA", "selB"]:
        setattr(C, nm, load_const(fw, C, nm))
    M = mix_bufs(fw, C, l, 128)
    G = gmlp_consts(fw, C, l, 128)
    A = attn_bufs(fw)
    M["junk"] = A["eb"][0]
    mixg = [fw.sbuf(f"mixg{i}", [128, 4, 128], BF16) for i in range(2)]
    mixn = [fw.sbuf(f"mixn{i}", [128, 8, 128], BF16) for i in range(2)]
    for t in range(NT):
        out_gm = C.out_gm[l, :, :] if t == NT - 1 else None
        emit_gmlp(fw, C, G, l, t, 128, C.Htok, C.HfT, mixg[t % 2], out_gm)
        emit_attn_prompt(fw, C, A, l, t, mixn[t % 2])
        emit_mix_ln(fw, C, M, l, t, 128, x_src, mixg[t % 2], mixn[t % 2], C.convT, X1)
    if samp is not None:
        xs_src, XS1, mixn_s, convTs = samp
        emit_gmlp(fw, C, G, l, 0, TS, C.HtokS, C.HfTS, mixg[0], C.out_sgm[l, :, :])
        emit_mix_ln(fw, C, M, l, 0, TS, xs_src, mixg[0], mixn_s, convTs, XS1)
    fw.pop()


INPUT_SPECS = [
    ("w_in", [DEPTH, D, IN_W]), ("gm_ln_g", [DEPTH, 512]), ("gm_ln_b", [DEPTH, 512]), ("gm_ws", [DEPTH, 4, 128, 128]), ("gm_bs", [DEPTH, 4, 128]),
    ("cmp_wk", [DEPTH, 2, 32]), ("cmp_wv", [DEPTH, 2, 32]), ("conv_dw", [DEPTH, 31, 512]), ("conv_db", [DEPTH, 512]),
    ("conv_ln_g", [DEPTH, 512]), ("conv_ln_b", [DEPTH, 512]), ("conv_pw", [DEPTH, 512, 512]), ("w_out", [DEPTH, D, D]),
    ("ln1_g", [DEPTH, D]), ("ln1_b", [DEPTH, D]), ("peer_wq", [DEPTH, D, D]), ("peer_subkeys", [DEPTH, 8, 2, 128, 128]),
    ("ln2_g", [DEPTH, D]), ("ln2_b", [DEPTH, D]), ("peer_u", [DEPTH, 16384, D]), ("peer_v", [DEPTH, 16384, D]),
]


def phase_peer(fw, C, l, jobs, NB=6):
    fw.push()
    wq = fw.sbuf("wq", [128, 16, D], BF16)
    for c4 in range(4):
        fw.dma(fw.pool, lambda h: h.dma_start(out=wq[:, c4 * 4:(c4 + 1) * 4, :], in_=C.peer_wq[l, c4 * 512:(c4 + 1) * 512, :].rearrange("(c p) n -> p c n", p=128)),
               reads=[C.peer_wq], writes=[wq], join=True)
    skr = fw.sbuf("skr", [128, 16, 128], F32)
    fw.dma(fw.sp, lambda h: h.dma_start(out=skr[:, :, :], in_=C.peer_subkeys[l, :, :, :, :].rearrange("h p k d -> k (h p) d")), reads=[C.peer_subkeys], writes=[skr])
    skT = fw.sbuf("skT", [128, 16, 128], BF16)
    for q4 in range(4):
        ps = C.ps[q4]
        for j in range(4):
            fw.op(fw.pe, lambda e: e.transpose(out=ps[:, j * 128:(j + 1) * 128], in_=skr[:, q4 * 4 + j, :], identity=C.ident[:, :]), reads=[skr, C.ident], writes=[ps], join=(j > 0))
        fw.op(fw.dve, lambda e: e.tensor_copy(out=skT[:, q4 * 4:(q4 + 1) * 4, :], in_=ps[:, :].rearrange("p (j k) -> p j k", j=4)), reads=[ps], writes=[skT], join=True)
    ln2g = load_bcast(fw, "ln2g", C.ln2_g[l, :], D, C.ln2_g, 128)
    ln2b = load_bcast(fw, "ln2b", C.ln2_b[l, :], D, C.ln2_b, 128)
    iota = load_const(fw, C, "iota16")
    x1 = fw.sbuf("px1", [128, D], F32)
    x1T = fw.sbuf("px1T", [128, 16, 128], BF16)
    qTb = fw.sbuf("pqT", [128, 16, 128], BF16)
    sb = fw.sbuf("psc", [128, 16, 128], F32)
    m = fw.sbuf("pm", [128, 16, 16], F32)
    ix = fw.sbuf("pix", [128, 16, 16], U32)
    ixf = fw.sbuf("pixf", [128, 16, 16], F32)
    tmp = fw.sbuf("ptmp", [128, 256], F32)
    cand = fw.sbuf("pcand", [128, 8, 256], F32)
    oh = fw.sbuf("poh", [128, 8, 256], F32)
    tm = fw.sbuf("ptm", [128, 8, 16], F32)
    pos = fw.sbuf("ppos", [128, 8, 16], U32)
    pa = fw.sbuf("ppa", [128, 8, 16], U32)
    paf = fw.sbuf("ppaf", [128, 2, 128], F32)
    isel = fw.sbuf("pisel", [128, 2, 128], F32)
    eid = fw.sbuf("peid", [128, 128], I32)
    gate = fw.sbuf("pgate", [128, 8, 16], F32)
    gst = fw.sbuf("pgst", [128, 16], F32)
    actp = fw.sbuf("pactp", [128, 128], F32)
    wgt = fw.sbuf("pwgt", [128, 128], F32)
    gtmp = fw.sbuf("pgtmp", [128, 128], F32)
    y = fw.sbuf("py", [128, D], F32)
    junk = fw.sbuf("pjunk", [128, D], BF16)
    lnst = fw.sbuf("plnst", [128, 8], F32)
    ring = [fw.sbuf(f"pring{i}", [128, D], F32) for i in range(NB)]
    u_rows = C.peer_u[:, :, :].rearrange("l n d -> (l n) d")
    v_rows = C.peer_v[:, :, :].rearrange("l n d -> (l n) d")
    gi = 0
    for (X1, P, X2, t, ridx) in [(j[0], j[2], j[3], t_, j[4] if len(j) > 4 else None) for j in jobs for t_ in range(j[1])]:
        if ridx is None:
            fw.dma(fw.sp, lambda h: h.dma_start(out=x1[0:P, :], in_=X1[t * 128:t * 128 + P, :]), reads=[X1], writes=[x1])
        else:
            fw.dma(fw.pool, lambda h: h.indirect_dma_start(out=x1[0:P, :], out_offset=None, in_=X1[:, :], in_offset=bass.IndirectOffsetOnAxis(ap=ridx[0:P, t:t + 1], axis=0)),
                   reads=[X1, ridx], writes=[x1])
        for cb in range(4):
            ps = C.ps[cb]
            for k in range(4):
                c = cb * 4 + k
                fw.op(fw.pe, lambda e: e.transpose(out=ps[:, k * 128:k * 128 + P], in_=x1[0:P, c * 128:(c + 1) * 128], identity=C.ident[0:P, 0:P]),
                      reads=[x1, C.ident], writes=[ps], join=(k > 0))
            src = ps[:, :].rearrange("p (k q) -> p k q", k=4)[:, :, 0:P]
            fw.op(fw.act if cb % 2 else fw.dve, (lambda e: e.copy(out=x1T[:, cb * 4:(cb + 1) * 4, 0:P], in_=src)) if cb % 2 else (lambda e: e.tensor_copy(out=x1T[:, cb * 4:(cb + 1) * 4, 0:P], in_=src)),
                  reads=[ps], writes=[x1T], join=True)
        for q4 in range(4):
            ps = C.ps[q4]
            for j in range(4):
                hp = q4 * 4 + j
                for c in range(16):
                    fw.op(fw.pe, lambda e: e.matmul(ps[:, j * 128:j * 128 + P], lhsT=wq[:, c, hp * 128:(hp + 1) * 128], rhs=x1T[:, c, 0:P], start=(c == 0), stop=(c == 15)),
                          reads=[wq, x1T], writes=[ps], join=not (j == 0 and c == 0))
            src = ps[:, :].rearrange("p (j q) -> p j q", j=4)[:, :, 0:P]
            fw.op(fw.act if q4 % 2 else fw.dve, (lambda e: e.copy(out=qTb[:, q4 * 4:(q4 + 1) * 4, 0:P], in_=src)) if q4 % 2 else (lambda e: e.tensor_copy(out=qTb[:, q4 * 4:(q4 + 1) * 4, 0:P], in_=src)),
                  reads=[ps], writes=[qTb], join=True)
        for q4 in range(4):
            ps = C.ps[q4]
            for j in range(4):
                hp = q4 * 4 + j
                fw.op(fw.pe, lambda e: e.matmul(ps[0:P, j * 128:(j + 1) * 128], lhsT=qTb[:, hp, 0:P], rhs=skT[:, hp, :], start=True, stop=True),
                      reads=[qTb, skT], writes=[ps], join=(j > 0))
            fw.op(fw.act if q4 % 2 else fw.dve, (lambda e: e.copy(out=sb[0:P, q4 * 4:(q4 + 1) * 4, :], in_=ps[0:P, :].rearrange("p (j k) -> p j k", j=4))) if q4 % 2 else
                  (lambda e: e.tensor_copy(out=sb[0:P, q4 * 4:(q4 + 1) * 4, :], in_=ps[0:P, :].rearrange("p (j k) -> p j k", j=4))), reads=[ps], writes=[sb], join=True)
        for hp in range(16):
            fw.op(fw.dve, lambda e: e.max(out=m[0:P, hp, 0:8], in_=sb[0:P, hp, :]), reads=[sb], writes=[m])
            fw.op(fw.dve, lambda e: e.match_replace(out=tmp[0:P, 0:128], in_to_replace=m[0:P, hp, 0:8], in_values=sb[0:P, hp, :], imm_value=-1e30), reads=[sb, m], writes=[tmp])
            fw.op(fw.dve, lambda e: e.max(out=m[0:P, hp, 8:16], in_=tmp[0:P, 0:128]), reads=[tmp], writes=[m])
            fw.op(fw.dve, lambda e: e.max_index(out=ix[0:P, hp, 0:8], in_max=m[0:P, hp, 0:8], in_values=sb[0:P, hp, :]), reads=[sb, m], writes=[ix])
            fw.op(fw.dve, lambda e: e.max_index(out=ix[0:P, hp, 8:16], in_max=m[0:P, hp, 8:16], in_values=tmp[0:P, 0:128]), reads=[tmp, m], writes=[ix])
        fw.op(fw.dve, lambda e: e.tensor_copy(out=ixf[0:P, :, :], in_=ix[0:P, :, :]), reads=[ix], writes=[ixf])
        m4 = m[0:P, :, :].rearrange("p (h two) k -> p h two k", two=2)
        i4 = ixf[0:P, :, :].rearrange("p (h two) k -> p h two k", two=2)
        c4v = cand[0:P, :, :].rearrange("p h (a b) -> p h a b", a=16)
        fw.op(fw.dve, lambda e: e.tensor_tensor(out=c4v, in0=m4[:, :, 0, :].unsqueeze(3).to_broadcast([P, 8, 16, 16]),
                                                in1=m4[:, :, 1, :].unsqueeze(2).to_broadcast([P, 8, 16, 16]), op=ALU.add), reads=[m], writes=[cand])
        for h in range(8):
            fw.op(fw.dve, lambda e: e.max(out=tm[0:P, h, 0:8], in_=cand[0:P, h, :]), reads=[cand], writes=[tm])
            fw.op(fw.dve, lambda e: e.match_replace(out=tmp[0:P, :], in_to_replace=tm[0:P, h, 0:8], in_values=cand[0:P, h, :], imm_value=-1e30), reads=[cand, tm], writes=[tmp])
            fw.op(fw.dve, lambda e: e.max(out=tm[0:P, h, 8:16], in_=tmp[0:P, :]), reads=[tmp], writes=[tm])
            fw.op(fw.dve, lambda e: e.max_index(out=pos[0:P, h, 0:8], in_max=tm[0:P, h, 0:8], in_values=cand[0:P, h, :]), reads=[cand, tm], writes=[pos])
            fw.op(fw.dve, lambda e: e.max_index(out=pos[0:P, h, 8:16], in_max=tm[0:P, h, 8:16], in_values=tmp[0:P, :]), reads=[tmp, tm], writes=[pos])
        fw.op(fw.dve, lambda e: e.tensor_single_scalar(out=pa[0:P, :, :], in_=pos[0:P, :, :], scalar=4, op=ALU.logical_shift_right), reads=[pos], writes=[pa])
        fw.op(fw.dve, lambda e: e.tensor_copy(out=paf[0:P, 0, :], in_=pa[0:P, :, :].rearrange("p h k -> p (h k)")), reads=[pa], writes=[paf])
        fw.op(fw.dve, lambda e: e.tensor_single_scalar(out=pa[0:P, :, :], in_=pos[0:P, :, :], scalar=15, op=ALU.bitwise_and), reads=[pos, paf], writes=[pa])
        fw.op(fw.dve, lambda e: e.tensor_copy(out=paf[0:P, 1, :], in_=pa[0:P, :, :].rearrange("p h k -> p (h k)")), reads=[pa], writes=[paf])
        for w_ in range(2):
            o4 = oh[0:P, :, :].rearrange("p h (k a) -> p h k a", a=16)
            sel = paf[0:P, w_, :].rearrange("p (h k) -> p h k", h=8)
            fw.op(fw.dve, lambda e: e.tensor_tensor(out=o4, in0=sel.unsqueeze(3).to_broadcast([P, 8, 16, 16]),
                                                    in1=iota[0:P, :].unsqueeze(1).unsqueeze(1).to_broadcast([P, 8, 16, 16]), op=ALU.is_equal), reads=[paf, iota], writes=[oh])
            fw.op(fw.dve, lambda e: e.tensor_tensor(out=o4, in0=o4, in1=i4[:, :, w_, :].unsqueeze(2).to_broadcast([P, 8, 16, 16]), op=ALU.mult), reads=[oh, ixf], writes=[oh])
            fw.op(fw.dve, lambda e: e.tensor_reduce(out=isel[0:P, w_, :], in_=oh[0:P, :, :].rearrange("p h (k a) -> p (h k) a", a=16), axis=AX.X, op=ALU.add), reads=[oh], writes=[isel])
        fw.op(fw.dve, lambda e: e.scalar_tensor_tensor(out=eid[0:P, :], in0=isel[0:P, 0, :], scalar=128.0, in1=isel[0:P, 1, :], op0=ALU.mult, op1=ALU.add), reads=[isel], writes=[eid])
        fw.op(fw.dve, lambda e: e.tensor_tensor(out=gate[0:P, :, :], in0=tm[0:P, :, :], in1=tm[0:P, :, 0:1].to_broadcast([P, 8, 16]), op=ALU.subtract), reads=[tm], writes=[gate])
        fw.op(fw.act, lambda e: e.activation(out=gate[0:P, :, :], in_=gate[0:P, :, :], func=AF.Exp), reads=[gate], writes=[gate])
        fw.op(fw.dve, lambda e: e.tensor_reduce(out=gst[0:P, 0:8], in_=gate[0:P, :, :], axis=AX.X, op=ALU.add), reads=[gate], writes=[gst])
        fw.op(fw.dve, lambda e: e.reciprocal(out=gst[0:P, 8:16], in_=gst[0:P, 0:8]), reads=[gst], writes=[gst])
        fw.op(fw.dve, lambda e: e.tensor_tensor(out=gate[0:P, :, :], in0=gate[0:P, :, :], in1=gst[0:P, 8:16].unsqueeze(2).to_broadcast([P, 8, 16]), op=ALU.mult), reads=[gate, gst], writes=[gate])
        fw.op(fw.dve, lambda e: e.memset(actp[0:P, :], 0.0), writes=[actp])
        for s_ in range(128):
            rb = ring[gi % NB]
            gi += 1
            fw.dma(fw.pool, lambda h: h.indirect_dma_start(out=rb[0:P, :], out_offset=None, in_=u_rows, in_offset=bass.IndirectOffsetOnAxis(ap=eid[0:P, s_:s_ + 1], axis=0),
                                                         element_offset=l * 16384 * D), reads=[eid, C.peer_u], writes=[rb])
            fw.op(fw.dve, lambda e: e.scalar_tensor_tensor(out=junk[0:P, :], in0=rb[0:P, :], scalar=1.0, in1=x1[0:P, :], op0=ALU.mult, op1=ALU.mult, accum_out=actp[0:P, s_:s_ + 1]),
                  reads=[rb, x1], writes=[junk, actp], join=True)
        emit_gelu(fw, fw.dve, fw.act, wgt[0:P, :], actp[0:P, :], gtmp[0:P, :], [actp], [wgt], gtmp)
        fw.op(fw.dve, lambda e: e.tensor_tensor(out=wgt[0:P, :], in0=wgt[0:P, :], in1=gate[0:P, :, :].rearrange("p h k -> p (h k)"), op=ALU.mult), reads=[wgt, gate], writes=[wgt])
        for s_ in range(128):
            rb = ring[gi % NB]
            gi += 1
            fw.dma(fw.pool, lambda h: h.indirect_dma_start(out=rb[0:P, :], out_offset=None, in_=v_rows, in_offset=bass.IndirectOffsetOnAxis(ap=eid[0:P, s_:s_ + 1], axis=0),
                                                         element_offset=l * 16384 * D), reads=[eid, C.peer_v], writes=[rb])
            if s_ == 0:
                fw.op(fw.dve, lambda e: e.tensor_scalar(out=y[0:P, :], in0=rb[0:P, :], scalar1=wgt[0:P, 0:1], scalar2=None, op0=ALU.mult), reads=[rb, wgt], writes=[y])
            else:
                fw.op(fw.dve, lambda e: e.scalar_tensor_tensor(out=y[0:P, :], in0=rb[0:P, :], scalar=wgt[0:P, s_:s_ + 1], in1=y[0:P, :], op0=ALU.mult, op1=ALU.add),
                      reads=[rb, wgt, y], writes=[y])
        fw.op(fw.dve, lambda e: e.scalar_tensor_tensor(out=y[0:P, :], in0=x1[0:P, :], scalar=ALPHA, in1=y[0:P, :], op0=ALU.mult, op1=ALU.add), reads=[x1, y], writes=[y])
        emit_ln_free(fw, C, y, P, ln2g, ln2b, junk, lnst)
        fw.dma(fw.sp, lambda h: h.dma_start(out=X2[t * 128:t * 128 + P, :], in_=y[0:P, :]), reads=[y], writes=[X2], join=True)
    fw.pop()


def sample_consts():
    c = {}
    rows = np.arange(64)
    tok = rows % 8
    g = rows // 32
    c["msum"] = ((g[:, None] == g[None, :]) & (tok[:, None] == tok[None, :])).astype(np.float32)
    sA = np.ones((64, 257), np.float32); sB = np.zeros((64, 257), np.float32)
    for col in (0, 255, 256):
        sA[:, col] = 0.0; sB[:, col] = 1.0e4
    c["ssA"] = sA; c["ssB"] = sB
    c["caus8"] = np.where(np.arange(8)[None, :] <= tok[:, None], 0.0, NEG).astype(np.float32)
    idx = np.arange(520)[None, :]
    ok = np.where(idx < 512, idx >= tok[:, None], (idx - 512) <= tok[:, None])
    c["wbias"] = np.where(ok, 0.0, NEG).astype(np.float32)
    return c


SCONST_SHAPES = {"msum": [64, 64], "ssA": [64, 257], "ssB": [64, 257], "caus8": [64, 8], "wbias": [64, 520]}
CONST_SHAPES.update(SCONST_SHAPES)


def emit_rows_softmax(fw, sbuf_s, n, st, gate_ap, gate_buf, R=64):
    fw.op(fw.dve, lambda e: e.tensor_reduce(out=st[0:R, 0:1], in_=sbuf_s[0:R, 0:n], axis=AX.X, op=ALU.max, negate=True), reads=[sbuf_s], writes=[st])
    fw.op(fw.act, lambda e: e.activation(out=sbuf_s[0:R, 0:n], in_=sbuf_s[0:R, 0:n], func=AF.Exp, bias=st[0:R, 0:1], scale=1.0, accum_out=st[0:R, 1:2]),
          reads=[sbuf_s, st], writes=[sbuf_s, st])
    fw.op(fw.dve, lambda e: e.reciprocal(out=st[0:R, 2:3], in_=st[0:R, 1:2]), reads=[st], writes=[st])
    if gate_ap is not None:
        fw.op(fw.dve, lambda e: e.tensor_tensor(out=st[0:R, 2:3], in0=st[0:R, 2:3], in1=gate_ap, op=ALU.mult), reads=[st, gate_buf], writes=[st])
    fw.op(fw.dve, lambda e: e.tensor_scalar(out=sbuf_s[0:R, 0:n], in0=sbuf_s[0:R, 0:n], scalar1=st[0:R, 2:3], scalar2=None, op0=ALU.mult), reads=[sbuf_s, st], writes=[sbuf_s])


def phase_nsa_sample(fw, C, l, HtokS, mixn_s):
    fw.push()
    R = 64
    cst = {nm: load_const(fw, C, nm) for nm in SCONST_SHAPES}
    pti = fw.sbuf("pti", [128, 1], I32)
    fw.dma(fw.sp, lambda h: h.dma_start(out=pti[:, :], in_=C.pt[:, :]), reads=[C.pt], writes=[pti])
    idx8 = fw.sbuf("idx8", [128, 1], I32)
    fw.op(fw.dve, lambda e: e.tensor_scalar(out=idx8[:, :], in0=pti[:, :], scalar1=8.0, scalar2=None, op0=ALU.mult), reads=[pti], writes=[idx8])
    kv = fw.sbuf("skv", [TS, 1536], F32)
    rt = fw.sbuf("srt", [TS, 4, 64], F32)
    fw.dma(fw.sp, lambda h: h.dma_start(out=kv[:, :], in_=HtokS[0:TS, 1536:3072]), reads=[HtokS], writes=[kv])
    fw.dma(fw.sp, lambda h: h.dma_start(out=rt[:, :, :], in_=C.cd["rope_s"][:, :, :]), reads=[C.cd["rope_s"]], writes=[rt])
    rtmp = [fw.sbuf(f"srtmp{i}", [TS, 512], F32) for i in range(4)]
    kview = kv[:, :].rearrange("p (a v g h d) -> p a v g h d", a=3, v=2, g=2, h=2, d=64)
    tv = [r[:, 0:384].rearrange("p (a g d) -> p a g d", a=3, g=2) for r in rtmp]
    emit_rope(fw, kview[:, :, 0, :, 0, :], kview[:, :, 0, :, 1, :], rt[:, 0, :], rt[:, 1, :], tv, [TS, 3, 2, 64], [rt], kv, rtmp)
    for a in range(6):
        fw.dma(fw.sp, lambda h: h.dma_start(out=C.out_skv[l, a, :, :], in_=kv[:, a * 256:(a + 1) * 256]), reads=[kv], writes=[C.out_skv], join=True)
    for wi in range(2):
        fw.dma(fw.sp, lambda h: h.dma_start(out=C.out_swin[l, wi, 0:504, :], in_=C.cwin[wi][l, 8:512, :]), reads=[C.cwin[wi]], writes=[C.out_swin], join=True)
        fw.dma(fw.sp, lambda h: h.dma_start(out=C.out_swin[l, wi, 504:512, :], in_=kv[:, 1024 + wi * 256:1280 + wi * 256]), reads=[kv], writes=[C.out_swin], join=True)
    kvb = fw.sbuf("skvb", [TS, 1536], BF16)
    fw.op(fw.act, lambda e: e.copy(out=kvb[:, :], in_=kv[:, :]), reads=[kv], writes=[kvb])
    psb = C.psb[0]
    for i, (a, g) in enumerate([(1, 0), (1, 1), (2, 0), (2, 1)]):
        c0 = a * 512 + g * 128
        fw.op(fw.pe, lambda e: e.transpose(out=psb[:, i * 8:(i + 1) * 8], in_=kvb[:, c0:c0 + 128], identity=C.identb[0:TS, 0:TS]), reads=[kvb, C.identb], writes=[psb], join=(i > 0))
    knT = fw.sbuf("knT", [128, 32], BF16)
    fw.op(fw.dve, lambda e: e.tensor_copy(out=knT[:, :], in_=psb[:, 0:32]), reads=[psb], writes=[knT])
    qt = fw.sbuf("sqt", [TS, 1024], F32)
    gt = fw.sbuf("sgt", [TS, 24], F32)
    fw.dma(fw.sp, lambda h: h.dma_start(out=qt[:, :], in_=HtokS[0:TS, 512:1536]), reads=[HtokS], writes=[qt])
    fw.dma(fw.sp, lambda h: h.dma_start(out=gt[:, :], in_=HtokS[0:TS, 3072:3096]), reads=[HtokS], writes=[gt])
    qv = qt[:, :].rearrange("p (h x d) -> p h x d", h=8, x=2)
    tv2 = [r[:, :].rearrange("p (h d) -> p h d", h=8) for r in rtmp]
    emit_rope(fw, qv[:, :, 0, :], qv[:, :, 1, :], rt[:, 2, :], rt[:, 3, :], tv2, [TS, 8, 64], [rt], qt, rtmp)
    qb = fw.sbuf("sqb", [TS, 1024], BF16)
    fw.op(fw.act, lambda e: e.copy(out=qb[:, :], in_=qt[:, :]), reads=[qt], writes=[qb])
    psb = C.psb[1]
    for h in range(8):
        fw.op(fw.pe, lambda e: e.transpose(out=psb[:, h * 8:(h + 1) * 8], in_=qb[:, h * 128:(h + 1) * 128], identity=C.identb[0:TS, 0:TS]), reads=[qb, C.identb], writes=[psb], join=(h > 0))
    Q = [fw.sbuf(f"sQ{g}", [128, 64], BF16) for g in range(2)]
    for g in range(2):
        fw.op(fw.dve, lambda e: e.memset(Q[g][:, :], 0.0), writes=[Q[g]])
        fw.op(fw.dve, lambda e: e.tensor_copy(out=Q[g][:, g * 32:(g + 1) * 32], in_=psb[:, g * 32:(g + 1) * 32]), reads=[psb], writes=[Q[g]])
    fw.op(fw.act, lambda e: e.activation(out=gt[:, :], in_=gt[:, :], func=AF.Sigmoid), reads=[gt], writes=[gt])
    fw.dma(fw.sp, lambda h: h.dma_start(out=C.gscr[:, :], in_=gt[:, :]), reads=[gt], writes=[C.gscr])
    g64 = fw.sbuf("g64", [64, 3], F32)
    for h in range(8):
        fw.dma(fw.sp, lambda h_: h_.dma_start(out=g64[h * 8:(h + 1) * 8, :], in_=C.gscr[:, h * 3:(h + 1) * 3]), reads=[C.gscr], writes=[g64], join=True)
    st = fw.sbuf("sst", [64, 4], F32)
    wsm = []
    for wi, wsrc in enumerate([C.cmp_wk, C.cmp_wv]):
        w_ = fw.sbuf(f"wsm{wi}", [128, 64], F32)
        fw.dma(fw.sp, lambda h: h.dma_start(out=w_[:, :], in_=wsrc[l, :, :].rearrange("g j -> (g j)").partition_broadcast(128)), reads=[wsrc], writes=[w_])
        wsm.append(w_)
    chunk = [fw.sbuf(f"chunk{i}", [128, 4096], F32) for i in range(2)]
    cacc = [fw.sbuf(f"cacc{i}", [128, 4, 256], F32) for i in range(2)]
    red = fw.sbuf("sred", [128, 256], F32)
    ci = 0
    for wi, cache in enumerate([C.c_cmp_k, C.c_cmp_v]):
        for jb8 in range(8):
            ch = chunk[ci % 2]
            ci += 1
            fw.dma(fw.pool, lambda h: h.indirect_dma_start(out=ch[:, :], out_offset=None, in_=cache[:, :], in_offset=bass.IndirectOffsetOnAxis(ap=idx8[:, 0:1], axis=0),
                                                         element_offset=(l * 1280 * 8 + jb8) * 4096), reads=[idx8, cache], writes=[ch])
            jb, j0 = jb8 // 2, (jb8 % 2) * 16
            wv_ = wsm[wi][:, :].rearrange("p (g j) -> p j g", g=2)[:, j0:j0 + 16, :].unsqueeze(3).to_broadcast([128, 16, 2, 128])
            chv4 = ch[:, :].rearrange("p (j g d) -> p j g d", j=16, g=2)
            fw.op(fw.dve, lambda e: e.tensor_tensor(out=chv4, in0=chv4, in1=wv_, op=ALU.mult), reads=[ch, wsm[wi]], writes=[ch])
            if j0 == 0:
                fw.op(fw.dve, lambda e: e.tensor_reduce(out=cacc[wi][:, jb, :], in_=ch[:, :].rearrange("p (j c) -> p c j", j=16), axis=AX.X, op=ALU.add), reads=[ch], writes=[cacc[wi]], join=True)
            else:
                fw.op(fw.dve, lambda e: e.tensor_reduce(out=red[:, :], in_=ch[:, :].rearrange("p (j c) -> p c j", j=16), axis=AX.X, op=ALU.add), reads=[ch], writes=[red])
                fw.op(fw.dve, lambda e: e.tensor_tensor(out=cacc[wi][:, jb, :], in0=cacc[wi][:, jb, :], in1=red[:, :], op=ALU.add), reads=[cacc[wi], red], writes=[cacc[wi]])
    kcT = fw.sbuf("skcT", [128, 2, 512], BF16)
    for g in range(2):
        ps = C.ps[g]
        for jb in range(4):
            fw.op(fw.pe, lambda e: e.transpose(out=ps[:, jb * 128:(jb + 1) * 128], in_=cacc[0][:, jb, g * 128:(g + 1) * 128], identity=C.ident[:, :]), reads=[cacc[0], C.ident], writes=[ps], join=(jb > 0))
        fw.op(fw.dve, lambda e: e.tensor_copy(out=kcT[:, g, :], in_=ps[:, :]), reads=[ps], writes=[kcT], join=True)
    vcb = fw.sbuf("svcb", [128, 4, 256], BF16)
    fw.op(fw.act, lambda e: e.copy(out=vcb[:, :, :], in_=cacc[1][:, :, :]), reads=[cacc[1]], writes=[vcb])
    ps = C.ps[2]
    for g in range(2):
        fw.op(fw.pe, lambda e: e.matmul(ps[0:R, :], lhsT=Q[g][:, :], rhs=kcT[:, g, :], start=(g == 0), stop=(g == 1)), reads=[Q[g], kcT], writes=[ps], join=(g > 0))
    pc = fw.sbuf("spc", [64, 512], F32)
    fw.op(fw.act, lambda e: e.copy(out=pc[:, :], in_=ps[0:R, :]), reads=[ps], writes=[pc])
    emit_rows_softmax(fw, pc, 512, st, None, None)
    ps = C.ps[3]
    fw.op(fw.pe, lambda e: e.matmul(ps[0:R, :], lhsT=cst["msum"][:, :], rhs=pc[:, :], start=True, stop=True), reads=[cst["msum"], pc], writes=[ps])
    impr = fw.sbuf("simpr", [64, 512], F32)
    fw.op(fw.act, lambda e: e.copy(out=impr[:, :], in_=ps[0:R, :]), reads=[ps], writes=[impr])
    sco = fw.sbuf("ssco", [64, 264], F32)
    sco2 = fw.sbuf("ssco2", [64, 264], F32)
    iv = impr[:, :].rearrange("r (hb two p) -> r hb two p", hb=2, two=2)
    fw.op(fw.dve, lambda e: e.memset(sco[:, 256:264], 0.0), writes=[sco])
    fw.op(fw.dve, lambda e: e.tensor_tensor(out=sco[:, 0:256].rearrange("r (hb p) -> r hb p", hb=2), in0=iv[:, :, 0, :], in1=iv[:, :, 1, :], op=ALU.add), reads=[impr], writes=[sco], join=True)
    fw.op(fw.dve, lambda e: e.tensor_tensor(out=sco[:, 0:257], in0=sco[:, 0:257], in1=cst["ssA"][:, :], op=ALU.mult), reads=[sco, cst["ssA"]], writes=[sco])
    fw.op(fw.dve, lambda e: e.tensor_tensor(out=sco[:, 0:257], in0=sco[:, 0:257], in1=cst["ssB"][:, :], op=ALU.add), reads=[sco, cst["ssB"]], writes=[sco])
    m8 = fw.sbuf("sm8", [64, 16], F32)
    fw.op(fw.dve, lambda e: e.max(out=m8[:, 0:8], in_=sco[:, 0:257]), reads=[sco], writes=[m8])
    fw.op(fw.dve, lambda e: e.match_replace(out=sco2[:, 0:257], in_to_replace=m8[:, 0:8], in_values=sco[:, 0:257], imm_value=-1e30), reads=[sco, m8], writes=[sco2])
    fw.op(fw.dve, lambda e: e.max(out=m8[:, 8:16], in_=sco2[:, 0:257]), reads=[sco2], writes=[m8])
    bsel = fw.sbuf("sbsel", [64, 257], F32)
    fw.op(fw.dve, lambda e: e.tensor_scalar(out=bsel[:, :], in0=sco[:, 0:257], scalar1=m8[:, 15:16], scalar2=None, op0=ALU.is_ge), reads=[sco, m8], writes=[bsel])
    fw.op(fw.dve, lambda e: e.tensor_scalar(out=bsel[:, :], in0=bsel[:, :], scalar1=-NEG, scalar2=NEG, op0=ALU.mult, op1=ALU.add), reads=[bsel], writes=[bsel])
    fw.op(fw.dve, lambda e: e.tensor_scalar(out=pc[:, :], in0=pc[:, :], scalar1=g64[:, 0:1], scalar2=None, op0=ALU.mult), reads=[pc, g64], writes=[pc])
    pcT = fw.sbuf("spcT", [128, 4, 64], BF16)
    ps = C.ps[0]
    for jb in range(4):
        fw.op(fw.pe, lambda e: e.transpose(out=ps[:, jb * 64:(jb + 1) * 64], in_=pc[:, jb * 128:(jb + 1) * 128], identity=C.ident[0:R, 0:R]), reads=[pc, C.ident], writes=[ps], join=(jb > 0))
    fw.op(fw.dve, lambda e: e.tensor_copy(out=pcT[:, :, :], in_=ps[:, 0:256].rearrange("p (j r) -> p j r", j=4)), reads=[ps], writes=[pcT])
    Ss = fw.sbuf("sS", [64, 16392], F32)
    vt = [fw.sbuf(f"svt{i}", [128, 16, 256], BF16) for i in range(2)]
    ktile = [fw.sbuf(f"sktile{i}", [128, 512], BF16) for i in range(4)]
    ki = 0
    for jb8 in range(8):
        ch = chunk[ci % 2]
        ci += 1
        fw.dma(fw.pool, lambda h: h.indirect_dma_start(out=ch[:, :], out_offset=None, in_=C.c_slc_k[:, :], in_offset=bass.IndirectOffsetOnAxis(ap=idx8[:, 0:1], axis=0),
                                                     element_offset=(l * 1280 * 8 + jb8) * 4096), reads=[idx8, C.c_slc_k], writes=[ch])
        chv = ch[:, :].rearrange("p (j g d) -> p j g d", j=16, g=2)
        for j4 in range(4):
            kts = []
            for g in range(2):
                pst = C.ps[(ki % 2) * 2 + g]
                for jj in range(4):
                    fw.op(fw.pe, lambda e: e.transpose(out=pst[:, jj * 128:(jj + 1) * 128], in_=chv[:, j4 * 4 + jj, g, :], identity=C.ident[:, :]), reads=[ch, C.ident], writes=[pst], join=(jj > 0))
                kt_ = ktile[(ki % 2) * 2 + g]
                if g == 0:
                    fw.op(fw.dve, lambda e: e.tensor_copy(out=kt_[:, :], in_=pst[:, :]), reads=[pst], writes=[kt_])
                else:
                    fw.op(fw.act, lambda e: e.copy(out=kt_[:, :], in_=pst[:, :]), reads=[pst], writes=[kt_])
                kts.append(kt_)
            pss = C.pacc[ki % 2]
            for g in range(2):
                fw.op(fw.pe, lambda e: e.matmul(pss[0:R, :], lhsT=Q[g][:, :], rhs=kts[g][:, :], start=(g == 0), stop=(g == 1)), reads=[Q[g], kts[g]], writes=[pss], join=(g > 0))
            jabs = jb8 * 16 + j4 * 4
            hb = 1 if jabs >= 64 else 0
            fw.op(fw.dve, lambda e: e.tensor_tensor(out=Ss[:, jabs * 128:(jabs + 4) * 128].rearrange("r (j p) -> r j p", j=4), in0=pss[0:R, :].rearrange("r (j p) -> r j p", j=4),
                                                    in1=bsel[:, hb * 128:(hb + 1) * 128].unsqueeze(1).to_broadcast([R, 4, 128]), op=ALU.add), reads=[pss, bsel], writes=[Ss], join=True)
            ki += 1
    pss = C.pacc[0]
    for g in range(2):
        fw.op(fw.pe, lambda e: e.matmul(pss[0:R, 0:8], lhsT=Q[g][:, :], rhs=knT[:, g * 8:(g + 1) * 8], start=(g == 0), stop=(g == 1)), reads=[Q[g], knT], writes=[pss], join=(g > 0))
    fw.op(fw.dve, lambda e: e.scalar_tensor_tensor(out=Ss[:, 16384:16392], in0=pss[0:R, 0:8], scalar=bsel[:, 256:257], in1=cst["caus8"][:, :], op0=ALU.add, op1=ALU.add),
          reads=[pss, bsel, cst["caus8"]], writes=[Ss], join=True)
    emit_rows_softmax(fw, Ss, 16392, st, g64[:, 1:2], g64)
    pT = fw.sbuf("spT", [128, 128, 64], BF16)
    for j8 in range(16):
        ps = C.ps[j8 % 4]
        for jj in range(8):
            j = j8 * 8 + jj
            fw.op(fw.pe, lambda e: e.transpose(out=ps[:, jj * 64:(jj + 1) * 64], in_=Ss[:, j * 128:(j + 1) * 128], identity=C.ident[0:R, 0:R]), reads=[Ss, C.ident], writes=[ps], join=(jj > 0))
        fw.op(fw.act if j8 % 2 else fw.dve, (lambda e: e.copy(out=pT[:, j8 * 8:(j8 + 1) * 8, :], in_=ps[:, :].rearrange("p (j r) -> p j r", j=8))) if j8 % 2 else
              (lambda e: e.tensor_copy(out=pT[:, j8 * 8:(j8 + 1) * 8, :], in_=ps[:, :].rearrange("p (j r) -> p j r", j=8))), reads=[ps], writes=[pT], join=True)
    ps = C.ps[0]
    fw.op(fw.pe, lambda e: e.transpose(out=ps[0:8, 0:64], in_=Ss[:, 16384:16392], identity=C.ident[0:R, 0:R]), reads=[Ss, C.ident], writes=[ps])
    pTn = fw.sbuf("spTn", [8, 64], BF16)
    fw.op(fw.dve, lambda e: e.tensor_copy(out=pTn[:, :], in_=ps[0:8, 0:64]), reads=[ps], writes=[pTn])
    wk = fw.sbuf("swk", [128, 4, 256], F32)
    wv = fw.sbuf("swv", [128, 4, 256], F32)
    fw.dma(fw.sp, lambda h: h.dma_start(out=wk[:, :, :], in_=C.cwin[0][l, :, :].rearrange("(a p) c -> p a c", p=128)), reads=[C.cwin[0]], writes=[wk])
    fw.dma(fw.sp, lambda h: h.dma_start(out=wv[:, :, :], in_=C.cwin[1][l, :, :].rearrange("(a p) c -> p a c", p=128)), reads=[C.cwin[1]], writes=[wv])
    wvb = fw.sbuf("swvb", [128, 4, 256], BF16)
    fw.op(fw.act, lambda e: e.copy(out=wvb[:, :, :], in_=wv[:, :, :]), reads=[wv], writes=[wvb])
    wkT = fw.sbuf("swkT", [128, 2, 512], BF16)
    for g in range(2):
        ps = C.ps[1 + g]
        for a in range(4):
            fw.op(fw.pe, lambda e: e.transpose(out=ps[:, a * 128:(a + 1) * 128], in_=wk[:, a, g * 128:(g + 1) * 128], identity=C.ident[:, :]), reads=[wk, C.ident], writes=[ps], join=(a > 0))
        fw.op(fw.dve, lambda e: e.tensor_copy(out=wkT[:, g, :], in_=ps[:, :]), reads=[ps], writes=[wkT], join=True)
    Sw = fw.sbuf("sSw", [64, 520], F32)
    pss = C.pacc[1]
    for g in range(2):
        fw.op(fw.pe, lambda e: e.matmul(pss[0:R, :], lhsT=Q[g][:, :], rhs=wkT[:, g, :], start=(g == 0), stop=(g == 1)), reads=[Q[g], wkT], writes=[pss], join=(g > 0))
    fw.op(fw.dve, lambda e: e.tensor_tensor(out=Sw[:, 0:512], in0=pss[0:R, :], in1=cst["wbias"][:, 0:512], op=ALU.add), reads=[pss, cst["wbias"]], writes=[Sw], join=True)
    pss = C.pacc[0]
    for g in range(2):
        fw.op(fw.pe, lambda e: e.matmul(pss[0:R, 0:8], lhsT=Q[g][:, :], rhs=knT[:, 16 + g * 8:16 + (g + 1) * 8], start=(g == 0), stop=(g == 1)), reads=[Q[g], knT], writes=[pss], join=(g > 0))
    fw.op(fw.dve, lambda e: e.tensor_tensor(out=Sw[:, 512:520], in0=pss[0:R, 0:8], in1=cst["wbias"][:, 512:520], op=ALU.add), reads=[pss, cst["wbias"]], writes=[Sw], join=True)
    emit_rows_softmax(fw, Sw, 520, st, g64[:, 2:3], g64)
    pwT = fw.sbuf("spwT", [128, 4, 64], BF16)
    ps = C.ps[3]
    for a in range(4):
        fw.op(fw.pe, lambda e: e.transpose(out=ps[:, a * 64:(a + 1) * 64], in_=Sw[:, a * 128:(a + 1) * 128], identity=C.ident[0:R, 0:R]), reads=[Sw, C.ident], writes=[ps], join=(a > 0))
    fw.op(fw.pe, lambda e: e.transpose(out=ps[0:8, 256:320], in_=Sw[:, 512:520], identity=C.ident[0:R, 0:R]), reads=[Sw, C.ident], writes=[ps], join=True)
    fw.op(fw.dve, lambda e: e.tensor_copy(out=pwT[:, :, :], in_=ps[:, 0:256].rearrange("p (a r) -> p a r", a=4)), reads=[ps], writes=[pwT])
    pwTn = fw.sbuf("spwTn", [8, 64], BF16)
    fw.op(fw.dve, lambda e: e.tensor_copy(out=pwTn[:, :], in_=ps[0:8, 256:320]), reads=[ps], writes=[pwTn])
    accs = [C.pacc[0], C.pacc[1]]
    for g in range(2):
        gs = slice(g * 32, (g + 1) * 32)
        for jb in range(4):
            fw.op(fw.pe, lambda e: e.matmul(accs[g][:, 0:32], lhsT=vcb[:, jb, g * 128:(g + 1) * 128], rhs=pcT[:, jb, gs], start=(jb == 0), stop=False), reads=[vcb, pcT], writes=[accs[g]], join=(jb > 0))
    for jb8 in range(8):
        ch = chunk[ci % 2]
        v_ = vt[ci % 2]
        ci += 1
        fw.dma(fw.pool, lambda h: h.indirect_dma_start(out=ch[:, :], out_offset=None, in_=C.c_slc_v[:, :], in_offset=bass.IndirectOffsetOnAxis(ap=idx8[:, 0:1], axis=0),
                                                     element_offset=(l * 1280 * 8 + jb8) * 4096), reads=[idx8, C.c_slc_v], writes=[ch])
        fw.op(fw.act if jb8 % 2 else fw.dve, (lambda e: e.copy(out=v_[:, :, :], in_=ch[:, :].rearrange("p (j c) -> p j c", j=16))) if jb8 % 2 else
              (lambda e: e.tensor_copy(out=v_[:, :, :], in_=ch[:, :].rearrange("p (j c) -> p j c", j=16))), reads=[ch], writes=[v_])
        for g in range(2):
            gs = slice(g * 32, (g + 1) * 32)
            for jj in range(16):
                j = jb8 * 16 + jj
                fw.op(fw.pe, lambda e: e.matmul(accs[g][:, 0:32], lhsT=v_[:, jj, g * 128:(g + 1) * 128], rhs=pT[:, j, gs], start=False, stop=False), reads=[v_, pT], writes=[accs[g]], join=True)
    for g in range(2):
        acc = accs[g]
        gs = slice(g * 32, (g + 1) * 32)
        fw.op(fw.pe, lambda e: e.matmul(acc[:, 0:32], lhsT=kvb[:, 768 + g * 128:896 + g * 128], rhs=pTn[:, gs], start=False, stop=False), reads=[kvb, pTn], writes=[acc], join=True)
        for a in range(4):
            fw.op(fw.pe, lambda e: e.matmul(acc[:, 0:32], lhsT=wvb[:, a, g * 128:(g + 1) * 128], rhs=pwT[:, a, gs], start=False, stop=False), reads=[wvb, pwT], writes=[acc], join=True)
        fw.op(fw.pe, lambda e: e.matmul(acc[:, 0:32], lhsT=kvb[:, 1280 + g * 128:1408 + g * 128], rhs=pwTn[:, gs], start=False, stop=True), reads=[kvb, pwTn], writes=[acc], join=True)
        fw.op(fw.dve, lambda e: e.tensor_copy(out=mixn_s[:, g * 4:(g + 1) * 4, 0:TS], in_=acc[:, 0:32].rearrange("p (r t) -> p r t", r=4)), reads=[acc], writes=[mixn_s], join=True)
    fw.pop()


def build(mode="full"):
    nc = bass.Bass("TRN2", target_bir_lowering=False)
    fw = FW(nc)
    C = Ctx()
    dbg = mode != "full"
    nlayers = 1 if dbg else DEPTH
    def inp(name, shape, dtype=F32):
        return fw.dram(name, shape, dtype, kind="ExternalInput")
    C.xp = inp("xp", [S, D])
    C.xs = inp("xs", [TS, D])
    C.pt = inp("pt", [128, 1], I32)
    C.c_cmp_k = inp("c_cmp_k", [DEPTH * 1280 * 8, 4096])
    C.c_cmp_v = inp("c_cmp_v", [DEPTH * 1280 * 8, 4096])
    C.c_slc_k = inp("c_slc_k", [DEPTH * 1280 * 8, 4096])
    C.c_slc_v = inp("c_slc_v", [DEPTH * 1280 * 8, 4096])
    C.cwin = [inp("c_win_k", [DEPTH, 512, 256]), inp("c_win_v", [DEPTH, 512, 256])]
    C.state_conv = inp("state_conv", [DEPTH, 30, 512])
    for nm, shp in INPUT_SPECS:
        setattr(C, nm, inp(nm, shp))
    C.cd = {nm: inp(nm, shp) for nm, shp in CONST_SHAPES.items()}
    C.out_kv = fw.dram("o_pkv", [DEPTH, 6, S, 256], F32, kind="ExternalOutput")
    C.out_conv_buf = fw.dram("o_pconv", [DEPTH, 30, 512], F32, kind="ExternalOutput")
    C.out_gm = fw.dram("o_pgm", [DEPTH, 128, 512], F32, kind="ExternalOutput")
    C.out_gm_buf = C.out_gm
    C.out_skv = fw.dram("o_skv", [DEPTH, 6, TS, 256], F32, kind="ExternalOutput")
    C.out_swin = fw.dram("o_swin", [DEPTH, 2, 512, 256], F32, kind="ExternalOutput")
    C.out_sconv = fw.dram("o_sconv", [DEPTH, 30, 512], F32, kind="ExternalOutput")
    C.out_sgm = fw.dram("o_sgm", [DEPTH, TS, 512], F32, kind="ExternalOutput")
    C.out_y = fw.dram("o_y", [S // 2, D], F32, kind="ExternalOutput")
    C.prow_d = inp("prow", [128, NT // 2], I32)
    C.prow = fw.sbuf("prow_sb", [128, NT // 2], I32)
    fw.dma(fw.sp, lambda h: h.dma_start(out=C.prow[:, :], in_=C.prow_d[:, :]), reads=[C.prow_d], writes=[C.prow])
    C.out_ys = fw.dram("o_ys", [TS, D], F32, kind="ExternalOutput")
    C.ident = load_const(fw, C, "ident")
    C.identb = fw.sbuf("identb", [128, 128], BF16)
    fw.op(fw.dve, lambda e: e.tensor_copy(out=C.identb[:, :], in_=C.ident[:, :]), reads=[C.ident], writes=[C.identb])
    C.epsc = fw.sbuf("epsc", [128, 1], F32)
    fw.op(fw.dve, lambda e: e.memset(C.epsc[:, :], LN_EPS), writes=[C.epsc])
    for nm in ["causal", "acausal", "mask4", "triu", "onesdiv"]:
        setattr(C, nm, load_const(fw, C, nm))
    C.ps = [fw.psum(f"ps{i}", [128, 512], F32) for i in range(4)]
    C.psb = [fw.psum(f"psb{i}", [128, 1024], BF16) for i in range(2)]
    C.pacc = [fw.psum(f"pacc{i}", [128, 512], F32) for i in range(2)]
    okind = "ExternalOutput" if dbg else "Internal"
    C.Htok = fw.dram("Htok", [S, NTOKC], F32)
    C.HfT = fw.dram("HfT", [1536, S], F32)
    C.HtokS = fw.dram("HtokS", [TS, NTOKC], F32)
    C.HfTS = fw.dram("HfTS", [1536, TS], F32)
    C.gscr = fw.dram("gscr", [TS, 24], F32)
    C.X1 = fw.dram("X1", [S, D], F32, kind=okind)
    C.XS1 = fw.dram("XS1", [TS, D], F32, kind=okind)
    X2 = [fw.dram("X2a", [S, D], F32), C.out_y] if not dbg else [C.out_y]
    XS2 = [fw.dram("XS2a", [TS, D], F32), C.out_ys] if not dbg else [C.out_ys]
    mixn_s = fw.sbuf("mixn_s", [128, 8, TS], BF16)
    convTs = fw.sbuf("convTs", [128, 4, TS], BF16)
    x_src, xs_src = C.xp, C.xs
    for l in range(nlayers):
        fw.push()
        C.xin = [fw.sbuf(f"xin{i}", [128, D], F32) for i in range(2)]
        C.wbuf = [fw.sbuf(f"wbuf{i}", [128, 16, 512], BF16) for i in range(2)]
        C.hbuf = [fw.sbuf(f"hbuf{i}", [128, 512], F32) for i in range(4)]
        C.htmp = fw.sbuf("htmp", [128, 512], F32)
        xT = fw.sbuf("xT", [128, 16, S], BF16)
        phase_proj(fw, C, l, x_src, NT, C.Htok, C.HfT, xT, "p")
        phase_proj(fw, C, l, xs_src, 0, C.HtokS, C.HfTS, xT, "s")
        fw.pop()
        phase_nsa_sample(fw, C, l, C.HtokS, mixn_s)
        fw.push()
        C.kT = fw.sbuf("kT", [128, 4, S], BF16)
        C.V = fw.sbuf("V", [128, NT, 2, 256], BF16)
        C.kcmpT = fw.sbuf("kcmpT", [128, 2, 64], BF16)
        C.vcmp = fw.sbuf("vcmp", [64, 256], BF16)
        C.convT = fw.sbuf("convT", [128, 4, S], BF16)
        fw.push()
        phase_kv_prompt(fw, C, l)
        fw.pop()
        phase_conv(fw, C, l, C.HfT, S, None, C.out_conv_buf[l, :, :], C.convT)
        phase_conv(fw, C, l, C.HfTS, TS, C.state_conv[l, :, :], C.out_sconv[l, :, :], convTs)
        phase_mix_prompt(fw, C, l, x_src, C.X1, (xs_src, C.XS1, mixn_s, convTs))
        fw.pop()
        if l == nlayers - 1:
            phase_peer(fw, C, l, [(C.X1, NT // 2 if not dbg else 1, 128, X2[l], C.prow), (C.XS1, 1, TS, XS2[l])])
        else:
            phase_peer(fw, C, l, [(C.X1, NT, 128, X2[l]), (C.XS1, 1, TS, XS2[l])])
        x_src, xs_src = X2[l], XS2[l]
    fw.finish()
    return nc


def core_inputs(inputs, c):
    b = c // 2
    m = {"xp": np.ascontiguousarray(inputs["x_prompt"][b]), "xs": np.ascontiguousarray(inputs["x_sample"][c]),
         "pt": np.ascontiguousarray(inputs["page_table"][c].reshape(128, 1)).astype(np.int32)}
    for nm, key in [("c_cmp_k", "cache_cmp_k"), ("c_cmp_v", "cache_cmp_v"), ("c_slc_k", "cache_slc_k"), ("c_slc_v", "cache_slc_v")]:
        m[nm] = np.asarray(inputs[key]).reshape(DEPTH * 1280 * 8, 4096)
    m["c_win_k"] = np.ascontiguousarray(np.asarray(inputs["cache_win_k"])[:, c].reshape(DEPTH, 512, 256))
    m["c_win_v"] = np.ascontiguousarray(np.asarray(inputs["cache_win_v"])[:, c].reshape(DEPTH, 512, 256))
    m["state_conv"] = np.ascontiguousarray(np.asarray(inputs["state_conv"])[:, c])
    hh = c % 2
    m["prow"] = ((hh * (NT // 2) + np.arange(NT // 2)[None, :]) * 128 + np.arange(128)[:, None]).astype(np.int32)
    for nm, shp in INPUT_SPECS:
        m[nm] = np.asarray(inputs[nm])
    m.update(host_consts())
    return m


_NC_CACHE = {}


def kernel(**inputs):
    n = 8
    if "nc" not in _NC_CACHE:
        _NC_CACHE["nc"] = build("full")
    nc = _NC_CACHE["nc"]
    in_maps = [core_inputs(inputs, c) for c in range(n)]
    res = run_bass_kernel_spmd(nc, in_maps, core_ids=list(range(n))).results
    B = 4
    ev = [res[2 * b] for b in range(B)]
    y_prompt = np.stack([np.concatenate([res[2 * b]["o_y"], res[2 * b + 1]["o_y"]], 0) for b in range(B)], 0).astype(np.float32)
    y_sample = np.stack([res[c]["o_ys"] for c in range(n)], 0).astype(np.float32)
    pkv = np.stack([r["o_pkv"] for r in ev], 0)
    def pk(a, rows=None):
        t = pkv[:, :, a]
        if rows is not None:
            t = t[:, :, rows:]
        t = np.transpose(t, (1, 0, 2, 3))
        return np.ascontiguousarray(t.reshape(t.shape[0], t.shape[1], t.shape[2], 2, 128)).astype(np.float32)
    p_conv = np.ascontiguousarray(np.stack([r["o_pconv"] for r in ev], 1)).astype(np.float32)
    p_gm = np.ascontiguousarray(np.stack([r["o_pgm"] for r in ev], 1)).astype(np.float32)
    skv = np.stack([res[c]["o_skv"] for c in range(n)], 0)
    def sk(a):
        t = np.transpose(skv[:, :, a], (1, 0, 2, 3))
        return np.ascontiguousarray(t.reshape(DEPTH, n, TS, 2, 128)).astype(np.float32)
    swin = np.stack([res[c]["o_swin"] for c in range(n)], 0)
    def sw(a):
        t = np.transpose(swin[:, :, a], (1, 0, 2, 3))
        return np.ascontiguousarray(t.reshape(DEPTH, n, 512, 2, 128)).astype(np.float32)
    s_conv = np.ascontiguousarray(np.stack([res[c]["o_sconv"] for c in range(n)], 1)).astype(np.float32)
    s_gm = np.ascontiguousarray(np.stack([res[c]["o_sgm"] for c in range(n)], 1)).astype(np.float32)
    return (y_prompt, y_sample, pk(0), pk(1), pk(2), pk(3), pk(4, S - 512), pk(5, S - 512), p_conv, p_gm,
            sk(0), sk(1), sk(2), sk(3), sw(0), sw(1), s_conv, s_gm)
```

```python
import numpy as np
from contextlib import ExitStack
import concourse.bass as bass
import concourse.mybir as mybir
from concourse.bass_utils import run_bass_kernel_spmd

F32 = mybir.dt.float32
BF16 = mybir.dt.bfloat16
I32 = mybir.dt.int32
U32 = mybir.dt.uint32
AF = mybir.ActivationFunctionType
ALU = mybir.AluOpType
AX = mybir.AxisListType


class Buf:
    def __init__(self, t, name):
        self.t = t
        self.name = name
        self.w = {}
        self.r = {}

    def __getitem__(self, idx):
        return self.t[idx]


class Eng:
    def __init__(self, fw, name, h, pe=False, ndma=0):
        self.fw = fw
        self.name = name
        self.h = h
        self.pe = pe
        self.sem = fw.new_sem(name + "_prog")
        self.n = 0
        self.seen = {}
        self.dma_sems = [fw.new_sem(f"{name}_d{i}") for i in range(ndma)]
        self.dma_n = 0

    def wait(self, tok):
        key, sem, val = tok
        if self.seen.get(key, 0) >= val:
            return
        self.h.wait_ge(sem, val)
        self.seen[key] = val


class FW:
    def __init__(self, nc):
        self.nc = nc
        self.es0 = ExitStack()
        self.es = self.es0
        self.es_stack = []
        self.sems = []
        self.pe = Eng(self, "pe", nc.tensor, pe=True)
        self.act = Eng(self, "act", nc.scalar, ndma=8)
        self.dve = Eng(self, "dve", nc.vector)
        self.pool = Eng(self, "pool", nc.gpsimd, ndma=24)
        self.sp = Eng(self, "sp", nc.sync, ndma=24)
        self.engs = [self.pe, self.act, self.dve, self.pool, self.sp]
        self.bufs = []

    def new_sem(self, name):
        s = self.es0.enter_context(self.nc.semaphore(name))
        self.sems.append(s)
        return s

    def push(self):
        self.es_stack.append((self.es, len(self.bufs)))
        self.es = ExitStack()

    def pop(self):
        self.barrier()
        self.es.close()
        self.es, nb = self.es_stack.pop()
        del self.bufs[nb:]

    def sbuf(self, name, shape, dtype=F32):
        self.uid = getattr(self, "uid", 0) + 1
        name = f"{name}_{self.uid}"
        t = self.es.enter_context(self.nc.sbuf_tensor(name, list(shape), dtype))
        b = Buf(t, name)
        self.bufs.append(b)
        return b

    def psum(self, name, shape, dtype=F32):
        t = self.es.enter_context(self.nc.psum_tensor(name, list(shape), dtype))
        b = Buf(t, name)
        self.bufs.append(b)
        return b

    def dram(self, name, shape, dtype=F32, kind="Internal"):
        t = self.nc.dram_tensor(name, list(shape), dtype, kind=kind).ap()
        b = Buf(t, name)
        self.bufs.append(b)
        return b

    def view(self, ap, name="v"):
        b = Buf(ap, name)
        self.bufs.append(b)
        return b

    def _waits(self, eng, reads, writes, join, is_dma):
        for b in reads:
            for tok in b.w.values():
                if tok[0] == eng.name and not is_dma and eng.pe:
                    continue
                eng.wait(tok)
        for b in writes:
            if not join:
                for tok in b.w.values():
                    if tok[0] == eng.name and not is_dma:
                        continue
                    eng.wait(tok)
            for tok in b.r.values():
                if tok[0] == eng.name and not is_dma:
                    continue
                eng.wait(tok)

    def _record(self, tok, reads, writes, join):
        for b in writes:
            if join:
                b.w[tok[0]] = tok
            else:
                b.w = {tok[0]: tok}
            b.r = {}
        for b in reads:
            if b not in writes:
                b.r[tok[0]] = tok

    def op(self, eng, fn, reads=(), writes=(), join=False):
        self._waits(eng, reads, writes, join, False)
        inst = fn(eng.h)
        eng.n += 1
        inst.then_inc(eng.sem, 1)
        tok = (eng.name, eng.sem, eng.n)
        self._record(tok, reads, writes, join)
        return inst

    def dma(self, q, fn, reads=(), writes=(), join=False):
        self._waits(q, reads, writes, join, True)
        k = len(q.dma_sems)
        i = q.dma_n % k
        gen = q.dma_n // k
        sem = q.dma_sems[i]
        key = f"{q.name}_d{i}"
        if gen > 0:
            q.wait((key, sem, 16 * gen))
        inst = fn(q.h)
        inst.then_inc(sem, 16)
        q.dma_n += 1
        tok = (key, sem, 16 * (gen + 1))
        self._record(tok, reads, writes, join)
        return inst

    def barrier(self):
        toks = []
        for e in self.engs:
            if e.n:
                toks.append((e.name, e.sem, e.n))
            k = len(e.dma_sems)
            for i in range(min(k, e.dma_n)):
                cnt = (e.dma_n - 1 - i) // k + 1
                toks.append((f"{e.name}_d{i}", e.dma_sems[i], 16 * cnt))
        for e in self.engs:
            for tok in toks:
                if tok[0] == e.name:
                    continue
                e.wait(tok)
        for b in self.bufs:
            b.w = {}
            b.r = {}

    def finish(self):
        self.barrier()


D = 2048
S = 2048
NT = S // 128
DEPTH = 2
TS = 8
IN_W = 4632
TOK0, TOK1 = 512, 3608
NTOKC = TOK1 - TOK0
ALPHA = float((2 * DEPTH) ** 0.25)
LN_EPS = 1e-5
GELU_C = 0.7978845608028654
NEG = -30000.0


class Ctx:
    pass


def emit_gelu(fw, eng_dve, eng_act, out_ap, in_ap, tmp_ap, bufs_r, bufs_w, tmpbuf):
    fw.op(eng_act, lambda e: e.activation(out=tmp_ap, in_=in_ap, func=AF.Square), reads=bufs_r, writes=[tmpbuf])
    fw.op(eng_dve, lambda e: e.tensor_scalar(out=tmp_ap, in0=tmp_ap, scalar1=0.044715, scalar2=1.0, op0=ALU.mult, op1=ALU.add),
          reads=[tmpbuf], writes=[tmpbuf])
    fw.op(eng_dve, lambda e: e.tensor_tensor(out=tmp_ap, in0=tmp_ap, in1=in_ap, op=ALU.mult), reads=bufs_r + [tmpbuf], writes=[tmpbuf])
    fw.op(eng_act, lambda e: e.activation(out=tmp_ap, in_=tmp_ap, func=AF.Sigmoid, scale=2.0 * GELU_C), reads=[tmpbuf], writes=[tmpbuf])
    fw.op(eng_dve, lambda e: e.tensor_tensor(out=out_ap, in0=tmp_ap, in1=in_ap, op=ALU.mult), reads=bufs_r + [tmpbuf], writes=bufs_w)


def phase_proj(fw, C, l, x_src, nt, Htok, HfT, xT, tagp):
    nc = fw.nc
    ntok = nt * 128 if nt > 0 else TS
    P = 128 if nt > 0 else TS
    ntile = max(nt, 1)
    for t in range(ntile):
        xt = C.xin[t % 2]
        fw.dma(fw.sp, lambda h: h.dma_start(out=xt[0:P, :], in_=x_src[t * 128:t * 128 + P, :]), reads=[x_src], writes=[xt])
        for cb in range(4):
            ps = C.ps[(t * 4 + cb) % 4]
            for k in range(4):
                c = cb * 4 + k
                fw.op(fw.pe, lambda e: e.transpose(out=ps[:, k * 128:k * 128 + P], in_=xt[0:P, c * 128:(c + 1) * 128], identity=C.ident[0:P, 0:P]),
                      reads=[xt, C.ident], writes=[ps], join=(k > 0))
            eng = fw.dve if cb % 2 == 0 else fw.act
            src = ps[:, :].rearrange("p (k q) -> p k q", k=4)[:, :, 0:P]
            dst = xT[:, cb * 4:(cb + 1) * 4, t * 128:t * 128 + P]
            if eng is fw.dve:
                fw.op(eng, lambda e: e.tensor_copy(out=dst, in_=src), reads=[ps], writes=[xT], join=True)
            else:
                fw.op(eng, lambda e: e.copy(out=dst, in_=src), reads=[ps], writes=[xT], join=True)
    w_in = C.w_in
    ncol = [(TOK0 + j * 512, min(512, TOK1 - (TOK0 + j * 512))) for j in range((NTOKC + 511) // 512)]
    it = 0
    for j, (c0, cw) in enumerate(ncol):
        wb = C.wbuf[j % 2]
        fw.dma(fw.pool, lambda h: h.dma_start(out=wb[:, :, 0:cw], in_=w_in[l, :, c0:c0 + cw].rearrange("(c p) n -> p c n", p=128)),
               reads=[w_in], writes=[wb])
        for t in range(ntile):
            ps = C.ps[it % 4]
            for c in range(16):
                fw.op(fw.pe, lambda e: e.matmul(ps[0:P, 0:cw], lhsT=xT[:, c, t * 128:t * 128 + P], rhs=wb[:, c, 0:cw], start=(c == 0), stop=(c == 15)),
                      reads=[xT, wb], writes=[ps], join=(c > 0))
            hb = C.hbuf[it % 4]
            if it % 2 == 0:
                fw.op(fw.dve, lambda e: e.tensor_copy(out=hb[0:P, 0:cw], in_=ps[0:P, 0:cw]), reads=[ps], writes=[hb])
            else:
                fw.op(fw.act, lambda e: e.copy(out=hb[0:P, 0:cw], in_=ps[0:P, 0:cw]), reads=[ps], writes=[hb])
            fw.dma(fw.sp, lambda h: h.dma_start(out=Htok[t * 128:t * 128 + P, c0 - TOK0:c0 - TOK0 + cw], in_=hb[0:P, 0:cw]),
                   reads=[hb], writes=[Htok], join=True)
            it += 1
    fcols = [(0, 0), (128, 128), (256, 256), (384, 384)] + [(3608 + i * 128, 512 + i * 128) for i in range(8)]
    TB = 512 if nt > 0 else TS
    ntb = max(ntok // 512, 1)
    for j, (c0, r0) in enumerate(fcols):
        wb = C.wbuf[j % 2]
        fw.dma(fw.pool, lambda h: h.dma_start(out=wb[:, :, 0:128], in_=w_in[l, :, c0:c0 + 128].rearrange("(c p) n -> p c n", p=128)),
               reads=[w_in], writes=[wb])
        for tb in range(ntb):
            ps = C.ps[it % 4]
            for c in range(16):
                fw.op(fw.pe, lambda e: e.matmul(ps[:, 0:TB], lhsT=wb[:, c, 0:128], rhs=xT[:, c, tb * 512:tb * 512 + TB], start=(c == 0), stop=(c == 15)),
                      reads=[xT, wb], writes=[ps], join=(c > 0))
            hb = C.hbuf[it % 4]
            if r0 < 512:
                tmp = C.htmp
                emit_gelu(fw, fw.dve, fw.act, hb[:, 0:TB], ps[:, 0:TB], tmp[:, 0:TB], [ps], [hb], tmp)
            elif it % 2 == 0:
                fw.op(fw.dve, lambda e: e.tensor_copy(out=hb[:, 0:TB], in_=ps[:, 0:TB]), reads=[ps], writes=[hb])
            else:
                fw.op(fw.act, lambda e: e.copy(out=hb[:, 0:TB], in_=ps[:, 0:TB]), reads=[ps], writes=[hb])
            fw.dma(fw.sp, lambda h: h.dma_start(out=HfT[r0:r0 + 128, tb * 512:tb * 512 + TB], in_=hb[:, 0:TB]),
                   reads=[hb], writes=[HfT], join=True)
            it += 1


def host_consts():
    c = {}
    c["ident"] = np.eye(128, dtype=np.float32)
    half = 64
    inv = (np.float32(10000.0) ** (-np.arange(half, dtype=np.float32) / np.float32(half))).astype(np.float32)
    def rope_tab(pos):
        ang = pos.astype(np.float32)[:, None] * inv[None, :]
        cs, sn = np.cos(ang).astype(np.float32), np.sin(ang).astype(np.float32)
        sc = np.float32(128 ** -0.5)
        return np.stack([cs, sn, cs * sc, sn * sc], axis=1).astype(np.float32)
    c["rope_p"] = rope_tab(np.arange(S))
    c["rope_s"] = rope_tab(16384 + np.arange(TS))
    q = np.arange(128)[:, None, None]
    t = np.arange(NT)[None, :, None]
    n = np.arange(64)[None, None, :]
    qpos = 128 * t + q
    c["cmpbias"] = np.where(32 * n + 31 <= qpos, 0.0, NEG).astype(np.float32)
    c["cmpvalid"] = (qpos[:, :, 0] >= 31).astype(np.float32)
    m = np.arange(32)[None, None, :]
    cur = qpos // 64
    valid = 64 * m <= qpos
    forced = (m == 0) | (m == cur) | (m == cur - 1)
    c["selA"] = (valid & ~forced).astype(np.float32)
    c["selB"] = np.where(valid, np.where(forced, 1.0e4, 0.0), -1.0e9).astype(np.float32)
    qq = np.arange(128)[:, None]; kk = np.arange(128)[None, :]
    c["causal"] = np.where(kk <= qq, 0.0, NEG).astype(np.float32)
    c["acausal"] = np.where(kk >= qq, 0.0, NEG).astype(np.float32)
    tok = np.arange(128)[:, None]
    c["mask4"] = (tok // 32 == np.arange(4)[None, :]).astype(np.float32)
    c["maskpad"] = (np.arange(64)[None, None, :] == 4 * np.arange(NT)[None, :, None] + (tok // 32)[:, :, None]).astype(np.float32)
    c["triu"] = (qq <= kk).astype(np.float32)
    c["onesdiv"] = np.full((128, 128), 1.0 / 128, np.float32)
    c["iota16"] = np.tile(np.arange(16, dtype=np.float32)[None, :], (128, 1))
    c.update(sample_consts())
    return c


CONST_SHAPES = {"ident": [128, 128], "rope_p": [S, 4, 64], "rope_s": [TS, 4, 64], "cmpbias": [128, NT, 64], "cmpvalid": [128, NT],
                "selA": [128, NT, 32], "selB": [128, NT, 32], "causal": [128, 128], "acausal": [128, 128], "mask4": [128, 4],
                "maskpad": [128, NT, 64], "triu": [128, 128], "onesdiv": [128, 128], "iota16": [128, 16]}


def load_const(fw, C, name, dtype=F32):
    shp = CONST_SHAPES[name]
    b = fw.sbuf("c_" + name, shp, F32)
    src = C.cd[name]
    idx = tuple(slice(None) for _ in shp)
    fw.dma(fw.sp, lambda h: h.dma_start(out=b[idx], in_=src[idx]), reads=[src], writes=[b])
    return b


def load_colvec(fw, C, vec_ap, n, name, srcbuf):
    rows = fw.sbuf(name + "_r", [n, 128], F32)
    fw.dma(fw.sp, lambda h: h.dma_start(out=rows[:, :], in_=vec_ap.rearrange("(j p) -> j p", p=128)), reads=[srcbuf], writes=[rows])
    ps = C.ps[0]
    fw.op(fw.pe, lambda e: e.transpose(out=ps[:, 0:n], in_=rows[0:n, :], identity=C.ident[0:n, 0:n]), reads=[rows, C.ident], writes=[ps])
    col = fw.sbuf(name, [128, n], F32)
    fw.op(fw.dve, lambda e: e.tensor_copy(out=col[:, :], in_=ps[:, 0:n]), reads=[ps], writes=[col])
    return col


def emit_rope(fw, x1, x2, cs, sn, tmp, shape, rbufs, xbuf, tmpbuf):
    nd = len(shape)
    def bc(a):
        v = a
        for _ in range(nd - 2):
            v = v.unsqueeze(1)
        return v.to_broadcast(list(shape))
    t1, t2, t3, t4 = tmp
    fw.op(fw.dve, lambda e: e.tensor_tensor(out=t1, in0=x1, in1=bc(cs), op=ALU.mult), reads=[xbuf] + rbufs, writes=[tmpbuf[0]])
    fw.op(fw.dve, lambda e: e.tensor_tensor(out=t2, in0=x2, in1=bc(sn), op=ALU.mult), reads=[xbuf] + rbufs, writes=[tmpbuf[1]])
    fw.op(fw.dve, lambda e: e.tensor_tensor(out=t3, in0=x2, in1=bc(cs), op=ALU.mult), reads=[xbuf] + rbufs, writes=[tmpbuf[2]])
    fw.op(fw.dve, lambda e: e.tensor_tensor(out=t4, in0=x1, in1=bc(sn), op=ALU.mult), reads=[xbuf] + rbufs, writes=[tmpbuf[3]])
    fw.op(fw.dve, lambda e: e.tensor_tensor(out=x1, in0=t1, in1=t2, op=ALU.subtract), reads=[tmpbuf[0], tmpbuf[1], tmpbuf[2], tmpbuf[3]], writes=[xbuf])
    fw.op(fw.dve, lambda e: e.tensor_tensor(out=x2, in0=t3, in1=t4, op=ALU.add), reads=[tmpbuf[2], tmpbuf[3]], writes=[xbuf])


def phase_kv_prompt(fw, C, l):
    P = 128
    C.maskpad = load_const(fw, C, "maskpad")
    wcol = fw.sbuf("wcol", [128, 4], F32)
    for wi, wsrc in enumerate([C.cmp_wk, C.cmp_wv]):
        for g in range(2):
            for blk in range(4):
                fw.dma(fw.sp, lambda h: h.dma_start(out=wcol[blk * 32:(blk + 1) * 32, wi * 2 + g:wi * 2 + g + 1],
                                                    in_=wsrc[l, g, :].rearrange("(j o) -> j o", o=1)), reads=[wsrc], writes=[wcol], join=True)
    Wck = fw.sbuf("Wck", [128, 2, 4], BF16)
    WcvPad = fw.sbuf("WcvPad", [128, 2, NT * 64], BF16)
    for g in range(2):
        fw.op(fw.dve, lambda e: e.tensor_scalar(out=Wck[:, g, :], in0=C.mask4[:, :], scalar1=wcol[:, g:g + 1], scalar2=None, op0=ALU.mult),
              reads=[C.mask4, wcol], writes=[Wck], join=True)
        fw.op(fw.dve, lambda e: e.tensor_scalar(out=WcvPad[:, g, :], in0=C.maskpad[:, :, :].rearrange("p t n -> p (t n)"), scalar1=wcol[:, 2 + g:3 + g], scalar2=None, op0=ALU.mult),
              reads=[C.maskpad, wcol], writes=[WcvPad], join=True)
    vacc = fw.sbuf("vacc", [64, 256], F32)
    kvin = [fw.sbuf(f"kvin{i}", [128, 1536], F32) for i in range(2)]
    kvb = [fw.sbuf(f"kvb{i}", [128, 1536], BF16) for i in range(2)]
    rtab = [fw.sbuf(f"rtab{i}", [128, 4, 64], F32) for i in range(2)]
    rtmp = [fw.sbuf(f"rtmp{i}", [128, 384], F32) for i in range(4)]
    for t in range(NT):
        kv = kvin[t % 2]
        kb = kvb[t % 2]
        rt = rtab[t % 2]
        fw.dma(fw.sp, lambda h: h.dma_start(out=kv[:, :], in_=C.Htok[t * 128:(t + 1) * 128, 1536:3072]), reads=[C.Htok], writes=[kv])
        fw.dma(fw.sp, lambda h: h.dma_start(out=rt[:, :, :], in_=C.cd["rope_p"][t * 128:(t + 1) * 128, :, :]), reads=[C.cd["rope_p"]], writes=[rt])
        kview = kv[:, :].rearrange("p (a v g h d) -> p a v g h d", a=3, v=2, g=2, h=2, d=64)
        x1 = kview[:, :, 0, :, 0, :]
        x2 = kview[:, :, 0, :, 1, :]
        tv = [r[:, :].rearrange("p (a g d) -> p a g d", a=3, g=2) for r in rtmp]
        emit_rope(fw, x1, x2, rt[:, 0, :], rt[:, 1, :], tv, [128, 3, 2, 64], [rt], kv, rtmp)
        for a in range(6):
            fw.dma(fw.sp, lambda h: h.dma_start(out=C.out_kv[l, a, t * 128:(t + 1) * 128, :], in_=kv[:, a * 256:(a + 1) * 256]),
                   reads=[kv], writes=[C.out_kv], join=True)
        fw.op(fw.act, lambda e: e.copy(out=kb[:, :], in_=kv[:, :]), reads=[kv], writes=[kb])
        psb = C.psb[t % 2]
        for i, (a, g) in enumerate([(1, 0), (1, 1), (2, 0), (2, 1)]):
            c0 = a * 512 + g * 128
            fw.op(fw.pe, lambda e: e.transpose(out=psb[:, i * 128:(i + 1) * 128], in_=kb[:, c0:c0 + 128], identity=C.identb[:, :]),
                  reads=[kb, C.identb], writes=[psb], join=(i > 0))
        fw.op(fw.dve, lambda e: e.tensor_copy(out=C.kT[:, :, t * 128:(t + 1) * 128], in_=psb[:, 0:512].rearrange("p (i q) -> p i q", i=4)),
              reads=[psb], writes=[C.kT], join=True)
        ps = C.ps[t % 4]
        for g in range(2):
            fw.op(fw.pe, lambda e: e.matmul(ps[:, g * 4:(g + 1) * 4], lhsT=kb[:, g * 128:(g + 1) * 128], rhs=Wck[:, g, :], start=True, stop=True),
                  reads=[kb, Wck], writes=[ps], join=(g > 0))
        for g in range(2):
            fw.op(fw.pe, lambda e: e.matmul(ps[0:64, 128 + g * 128:256 + g * 128], lhsT=WcvPad[:, g, t * 64:(t + 1) * 64], rhs=kb[:, 256 + g * 128:384 + g * 128], start=True, stop=True),
                  reads=[kb, WcvPad], writes=[ps], join=True)
        fw.op(fw.dve, lambda e: e.tensor_copy(out=C.kcmpT[:, :, t * 4:(t + 1) * 4], in_=ps[:, 0:8].rearrange("p (g n) -> p g n", g=2)),
              reads=[ps], writes=[C.kcmpT], join=True)
        if t == 0:
            fw.op(fw.dve, lambda e: e.tensor_copy(out=vacc[:, :], in_=ps[0:64, 128:384]), reads=[ps], writes=[vacc])
        else:
            fw.op(fw.dve, lambda e: e.tensor_tensor(out=vacc[:, :], in0=vacc[:, :], in1=ps[0:64, 128:384], op=ALU.add), reads=[ps, vacc], writes=[vacc])
        vsrc = kb[:, 512:1536].rearrange("p (a v c) -> p a v c", a=2, v=2)[:, :, 1, :]
        fw.op(fw.pool, lambda e: e.tensor_copy(out=C.V[:, t, :, :], in_=vsrc), reads=[kb], writes=[C.V], join=True)
    fw.op(fw.dve, lambda e: e.tensor_copy(out=C.vcmp[:, :], in_=vacc[:, :]), reads=[vacc], writes=[C.vcmp])


def emit_ln_partition(fw, C, y, blk_cols, g_ap, b_ap, gb_bufs, out_ap_fn, tmpA, tmpB, func=None):
    ncols = blk_cols
    for c0 in range(0, ncols, 512):
        cw = min(512, ncols - c0)
        ps1 = C.ps[0]
        ps2 = C.ps[1]
        ysl = y[:, c0:c0 + cw]
        fw.op(fw.pe, lambda e: e.matmul(ps1[:, 0:cw], lhsT=C.onesdiv[:, :], rhs=ysl, start=True, stop=True), reads=[C.onesdiv, y], writes=[ps1])
        fw.op(fw.dve, lambda e: e.tensor_tensor(out=ysl, in0=ysl, in1=ps1[:, 0:cw], op=ALU.subtract), reads=[ps1, y], writes=[y])
        fw.op(fw.act, lambda e: e.activation(out=tmpA[:, 0:cw], in_=ysl, func=AF.Square), reads=[y], writes=[tmpA])
        fw.op(fw.pe, lambda e: e.matmul(ps2[:, 0:cw], lhsT=C.onesdiv[:, :], rhs=tmpA[:, 0:cw], start=True, stop=True), reads=[C.onesdiv, tmpA], writes=[ps2])
        fw.op(fw.act, lambda e: e.activation(out=tmpB[:, 0:cw], in_=ps2[:, 0:cw], func=AF.Sqrt, bias=C.epsc[:, 0:1], scale=1.0), reads=[ps2, C.epsc], writes=[tmpB])
        fw.op(fw.dve, lambda e: e.reciprocal(out=tmpB[:, 0:cw], in_=tmpB[:, 0:cw]), reads=[tmpB], writes=[tmpB])
        fw.op(fw.dve, lambda e: e.tensor_tensor(out=tmpA[:, 0:cw], in0=ysl, in1=tmpB[:, 0:cw], op=ALU.mult), reads=[y, tmpB], writes=[tmpA])
        oap, obuf = out_ap_fn(c0, cw)
        if func is None:
            fw.op(fw.dve, lambda e: e.tensor_scalar(out=oap, in0=tmpA[:, 0:cw], scalar1=g_ap, scalar2=b_ap, op0=ALU.mult, op1=ALU.add),
                  reads=[tmpA] + gb_bufs, writes=[obuf], join=True)
        else:
            fw.op(fw.dve, lambda e: e.tensor_scalar(out=tmpA[:, 0:cw], in0=tmpA[:, 0:cw], scalar1=g_ap, scalar2=b_ap, op0=ALU.mult, op1=ALU.add),
                  reads=[tmpA] + gb_bufs, writes=[tmpA])
            fw.op(fw.act, lambda e: e.activation(out=oap, in_=tmpA[:, 0:cw], func=func), reads=[tmpA], writes=[obuf], join=True)


def phase_conv(fw, C, l, HfT, T, hist_src, out_conv, convT):
    fw.push()
    dwr = fw.sbuf("dwr", [31, 512], F32)
    fw.dma(fw.sp, lambda h: h.dma_start(out=dwr[:, :], in_=C.conv_dw[l, :, :]), reads=[C.conv_dw], writes=[dwr])
    dwT = fw.sbuf("dwT", [128, 4, 31], F32)
    ps = C.ps[2]
    for cc in range(4):
        fw.op(fw.pe, lambda e: e.transpose(out=ps[:, cc * 32:cc * 32 + 31], in_=dwr[0:31, cc * 128:(cc + 1) * 128], identity=C.ident[0:31, 0:31]),
              reads=[dwr, C.ident], writes=[ps], join=(cc > 0))
    fw.op(fw.dve, lambda e: e.tensor_copy(out=dwT[:, :, :], in_=ps[:, 0:128].rearrange("p (c k) -> p c k", c=4)[:, :, 0:31]), reads=[ps], writes=[dwT])
    dbc = load_colvec(fw, C, C.conv_db[l, :], 4, "dbc", C.conv_db)
    lng = load_colvec(fw, C, C.conv_ln_g[l, :], 4, "clng", C.conv_ln_g)
    lnb = load_colvec(fw, C, C.conv_ln_b[l, :], 4, "clnb", C.conv_ln_b)
    pw = fw.sbuf("pw", [128, 4, 512], BF16)
    fw.dma(fw.pool, lambda h: h.dma_start(out=pw[:, :, :], in_=C.conv_pw[l, :, :].rearrange("(c p) n -> p c n", p=128)), reads=[C.conv_pw], writes=[pw])
    zT = fw.sbuf("zT", [128, 4, T], BF16)
    pcs = fw.sbuf("pcs", [30, 512], F32)
    psc = C.ps[3]
    sets = []
    for i in range(2):
        sets.append(dict(ca=fw.sbuf(f"cca{i}", [128, T], F32), cb=fw.sbuf(f"ccb{i}", [128, T], F32),
                         cseq=fw.sbuf(f"cseq{i}", [128, 30 + T], F32), y=fw.sbuf(f"cy{i}", [128, T], F32)))
    tmpA = fw.sbuf("ctmpA", [128, 512], F32)
    tmpB = fw.sbuf("ctmpB", [128, 512], F32)
    hs = None
    if hist_src is not None:
        hs = fw.sbuf("hist_r", [30, 512], F32)
        fw.dma(fw.sp, lambda h: h.dma_start(out=hs[:, :], in_=hist_src), reads=[C.state_conv], writes=[hs])
    for cc in range(4):
        s_ = sets[cc % 2]
        ca, cb, cseq, y = s_["ca"], s_["cb"], s_["cseq"], s_["y"]
        fw.dma(fw.sp, lambda h: h.dma_start(out=ca[:, :], in_=HfT[512 + cc * 128:640 + cc * 128, 0:T]), reads=[HfT], writes=[ca])
        fw.dma(fw.sp, lambda h: h.dma_start(out=cb[:, :], in_=HfT[1024 + cc * 128:1152 + cc * 128, 0:T]), reads=[HfT], writes=[cb])
        fw.op(fw.act, lambda e: e.activation(out=cb[:, :], in_=cb[:, :], func=AF.Sigmoid), reads=[cb], writes=[cb])
        if hs is None:
            fw.op(fw.pool, lambda e: e.memset(cseq[:, 0:30], 0.0), writes=[cseq])
        else:
            pst = C.ps[0]
            fw.op(fw.pe, lambda e: e.transpose(out=pst[:, 0:30], in_=hs[0:30, cc * 128:(cc + 1) * 128], identity=C.ident[0:30, 0:30]), reads=[hs, C.ident], writes=[pst])
            fw.op(fw.dve, lambda e: e.tensor_copy(out=cseq[:, 0:30], in_=pst[:, 0:30]), reads=[pst], writes=[cseq])
        fw.op(fw.dve, lambda e: e.tensor_tensor(out=cseq[:, 30:30 + T], in0=ca[:, :], in1=cb[:, :], op=ALU.mult), reads=[ca, cb], writes=[cseq], join=True)
        fw.op(fw.pe, lambda e: e.transpose(out=psc[0:30, cc * 128:(cc + 1) * 128], in_=cseq[:, T:T + 30], identity=C.ident[:, :]),
              reads=[cseq, C.ident], writes=[psc], join=(cc > 0))
        eng = fw.dve
        fw.op(eng, lambda e: e.tensor_scalar(out=y[:, :], in0=cseq[:, 0:T], scalar1=dwT[:, cc, 0:1], scalar2=dbc[:, cc:cc + 1], op0=ALU.mult, op1=ALU.add),
              reads=[cseq, dwT, dbc], writes=[y])
        for k in range(1, 31):
            fw.op(eng, lambda e: e.scalar_tensor_tensor(out=y[:, :], in0=cseq[:, k:k + T], scalar=dwT[:, cc, k:k + 1], in1=y[:, :], op0=ALU.mult, op1=ALU.add),
                  reads=[cseq, dwT, y], writes=[y])
        emit_ln_partition(fw, C, y, T, lng[:, cc:cc + 1], lnb[:, cc:cc + 1], [lng, lnb],
                          lambda c0, cw: (zT[:, cc, c0:c0 + cw], zT), tmpA, tmpB, func=AF.Silu)
    fw.op(fw.dve, lambda e: e.tensor_copy(out=pcs[:, :], in_=psc[0:30, 0:512]), reads=[psc], writes=[pcs])
    fw.dma(fw.sp, lambda h: h.dma_start(out=out_conv, in_=pcs[:, :]), reads=[pcs], writes=[C.out_conv_buf])
    it = 0
    for co in range(4):
        for c0 in range(0, T, 512):
            cw = min(512, T - c0)
            ps = C.ps[it % 4]
            for cc in range(4):
                fw.op(fw.pe, lambda e: e.matmul(ps[:, 0:cw], lhsT=pw[:, cc, co * 128:(co + 1) * 128], rhs=zT[:, cc, c0:c0 + cw], start=(cc == 0), stop=(cc == 3)),
                      reads=[pw, zT], writes=[ps], join=(cc > 0))
            if it % 2 == 0:
                fw.op(fw.dve, lambda e: e.tensor_copy(out=convT[:, co, c0:c0 + cw], in_=ps[:, 0:cw]), reads=[ps], writes=[convT], join=True)
            else:
                fw.op(fw.act, lambda e: e.copy(out=convT[:, co, c0:c0 + cw], in_=ps[:, 0:cw]), reads=[ps], writes=[convT], join=True)
            it += 1
    fw.pop()


def emit_ln_free(fw, C, x, P, g_bc, b_bc, junk, stat):
    xs = x[0:P, :]
    fw.op(fw.dve, lambda e: e.tensor_reduce(out=stat[0:P, 0:1], in_=xs, axis=AX.X, op=ALU.add), reads=[x], writes=[stat])
    fw.op(fw.dve, lambda e: e.tensor_scalar(out=stat[0:P, 1:2], in0=stat[0:P, 0:1], scalar1=1.0 / D, scalar2=None, op0=ALU.mult), reads=[stat], writes=[stat])
    fw.op(fw.dve, lambda e: e.tensor_scalar(out=xs, in0=xs, scalar1=stat[0:P, 1:2], scalar2=None, op0=ALU.subtract), reads=[x, stat], writes=[x])
    fw.op(fw.act, lambda e: e.activation(out=junk[0:P, :], in_=xs, func=AF.Square, accum_out=stat[0:P, 2:3]), reads=[x], writes=[junk, stat])
    fw.op(fw.act, lambda e: e.activation(out=stat[0:P, 3:4], in_=stat[0:P, 2:3], func=AF.Sqrt, bias=C.epsc[0:P, 0:1], scale=1.0 / D), reads=[stat, C.epsc], writes=[stat])
    fw.op(fw.dve, lambda e: e.reciprocal(out=stat[0:P, 4:5], in_=stat[0:P, 3:4]), reads=[stat], writes=[stat])
    fw.op(fw.dve, lambda e: e.scalar_tensor_tensor(out=xs, in0=xs, scalar=stat[0:P, 4:5], in1=g_bc[0:P, :], op0=ALU.mult, op1=ALU.mult), reads=[x, stat, g_bc], writes=[x])
    fw.op(fw.dve, lambda e: e.tensor_tensor(out=xs, in0=xs, in1=b_bc[0:P, :], op=ALU.add), reads=[x, b_bc], writes=[x])


def load_bcast(fw, name, vec_ap, n, srcbuf, P=128):
    b = fw.sbuf(name, [128, n], F32)
    fw.dma(fw.sp, lambda h: h.dma_start(out=b[0:P, :], in_=vec_ap.partition_broadcast(P)), reads=[srcbuf], writes=[b])
    return b


def emit_gmlp(fw, C, G, l, t, P, Htok, HfT, mixg, out_gm_ap):
    gv = G["gvin"][t % 2]
    gt = G["gtmp"]
    st = G["gstat"]
    fw.dma(fw.sp, lambda h: h.dma_start(out=gv[0:P, :], in_=Htok[t * 128:t * 128 + P, 0:512]), reads=[Htok], writes=[gv])
    emit_gelu(fw, fw.dve, fw.act, gv[0:P, :], gv[0:P, :], gt[0:P, :], [gv], [gv], gt)
    g3 = gv[0:P, :].rearrange("p (h c) -> p h c", h=4)
    t3 = gt[0:P, :].rearrange("p (h c) -> p h c", h=4)
    fw.op(fw.dve, lambda e: e.tensor_reduce(out=st[0:P, 0:4], in_=g3, axis=AX.X, op=ALU.add), reads=[gv], writes=[st])
    fw.op(fw.dve, lambda e: e.tensor_scalar(out=st[0:P, 4:8], in0=st[0:P, 0:4], scalar1=1.0 / 128, scalar2=None, op0=ALU.mult), reads=[st], writes=[st])
    fw.op(fw.dve, lambda e: e.tensor_tensor(out=g3, in0=g3, in1=st[0:P, 4:8].unsqueeze(2).to_broadcast([P, 4, 128]), op=ALU.subtract), reads=[gv, st], writes=[gv])
    fw.op(fw.dve, lambda e: e.tensor_tensor(out=t3, in0=g3, in1=g3, op=ALU.mult), reads=[gv], writes=[gt])
    fw.op(fw.dve, lambda e: e.tensor_reduce(out=st[0:P, 8:12], in_=t3, axis=AX.X, op=ALU.add), reads=[gt], writes=[st])
    fw.op(fw.act, lambda e: e.activation(out=st[0:P, 12:16], in_=st[0:P, 8:12], func=AF.Sqrt, bias=C.epsc[0:P, 0:1], scale=1.0 / 128), reads=[st, C.epsc], writes=[st])
    fw.op(fw.dve, lambda e: e.reciprocal(out=st[0:P, 16:20], in_=st[0:P, 12:16]), reads=[st], writes=[st])
    fw.op(fw.dve, lambda e: e.tensor_tensor(out=g3, in0=g3, in1=st[0:P, 16:20].unsqueeze(2).to_broadcast([P, 4, 128]), op=ALU.mult), reads=[gv, st], writes=[gv])
    fw.op(fw.dve, lambda e: e.tensor_tensor(out=gv[0:P, :], in0=gv[0:P, :], in1=G["gmg"][0:P, :], op=ALU.mult), reads=[gv, G["gmg"]], writes=[gv])
    fw.op(fw.dve, lambda e: e.tensor_tensor(out=gv[0:P, :], in0=gv[0:P, :], in1=G["gmb"][0:P, :], op=ALU.add), reads=[gv, G["gmb"]], writes=[gv])
    if out_gm_ap is not None:
        fw.dma(fw.sp, lambda h: h.dma_start(out=out_gm_ap, in_=gv[0:P, :]), reads=[gv], writes=[C.out_gm_buf], join=True)
    vnb = G["vnb"]
    fw.op(fw.act, lambda e: e.copy(out=vnb[0:P, :], in_=gv[0:P, :]), reads=[gv], writes=[vnb])
    ps = C.ps[0]
    for h in range(4):
        fw.op(fw.pe, lambda e: e.matmul(ps[:, h * 128:h * 128 + P], lhsT=vnb[0:P, h * 128:(h + 1) * 128], rhs=G["trilWT"][0:P, h, 0:P], start=True, stop=True),
              reads=[vnb, G["trilWT"]], writes=[ps], join=(h > 0))
    ut = G["ut"][t % 2]
    fw.dma(fw.sp, lambda h_: h_.dma_start(out=ut[:, :, 0:P], in_=HfT[0:512, t * 128:t * 128 + P].rearrange("(h c) q -> c h q", h=4)), reads=[HfT], writes=[ut])
    sv = gt[:, :].rearrange("p (h c) -> p h c", h=4)[:, :, 0:P]
    fw.op(fw.dve, lambda e: e.tensor_tensor(out=sv, in0=ps[:, 0:512].rearrange("p (h c) -> p h c", h=4)[:, :, 0:P], in1=G["bsb"][:, :, 0:P], op=ALU.add),
          reads=[ps, G["bsb"]], writes=[gt])
    fw.op(fw.dve, lambda e: e.tensor_tensor(out=mixg[:, :, 0:P], in0=sv, in1=ut[:, :, 0:P], op=ALU.mult), reads=[gt, ut], writes=[mixg])


def gmlp_consts(fw, C, l, P):
    G = {}
    G["gvin"] = [fw.sbuf("gvin0", [128, 512], F32)] * 2
    G["gtmp"] = fw.sbuf("gtmp", [128, 512], F32)
    G["gstat"] = fw.sbuf("gstat", [128, 20], F32)
    G["vnb"] = fw.sbuf("vnb", [128, 512], BF16)
    G["ut"] = [fw.sbuf("ut0", [128, 4, 128], F32)] * 2
    G["gmg"] = load_bcast(fw, "gmg", C.gm_ln_g[l, :], 512, C.gm_ln_g, P)
    G["gmb"] = load_bcast(fw, "gmb", C.gm_ln_b[l, :], 512, C.gm_ln_b, P)
    bsb = fw.sbuf("bsb", [128, 4, 128], F32)
    fw.dma(fw.sp, lambda h: h.dma_start(out=bsb[:, :, :].rearrange("p h i -> p (h i)"), in_=C.gm_bs[l, :, :].rearrange("h i -> (h i)").partition_broadcast(128)),
           reads=[C.gm_bs], writes=[bsb])
    G["bsb"] = bsb
    wsr = fw.sbuf("wsr", [128, 4, 128], F32)
    fw.dma(fw.sp, lambda h: h.dma_start(out=wsr[:, :, :], in_=C.gm_ws[l, :, :, :].rearrange("h i j -> i h j")), reads=[C.gm_ws], writes=[wsr])
    ps = C.ps[1]
    for h in range(4):
        fw.op(fw.pe, lambda e: e.transpose(out=ps[:, h * 128:(h + 1) * 128], in_=wsr[:, h, :], identity=C.ident[:, :]), reads=[wsr, C.ident], writes=[ps], join=(h > 0))
    tw = fw.sbuf("trilWT", [128, 4, 128], BF16)
    fw.op(fw.dve, lambda e: e.tensor_tensor(out=tw[:, :, :], in0=ps[:, 0:512].rearrange("p (h i) -> p h i", h=4), in1=C.triu[:, :].unsqueeze(1).to_broadcast([128, 4, 128]), op=ALU.mult),
          reads=[ps, C.triu], writes=[tw])
    G["trilWT"] = tw
    return G


def emit_softmax_pv(fw, C, A, sbuf_s, nk, gate_ap, gate_buf, acc, v_fn, first, last, tagi, part="all"):
    st = A["st"][tagi]
    eb = A["eb"][tagi]
    pT = A["pT"][tagi]
    if part in ("all", "sm"):
      fw.op(fw.dve, lambda e: e.tensor_reduce(out=st[:, 0:1], in_=sbuf_s[:, 0:nk], axis=AX.X, op=ALU.max, negate=True), reads=[sbuf_s], writes=[st])
      fw.op(fw.act, lambda e: e.activation(out=eb[:, 0:nk], in_=sbuf_s[:, 0:nk], func=AF.Exp, bias=st[:, 0:1], scale=1.0, accum_out=st[:, 1:2]),
          reads=[sbuf_s, st], writes=[eb, st])
      fw.op(fw.dve, lambda e: e.reciprocal(out=st[:, 2:3], in_=st[:, 1:2]), reads=[st], writes=[st])
      fw.op(fw.dve, lambda e: e.tensor_tensor(out=st[:, 3:4], in0=st[:, 2:3], in1=gate_ap, op=ALU.mult), reads=[st, gate_buf], writes=[st])
      fw.op(fw.act, lambda e: e.mul(out=eb[:, 0:nk], in_=eb[:, 0:nk], mul=st[:, 3:4]), reads=[eb, st], writes=[eb])
    if part == "sm":
        return
    nkt = nk // 128
    for k0 in range(0, nkt, 8):
        kn = min(8, nkt - k0)
        psb = C.psb[A["psbi"] % 2]
        A["psbi"] += 1
        for j in range(kn):
            kt = k0 + j
            fw.op(fw.pe, lambda e: e.transpose(out=psb[:, j * 128:(j + 1) * 128], in_=eb[:, kt * 128:(kt + 1) * 128], identity=C.identb[:, :]),
                  reads=[eb, C.identb], writes=[psb], join=(j > 0))
        if (k0 // 8) % 2 == 0:
            fw.op(fw.dve, lambda e: e.tensor_copy(out=pT[:, k0 * 128:(k0 + kn) * 128], in_=psb[:, 0:kn * 128]), reads=[psb], writes=[pT], join=True)
        else:
            fw.op(fw.act, lambda e: e.copy(out=pT[:, k0 * 128:(k0 + kn) * 128], in_=psb[:, 0:kn * 128]), reads=[psb], writes=[pT], join=True)
    for kt in range(nkt):
        fw.op(fw.pe, lambda e: e.matmul(acc[:, 0:128], lhsT=v_fn(kt), rhs=pT[:, kt * 128:(kt + 1) * 128], start=(first and kt == 0), stop=(last and kt == nkt - 1)),
              reads=[C.V, pT], writes=[acc], join=not (first and kt == 0))


def emit_attn_prompt(fw, C, A, l, t, mixn):
    qt = A["qt"][t % 2]
    gt = A["gate"][t % 2]
    rt = A["rt"][t % 2]
    fw.dma(fw.sp, lambda h: h.dma_start(out=qt[:, :], in_=C.Htok[t * 128:(t + 1) * 128, 512:1536]), reads=[C.Htok], writes=[qt])
    fw.dma(fw.sp, lambda h: h.dma_start(out=gt[:, :], in_=C.Htok[t * 128:(t + 1) * 128, 3072:3096]), reads=[C.Htok], writes=[gt])
    fw.dma(fw.sp, lambda h: h.dma_start(out=rt[:, :, :], in_=C.cd["rope_p"][t * 128:(t + 1) * 128, :, :]), reads=[C.cd["rope_p"]], writes=[rt])
    qv = qt[:, :].rearrange("p (h x d) -> p h x d", h=8, x=2)
    tv = [A["ssb"][:, i * 512:(i + 1) * 512].rearrange("p (h d) -> p h d", h=8) for i in range(4)]
    emit_rope(fw, qv[:, :, 0, :], qv[:, :, 1, :], rt[:, 2, :], rt[:, 3, :], tv, [128, 8, 64], [rt], qt, A["rtmp"])
    qb = A["qb"]
    fw.op(fw.act, lambda e: e.copy(out=qb[:, :], in_=qt[:, :]), reads=[qt], writes=[qb])
    psb = C.psb[A["psbi"] % 2]
    A["psbi"] += 1
    for h in range(8):
        fw.op(fw.pe, lambda e: e.transpose(out=psb[:, h * 128:(h + 1) * 128], in_=qb[:, h * 128:(h + 1) * 128], identity=C.identb[:, :]),
              reads=[qb, C.identb], writes=[psb], join=(h > 0))
    qT = A["qT"]
    fw.op(fw.dve, lambda e: e.tensor_copy(out=qT[:, :], in_=psb[:, :]), reads=[psb], writes=[qT])
    sig = A["sig"]
    fw.op(fw.act, lambda e: e.activation(out=sig[:, :], in_=gt[:, :], func=AF.Sigmoid), reads=[gt], writes=[sig])
    sig3 = sig[:, :].rearrange("p (h b) -> p h b", b=3)
    sc, ec, st4 = A["sc"], A["ec"], A["st4"]
    for g in range(2):
        psc = C.ps[0]
        for r in range(4):
            fw.op(fw.pe, lambda e: e.matmul(psc[:, r * 64:(r + 1) * 64], lhsT=qT[:, (g * 4 + r) * 128:(g * 4 + r + 1) * 128], rhs=C.kcmpT[:, g, :], start=True, stop=True),
                  reads=[qT, C.kcmpT], writes=[psc], join=(r > 0))
        sc3 = sc[:, :].rearrange("p (r n) -> p r n", r=4)
        ec3 = ec[:, :].rearrange("p (r n) -> p r n", r=4)
        fw.op(fw.dve, lambda e: e.tensor_tensor(out=sc3, in0=psc[:, 0:256].rearrange("p (r n) -> p r n", r=4),
                                                in1=C.cmpbias[:, t, :].unsqueeze(1).to_broadcast([128, 4, 64]), op=ALU.add), reads=[psc, C.cmpbias], writes=[sc])
        fw.op(fw.dve, lambda e: e.tensor_reduce(out=st4[:, 0:4], in_=sc3, axis=AX.X, op=ALU.max), reads=[sc], writes=[st4])
        fw.op(fw.dve, lambda e: e.tensor_tensor(out=sc3, in0=sc3, in1=st4[:, 0:4].unsqueeze(2).to_broadcast([128, 4, 64]), op=ALU.subtract), reads=[sc, st4], writes=[sc])
        fw.op(fw.act, lambda e: e.activation(out=ec[:, :], in_=sc[:, :], func=AF.Exp), reads=[sc], writes=[ec])
        fw.op(fw.dve, lambda e: e.tensor_reduce(out=st4[:, 4:8], in_=ec3, axis=AX.X, op=ALU.add), reads=[ec], writes=[st4])
        fw.op(fw.dve, lambda e: e.tensor_scalar(out=st4[:, 4:8], in0=st4[:, 4:8], scalar1=1e-30, scalar2=None, op0=ALU.max), reads=[st4], writes=[st4])
        fw.op(fw.dve, lambda e: e.reciprocal(out=st4[:, 8:12], in_=st4[:, 4:8]), reads=[st4], writes=[st4])
        fw.op(fw.dve, lambda e: e.tensor_scalar(out=st4[:, 8:12], in0=st4[:, 8:12], scalar1=C.cmpvalid[:, t:t + 1], scalar2=None, op0=ALU.mult), reads=[st4, C.cmpvalid], writes=[st4])
        fw.op(fw.dve, lambda e: e.tensor_tensor(out=ec3, in0=ec3, in1=st4[:, 8:12].unsqueeze(2).to_broadcast([128, 4, 64]), op=ALU.mult), reads=[ec, st4], writes=[ec])
        imp = A["imp"]
        fw.op(fw.dve, lambda e: e.tensor_reduce(out=imp[:, 0:64], in_=ec[:, :].rearrange("p (r n) -> p n r", r=4), axis=AX.X, op=ALU.add), reads=[ec], writes=[imp])
        fw.op(fw.dve, lambda e: e.tensor_reduce(out=imp[:, 64:96], in_=imp[:, 0:64].rearrange("p (m two) -> p m two", two=2), axis=AX.X, op=ALU.add), reads=[imp], writes=[imp])
        fw.op(fw.dve, lambda e: e.tensor_tensor(out=imp[:, 64:96], in0=imp[:, 64:96], in1=C.selA[:, t, :], op=ALU.mult), reads=[imp, C.selA], writes=[imp])
        fw.op(fw.dve, lambda e: e.tensor_tensor(out=imp[:, 64:96], in0=imp[:, 64:96], in1=C.selB[:, t, :], op=ALU.add), reads=[imp, C.selB], writes=[imp])
        m8 = A["m8"]
        fw.op(fw.dve, lambda e: e.max(out=m8[:, 0:8], in_=imp[:, 64:96]), reads=[imp], writes=[m8])
        fw.op(fw.dve, lambda e: e.match_replace(out=imp[:, 96:128], in_to_replace=m8[:, 0:8], in_values=imp[:, 64:96], imm_value=-1e30), reads=[imp, m8], writes=[imp])
        fw.op(fw.dve, lambda e: e.max(out=m8[:, 8:16], in_=imp[:, 96:128]), reads=[imp], writes=[m8])
        bsel = A["bsel"]
        fw.op(fw.dve, lambda e: e.tensor_scalar(out=bsel[:, :], in0=imp[:, 64:96], scalar1=m8[:, 15:16], scalar2=None, op0=ALU.is_ge), reads=[imp, m8], writes=[bsel])
        fw.op(fw.dve, lambda e: e.tensor_scalar(out=bsel[:, :], in0=bsel[:, :], scalar1=-NEG, scalar2=NEG, op0=ALU.mult, op1=ALU.add), reads=[bsel], writes=[bsel])
        pcb = A["pcb"]
        fw.op(fw.dve, lambda e: e.tensor_tensor(out=pcb[:, :].rearrange("p (r n) -> p r n", r=4), in0=ec3,
                                                in1=sig3[:, g * 4:(g + 1) * 4, 0:1].to_broadcast([128, 4, 64]), op=ALU.mult), reads=[ec, sig], writes=[pcb])
        psb = C.psb[A["psbi"] % 2]
        A["psbi"] += 1
        for r in range(4):
            fw.op(fw.pe, lambda e: e.transpose(out=psb[0:64, r * 128:(r + 1) * 128], in_=pcb[:, r * 64:(r + 1) * 64], identity=C.identb[:, :]),
                  reads=[pcb, C.identb], writes=[psb], join=(r > 0))
        pTc = A["pTc"]
        fw.op(fw.act, lambda e: e.copy(out=pTc[0:64, :], in_=psb[0:64, 0:512]), reads=[psb], writes=[pTc])
        for r in range(4):
            h = g * 4 + r
            acc = C.pacc[h % 2]
            fw.op(fw.pe, lambda e: e.matmul(acc[:, 0:128], lhsT=C.vcmp[0:64, g * 128:(g + 1) * 128], rhs=pTc[0:64, r * 128:(r + 1) * 128], start=True, stop=False),
                  reads=[C.vcmp, pTc], writes=[acc])
            nk = (t + 1) * 128
            ssb = A["ssb"]
            for ci, c0 in enumerate(range(0, nk, 512)):
                cw = min(512, nk - c0)
                ps = C.ps[1 + (ci % 3)]
                fw.op(fw.pe, lambda e: e.matmul(ps[:, 0:cw], lhsT=qT[:, h * 128:(h + 1) * 128], rhs=C.kT[:, g, c0:c0 + cw], start=True, stop=True),
                      reads=[qT, C.kT], writes=[ps])
                nb = cw // 64
                fw.op(fw.dve, lambda e: e.tensor_tensor(out=ssb[:, c0:c0 + cw].rearrange("p (m j) -> p m j", j=64), in0=ps[:, 0:cw].rearrange("p (m j) -> p m j", j=64),
                                                        in1=bsel[:, c0 // 64:c0 // 64 + nb].unsqueeze(2).to_broadcast([128, nb, 64]), op=ALU.add),
                      reads=[ps, bsel], writes=[ssb], join=True)
            fw.op(fw.dve, lambda e: e.tensor_tensor(out=ssb[:, t * 128:(t + 1) * 128], in0=ssb[:, t * 128:(t + 1) * 128], in1=C.causal[:, :], op=ALU.add),
                  reads=[ssb, C.causal], writes=[ssb])
            kt0 = max(0, t - 4)
            nkw = (t - kt0 + 1) * 128
            swb = A["swb"]
            for ci, c0 in enumerate(range(0, nkw, 512)):
                cw = min(512, nkw - c0)
                ps = C.ps[1 + (ci % 3)]
                fw.op(fw.pe, lambda e: e.matmul(ps[:, 0:cw], lhsT=qT[:, h * 128:(h + 1) * 128], rhs=C.kT[:, 2 + g, kt0 * 128 + c0:kt0 * 128 + c0 + cw], start=True, stop=True),
                      reads=[qT, C.kT], writes=[ps])
                fw.op(fw.act, lambda e: e.copy(out=swb[:, c0:c0 + cw], in_=ps[:, 0:cw]), reads=[ps], writes=[swb], join=True)
            fw.op(fw.dve, lambda e: e.tensor_tensor(out=swb[:, nkw - 128:nkw], in0=swb[:, nkw - 128:nkw], in1=C.causal[:, :], op=ALU.add), reads=[swb, C.causal], writes=[swb])
            if t >= 4:
                fw.op(fw.dve, lambda e: e.tensor_tensor(out=swb[:, 0:128], in0=swb[:, 0:128], in1=C.acausal[:, :], op=ALU.add), reads=[swb, C.acausal], writes=[swb])
            emit_softmax_pv(fw, C, A, ssb, nk, sig3[:, h, 1:2], sig, acc, None, False, False, 0, part="sm")
            emit_softmax_pv(fw, C, A, swb, nkw, sig3[:, h, 2:3], sig, acc, None, False, True, 1, part="sm")
            emit_softmax_pv(fw, C, A, ssb, nk, sig3[:, h, 1:2], sig, acc, lambda kt: C.V[:, kt, 0, g * 128:(g + 1) * 128], False, False, 0, part="pv")
            emit_softmax_pv(fw, C, A, swb, nkw, sig3[:, h, 2:3], sig, acc, lambda kt: C.V[:, kt0 + kt, 1, g * 128:(g + 1) * 128], False, True, 1, part="pv")
            if h % 2 == 0:
                fw.op(fw.dve, lambda e: e.tensor_copy(out=mixn[:, h, :], in_=acc[:, 0:128]), reads=[acc], writes=[mixn], join=True)
            else:
                fw.op(fw.act, lambda e: e.copy(out=mixn[:, h, :], in_=acc[:, 0:128]), reads=[acc], writes=[mixn], join=True)


def attn_bufs(fw):
    A = {"psbi": 0}
    A["qt"] = [fw.sbuf("qt0", [128, 1024], F32)] * 2
    A["gate"] = [fw.sbuf(f"gate{i}", [128, 24], F32) for i in range(2)]
    A["rt"] = [fw.sbuf(f"art{i}", [128, 4, 64], F32) for i in range(2)]

    A["qb"] = fw.sbuf("qb", [128, 1024], BF16)
    A["qT"] = fw.sbuf("qT", [128, 1024], BF16)
    A["sig"] = fw.sbuf("sig", [128, 24], F32)
    A["sc"] = fw.sbuf("sc", [128, 256], F32)
    A["ec"] = fw.sbuf("ec", [128, 256], F32)
    A["st4"] = fw.sbuf("st4", [128, 12], F32)
    A["st"] = [fw.sbuf("ast0", [128, 4], F32), fw.sbuf("ast1", [128, 4], F32)]
    A["imp"] = fw.sbuf("imp", [128, 128], F32)
    A["m8"] = fw.sbuf("m8", [128, 16], F32)
    A["bsel"] = fw.sbuf("bsel", [128, 32], F32)
    A["pcb"] = fw.sbuf("pcb", [128, 256], BF16)
    A["pTc"] = fw.sbuf("pTc", [128, 512], BF16)
    A["ssb"] = fw.sbuf("ssb", [128, S], F32)
    A["rtmp"] = [A["ssb"]] * 4
    A["swb"] = fw.sbuf("swb", [128, 640], F32)
    A["eb"] = [fw.sbuf("eb", [128, S], BF16), fw.sbuf("ebw", [128, 640], BF16)]
    A["pT"] = [fw.sbuf("pT", [128, S], BF16), fw.sbuf("pTw", [128, 640], BF16)]
    return A


def emit_mix_ln(fw, C, M, l, t, P, x_src, mixg, mixn, convT, X1):
    xt = M["xres"][0]
    fw.dma(fw.sp, lambda h: h.dma_start(out=xt[0:P, :], in_=x_src[t * 128:t * 128 + P, :]), reads=[x_src], writes=[xt])
    x1 = xt
    for n4 in range(4):
        ps = C.ps[n4]
        for c in range(16):
            if c < 4:
                lt, lb = mixg[:, c, 0:P], mixg
            elif c < 12:
                lt, lb = mixn[:, c - 4, 0:P], mixn
            else:
                lt, lb = convT[:, c - 12, t * 128:t * 128 + P], convT
            fw.op(fw.pe, lambda e: e.matmul(ps[0:P, :], lhsT=lt, rhs=M["wout"][:, c, n4 * 512:(n4 + 1) * 512], start=(c == 0), stop=(c == 15)),
                  reads=[lb, M["wout"]], writes=[ps], join=(c > 0))
        fw.op(fw.dve, lambda e: e.scalar_tensor_tensor(out=x1[0:P, n4 * 512:(n4 + 1) * 512], in0=xt[0:P, n4 * 512:(n4 + 1) * 512], scalar=ALPHA, in1=ps[0:P, :], op0=ALU.mult, op1=ALU.add),
              reads=[xt, ps], writes=[x1])
    emit_ln_free(fw, C, x1, P, M["ln1g"], M["ln1b"], M["junk"], M["lnst"])
    fw.dma(fw.sp, lambda h: h.dma_start(out=X1[t * 128:t * 128 + P, :], in_=x1[0:P, :]), reads=[x1], writes=[X1], join=True)


def mix_bufs(fw, C, l, P):
    M = {}
    wout = fw.sbuf("wout", [128, 16, D], BF16)
    for c4 in range(4):
        fw.dma(fw.pool, lambda h: h.dma_start(out=wout[:, c4 * 4:(c4 + 1) * 4, :], in_=C.w_out[l, c4 * 512:(c4 + 1) * 512, :].rearrange("(c p) n -> p c n", p=128)),
               reads=[C.w_out], writes=[wout], join=True)
    M["wout"] = wout
    M["xres"] = [fw.sbuf("xres0", [128, D], F32)]
    M["x1"] = M["xres"]
    M["junk"] = None
    M["lnst"] = fw.sbuf("lnst", [128, 8], F32)
    M["ln1g"] = load_bcast(fw, "ln1g", C.ln1_g[l, :], D, C.ln1_g, P)
    M["ln1b"] = load_bcast(fw, "ln1b", C.ln1_b[l, :], D, C.ln1_b, P)
    return M


def phase_mix_prompt(fw, C, l, x_src, X1, samp=None):
    fw.push()
    for nm in ["cmpbias", "cmpvalid", "selA", "selB"]:
        setattr(C, nm, load_const(fw, C, nm))
    M = mix_bufs(fw, C, l, 128)
    G = gmlp_consts(fw, C, l, 128)
    A = attn_bufs(fw)
    M["junk"] = A["eb"][0]
    mixg = [fw.sbuf(f"mixg{i}", [128, 4, 128], BF16) for i in range(2)]
    mixn = [fw.sbuf(f"mixn{i}", [128, 8, 128], BF16) for i in range(2)]
    for t in range(NT):
        for tab, dst in ((C.peer_u, C.UB), (C.peer_v, C.VB)):
            for k in (2 * t, 2 * t + 1):
                fw.dma(fw.pool, lambda h: h.dma_start(out=dst[l * 16384 + k * 512:l * 16384 + (k + 1) * 512, :], in_=tab[l, k * 512:(k + 1) * 512, :]),
                       reads=[tab], writes=[dst], join=True)
        out_gm = C.out_gm[l, :, :] if t == NT - 1 else None
        emit_gmlp(fw, C, G, l, t, 128, C.Htok, C.HfT, mixg[t % 2], out_gm)
        emit_attn_prompt(fw, C, A, l, t, mixn[t % 2])
        emit_mix_ln(fw, C, M, l, t, 128, x_src, mixg[t % 2], mixn[t % 2], C.convT, X1)
    if samp is not None:
        xs_src, XS1, mixn_s, convTs = samp
        emit_gmlp(fw, C, G, l, 0, TS, C.HtokS, C.HfTS, mixg[0], C.out_sgm[l, :, :])
        emit_mix_ln(fw, C, M, l, 0, TS, xs_src, mixg[0], mixn_s, convTs, XS1)
    fw.pop()


INPUT_SPECS = [
    ("w_in", [DEPTH, D, IN_W]), ("gm_ln_g", [DEPTH, 512]), ("gm_ln_b", [DEPTH, 512]), ("gm_ws", [DEPTH, 4, 128, 128]), ("gm_bs", [DEPTH, 4, 128]),
    ("cmp_wk", [DEPTH, 2, 32]), ("cmp_wv", [DEPTH, 2, 32]), ("conv_dw", [DEPTH, 31, 512]), ("conv_db", [DEPTH, 512]),
    ("conv_ln_g", [DEPTH, 512]), ("conv_ln_b", [DEPTH, 512]), ("conv_pw", [DEPTH, 512, 512]), ("w_out", [DEPTH, D, D]),
    ("ln1_g", [DEPTH, D]), ("ln1_b", [DEPTH, D]), ("peer_wq", [DEPTH, D, D]), ("peer_subkeys", [DEPTH, 8, 2, 128, 128]),
    ("ln2_g", [DEPTH, D]), ("ln2_b", [DEPTH, D]), ("peer_u", [DEPTH, 16384, D]), ("peer_v", [DEPTH, 16384, D]),
]


def phase_peer(fw, C, l, jobs, NB=12):
    fw.push()
    wq = fw.sbuf("wq", [128, 16, D], BF16)
    for c4 in range(4):
        fw.dma(fw.pool, lambda h: h.dma_start(out=wq[:, c4 * 4:(c4 + 1) * 4, :], in_=C.peer_wq[l, c4 * 512:(c4 + 1) * 512, :].rearrange("(c p) n -> p c n", p=128)),
               reads=[C.peer_wq], writes=[wq], join=True)
    skr = fw.sbuf("skr", [128, 16, 128], F32)
    fw.dma(fw.sp, lambda h: h.dma_start(out=skr[:, :, :], in_=C.peer_subkeys[l, :, :, :, :].rearrange("h p k d -> k (h p) d")), reads=[C.peer_subkeys], writes=[skr])
    skT = fw.sbuf("skT", [128, 16, 128], BF16)
    for q4 in range(4):
        ps = C.ps[q4]
        for j in range(4):
            fw.op(fw.pe, lambda e: e.transpose(out=ps[:, j * 128:(j + 1) * 128], in_=skr[:, q4 * 4 + j, :], identity=C.ident[:, :]), reads=[skr, C.ident], writes=[ps], join=(j > 0))
        fw.op(fw.dve, lambda e: e.tensor_copy(out=skT[:, q4 * 4:(q4 + 1) * 4, :], in_=ps[:, :].rearrange("p (j k) -> p j k", j=4)), reads=[ps], writes=[skT], join=True)
    ln2g = load_bcast(fw, "ln2g", C.ln2_g[l, :], D, C.ln2_g, 128)
    ln2b = load_bcast(fw, "ln2b", C.ln2_b[l, :], D, C.ln2_b, 128)
    iota = load_const(fw, C, "iota16")
    x1 = fw.sbuf("px1", [128, D], F32)
    x1T = fw.sbuf("px1T", [128, 16, 128], BF16)
    qTb = fw.sbuf("pqT", [128, 16, 128], BF16)
    sb = fw.sbuf("psc", [128, 16, 128], F32)
    m = fw.sbuf("pm", [128, 16, 16], F32)
    ix = fw.sbuf("pix", [128, 16, 16], U32)
    ixf = fw.sbuf("pixf", [128, 16, 16], F32)
    tmp = fw.sbuf("ptmp", [128, 256], F32)
    cand = fw.sbuf("pcand", [128, 8, 256], F32)
    oh = fw.sbuf("poh", [128, 8, 256], F32)
    tm = fw.sbuf("ptm", [128, 8, 16], F32)
    pos = fw.sbuf("ppos", [128, 8, 16], U32)
    pa = fw.sbuf("ppa", [128, 8, 16], U32)
    paf = fw.sbuf("ppaf", [128, 2, 128], F32)
    isel = fw.sbuf("pisel", [128, 2, 128], F32)
    eid = fw.sbuf("peid", [128, 128], I32)
    gate = fw.sbuf("pgate", [128, 8, 16], F32)
    gst = fw.sbuf("pgst", [128, 16], F32)
    actp = fw.sbuf("pactp", [128, 128], F32)
    wgt = fw.sbuf("pwgt", [128, 128], F32)
    gtmp = fw.sbuf("pgtmp", [128, 128], F32)
    y = fw.sbuf("py", [128, D], F32)
    junk = fw.sbuf("pjunk", [128, D], BF16)
    lnst = fw.sbuf("plnst", [128, 8], F32)
    ring = [fw.sbuf(f"pring{i}", [128, D], BF16) for i in range(NB)]
    u_rows = C.UB[:, :]
    v_rows = C.VB[:, :]
    gi = 0
    for (X1, P, X2, t, ridx) in [(j[0], j[2], j[3], t_, j[4] if len(j) > 4 else None) for j in jobs for t_ in range(j[1])]:
        if ridx is None:
            fw.dma(fw.sp, lambda h: h.dma_start(out=x1[0:P, :], in_=X1[t * 128:t * 128 + P, :]), reads=[X1], writes=[x1])
        else:
            fw.dma(fw.pool, lambda h: h.indirect_dma_start(out=x1[0:P, :], out_offset=None, in_=X1[:, :], in_offset=bass.IndirectOffsetOnAxis(ap=ridx[0:P, t:t + 1], axis=0)),
                   reads=[X1, ridx], writes=[x1])
        for cb in range(4):
            ps = C.ps[cb]
            for k in range(4):
                c = cb * 4 + k
                fw.op(fw.pe, lambda e: e.transpose(out=ps[:, k * 128:k * 128 + P], in_=x1[0:P, c * 128:(c + 1) * 128], identity=C.ident[0:P, 0:P]),
                      reads=[x1, C.ident], writes=[ps], join=(k > 0))
            src = ps[:, :].rearrange("p (k q) -> p k q", k=4)[:, :, 0:P]
            fw.op(fw.act if cb % 2 else fw.dve, (lambda e: e.copy(out=x1T[:, cb * 4:(cb + 1) * 4, 0:P], in_=src)) if cb % 2 else (lambda e: e.tensor_copy(out=x1T[:, cb * 4:(cb + 1) * 4, 0:P], in_=src)),
                  reads=[ps], writes=[x1T], join=True)
        for q4 in range(4):
            ps = C.ps[q4]
            for j in range(4):
                hp = q4 * 4 + j
                for c in range(16):
                    fw.op(fw.pe, lambda e: e.matmul(ps[:, j * 128:j * 128 + P], lhsT=wq[:, c, hp * 128:(hp + 1) * 128], rhs=x1T[:, c, 0:P], start=(c == 0), stop=(c == 15)),
                          reads=[wq, x1T], writes=[ps], join=not (j == 0 and c == 0))
            src = ps[:, :].rearrange("p (j q) -> p j q", j=4)[:, :, 0:P]
            fw.op(fw.act if q4 % 2 else fw.dve, (lambda e: e.copy(out=qTb[:, q4 * 4:(q4 + 1) * 4, 0:P], in_=src)) if q4 % 2 else (lambda e: e.tensor_copy(out=qTb[:, q4 * 4:(q4 + 1) * 4, 0:P], in_=src)),
                  reads=[ps], writes=[qTb], join=True)
        for q4 in range(4):
            ps = C.ps[q4]
            for j in range(4):
                hp = q4 * 4 + j
                fw.op(fw.pe, lambda e: e.matmul(ps[0:P, j * 128:(j + 1) * 128], lhsT=qTb[:, hp, 0:P], rhs=skT[:, hp, :], start=True, stop=True),
                      reads=[qTb, skT], writes=[ps], join=(j > 0))
            fw.op(fw.act if q4 % 2 else fw.dve, (lambda e: e.copy(out=sb[0:P, q4 * 4:(q4 + 1) * 4, :], in_=ps[0:P, :].rearrange("p (j k) -> p j k", j=4))) if q4 % 2 else
                  (lambda e: e.tensor_copy(out=sb[0:P, q4 * 4:(q4 + 1) * 4, :], in_=ps[0:P, :].rearrange("p (j k) -> p j k", j=4))), reads=[ps], writes=[sb], join=True)
        for hp in range(16):
            fw.op(fw.dve, lambda e: e.max(out=m[0:P, hp, 0:8], in_=sb[0:P, hp, :]), reads=[sb], writes=[m])
            fw.op(fw.dve, lambda e: e.match_replace(out=tmp[0:P, 0:128], in_to_replace=m[0:P, hp, 0:8], in_values=sb[0:P, hp, :], imm_value=-1e30), reads=[sb, m], writes=[tmp])
            fw.op(fw.dve, lambda e: e.max(out=m[0:P, hp, 8:16], in_=tmp[0:P, 0:128]), reads=[tmp], writes=[m])
            fw.op(fw.dve, lambda e: e.max_index(out=ix[0:P, hp, 0:8], in_max=m[0:P, hp, 0:8], in_values=sb[0:P, hp, :]), reads=[sb, m], writes=[ix])
            fw.op(fw.dve, lambda e: e.max_index(out=ix[0:P, hp, 8:16], in_max=m[0:P, hp, 8:16], in_values=tmp[0:P, 0:128]), reads=[tmp, m], writes=[ix])
        fw.op(fw.dve, lambda e: e.tensor_copy(out=ixf[0:P, :, :], in_=ix[0:P, :, :]), reads=[ix], writes=[ixf])
        m4 = m[0:P, :, :].rearrange("p (h two) k -> p h two k", two=2)
        i4 = ixf[0:P, :, :].rearrange("p (h two) k -> p h two k", two=2)
        c4v = cand[0:P, :, :].rearrange("p h (a b) -> p h a b", a=16)
        fw.op(fw.dve, lambda e: e.tensor_tensor(out=c4v, in0=m4[:, :, 0, :].unsqueeze(3).to_broadcast([P, 8, 16, 16]),
                                                in1=m4[:, :, 1, :].unsqueeze(2).to_broadcast([P, 8, 16, 16]), op=ALU.add), reads=[m], writes=[cand])
        for h in range(8):
            fw.op(fw.dve, lambda e: e.max(out=tm[0:P, h, 0:8], in_=cand[0:P, h, :]), reads=[cand], writes=[tm])
            fw.op(fw.dve, lambda e: e.match_replace(out=tmp[0:P, :], in_to_replace=tm[0:P, h, 0:8], in_values=cand[0:P, h, :], imm_value=-1e30), reads=[cand, tm], writes=[tmp])
            fw.op(fw.dve, lambda e: e.max(out=tm[0:P, h, 8:16], in_=tmp[0:P, :]), reads=[tmp], writes=[tm])
            fw.op(fw.dve, lambda e: e.max_index(out=pos[0:P, h, 0:8], in_max=tm[0:P, h, 0:8], in_values=cand[0:P, h, :]), reads=[cand, tm], writes=[pos])
            fw.op(fw.dve, lambda e: e.max_index(out=pos[0:P, h, 8:16], in_max=tm[0:P, h, 8:16], in_values=tmp[0:P, :]), reads=[tmp, tm], writes=[pos])
        fw.op(fw.dve, lambda e: e.tensor_single_scalar(out=pa[0:P, :, :], in_=pos[0:P, :, :], scalar=4, op=ALU.logical_shift_right), reads=[pos], writes=[pa])
        fw.op(fw.dve, lambda e: e.tensor_copy(out=paf[0:P, 0, :], in_=pa[0:P, :, :].rearrange("p h k -> p (h k)")), reads=[pa], writes=[paf])
        fw.op(fw.dve, lambda e: e.tensor_single_scalar(out=pa[0:P, :, :], in_=pos[0:P, :, :], scalar=15, op=ALU.bitwise_and), reads=[pos, paf], writes=[pa])
        fw.op(fw.dve, lambda e: e.tensor_copy(out=paf[0:P, 1, :], in_=pa[0:P, :, :].rearrange("p h k -> p (h k)")), reads=[pa], writes=[paf])
        for w_ in range(2):
            o4 = oh[0:P, :, :].rearrange("p h (k a) -> p h k a", a=16)
            sel = paf[0:P, w_, :].rearrange("p (h k) -> p h k", h=8)
            fw.op(fw.dve, lambda e: e.tensor_tensor(out=o4, in0=sel.unsqueeze(3).to_broadcast([P, 8, 16, 16]),
                                                    in1=iota[0:P, :].unsqueeze(1).unsqueeze(1).to_broadcast([P, 8, 16, 16]), op=ALU.is_equal), reads=[paf, iota], writes=[oh])
            fw.op(fw.dve, lambda e: e.tensor_tensor(out=o4, in0=o4, in1=i4[:, :, w_, :].unsqueeze(2).to_broadcast([P, 8, 16, 16]), op=ALU.mult), reads=[oh, ixf], writes=[oh])
            fw.op(fw.dve, lambda e: e.tensor_reduce(out=isel[0:P, w_, :], in_=oh[0:P, :, :].rearrange("p h (k a) -> p (h k) a", a=16), axis=AX.X, op=ALU.add), reads=[oh], writes=[isel])
        fw.op(fw.dve, lambda e: e.scalar_tensor_tensor(out=eid[0:P, :], in0=isel[0:P, 0, :], scalar=128.0, in1=isel[0:P, 1, :], op0=ALU.mult, op1=ALU.add), reads=[isel], writes=[eid])
        fw.op(fw.dve, lambda e: e.tensor_tensor(out=gate[0:P, :, :], in0=tm[0:P, :, :], in1=tm[0:P, :, 0:1].to_broadcast([P, 8, 16]), op=ALU.subtract), reads=[tm], writes=[gate])
        fw.op(fw.act, lambda e: e.activation(out=gate[0:P, :, :], in_=gate[0:P, :, :], func=AF.Exp), reads=[gate], writes=[gate])
        fw.op(fw.dve, lambda e: e.tensor_reduce(out=gst[0:P, 0:8], in_=gate[0:P, :, :], axis=AX.X, op=ALU.add), reads=[gate], writes=[gst])
        fw.op(fw.dve, lambda e: e.reciprocal(out=gst[0:P, 8:16], in_=gst[0:P, 0:8]), reads=[gst], writes=[gst])
        fw.op(fw.dve, lambda e: e.tensor_tensor(out=gate[0:P, :, :], in0=gate[0:P, :, :], in1=gst[0:P, 8:16].unsqueeze(2).to_broadcast([P, 8, 16]), op=ALU.mult), reads=[gate, gst], writes=[gate])
        fw.op(fw.dve, lambda e: e.memset(actp[0:P, :], 0.0), writes=[actp])
        for s_ in range(128):
            rb = ring[gi % NB]
            gi += 1
            fw.dma(fw.pool, lambda h: h.indirect_dma_start(out=rb[0:P, :], out_offset=None, in_=u_rows, in_offset=bass.IndirectOffsetOnAxis(ap=eid[0:P, s_:s_ + 1], axis=0),
                                                         element_offset=l * 16384 * D), reads=[eid, C.UB], writes=[rb])
            fw.op(fw.dve, lambda e: e.scalar_tensor_tensor(out=junk[0:P, :], in0=rb[0:P, :], scalar=1.0, in1=x1[0:P, :], op0=ALU.mult, op1=ALU.mult, accum_out=actp[0:P, s_:s_ + 1]),
                  reads=[rb, x1], writes=[junk, actp], join=True)
        emit_gelu(fw, fw.dve, fw.act, wgt[0:P, :], actp[0:P, :], gtmp[0:P, :], [actp], [wgt], gtmp)
        fw.op(fw.dve, lambda e: e.tensor_tensor(out=wgt[0:P, :], in0=wgt[0:P, :], in1=gate[0:P, :, :].rearrange("p h k -> p (h k)"), op=ALU.mult), reads=[wgt, gate], writes=[wgt])
        for s_ in range(128):
            rb = ring[gi % NB]
            gi += 1
            fw.dma(fw.pool, lambda h: h.indirect_dma_start(out=rb[0:P, :], out_offset=None, in_=v_rows, in_offset=bass.IndirectOffsetOnAxis(ap=eid[0:P, s_:s_ + 1], axis=0),
                                                         element_offset=l * 16384 * D), reads=[eid, C.VB], writes=[rb])
            if s_ == 0:
                fw.op(fw.dve, lambda e: e.tensor_scalar(out=y[0:P, :], in0=rb[0:P, :], scalar1=wgt[0:P, 0:1], scalar2=None, op0=ALU.mult), reads=[rb, wgt], writes=[y])
            else:
                fw.op(fw.dve, lambda e: e.scalar_tensor_tensor(out=y[0:P, :], in0=rb[0:P, :], scalar=wgt[0:P, s_:s_ + 1], in1=y[0:P, :], op0=ALU.mult, op1=ALU.add),
                      reads=[rb, wgt, y], writes=[y])
        fw.op(fw.dve, lambda e: e.scalar_tensor_tensor(out=y[0:P, :], in0=x1[0:P, :], scalar=ALPHA, in1=y[0:P, :], op0=ALU.mult, op1=ALU.add), reads=[x1, y], writes=[y])
        emit_ln_free(fw, C, y, P, ln2g, ln2b, junk, lnst)
        fw.dma(fw.sp, lambda h: h.dma_start(out=X2[t * 128:t * 128 + P, :], in_=y[0:P, :]), reads=[y], writes=[X2], join=True)
    fw.pop()


def sample_consts():
    c = {}
    rows = np.arange(64)
    tok = rows % 8
    g = rows // 32
    c["msum"] = ((g[:, None] == g[None, :]) & (tok[:, None] == tok[None, :])).astype(np.float32)
    sA = np.ones((64, 257), np.float32); sB = np.zeros((64, 257), np.float32)
    for col in (0, 255, 256):
        sA[:, col] = 0.0; sB[:, col] = 1.0e4
    c["ssA"] = sA; c["ssB"] = sB
    c["caus8"] = np.where(np.arange(8)[None, :] <= tok[:, None], 0.0, NEG).astype(np.float32)
    idx = np.arange(520)[None, :]
    ok = np.where(idx < 512, idx >= tok[:, None], (idx - 512) <= tok[:, None])
    c["wbias"] = np.where(ok, 0.0, NEG).astype(np.float32)
    return c


SCONST_SHAPES = {"msum": [64, 64], "ssA": [64, 257], "ssB": [64, 257], "caus8": [64, 8], "wbias": [64, 520]}
CONST_SHAPES.update(SCONST_SHAPES)


def emit_rows_softmax(fw, sbuf_s, n, st, gate_ap, gate_buf, R=64):
    fw.op(fw.dve, lambda e: e.tensor_reduce(out=st[0:R, 0:1], in_=sbuf_s[0:R, 0:n], axis=AX.X, op=ALU.max, negate=True), reads=[sbuf_s], writes=[st])
    fw.op(fw.act, lambda e: e.activation(out=sbuf_s[0:R, 0:n], in_=sbuf_s[0:R, 0:n], func=AF.Exp, bias=st[0:R, 0:1], scale=1.0, accum_out=st[0:R, 1:2]),
          reads=[sbuf_s, st], writes=[sbuf_s, st])
    fw.op(fw.dve, lambda e: e.reciprocal(out=st[0:R, 2:3], in_=st[0:R, 1:2]), reads=[st], writes=[st])
    if gate_ap is not None:
        fw.op(fw.dve, lambda e: e.tensor_tensor(out=st[0:R, 2:3], in0=st[0:R, 2:3], in1=gate_ap, op=ALU.mult), reads=[st, gate_buf], writes=[st])
    fw.op(fw.dve, lambda e: e.tensor_scalar(out=sbuf_s[0:R, 0:n], in0=sbuf_s[0:R, 0:n], scalar1=st[0:R, 2:3], scalar2=None, op0=ALU.mult), reads=[sbuf_s, st], writes=[sbuf_s])


def phase_nsa_sample(fw, C, l, HtokS, mixn_s):
    fw.push()
    R = 64
    cst = {nm: load_const(fw, C, nm) for nm in SCONST_SHAPES}
    pti = fw.sbuf("pti", [128, 1], I32)
    fw.dma(fw.sp, lambda h: h.dma_start(out=pti[:, :], in_=C.pt[:, :]), reads=[C.pt], writes=[pti])
    idx8 = fw.sbuf("idx8", [128, 1], I32)
    fw.op(fw.dve, lambda e: e.tensor_scalar(out=idx8[:, :], in0=pti[:, :], scalar1=8.0, scalar2=None, op0=ALU.mult), reads=[pti], writes=[idx8])
    kv = fw.sbuf("skv", [TS, 1536], F32)
    rt = fw.sbuf("srt", [TS, 4, 64], F32)
    fw.dma(fw.sp, lambda h: h.dma_start(out=kv[:, :], in_=HtokS[0:TS, 1536:3072]), reads=[HtokS], writes=[kv])
    fw.dma(fw.sp, lambda h: h.dma_start(out=rt[:, :, :], in_=C.cd["rope_s"][:, :, :]), reads=[C.cd["rope_s"]], writes=[rt])
    rtmp = [fw.sbuf(f"srtmp{i}", [TS, 512], F32) for i in range(4)]
    kview = kv[:, :].rearrange("p (a v g h d) -> p a v g h d", a=3, v=2, g=2, h=2, d=64)
    tv = [r[:, 0:384].rearrange("p (a g d) -> p a g d", a=3, g=2) for r in rtmp]
    emit_rope(fw, kview[:, :, 0, :, 0, :], kview[:, :, 0, :, 1, :], rt[:, 0, :], rt[:, 1, :], tv, [TS, 3, 2, 64], [rt], kv, rtmp)
    for a in range(6):
        fw.dma(fw.sp, lambda h: h.dma_start(out=C.out_skv[l, a, :, :], in_=kv[:, a * 256:(a + 1) * 256]), reads=[kv], writes=[C.out_skv], join=True)
    for wi in range(2):
        fw.dma(fw.sp, lambda h: h.dma_start(out=C.out_swin[l, wi, 0:504, :], in_=C.cwin[wi][l, 8:512, :]), reads=[C.cwin[wi]], writes=[C.out_swin], join=True)
        fw.dma(fw.sp, lambda h: h.dma_start(out=C.out_swin[l, wi, 504:512, :], in_=kv[:, 1024 + wi * 256:1280 + wi * 256]), reads=[kv], writes=[C.out_swin], join=True)
    kvb = fw.sbuf("skvb", [TS, 1536], BF16)
    fw.op(fw.act, lambda e: e.copy(out=kvb[:, :], in_=kv[:, :]), reads=[kv], writes=[kvb])
    psb = C.psb[0]
    for i, (a, g) in enumerate([(1, 0), (1, 1), (2, 0), (2, 1)]):
        c0 = a * 512 + g * 128
        fw.op(fw.pe, lambda e: e.transpose(out=psb[:, i * 8:(i + 1) * 8], in_=kvb[:, c0:c0 + 128], identity=C.identb[0:TS, 0:TS]), reads=[kvb, C.identb], writes=[psb], join=(i > 0))
    knT = fw.sbuf("knT", [128, 32], BF16)
    fw.op(fw.dve, lambda e: e.tensor_copy(out=knT[:, :], in_=psb[:, 0:32]), reads=[psb], writes=[knT])
    qt = fw.sbuf("sqt", [TS, 1024], F32)
    gt = fw.sbuf("sgt", [TS, 24], F32)
    fw.dma(fw.sp, lambda h: h.dma_start(out=qt[:, :], in_=HtokS[0:TS, 512:1536]), reads=[HtokS], writes=[qt])
    fw.dma(fw.sp, lambda h: h.dma_start(out=gt[:, :], in_=HtokS[0:TS, 3072:3096]), reads=[HtokS], writes=[gt])
    qv = qt[:, :].rearrange("p (h x d) -> p h x d", h=8, x=2)
    tv2 = [r[:, :].rearrange("p (h d) -> p h d", h=8) for r in rtmp]
    emit_rope(fw, qv[:, :, 0, :], qv[:, :, 1, :], rt[:, 2, :], rt[:, 3, :], tv2, [TS, 8, 64], [rt], qt, rtmp)
    qb = fw.sbuf("sqb", [TS, 1024], BF16)
    fw.op(fw.act, lambda e: e.copy(out=qb[:, :], in_=qt[:, :]), reads=[qt], writes=[qb])
    psb = C.psb[1]
    for h in range(8):
        fw.op(fw.pe, lambda e: e.transpose(out=psb[:, h * 8:(h + 1) * 8], in_=qb[:, h * 128:(h + 1) * 128], identity=C.identb[0:TS, 0:TS]), reads=[qb, C.identb], writes=[psb], join=(h > 0))
    Q = [fw.sbuf(f"sQ{g}", [128, 64], BF16) for g in range(2)]
    for g in range(2):
        fw.op(fw.dve, lambda e: e.memset(Q[g][:, :], 0.0), writes=[Q[g]])
        fw.op(fw.dve, lambda e: e.tensor_copy(out=Q[g][:, g * 32:(g + 1) * 32], in_=psb[:, g * 32:(g + 1) * 32]), reads=[psb], writes=[Q[g]])
    fw.op(fw.act, lambda e: e.activation(out=gt[:, :], in_=gt[:, :], func=AF.Sigmoid), reads=[gt], writes=[gt])
    fw.dma(fw.sp, lambda h: h.dma_start(out=C.gscr[:, :], in_=gt[:, :]), reads=[gt], writes=[C.gscr])
    g64 = fw.sbuf("g64", [64, 3], F32)
    for h in range(8):
        fw.dma(fw.sp, lambda h_: h_.dma_start(out=g64[h * 8:(h + 1) * 8, :], in_=C.gscr[:, h * 3:(h + 1) * 3]), reads=[C.gscr], writes=[g64], join=True)
    st = fw.sbuf("sst", [64, 4], F32)
    wsm = []
    for wi, wsrc in enumerate([C.cmp_wk, C.cmp_wv]):
        w_ = fw.sbuf(f"wsm{wi}", [128, 64], F32)
        fw.dma(fw.sp, lambda h: h.dma_start(out=w_[:, :], in_=wsrc[l, :, :].rearrange("g j -> (g j)").partition_broadcast(128)), reads=[wsrc], writes=[w_])
        wsm.append(w_)
    chunk = [fw.sbuf(f"chunk{i}", [128, 4096], F32) for i in range(2)]
    cacc = [fw.sbuf(f"cacc{i}", [128, 4, 256], F32) for i in range(2)]
    red = fw.sbuf("sred", [128, 256], F32)
    ci = 0
    for wi, cache in enumerate([C.c_cmp_k, C.c_cmp_v]):
        for jb8 in range(8):
            ch = chunk[ci % 2]
            ci += 1
            fw.dma(fw.pool, lambda h: h.indirect_dma_start(out=ch[:, :], out_offset=None, in_=cache[:, :], in_offset=bass.IndirectOffsetOnAxis(ap=idx8[:, 0:1], axis=0),
                                                         element_offset=(l * 1280 * 8 + jb8) * 4096), reads=[idx8, cache], writes=[ch])
            jb, j0 = jb8 // 2, (jb8 % 2) * 16
            wv_ = wsm[wi][:, :].rearrange("p (g j) -> p j g", g=2)[:, j0:j0 + 16, :].unsqueeze(3).to_broadcast([128, 16, 2, 128])
            chv4 = ch[:, :].rearrange("p (j g d) -> p j g d", j=16, g=2)
            fw.op(fw.dve, lambda e: e.tensor_tensor(out=chv4, in0=chv4, in1=wv_, op=ALU.mult), reads=[ch, wsm[wi]], writes=[ch])
            if j0 == 0:
                fw.op(fw.dve, lambda e: e.tensor_reduce(out=cacc[wi][:, jb, :], in_=ch[:, :].rearrange("p (j c) -> p c j", j=16), axis=AX.X, op=ALU.add), reads=[ch], writes=[cacc[wi]], join=True)
            else:
                fw.op(fw.dve, lambda e: e.tensor_reduce(out=red[:, :], in_=ch[:, :].rearrange("p (j c) -> p c j", j=16), axis=AX.X, op=ALU.add), reads=[ch], writes=[red])
                fw.op(fw.dve, lambda e: e.tensor_tensor(out=cacc[wi][:, jb, :], in0=cacc[wi][:, jb, :], in1=red[:, :], op=ALU.add), reads=[cacc[wi], red], writes=[cacc[wi]])
    kcT = fw.sbuf("skcT", [128, 2, 512], BF16)
    for g in range(2):
        ps = C.ps[g]
        for jb in range(4):
            fw.op(fw.pe, lambda e: e.transpose(out=ps[:, jb * 128:(jb + 1) * 128], in_=cacc[0][:, jb, g * 128:(g + 1) * 128], identity=C.ident[:, :]), reads=[cacc[0], C.ident], writes=[ps], join=(jb > 0))
        fw.op(fw.dve, lambda e: e.tensor_copy(out=kcT[:, g, :], in_=ps[:, :]), reads=[ps], writes=[kcT], join=True)
    vcb = fw.sbuf("svcb", [128, 4, 256], BF16)
    fw.op(fw.act, lambda e: e.copy(out=vcb[:, :, :], in_=cacc[1][:, :, :]), reads=[cacc[1]], writes=[vcb])
    ps = C.ps[2]
    for g in range(2):
        fw.op(fw.pe, lambda e: e.matmul(ps[0:R, :], lhsT=Q[g][:, :], rhs=kcT[:, g, :], start=(g == 0), stop=(g == 1)), reads=[Q[g], kcT], writes=[ps], join=(g > 0))
    pc = fw.sbuf("spc", [64, 512], F32)
    fw.op(fw.act, lambda e: e.copy(out=pc[:, :], in_=ps[0:R, :]), reads=[ps], writes=[pc])
    emit_rows_softmax(fw, pc, 512, st, None, None)
    ps = C.ps[3]
    fw.op(fw.pe, lambda e: e.matmul(ps[0:R, :], lhsT=cst["msum"][:, :], rhs=pc[:, :], start=True, stop=True), reads=[cst["msum"], pc], writes=[ps])
    impr = fw.sbuf("simpr", [64, 512], F32)
    fw.op(fw.act, lambda e: e.copy(out=impr[:, :], in_=ps[0:R, :]), reads=[ps], writes=[impr])
    sco = fw.sbuf("ssco", [64, 264], F32)
    sco2 = fw.sbuf("ssco2", [64, 264], F32)
    iv = impr[:, :].rearrange("r (hb two p) -> r hb two p", hb=2, two=2)
    fw.op(fw.dve, lambda e: e.memset(sco[:, 256:264], 0.0), writes=[sco])
    fw.op(fw.dve, lambda e: e.tensor_tensor(out=sco[:, 0:256].rearrange("r (hb p) -> r hb p", hb=2), in0=iv[:, :, 0, :], in1=iv[:, :, 1, :], op=ALU.add), reads=[impr], writes=[sco], join=True)
    fw.op(fw.dve, lambda e: e.tensor_tensor(out=sco[:, 0:257], in0=sco[:, 0:257], in1=cst["ssA"][:, :], op=ALU.mult), reads=[sco, cst["ssA"]], writes=[sco])
    fw.op(fw.dve, lambda e: e.tensor_tensor(out=sco[:, 0:257], in0=sco[:, 0:257], in1=cst["ssB"][:, :], op=ALU.add), reads=[sco, cst["ssB"]], writes=[sco])
    m8 = fw.sbuf("sm8", [64, 16], F32)
    fw.op(fw.dve, lambda e: e.max(out=m8[:, 0:8], in_=sco[:, 0:257]), reads=[sco], writes=[m8])
    fw.op(fw.dve, lambda e: e.match_replace(out=sco2[:, 0:257], in_to_replace=m8[:, 0:8], in_values=sco[:, 0:257], imm_value=-1e30), reads=[sco, m8], writes=[sco2])
    fw.op(fw.dve, lambda e: e.max(out=m8[:, 8:16], in_=sco2[:, 0:257]), reads=[sco2], writes=[m8])
    bsel = fw.sbuf("sbsel", [64, 257], F32)
    fw.op(fw.dve, lambda e: e.tensor_scalar(out=bsel[:, :], in0=sco[:, 0:257], scalar1=m8[:, 15:16], scalar2=None, op0=ALU.is_ge), reads=[sco, m8], writes=[bsel])
    fw.op(fw.dve, lambda e: e.tensor_scalar(out=bsel[:, :], in0=bsel[:, :], scalar1=-NEG, scalar2=NEG, op0=ALU.mult, op1=ALU.add), reads=[bsel], writes=[bsel])
    fw.op(fw.dve, lambda e: e.tensor_scalar(out=pc[:, :], in0=pc[:, :], scalar1=g64[:, 0:1], scalar2=None, op0=ALU.mult), reads=[pc, g64], writes=[pc])
    pcT = fw.sbuf("spcT", [128, 4, 64], BF16)
    ps = C.ps[0]
    for jb in range(4):
        fw.op(fw.pe, lambda e: e.transpose(out=ps[:, jb * 64:(jb + 1) * 64], in_=pc[:, jb * 128:(jb + 1) * 128], identity=C.ident[0:R, 0:R]), reads=[pc, C.ident], writes=[ps], join=(jb > 0))
    fw.op(fw.dve, lambda e: e.tensor_copy(out=pcT[:, :, :], in_=ps[:, 0:256].rearrange("p (j r) -> p j r", j=4)), reads=[ps], writes=[pcT])
    Ss = fw.sbuf("sS", [64, 16392], F32)
    vt = [fw.sbuf(f"svt{i}", [128, 16, 256], BF16) for i in range(2)]
    ktile = [fw.sbuf(f"sktile{i}", [128, 512], BF16) for i in range(4)]
    ki = 0
    for jb8 in range(8):
        ch = chunk[ci % 2]
        ci += 1
        fw.dma(fw.pool, lambda h: h.indirect_dma_start(out=ch[:, :], out_offset=None, in_=C.c_slc_k[:, :], in_offset=bass.IndirectOffsetOnAxis(ap=idx8[:, 0:1], axis=0),
                                                     element_offset=(l * 1280 * 8 + jb8) * 4096), reads=[idx8, C.c_slc_k], writes=[ch])
        chv = ch[:, :].rearrange("p (j g d) -> p j g d", j=16, g=2)
        for j4 in range(4):
            kts = []
            for g in range(2):
                pst = C.ps[(ki % 2) * 2 + g]
                for jj in range(4):
                    fw.op(fw.pe, lambda e: e.transpose(out=pst[:, jj * 128:(jj + 1) * 128], in_=chv[:, j4 * 4 + jj, g, :], identity=C.ident[:, :]), reads=[ch, C.ident], writes=[pst], join=(jj > 0))
                kt_ = ktile[(ki % 2) * 2 + g]
                if g == 0:
                    fw.op(fw.dve, lambda e: e.tensor_copy(out=kt_[:, :], in_=pst[:, :]), reads=[pst], writes=[kt_])
                else:
                    fw.op(fw.act, lambda e: e.copy(out=kt_[:, :], in_=pst[:, :]), reads=[pst], writes=[kt_])
                kts.append(kt_)
            pss = C.pacc[ki % 2]
            for g in range(2):
                fw.op(fw.pe, lambda e: e.matmul(pss[0:R, :], lhsT=Q[g][:, :], rhs=kts[g][:, :], start=(g == 0), stop=(g == 1)), reads=[Q[g], kts[g]], writes=[pss], join=(g > 0))
            jabs = jb8 * 16 + j4 * 4
            hb = 1 if jabs >= 64 else 0
            fw.op(fw.dve, lambda e: e.tensor_tensor(out=Ss[:, jabs * 128:(jabs + 4) * 128].rearrange("r (j p) -> r j p", j=4), in0=pss[0:R, :].rearrange("r (j p) -> r j p", j=4),
                                                    in1=bsel[:, hb * 128:(hb + 1) * 128].unsqueeze(1).to_broadcast([R, 4, 128]), op=ALU.add), reads=[pss, bsel], writes=[Ss], join=True)
            ki += 1
    pss = C.pacc[0]
    for g in range(2):
        fw.op(fw.pe, lambda e: e.matmul(pss[0:R, 0:8], lhsT=Q[g][:, :], rhs=knT[:, g * 8:(g + 1) * 8], start=(g == 0), stop=(g == 1)), reads=[Q[g], knT], writes=[pss], join=(g > 0))
    fw.op(fw.dve, lambda e: e.scalar_tensor_tensor(out=Ss[:, 16384:16392], in0=pss[0:R, 0:8], scalar=bsel[:, 256:257], in1=cst["caus8"][:, :], op0=ALU.add, op1=ALU.add),
          reads=[pss, bsel, cst["caus8"]], writes=[Ss], join=True)
    emit_rows_softmax(fw, Ss, 16392, st, g64[:, 1:2], g64)
    pT = fw.sbuf("spT", [128, 128, 64], BF16)
    for j8 in range(16):
        ps = C.ps[j8 % 4]
        for jj in range(8):
            j = j8 * 8 + jj
            fw.op(fw.pe, lambda e: e.transpose(out=ps[:, jj * 64:(jj + 1) * 64], in_=Ss[:, j * 128:(j + 1) * 128], identity=C.ident[0:R, 0:R]), reads=[Ss, C.ident], writes=[ps], join=(jj > 0))
        fw.op(fw.act if j8 % 2 else fw.dve, (lambda e: e.copy(out=pT[:, j8 * 8:(j8 + 1) * 8, :], in_=ps[:, :].rearrange("p (j r) -> p j r", j=8))) if j8 % 2 else
              (lambda e: e.tensor_copy(out=pT[:, j8 * 8:(j8 + 1) * 8, :], in_=ps[:, :].rearrange("p (j r) -> p j r", j=8))), reads=[ps], writes=[pT], join=True)
    ps = C.ps[0]
    fw.op(fw.pe, lambda e: e.transpose(out=ps[0:8, 0:64], in_=Ss[:, 16384:16392], identity=C.ident[0:R, 0:R]), reads=[Ss, C.ident], writes=[ps])
    pTn = fw.sbuf("spTn", [8, 64], BF16)
    fw.op(fw.dve, lambda e: e.tensor_copy(out=pTn[:, :], in_=ps[0:8, 0:64]), reads=[ps], writes=[pTn])
    wk = fw.sbuf("swk", [128, 4, 256], F32)
    wv = fw.sbuf("swv", [128, 4, 256], F32)
    fw.dma(fw.sp, lambda h: h.dma_start(out=wk[:, :, :], in_=C.cwin[0][l, :, :].rearrange("(a p) c -> p a c", p=128)), reads=[C.cwin[0]], writes=[wk])
    fw.dma(fw.sp, lambda h: h.dma_start(out=wv[:, :, :], in_=C.cwin[1][l, :, :].rearrange("(a p) c -> p a c", p=128)), reads=[C.cwin[1]], writes=[wv])
    wvb = fw.sbuf("swvb", [128, 4, 256], BF16)
    fw.op(fw.act, lambda e: e.copy(out=wvb[:, :, :], in_=wv[:, :, :]), reads=[wv], writes=[wvb])
    wkT = fw.sbuf("swkT", [128, 2, 512], BF16)
    for g in range(2):
        ps = C.ps[1 + g]
        for a in range(4):
            fw.op(fw.pe, lambda e: e.transpose(out=ps[:, a * 128:(a + 1) * 128], in_=wk[:, a, g * 128:(g + 1) * 128], identity=C.ident[:, :]), reads=[wk, C.ident], writes=[ps], join=(a > 0))
        fw.op(fw.dve, lambda e: e.tensor_copy(out=wkT[:, g, :], in_=ps[:, :]), reads=[ps], writes=[wkT], join=True)
    Sw = fw.sbuf("sSw", [64, 520], F32)
    pss = C.pacc[1]
    for g in range(2):
        fw.op(fw.pe, lambda e: e.matmul(pss[0:R, :], lhsT=Q[g][:, :], rhs=wkT[:, g, :], start=(g == 0), stop=(g == 1)), reads=[Q[g], wkT], writes=[pss], join=(g > 0))
    fw.op(fw.dve, lambda e: e.tensor_tensor(out=Sw[:, 0:512], in0=pss[0:R, :], in1=cst["wbias"][:, 0:512], op=ALU.add), reads=[pss, cst["wbias"]], writes=[Sw], join=True)
    pss = C.pacc[0]
    for g in range(2):
        fw.op(fw.pe, lambda e: e.matmul(pss[0:R, 0:8], lhsT=Q[g][:, :], rhs=knT[:, 16 + g * 8:16 + (g + 1) * 8], start=(g == 0), stop=(g == 1)), reads=[Q[g], knT], writes=[pss], join=(g > 0))
    fw.op(fw.dve, lambda e: e.tensor_tensor(out=Sw[:, 512:520], in0=pss[0:R, 0:8], in1=cst["wbias"][:, 512:520], op=ALU.add), reads=[pss, cst["wbias"]], writes=[Sw], join=True)
    emit_rows_softmax(fw, Sw, 520, st, g64[:, 2:3], g64)
    pwT = fw.sbuf("spwT", [128, 4, 64], BF16)
    ps = C.ps[3]
    for a in range(4):
        fw.op(fw.pe, lambda e: e.transpose(out=ps[:, a * 64:(a + 1) * 64], in_=Sw[:, a * 128:(a + 1) * 128], identity=C.ident[0:R, 0:R]), reads=[Sw, C.ident], writes=[ps], join=(a > 0))
    fw.op(fw.pe, lambda e: e.transpose(out=ps[0:8, 256:320], in_=Sw[:, 512:520], identity=C.ident[0:R, 0:R]), reads=[Sw, C.ident], writes=[ps], join=True)
    fw.op(fw.dve, lambda e: e.tensor_copy(out=pwT[:, :, :], in_=ps[:, 0:256].rearrange("p (a r) -> p a r", a=4)), reads=[ps], writes=[pwT])
    pwTn = fw.sbuf("spwTn", [8, 64], BF16)
    fw.op(fw.dve, lambda e: e.tensor_copy(out=pwTn[:, :], in_=ps[0:8, 256:320]), reads=[ps], writes=[pwTn])
    accs = [C.pacc[0], C.pacc[1]]
    for g in range(2):
        gs = slice(g * 32, (g + 1) * 32)
        for jb in range(4):
            fw.op(fw.pe, lambda e: e.matmul(accs[g][:, 0:32], lhsT=vcb[:, jb, g * 128:(g + 1) * 128], rhs=pcT[:, jb, gs], start=(jb == 0), stop=False), reads=[vcb, pcT], writes=[accs[g]], join=(jb > 0))
    for jb8 in range(8):
        ch = chunk[ci % 2]
        v_ = vt[ci % 2]
        ci += 1
        fw.dma(fw.pool, lambda h: h.indirect_dma_start(out=ch[:, :], out_offset=None, in_=C.c_slc_v[:, :], in_offset=bass.IndirectOffsetOnAxis(ap=idx8[:, 0:1], axis=0),
                                                     element_offset=(l * 1280 * 8 + jb8) * 4096), reads=[idx8, C.c_slc_v], writes=[ch])
        fw.op(fw.act if jb8 % 2 else fw.dve, (lambda e: e.copy(out=v_[:, :, :], in_=ch[:, :].rearrange("p (j c) -> p j c", j=16))) if jb8 % 2 else
              (lambda e: e.tensor_copy(out=v_[:, :, :], in_=ch[:, :].rearrange("p (j c) -> p j c", j=16))), reads=[ch], writes=[v_])
        for g in range(2):
            gs = slice(g * 32, (g + 1) * 32)
            for jj in range(16):
                j = jb8 * 16 + jj
                fw.op(fw.pe, lambda e: e.matmul(accs[g][:, 0:32], lhsT=v_[:, jj, g * 128:(g + 1) * 128], rhs=pT[:, j, gs], start=False, stop=False), reads=[v_, pT], writes=[accs[g]], join=True)
    for g in range(2):
        acc = accs[g]
        gs = slice(g * 32, (g + 1) * 32)
        fw.op(fw.pe, lambda e: e.matmul(acc[:, 0:32], lhsT=kvb[:, 768 + g * 128:896 + g * 128], rhs=pTn[:, gs], start=False, stop=False), reads=[kvb, pTn], writes=[acc], join=True)
        for a in range(4):
            fw.op(fw.pe, lambda e: e.matmul(acc[:, 0:32], lhsT=wvb[:, a, g * 128:(g + 1) * 128], rhs=pwT[:, a, gs], start=False, stop=False), reads=[wvb, pwT], writes=[acc], join=True)
        fw.op(fw.pe, lambda e: e.matmul(acc[:, 0:32], lhsT=kvb[:, 1280 + g * 128:1408 + g * 128], rhs=pwTn[:, gs], start=False, stop=True), reads=[kvb, pwTn], writes=[acc], join=True)
        fw.op(fw.dve, lambda e: e.tensor_copy(out=mixn_s[:, g * 4:(g + 1) * 4, 0:TS], in_=acc[:, 0:32].rearrange("p (r t) -> p r t", r=4)), reads=[acc], writes=[mixn_s], join=True)
    fw.pop()


def build(mode="full"):
    nc = bass.Bass("TRN2", target_bir_lowering=False)
    fw = FW(nc)
    C = Ctx()
    dbg = mode != "full"
    nlayers = 1 if dbg else DEPTH
    def inp(name, shape, dtype=F32):
        return fw.dram(name, shape, dtype, kind="ExternalInput")
    C.xp = inp("xp", [S, D])
    C.xs = inp("xs", [TS, D])
    C.pt = inp("pt", [128, 1], I32)
    C.c_cmp_k = inp("c_cmp_k", [DEPTH * 1280 * 8, 4096])
    C.c_cmp_v = inp("c_cmp_v", [DEPTH * 1280 * 8, 4096])
    C.c_slc_k = inp("c_slc_k", [DEPTH * 1280 * 8, 4096])
    C.c_slc_v = inp("c_slc_v", [DEPTH * 1280 * 8, 4096])
    C.cwin = [inp("c_win_k", [DEPTH, 512, 256]), inp("c_win_v", [DEPTH, 512, 256])]
    C.state_conv = inp("state_conv", [DEPTH, 30, 512])
    for nm, shp in INPUT_SPECS:
        setattr(C, nm, inp(nm, shp))
    C.cd = {nm: inp(nm, shp) for nm, shp in CONST_SHAPES.items()}
    C.out_kv = fw.dram("o_pkv", [DEPTH, 6, S, 256], F32, kind="ExternalOutput")
    C.out_conv_buf = fw.dram("o_pconv", [DEPTH, 30, 512], F32, kind="ExternalOutput")
    C.out_gm = fw.dram("o_pgm", [DEPTH, 128, 512], F32, kind="ExternalOutput")
    C.out_gm_buf = C.out_gm
    C.out_skv = fw.dram("o_skv", [DEPTH, 6, TS, 256], F32, kind="ExternalOutput")
    C.out_swin = fw.dram("o_swin", [DEPTH, 2, 512, 256], F32, kind="ExternalOutput")
    C.out_sconv = fw.dram("o_sconv", [DEPTH, 30, 512], F32, kind="ExternalOutput")
    C.out_sgm = fw.dram("o_sgm", [DEPTH, TS, 512], F32, kind="ExternalOutput")
    C.out_y = fw.dram("o_y", [S // 2, D], F32, kind="ExternalOutput")
    C.prow_d = inp("prow", [128, NT // 2], I32)
    C.prow = fw.sbuf("prow_sb", [128, NT // 2], I32)
    fw.dma(fw.sp, lambda h: h.dma_start(out=C.prow[:, :], in_=C.prow_d[:, :]), reads=[C.prow_d], writes=[C.prow])
    C.out_ys = fw.dram("o_ys", [TS, D], F32, kind="ExternalOutput")
    C.ident = load_const(fw, C, "ident")
    C.identb = fw.sbuf("identb", [128, 128], BF16)
    fw.op(fw.dve, lambda e: e.tensor_copy(out=C.identb[:, :], in_=C.ident[:, :]), reads=[C.ident], writes=[C.identb])
    C.epsc = fw.sbuf("epsc", [128, 1], F32)
    fw.op(fw.dve, lambda e: e.memset(C.epsc[:, :], LN_EPS), writes=[C.epsc])
    for nm in ["causal", "acausal", "mask4", "triu", "onesdiv"]:
        setattr(C, nm, load_const(fw, C, nm))
    C.ps = [fw.psum(f"ps{i}", [128, 512], F32) for i in range(4)]
    C.psb = [fw.psum(f"psb{i}", [128, 1024], BF16) for i in range(2)]
    C.pacc = [fw.psum(f"pacc{i}", [128, 512], F32) for i in range(2)]
    okind = "ExternalOutput" if dbg else "Internal"
    C.Htok = fw.dram("Htok", [S, NTOKC], F32)
    C.HfT = fw.dram("HfT", [1536, S], F32)
    C.HtokS = fw.dram("HtokS", [TS, NTOKC], F32)
    C.HfTS = fw.dram("HfTS", [1536, TS], F32)
    C.gscr = fw.dram("gscr", [TS, 24], F32)
    C.UB = fw.dram("UB", [DEPTH * 16384, D], BF16)
    C.VB = fw.dram("VB", [DEPTH * 16384, D], BF16)
    C.X1 = fw.dram("X1", [S, D], F32, kind=okind)
    C.XS1 = fw.dram("XS1", [TS, D], F32, kind=okind)
    X2 = [fw.dram("X2a", [S, D], F32), C.out_y] if not dbg else [C.out_y]
    XS2 = [fw.dram("XS2a", [TS, D], F32), C.out_ys] if not dbg else [C.out_ys]
    mixn_s = fw.sbuf("mixn_s", [128, 8, TS], BF16)
    convTs = fw.sbuf("convTs", [128, 4, TS], BF16)
    x_src, xs_src = C.xp, C.xs
    for l in range(nlayers):
        fw.push()
        C.xin = [fw.sbuf(f"xin{i}", [128, D], F32) for i in range(2)]
        C.wbuf = [fw.sbuf(f"wbuf{i}", [128, 16, 512], BF16) for i in range(2)]
        C.hbuf = [fw.sbuf(f"hbuf{i}", [128, 512], F32) for i in range(4)]
        C.htmp = fw.sbuf("htmp", [128, 512], F32)
        xT = fw.sbuf("xT", [128, 16, S], BF16)
        phase_proj(fw, C, l, x_src, NT, C.Htok, C.HfT, xT, "p")
        phase_proj(fw, C, l, xs_src, 0, C.HtokS, C.HfTS, xT, "s")
        fw.pop()
        phase_nsa_sample(fw, C, l, C.HtokS, mixn_s)
        fw.push()
        C.kT = fw.sbuf("kT", [128, 4, S], BF16)
        C.V = fw.sbuf("V", [128, NT, 2, 256], BF16)
        C.kcmpT = fw.sbuf("kcmpT", [128, 2, 64], BF16)
        C.vcmp = fw.sbuf("vcmp", [64, 256], BF16)
        C.convT = fw.sbuf("convT", [128, 4, S], BF16)
        fw.push()
        phase_kv_prompt(fw, C, l)
        fw.pop()
        phase_conv(fw, C, l, C.HfT, S, None, C.out_conv_buf[l, :, :], C.convT)
        phase_conv(fw, C, l, C.HfTS, TS, C.state_conv[l, :, :], C.out_sconv[l, :, :], convTs)
        phase_mix_prompt(fw, C, l, x_src, C.X1, (xs_src, C.XS1, mixn_s, convTs))
        fw.pop()
        if l == nlayers - 1:
            phase_peer(fw, C, l, [(C.X1, NT // 2 if not dbg else 1, 128, X2[l], C.prow), (C.XS1, 1, TS, XS2[l])])
        else:
            phase_peer(fw, C, l, [(C.X1, NT, 128, X2[l]), (C.XS1, 1, TS, XS2[l])])
        x_src, xs_src = X2[l], XS2[l]
    fw.finish()
    return nc


def core_inputs(inputs, c):
    b = c // 2
    m = {"xp": np.ascontiguousarray(inputs["x_prompt"][b]), "xs": np.ascontiguousarray(inputs["x_sample"][c]),
         "pt": np.ascontiguousarray(inputs["page_table"][c].reshape(128, 1)).astype(np.int32)}
    for nm, key in [("c_cmp_k", "cache_cmp_k"), ("c_cmp_v", "cache_cmp_v"), ("c_slc_k", "cache_slc_k"), ("c_slc_v", "cache_slc_v")]:
        m[nm] = np.asarray(inputs[key]).reshape(DEPTH * 1280 * 8, 4096)
    m["c_win_k"] = np.ascontiguousarray(np.asarray(inputs["cache_win_k"])[:, c].reshape(DEPTH, 512, 256))
    m["c_win_v"] = np.ascontiguousarray(np.asarray(inputs["cache_win_v"])[:, c].reshape(DEPTH, 512, 256))
    m["state_conv"] = np.ascontiguousarray(np.asarray(inputs["state_conv"])[:, c])
    hh = c % 2
    m["prow"] = ((hh * (NT // 2) + np.arange(NT // 2)[None, :]) * 128 + np.arange(128)[:, None]).astype(np.int32)
    for nm, shp in INPUT_SPECS:
        m[nm] = np.asarray(inputs[nm])
    m.update(host_consts())
    return m


_NC_CACHE = {}


def kernel(**inputs):
    n = 8
    if "nc" not in _NC_CACHE:
        _NC_CACHE["nc"] = build("full")
    nc = _NC_CACHE["nc"]
    in_maps = [core_inputs(inputs, c) for c in range(n)]
    res = run_bass_kernel_spmd(nc, in_maps, core_ids=list(range(n))).results
    B = 4
    ev = [res[2 * b] for b in range(B)]
    y_prompt = np.stack([np.concatenate([res[2 * b]["o_y"], res[2 * b + 1]["o_y"]], 0) for b in range(B)], 0).astype(np.float32)
    y_sample = np.stack([res[c]["o_ys"] for c in range(n)], 0).astype(np.float32)
    pkv = np.stack([r["o_pkv"] for r in ev], 0)
    def pk(a, rows=None):
        t = pkv[:, :, a]
        if rows is not None:
            t = t[:, :, rows:]
        t = np.transpose(t, (1, 0, 2, 3))
        return np.ascontiguousarray(t.reshape(t.shape[0], t.shape[1], t.shape[2], 2, 128)).astype(np.float32)
    p_conv = np.ascontiguousarray(np.stack([r["o_pconv"] for r in ev], 1)).astype(np.float32)
    p_gm = np.ascontiguousarray(np.stack([r["o_pgm"] for r in ev], 1)).astype(np.float32)
    skv = np.stack([res[c]["o_skv"] for c in range(n)], 0)
    def sk(a):
        t = np.transpose(skv[:, :, a], (1, 0, 2, 3))
        return np.ascontiguousarray(t.reshape(DEPTH, n, TS, 2, 128)).astype(np.float32)
    swin = np.stack([res[c]["o_swin"] for c in range(n)], 0)
    def sw(a):
        t = np.transpose(swin[:, :, a], (1, 0, 2, 3))
        return np.ascontiguousarray(t.reshape(DEPTH, n, 512, 2, 128)).astype(np.float32)
    s_conv = np.ascontiguousarray(np.stack([res[c]["o_sconv"] for c in range(n)], 1)).astype(np.float32)
    s_gm = np.ascontiguousarray(np.stack([res[c]["o_sgm"] for c in range(n)], 1)).astype(np.float32)
    return (y_prompt, y_sample, pk(0), pk(1), pk(2), pk(3), pk(4, S - 512), pk(5, S - 512), p_conv, p_gm,
            sk(0), sk(1), sk(2), sk(3), sw(0), sw(1), s_conv, s_gm)
```

```python
import numpy as np
from contextlib import ExitStack
import concourse.bass as bass
import concourse.mybir as mybir
from concourse.bass_utils import run_bass_kernel_spmd

F32 = mybir.dt.float32
BF16 = mybir.dt.bfloat16
I32 = mybir.dt.int32
U32 = mybir.dt.uint32
AF = mybir.ActivationFunctionType
ALU = mybir.AluOpType
AX = mybir.AxisListType


class Buf:
    def __init__(self, t, name):
        self.t = t
        self.name = name
        self.w = {}
        self.r = {}

    def __getitem__(self, idx):
        return self.t[idx]


class Eng:
    def __init__(self, fw, name, h, pe=False, ndma=0):
        self.fw = fw
        self.name = name
        self.h = h
        self.pe = pe
        self.sem = fw.new_sem(name + "_prog")
        self.n = 0
        self.seen = {}
        self.dma_sems = [fw.new_sem(f"{name}_d{i}") for i in range(ndma)]
        self.dma_n = 0

    def wait(self, tok):
        key, sem, val = tok
        if self.seen.get(key, 0) >= val:
            return
        self.h.wait_ge(sem, val)
        self.seen[key] = val


class FW:
    def __init__(self, nc):
        self.nc = nc
        self.es0 = ExitStack()
        self.es = self.es0
        self.es_stack = []
        self.sems = []
        self.pe = Eng(self, "pe", nc.tensor, pe=True)
        self.act = Eng(self, "act", nc.scalar, ndma=8)
        self.dve = Eng(self, "dve", nc.vector)
        self.pool = Eng(self, "pool", nc.gpsimd, ndma=24)
        self.sp = Eng(self, "sp", nc.sync, ndma=24)
        self.engs = [self.pe, self.act, self.dve, self.pool, self.sp]
        self.bufs = []

    def new_sem(self, name):
        s = self.es0.enter_context(self.nc.semaphore(name))
        self.sems.append(s)
        return s

    def push(self):
        self.es_stack.append((self.es, len(self.bufs)))
        self.es = ExitStack()

    def pop(self):
        self.barrier()
        self.es.close()
        self.es, nb = self.es_stack.pop()
        del self.bufs[nb:]

    def sbuf(self, name, shape, dtype=F32):
        self.uid = getattr(self, "uid", 0) + 1
        name = f"{name}_{self.uid}"
        t = self.es.enter_context(self.nc.sbuf_tensor(name, list(shape), dtype))
        b = Buf(t, name)
        self.bufs.append(b)
        return b

    def psum(self, name, shape, dtype=F32):
        t = self.es.enter_context(self.nc.psum_tensor(name, list(shape), dtype))
        b = Buf(t, name)
        self.bufs.append(b)
        return b

    def dram(self, name, shape, dtype=F32, kind="Internal"):
        t = self.nc.dram_tensor(name, list(shape), dtype, kind=kind).ap()
        b = Buf(t, name)
        self.bufs.append(b)
        return b

    def view(self, ap, name="v"):
        b = Buf(ap, name)
        self.bufs.append(b)
        return b

    def _waits(self, eng, reads, writes, join, is_dma):
        for b in reads:
            for tok in b.w.values():
                if tok[0] == eng.name and not is_dma and eng.pe:
                    continue
                eng.wait(tok)
        for b in writes:
            if not join:
                for tok in b.w.values():
                    if tok[0] == eng.name and not is_dma:
                        continue
                    eng.wait(tok)
            for tok in b.r.values():
                if tok[0] == eng.name and not is_dma:
                    continue
                eng.wait(tok)

    def _record(self, tok, reads, writes, join):
        for b in writes:
            if join:
                b.w[tok[0]] = tok
            else:
                b.w = {tok[0]: tok}
            b.r = {}
        for b in reads:
            if b not in writes:
                b.r[tok[0]] = tok

    def op(self, eng, fn, reads=(), writes=(), join=False):
        self._waits(eng, reads, writes, join, False)
        inst = fn(eng.h)
        eng.n += 1
        inst.then_inc(eng.sem, 1)
        tok = (eng.name, eng.sem, eng.n)
        self._record(tok, reads, writes, join)
        return inst

    def dma(self, q, fn, reads=(), writes=(), join=False):
        self._waits(q, reads, writes, join, True)
        k = len(q.dma_sems)
        i = q.dma_n % k
        gen = q.dma_n // k
        sem = q.dma_sems[i]
        key = f"{q.name}_d{i}"
        if gen > 0:
            q.wait((key, sem, 16 * gen))
        inst = fn(q.h)
        inst.then_inc(sem, 16)
        q.dma_n += 1
        tok = (key, sem, 16 * (gen + 1))
        self._record(tok, reads, writes, join)
        return inst

    def barrier(self):
        toks = []
        for e in self.engs:
            if e.n:
                toks.append((e.name, e.sem, e.n))
            k = len(e.dma_sems)
            for i in range(min(k, e.dma_n)):
                cnt = (e.dma_n - 1 - i) // k + 1
                toks.append((f"{e.name}_d{i}", e.dma_sems[i], 16 * cnt))
        for e in self.engs:
            for tok in toks:
                if tok[0] == e.name:
                    continue
                e.wait(tok)
        for b in self.bufs:
            b.w = {}
            b.r = {}

    def finish(self):
        self.barrier()


D = 2048
S = 2048
NT = S // 128
DEPTH = 2
TS = 8
IN_W = 4632
TOK0, TOK1 = 512, 3608
NTOKC = TOK1 - TOK0
ALPHA = float((2 * DEPTH) ** 0.25)
LN_EPS = 1e-5
GELU_C = 0.7978845608028654
NEG = -30000.0


class Ctx:
    pass


def emit_gelu(fw, eng_dve, eng_act, out_ap, in_ap, tmp_ap, bufs_r, bufs_w, tmpbuf):
    fw.op(eng_act, lambda e: e.activation(out=tmp_ap, in_=in_ap, func=AF.Square), reads=bufs_r, writes=[tmpbuf])
    fw.op(eng_dve, lambda e: e.tensor_scalar(out=tmp_ap, in0=tmp_ap, scalar1=0.044715, scalar2=1.0, op0=ALU.mult, op1=ALU.add),
          reads=[tmpbuf], writes=[tmpbuf])
    fw.op(eng_dve, lambda e: e.tensor_tensor(out=tmp_ap, in0=tmp_ap, in1=in_ap, op=ALU.mult), reads=bufs_r + [tmpbuf], writes=[tmpbuf])
    fw.op(eng_act, lambda e: e.activation(out=tmp_ap, in_=tmp_ap, func=AF.Sigmoid, scale=2.0 * GELU_C), reads=[tmpbuf], writes=[tmpbuf])
    fw.op(eng_dve, lambda e: e.tensor_tensor(out=out_ap, in0=tmp_ap, in1=in_ap, op=ALU.mult), reads=bufs_r + [tmpbuf], writes=bufs_w)


def phase_proj(fw, C, l, x_src, nt, Htok, HfT, xT, tagp):
    nc = fw.nc
    ntok = nt * 128 if nt > 0 else TS
    P = 128 if nt > 0 else TS
    ntile = max(nt, 1)
    for t in range(ntile):
        xt = C.xin[t % 2]
        fw.dma(fw.sp, lambda h: h.dma_start(out=xt[0:P, :], in_=x_src[t * 128:t * 128 + P, :]), reads=[x_src], writes=[xt])
        for cb in range(4):
            ps = C.ps[(t * 4 + cb) % 4]
            for k in range(4):
                c = cb * 4 + k
                fw.op(fw.pe, lambda e: e.transpose(out=ps[:, k * 128:k * 128 + P], in_=xt[0:P, c * 128:(c + 1) * 128], identity=C.ident[0:P, 0:P]),
                      reads=[xt, C.ident], writes=[ps], join=(k > 0))
            eng = fw.dve if cb % 2 == 0 else fw.act
            src = ps[:, :].rearrange("p (k q) -> p k q", k=4)[:, :, 0:P]
            dst = xT[:, cb * 4:(cb + 1) * 4, t * 128:t * 128 + P]
            if eng is fw.dve:
                fw.op(eng, lambda e: e.tensor_copy(out=dst, in_=src), reads=[ps], writes=[xT], join=True)
            else:
                fw.op(eng, lambda e: e.copy(out=dst, in_=src), reads=[ps], writes=[xT], join=True)
    w_in = C.w_in
    ncol = [(TOK0 + j * 512, min(512, TOK1 - (TOK0 + j * 512))) for j in range((NTOKC + 511) // 512)]
    it = 0
    for j, (c0, cw) in enumerate(ncol):
        wb = C.wbuf[j % 2]
        fw.dma(fw.pool, lambda h: h.dma_start(out=wb[:, :, 0:cw], in_=w_in[l, :, c0:c0 + cw].rearrange("(c p) n -> p c n", p=128)),
               reads=[w_in], writes=[wb])
        for t in range(ntile):
            ps = C.ps[it % 4]
            for c in range(16):
                fw.op(fw.pe, lambda e: e.matmul(ps[0:P, 0:cw], lhsT=xT[:, c, t * 128:t * 128 + P], rhs=wb[:, c, 0:cw], start=(c == 0), stop=(c == 15)),
                      reads=[xT, wb], writes=[ps], join=(c > 0))
            hb = C.hbuf[it % 4]
            if it % 2 == 0:
                fw.op(fw.dve, lambda e: e.tensor_copy(out=hb[0:P, 0:cw], in_=ps[0:P, 0:cw]), reads=[ps], writes=[hb])
            else:
                fw.op(fw.act, lambda e: e.copy(out=hb[0:P, 0:cw], in_=ps[0:P, 0:cw]), reads=[ps], writes=[hb])
            fw.dma(fw.sp, lambda h: h.dma_start(out=Htok[t * 128:t * 128 + P, c0 - TOK0:c0 - TOK0 + cw], in_=hb[0:P, 0:cw]),
                   reads=[hb], writes=[Htok], join=True)
            it += 1
    fcols = [(0, 0), (128, 128), (256, 256), (384, 384)] + [(3608 + i * 128, 512 + i * 128) for i in range(8)]
    TB = 512 if nt > 0 else TS
    ntb = max(ntok // 512, 1)
    for j, (c0, r0) in enumerate(fcols):
        wb = C.wbuf[j % 2]
        fw.dma(fw.pool, lambda h: h.dma_start(out=wb[:, :, 0:128], in_=w_in[l, :, c0:c0 + 128].rearrange("(c p) n -> p c n", p=128)),
               reads=[w_in], writes=[wb])
        for tb in range(ntb):
            ps = C.ps[it % 4]
            for c in range(16):
                fw.op(fw.pe, lambda e: e.matmul(ps[:, 0:TB], lhsT=wb[:, c, 0:128], rhs=xT[:, c, tb * 512:tb * 512 + TB], start=(c == 0), stop=(c == 15)),
                      reads=[xT, wb], writes=[ps], join=(c > 0))
            hb = C.hbuf[it % 4]
            if r0 < 512:
                tmp = C.htmp
                emit_gelu(fw, fw.dve, fw.act, hb[:, 0:TB], ps[:, 0:TB], tmp[:, 0:TB], [ps], [hb], tmp)
            elif it % 2 == 0:
                fw.op(fw.dve, lambda e: e.tensor_copy(out=hb[:, 0:TB], in_=ps[:, 0:TB]), reads=[ps], writes=[hb])
            else:
                fw.op(fw.act, lambda e: e.copy(out=hb[:, 0:TB], in_=ps[:, 0:TB]), reads=[ps], writes=[hb])
            fw.dma(fw.sp, lambda h: h.dma_start(out=HfT[r0:r0 + 128, tb * 512:tb * 512 + TB], in_=hb[:, 0:TB]),
                   reads=[hb], writes=[HfT], join=True)
            it += 1


def host_consts():
    c = {}
    c["ident"] = np.eye(128, dtype=np.float32)
    half = 64
    inv = (np.float32(10000.0) ** (-np.arange(half, dtype=np.float32) / np.float32(half))).astype(np.float32)
    def rope_tab(pos):
        ang = pos.astype(np.float32)[:, None] * inv[None, :]
        cs, sn = np.cos(ang).astype(np.float32), np.sin(ang).astype(np.float32)
        sc = np.float32(128 ** -0.5)
        return np.stack([cs, sn, cs * sc, sn * sc], axis=1).astype(np.float32)
    c["rope_p"] = rope_tab(np.arange(S))
    c["rope_s"] = rope_tab(16384 + np.arange(TS))
    q = np.arange(128)[:, None, None]
    t = np.arange(NT)[None, :, None]
    n = np.arange(64)[None, None, :]
    qpos = 128 * t + q
    c["cmpbias"] = np.where(32 * n + 31 <= qpos, 0.0, NEG).astype(np.float32)
    c["cmpvalid"] = (qpos[:, :, 0] >= 31).astype(np.float32)
    m = np.arange(32)[None, None, :]
    cur = qpos // 64
    valid = 64 * m <= qpos
    forced = (m == 0) | (m == cur) | (m == cur - 1)
    c["selA"] = (valid & ~forced).astype(np.float32)
    c["selB"] = np.where(valid, np.where(forced, 1.0e4, 0.0), -1.0e9).astype(np.float32)
    qq = np.arange(128)[:, None]; kk = np.arange(128)[None, :]
    c["causal"] = np.where(kk <= qq, 0.0, NEG).astype(np.float32)
    c["acausal"] = np.where(kk >= qq, 0.0, NEG).astype(np.float32)
    tok = np.arange(128)[:, None]
    c["mask4"] = (tok // 32 == np.arange(4)[None, :]).astype(np.float32)
    c["maskpad"] = (np.arange(64)[None, None, :] == 4 * np.arange(NT)[None, :, None] + (tok // 32)[:, :, None]).astype(np.float32)
    c["triu"] = (qq <= kk).astype(np.float32)
    c["onesdiv"] = np.full((128, 128), 1.0 / 128, np.float32)
    c["iota16"] = np.tile(np.arange(16, dtype=np.float32)[None, :], (128, 1))
    c.update(sample_consts())
    return c


CONST_SHAPES = {"ident": [128, 128], "rope_p": [S, 4, 64], "rope_s": [TS, 4, 64], "cmpbias": [128, NT, 64], "cmpvalid": [128, NT],
                "selA": [128, NT, 32], "selB": [128, NT, 32], "causal": [128, 128], "acausal": [128, 128], "mask4": [128, 4],
                "maskpad": [128, NT, 64], "triu": [128, 128], "onesdiv": [128, 128], "iota16": [128, 16]}


def load_const(fw, C, name, dtype=F32):
    shp = CONST_SHAPES[name]
    b = fw.sbuf("c_" + name, shp, F32)
    src = C.cd[name]
    idx = tuple(slice(None) for _ in shp)
    fw.dma(fw.sp, lambda h: h.dma_start(out=b[idx], in_=src[idx]), reads=[src], writes=[b])
    return b


def load_colvec(fw, C, vec_ap, n, name, srcbuf):
    rows = fw.sbuf(name + "_r", [n, 128], F32)
    fw.dma(fw.sp, lambda h: h.dma_start(out=rows[:, :], in_=vec_ap.rearrange("(j p) -> j p", p=128)), reads=[srcbuf], writes=[rows])
    ps = C.ps[0]
    fw.op(fw.pe, lambda e: e.transpose(out=ps[:, 0:n], in_=rows[0:n, :], identity=C.ident[0:n, 0:n]), reads=[rows, C.ident], writes=[ps])
    col = fw.sbuf(name, [128, n], F32)
    fw.op(fw.dve, lambda e: e.tensor_copy(out=col[:, :], in_=ps[:, 0:n]), reads=[ps], writes=[col])
    return col


def emit_rope(fw, x1, x2, cs, sn, tmp, shape, rbufs, xbuf, tmpbuf):
    nd = len(shape)
    def bc(a):
        v = a
        for _ in range(nd - 2):
            v = v.unsqueeze(1)
        return v.to_broadcast(list(shape))
    t1, t2, t3, t4 = tmp
    fw.op(fw.dve, lambda e: e.tensor_tensor(out=t1, in0=x1, in1=bc(cs), op=ALU.mult), reads=[xbuf] + rbufs, writes=[tmpbuf[0]])
    fw.op(fw.dve, lambda e: e.tensor_tensor(out=t2, in0=x2, in1=bc(sn), op=ALU.mult), reads=[xbuf] + rbufs, writes=[tmpbuf[1]])
    fw.op(fw.dve, lambda e: e.tensor_tensor(out=t3, in0=x2, in1=bc(cs), op=ALU.mult), reads=[xbuf] + rbufs, writes=[tmpbuf[2]])
    fw.op(fw.dve, lambda e: e.tensor_tensor(out=t4, in0=x1, in1=bc(sn), op=ALU.mult), reads=[xbuf] + rbufs, writes=[tmpbuf[3]])
    fw.op(fw.dve, lambda e: e.tensor_tensor(out=x1, in0=t1, in1=t2, op=ALU.subtract), reads=[tmpbuf[0], tmpbuf[1], tmpbuf[2], tmpbuf[3]], writes=[xbuf])
    fw.op(fw.dve, lambda e: e.tensor_tensor(out=x2, in0=t3, in1=t4, op=ALU.add), reads=[tmpbuf[2], tmpbuf[3]], writes=[xbuf])


def phase_kv_prompt(fw, C, l):
    P = 128
    C.maskpad = load_const(fw, C, "maskpad")
    wcol = fw.sbuf("wcol", [128, 4], F32)
    for wi, wsrc in enumerate([C.cmp_wk, C.cmp_wv]):
        for g in range(2):
            for blk in range(4):
                fw.dma(fw.sp, lambda h: h.dma_start(out=wcol[blk * 32:(blk + 1) * 32, wi * 2 + g:wi * 2 + g + 1],
                                                    in_=wsrc[l, g, :].rearrange("(j o) -> j o", o=1)), reads=[wsrc], writes=[wcol], join=True)
    Wck = fw.sbuf("Wck", [128, 2, 4], BF16)
    WcvPad = fw.sbuf("WcvPad", [128, 2, NT * 64], BF16)
    for g in range(2):
        fw.op(fw.dve, lambda e: e.tensor_scalar(out=Wck[:, g, :], in0=C.mask4[:, :], scalar1=wcol[:, g:g + 1], scalar2=None, op0=ALU.mult),
              reads=[C.mask4, wcol], writes=[Wck], join=True)
        fw.op(fw.dve, lambda e: e.tensor_scalar(out=WcvPad[:, g, :], in0=C.maskpad[:, :, :].rearrange("p t n -> p (t n)"), scalar1=wcol[:, 2 + g:3 + g], scalar2=None, op0=ALU.mult),
              reads=[C.maskpad, wcol], writes=[WcvPad], join=True)
    vacc = fw.sbuf("vacc", [64, 256], F32)
    kvin = [fw.sbuf(f"kvin{i}", [128, 1536], F32) for i in range(2)]
    kvb = [fw.sbuf(f"kvb{i}", [128, 1536], BF16) for i in range(2)]
    rtab = [fw.sbuf(f"rtab{i}", [128, 4, 64], F32) for i in range(2)]
    rtmp = [fw.sbuf(f"rtmp{i}", [128, 384], F32) for i in range(4)]
    for t in range(NT):
        kv = kvin[t % 2]
        kb = kvb[t % 2]
        rt = rtab[t % 2]
        fw.dma(fw.sp, lambda h: h.dma_start(out=kv[:, :], in_=C.Htok[t * 128:(t + 1) * 128, 1536:3072]), reads=[C.Htok], writes=[kv])
        fw.dma(fw.sp, lambda h: h.dma_start(out=rt[:, :, :], in_=C.cd["rope_p"][t * 128:(t + 1) * 128, :, :]), reads=[C.cd["rope_p"]], writes=[rt])
        kview = kv[:, :].rearrange("p (a v g h d) -> p a v g h d", a=3, v=2, g=2, h=2, d=64)
        x1 = kview[:, :, 0, :, 0, :]
        x2 = kview[:, :, 0, :, 1, :]
        tv = [r[:, :].rearrange("p (a g d) -> p a g d", a=3, g=2) for r in rtmp]
        emit_rope(fw, x1, x2, rt[:, 0, :], rt[:, 1, :], tv, [128, 3, 2, 64], [rt], kv, rtmp)
        for a in range(6):
            fw.dma(fw.sp, lambda h: h.dma_start(out=C.out_kv[l, a, t * 128:(t + 1) * 128, :], in_=kv[:, a * 256:(a + 1) * 256]),
                   reads=[kv], writes=[C.out_kv], join=True)
        fw.op(fw.act, lambda e: e.copy(out=kb[:, :], in_=kv[:, :]), reads=[kv], writes=[kb])
        psb = C.psb[t % 2]
        for i, (a, g) in enumerate([(1, 0), (1, 1), (2, 0), (2, 1)]):
            c0 = a * 512 + g * 128
            fw.op(fw.pe, lambda e: e.transpose(out=psb[:, i * 128:(i + 1) * 128], in_=kb[:, c0:c0 + 128], identity=C.identb[:, :]),
                  reads=[kb, C.identb], writes=[psb], join=(i > 0))
        fw.op(fw.dve, lambda e: e.tensor_copy(out=C.kT[:, :, t * 128:(t + 1) * 128], in_=psb[:, 0:512].rearrange("p (i q) -> p i q", i=4)),
              reads=[psb], writes=[C.kT], join=True)
        ps = C.ps[t % 4]
        for g in range(2):
            fw.op(fw.pe, lambda e: e.matmul(ps[:, g * 4:(g + 1) * 4], lhsT=kb[:, g * 128:(g + 1) * 128], rhs=Wck[:, g, :], start=True, stop=True),
                  reads=[kb, Wck], writes=[ps], join=(g > 0))
        for g in range(2):
            fw.op(fw.pe, lambda e: e.matmul(ps[0:64, 128 + g * 128:256 + g * 128], lhsT=WcvPad[:, g, t * 64:(t + 1) * 64], rhs=kb[:, 256 + g * 128:384 + g * 128], start=True, stop=True),
                  reads=[kb, WcvPad], writes=[ps], join=True)
        fw.op(fw.dve, lambda e: e.tensor_copy(out=C.kcmpT[:, :, t * 4:(t + 1) * 4], in_=ps[:, 0:8].rearrange("p (g n) -> p g n", g=2)),
              reads=[ps], writes=[C.kcmpT], join=True)
        if t == 0:
            fw.op(fw.dve, lambda e: e.tensor_copy(out=vacc[:, :], in_=ps[0:64, 128:384]), reads=[ps], writes=[vacc])
        else:
            fw.op(fw.dve, lambda e: e.tensor_tensor(out=vacc[:, :], in0=vacc[:, :], in1=ps[0:64, 128:384], op=ALU.add), reads=[ps, vacc], writes=[vacc])
        vsrc = kb[:, 512:1536].rearrange("p (a v c) -> p a v c", a=2, v=2)[:, :, 1, :]
        fw.op(fw.pool, lambda e: e.tensor_copy(out=C.V[:, t, :, :], in_=vsrc), reads=[kb], writes=[C.V], join=True)
    fw.op(fw.dve, lambda e: e.tensor_copy(out=C.vcmp[:, :], in_=vacc[:, :]), reads=[vacc], writes=[C.vcmp])


def emit_ln_partition(fw, C, y, blk_cols, g_ap, b_ap, gb_bufs, out_ap_fn, tmpA, tmpB, func=None):
    ncols = blk_cols
    for c0 in range(0, ncols, 512):
        cw = min(512, ncols - c0)
        ps1 = C.ps[0]
        ps2 = C.ps[1]
        ysl = y[:, c0:c0 + cw]
        fw.op(fw.pe, lambda e: e.matmul(ps1[:, 0:cw], lhsT=C.onesdiv[:, :], rhs=ysl, start=True, stop=True), reads=[C.onesdiv, y], writes=[ps1])
        fw.op(fw.dve, lambda e: e.tensor_tensor(out=ysl, in0=ysl, in1=ps1[:, 0:cw], op=ALU.subtract), reads=[ps1, y], writes=[y])
        fw.op(fw.act, lambda e: e.activation(out=tmpA[:, 0:cw], in_=ysl, func=AF.Square), reads=[y], writes=[tmpA])
        fw.op(fw.pe, lambda e: e.matmul(ps2[:, 0:cw], lhsT=C.onesdiv[:, :], rhs=tmpA[:, 0:cw], start=True, stop=True), reads=[C.onesdiv, tmpA], writes=[ps2])
        fw.op(fw.act, lambda e: e.activation(out=tmpB[:, 0:cw], in_=ps2[:, 0:cw], func=AF.Sqrt, bias=C.epsc[:, 0:1], scale=1.0), reads=[ps2, C.epsc], writes=[tmpB])
        fw.op(fw.dve, lambda e: e.reciprocal(out=tmpB[:, 0:cw], in_=tmpB[:, 0:cw]), reads=[tmpB], writes=[tmpB])
        fw.op(fw.dve, lambda e: e.tensor_tensor(out=tmpA[:, 0:cw], in0=ysl, in1=tmpB[:, 0:cw], op=ALU.mult), reads=[y, tmpB], writes=[tmpA])
        oap, obuf = out_ap_fn(c0, cw)
        if func is None:
            fw.op(fw.dve, lambda e: e.tensor_scalar(out=oap, in0=tmpA[:, 0:cw], scalar1=g_ap, scalar2=b_ap, op0=ALU.mult, op1=ALU.add),
                  reads=[tmpA] + gb_bufs, writes=[obuf], join=True)
        else:
            fw.op(fw.dve, lambda e: e.tensor_scalar(out=tmpA[:, 0:cw], in0=tmpA[:, 0:cw], scalar1=g_ap, scalar2=b_ap, op0=ALU.mult, op1=ALU.add),
                  reads=[tmpA] + gb_bufs, writes=[tmpA])
            fw.op(fw.act, lambda e: e.activation(out=oap, in_=tmpA[:, 0:cw], func=func), reads=[tmpA], writes=[obuf], join=True)


def phase_conv(fw, C, l, HfT, T, hist_src, out_conv, convT):
    fw.push()
    dwr = fw.sbuf("dwr", [31, 512], F32)
    fw.dma(fw.sp, lambda h: h.dma_start(out=dwr[:, :], in_=C.conv_dw[l, :, :]), reads=[C.conv_dw], writes=[dwr])
    dwT = fw.sbuf("dwT", [128, 4, 31], F32)
    ps = C.ps[2]
    for cc in range(4):
        fw.op(fw.pe, lambda e: e.transpose(out=ps[:, cc * 32:cc * 32 + 31], in_=dwr[0:31, cc * 128:(cc + 1) * 128], identity=C.ident[0:31, 0:31]),
              reads=[dwr, C.ident], writes=[ps], join=(cc > 0))
    fw.op(fw.dve, lambda e: e.tensor_copy(out=dwT[:, :, :], in_=ps[:, 0:128].rearrange("p (c k) -> p c k", c=4)[:, :, 0:31]), reads=[ps], writes=[dwT])
    dbc = load_colvec(fw, C, C.conv_db[l, :], 4, "dbc", C.conv_db)
    lng = load_colvec(fw, C, C.conv_ln_g[l, :], 4, "clng", C.conv_ln_g)
    lnb = load_colvec(fw, C, C.conv_ln_b[l, :], 4, "clnb", C.conv_ln_b)
    pw = fw.sbuf("pw", [128, 4, 512], BF16)
    fw.dma(fw.pool, lambda h: h.dma_start(out=pw[:, :, :], in_=C.conv_pw[l, :, :].rearrange("(c p) n -> p c n", p=128)), reads=[C.conv_pw], writes=[pw])
    zT = fw.sbuf("zT", [128, 4, T], BF16)
    pcs = fw.sbuf("pcs", [30, 512], F32)
    psc = C.ps[3]
    sets = []
    for i in range(2):
        sets.append(dict(ca=fw.sbuf(f"cca{i}", [128, T], F32), cb=fw.sbuf(f"ccb{i}", [128, T], F32),
                         cseq=fw.sbuf(f"cseq{i}", [128, 30 + T], F32), y=fw.sbuf(f"cy{i}", [128, T], F32)))
    tmpA = fw.sbuf("ctmpA", [128, 512], F32)
    tmpB = fw.sbuf("ctmpB", [128, 512], F32)
    hs = None
    if hist_src is not None:
        hs = fw.sbuf("hist_r", [30, 512], F32)
        fw.dma(fw.sp, lambda h: h.dma_start(out=hs[:, :], in_=hist_src), reads=[C.state_conv], writes=[hs])
    for cc in range(4):
        s_ = sets[cc % 2]
        ca, cb, cseq, y = s_["ca"], s_["cb"], s_["cseq"], s_["y"]
        fw.dma(fw.sp, lambda h: h.dma_start(out=ca[:, :], in_=HfT[512 + cc * 128:640 + cc * 128, 0:T]), reads=[HfT], writes=[ca])
        fw.dma(fw.sp, lambda h: h.dma_start(out=cb[:, :], in_=HfT[1024 + cc * 128:1152 + cc * 128, 0:T]), reads=[HfT], writes=[cb])
        fw.op(fw.act, lambda e: e.activation(out=cb[:, :], in_=cb[:, :], func=AF.Sigmoid), reads=[cb], writes=[cb])
        if hs is None:
            fw.op(fw.pool, lambda e: e.memset(cseq[:, 0:30], 0.0), writes=[cseq])
        else:
            pst = C.ps[0]
            fw.op(fw.pe, lambda e: e.transpose(out=pst[:, 0:30], in_=hs[0:30, cc * 128:(cc + 1) * 128], identity=C.ident[0:30, 0:30]), reads=[hs, C.ident], writes=[pst])
            fw.op(fw.dve, lambda e: e.tensor_copy(out=cseq[:, 0:30], in_=pst[:, 0:30]), reads=[pst], writes=[cseq])
        fw.op(fw.dve, lambda e: e.tensor_tensor(out=cseq[:, 30:30 + T], in0=ca[:, :], in1=cb[:, :], op=ALU.mult), reads=[ca, cb], writes=[cseq], join=True)
        fw.op(fw.pe, lambda e: e.transpose(out=psc[0:30, cc * 128:(cc + 1) * 128], in_=cseq[:, T:T + 30], identity=C.ident[:, :]),
              reads=[cseq, C.ident], writes=[psc], join=(cc > 0))
        eng = fw.dve
        fw.op(eng, lambda e: e.tensor_scalar(out=y[:, :], in0=cseq[:, 0:T], scalar1=dwT[:, cc, 0:1], scalar2=dbc[:, cc:cc + 1], op0=ALU.mult, op1=ALU.add),
              reads=[cseq, dwT, dbc], writes=[y])
        for k in range(1, 31):
            fw.op(eng, lambda e: e.scalar_tensor_tensor(out=y[:, :], in0=cseq[:, k:k + T], scalar=dwT[:, cc, k:k + 1], in1=y[:, :], op0=ALU.mult, op1=ALU.add),
                  reads=[cseq, dwT, y], writes=[y])
        emit_ln_partition(fw, C, y, T, lng[:, cc:cc + 1], lnb[:, cc:cc + 1], [lng, lnb],
                          lambda c0, cw: (zT[:, cc, c0:c0 + cw], zT), tmpA, tmpB, func=AF.Silu)
    fw.op(fw.dve, lambda e: e.tensor_copy(out=pcs[:, :], in_=psc[0:30, 0:512]), reads=[psc], writes=[pcs])
    fw.dma(fw.sp, lambda h: h.dma_start(out=out_conv, in_=pcs[:, :]), reads=[pcs], writes=[C.out_conv_buf])
    it = 0
    for co in range(4):
        for c0 in range(0, T, 512):
            cw = min(512, T - c0)
            ps = C.ps[it % 4]
            for cc in range(4):
                fw.op(fw.pe, lambda e: e.matmul(ps[:, 0:cw], lhsT=pw[:, cc, co * 128:(co + 1) * 128], rhs=zT[:, cc, c0:c0 + cw], start=(cc == 0), stop=(cc == 3)),
                      reads=[pw, zT], writes=[ps], join=(cc > 0))
            if it % 2 == 0:
                fw.op(fw.dve, lambda e: e.tensor_copy(out=convT[:, co, c0:c0 + cw], in_=ps[:, 0:cw]), reads=[ps], writes=[convT], join=True)
            else:
                fw.op(fw.act, lambda e: e.copy(out=convT[:, co, c0:c0 + cw], in_=ps[:, 0:cw]), reads=[ps], writes=[convT], join=True)
            it += 1
    fw.pop()


def emit_ln_free(fw, C, x, P, g_bc, b_bc, junk, stat):
    xs = x[0:P, :]
    fw.op(fw.dve, lambda e: e.tensor_reduce(out=stat[0:P, 0:1], in_=xs, axis=AX.X, op=ALU.add), reads=[x], writes=[stat])
    fw.op(fw.dve, lambda e: e.tensor_scalar(out=stat[0:P, 1:2], in0=stat[0:P, 0:1], scalar1=1.0 / D, scalar2=None, op0=ALU.mult), reads=[stat], writes=[stat])
    fw.op(fw.dve, lambda e: e.tensor_scalar(out=xs, in0=xs, scalar1=stat[0:P, 1:2], scalar2=None, op0=ALU.subtract), reads=[x, stat], writes=[x])
    fw.op(fw.act, lambda e: e.activation(out=junk[0:P, :], in_=xs, func=AF.Square, accum_out=stat[0:P, 2:3]), reads=[x], writes=[junk, stat])
    fw.op(fw.act, lambda e: e.activation(out=stat[0:P, 3:4], in_=stat[0:P, 2:3], func=AF.Sqrt, bias=C.epsc[0:P, 0:1], scale=1.0 / D), reads=[stat, C.epsc], writes=[stat])
    fw.op(fw.dve, lambda e: e.reciprocal(out=stat[0:P, 4:5], in_=stat[0:P, 3:4]), reads=[stat], writes=[stat])
    fw.op(fw.dve, lambda e: e.scalar_tensor_tensor(out=xs, in0=xs, scalar=stat[0:P, 4:5], in1=g_bc[0:P, :], op0=ALU.mult, op1=ALU.mult), reads=[x, stat, g_bc], writes=[x])
    fw.op(fw.dve, lambda e: e.tensor_tensor(out=xs, in0=xs, in1=b_bc[0:P, :], op=ALU.add), reads=[x, b_bc], writes=[x])


def load_bcast(fw, name, vec_ap, n, srcbuf, P=128):
    b = fw.sbuf(name, [128, n], F32)
    fw.dma(fw.sp, lambda h: h.dma_start(out=b[0:P, :], in_=vec_ap.partition_broadcast(P)), reads=[srcbuf], writes=[b])
    return b


def emit_gmlp(fw, C, G, l, t, P, Htok, HfT, mixg, out_gm_ap):
    gv = G["gvin"][t % 2]
    gt = G["gtmp"]
    st = G["gstat"]
    fw.dma(fw.sp, lambda h: h.dma_start(out=gv[0:P, :], in_=Htok[t * 128:t * 128 + P, 0:512]), reads=[Htok], writes=[gv])
    emit_gelu(fw, fw.dve, fw.act, gv[0:P, :], gv[0:P, :], gt[0:P, :], [gv], [gv], gt)
    g3 = gv[0:P, :].rearrange("p (h c) -> p h c", h=4)
    t3 = gt[0:P, :].rearrange("p (h c) -> p h c", h=4)
    fw.op(fw.dve, lambda e: e.tensor_reduce(out=st[0:P, 0:4], in_=g3, axis=AX.X, op=ALU.add), reads=[gv], writes=[st])
    fw.op(fw.dve, lambda e: e.tensor_scalar(out=st[0:P, 4:8], in0=st[0:P, 0:4], scalar1=1.0 / 128, scalar2=None, op0=ALU.mult), reads=[st], writes=[st])
    fw.op(fw.dve, lambda e: e.tensor_tensor(out=g3, in0=g3, in1=st[0:P, 4:8].unsqueeze(2).to_broadcast([P, 4, 128]), op=ALU.subtract), reads=[gv, st], writes=[gv])
    fw.op(fw.dve, lambda e: e.tensor_tensor(out=t3, in0=g3, in1=g3, op=ALU.mult), reads=[gv], writes=[gt])
    fw.op(fw.dve, lambda e: e.tensor_reduce(out=st[0:P, 8:12], in_=t3, axis=AX.X, op=ALU.add), reads=[gt], writes=[st])
    fw.op(fw.act, lambda e: e.activation(out=st[0:P, 12:16], in_=st[0:P, 8:12], func=AF.Sqrt, bias=C.epsc[0:P, 0:1], scale=1.0 / 128), reads=[st, C.epsc], writes=[st])
    fw.op(fw.dve, lambda e: e.reciprocal(out=st[0:P, 16:20], in_=st[0:P, 12:16]), reads=[st], writes=[st])
    fw.op(fw.dve, lambda e: e.tensor_tensor(out=g3, in0=g3, in1=st[0:P, 16:20].unsqueeze(2).to_broadcast([P, 4, 128]), op=ALU.mult), reads=[gv, st], writes=[gv])
    fw.op(fw.dve, lambda e: e.tensor_tensor(out=gv[0:P, :], in0=gv[0:P, :], in1=G["gmg"][0:P, :], op=ALU.mult), reads=[gv, G["gmg"]], writes=[gv])
    fw.op(fw.dve, lambda e: e.tensor_tensor(out=gv[0:P, :], in0=gv[0:P, :], in1=G["gmb"][0:P, :], op=ALU.add), reads=[gv, G["gmb"]], writes=[gv])
    if out_gm_ap is not None:
        fw.dma(fw.sp, lambda h: h.dma_start(out=out_gm_ap, in_=gv[0:P, :]), reads=[gv], writes=[C.out_gm_buf], join=True)
    vnb = G["vnb"]
    fw.op(fw.act, lambda e: e.copy(out=vnb[0:P, :], in_=gv[0:P, :]), reads=[gv], writes=[vnb])
    ps = C.ps[0]
    for h in range(4):
        fw.op(fw.pe, lambda e: e.matmul(ps[:, h * 128:h * 128 + P], lhsT=vnb[0:P, h * 128:(h + 1) * 128], rhs=G["trilWT"][0:P, h, 0:P], start=True, stop=True),
              reads=[vnb, G["trilWT"]], writes=[ps], join=(h > 0))
    ut = G["ut"][t % 2]
    fw.dma(fw.sp, lambda h_: h_.dma_start(out=ut[:, :, 0:P], in_=HfT[0:512, t * 128:t * 128 + P].rearrange("(h c) q -> c h q", h=4)), reads=[HfT], writes=[ut])
    sv = gt[:, :].rearrange("p (h c) -> p h c", h=4)[:, :, 0:P]
    fw.op(fw.dve, lambda e: e.tensor_tensor(out=sv, in0=ps[:, 0:512].rearrange("p (h c) -> p h c", h=4)[:, :, 0:P], in1=G["bsb"][:, :, 0:P], op=ALU.add),
          reads=[ps, G["bsb"]], writes=[gt])
    fw.op(fw.dve, lambda e: e.tensor_tensor(out=mixg[:, :, 0:P], in0=sv, in1=ut[:, :, 0:P], op=ALU.mult), reads=[gt, ut], writes=[mixg])


def gmlp_consts(fw, C, l, P):
    G = {}
    G["gvin"] = [fw.sbuf("gvin0", [128, 512], F32)] * 2
    G["gtmp"] = fw.sbuf("gtmp", [128, 512], F32)
    G["gstat"] = fw.sbuf("gstat", [128, 20], F32)
    G["vnb"] = fw.sbuf("vnb", [128, 512], BF16)
    G["ut"] = [fw.sbuf("ut0", [128, 4, 128], F32)] * 2
    G["gmg"] = load_bcast(fw, "gmg", C.gm_ln_g[l, :], 512, C.gm_ln_g, P)
    G["gmb"] = load_bcast(fw, "gmb", C.gm_ln_b[l, :], 512, C.gm_ln_b, P)
    bsb = fw.sbuf("bsb", [128, 4, 128], F32)
    fw.dma(fw.sp, lambda h: h.dma_start(out=bsb[:, :, :].rearrange("p h i -> p (h i)"), in_=C.gm_bs[l, :, :].rearrange("h i -> (h i)").partition_broadcast(128)),
           reads=[C.gm_bs], writes=[bsb])
    G["bsb"] = bsb
    wsr = fw.sbuf("wsr", [128, 4, 128], F32)
    fw.dma(fw.sp, lambda h: h.dma_start(out=wsr[:, :, :], in_=C.gm_ws[l, :, :, :].rearrange("h i j -> i h j")), reads=[C.gm_ws], writes=[wsr])
    ps = C.ps[1]
    for h in range(4):
        fw.op(fw.pe, lambda e: e.transpose(out=ps[:, h * 128:(h + 1) * 128], in_=wsr[:, h, :], identity=C.ident[:, :]), reads=[wsr, C.ident], writes=[ps], join=(h > 0))
    tw = fw.sbuf("trilWT", [128, 4, 128], BF16)
    fw.op(fw.dve, lambda e: e.tensor_tensor(out=tw[:, :, :], in0=ps[:, 0:512].rearrange("p (h i) -> p h i", h=4), in1=C.triu[:, :].unsqueeze(1).to_broadcast([128, 4, 128]), op=ALU.mult),
          reads=[ps, C.triu], writes=[tw])
    G["trilWT"] = tw
    return G


def emit_softmax_pv(fw, C, A, sbuf_s, nk, gate_ap, gate_buf, acc, v_fn, first, last, tagi, part="all"):
    st = A["st"][tagi]
    eb = A["eb"][tagi]
    pT = A["pT"][tagi]
    if part in ("all", "sm"):
      fw.op(fw.dve, lambda e: e.tensor_reduce(out=st[:, 0:1], in_=sbuf_s[:, 0:nk], axis=AX.X, op=ALU.max, negate=True), reads=[sbuf_s], writes=[st])
      fw.op(fw.act, lambda e: e.activation(out=eb[:, 0:nk], in_=sbuf_s[:, 0:nk], func=AF.Exp, bias=st[:, 0:1], scale=1.0, accum_out=st[:, 1:2]),
          reads=[sbuf_s, st], writes=[eb, st])
      fw.op(fw.dve, lambda e: e.reciprocal(out=st[:, 2:3], in_=st[:, 1:2]), reads=[st], writes=[st])
      fw.op(fw.dve, lambda e: e.tensor_tensor(out=st[:, 3:4], in0=st[:, 2:3], in1=gate_ap, op=ALU.mult), reads=[st, gate_buf], writes=[st])
      fw.op(fw.act, lambda e: e.mul(out=eb[:, 0:nk], in_=eb[:, 0:nk], mul=st[:, 3:4]), reads=[eb, st], writes=[eb])
    if part == "sm":
        return
    nkt = nk // 128
    for k0 in range(0, nkt, 8):
        kn = min(8, nkt - k0)
        psb = C.psb[A["psbi"] % 2]
        A["psbi"] += 1
        for j in range(kn):
            kt = k0 + j
            fw.op(fw.pe, lambda e: e.transpose(out=psb[:, j * 128:(j + 1) * 128], in_=eb[:, kt * 128:(kt + 1) * 128], identity=C.identb[:, :]),
                  reads=[eb, C.identb], writes=[psb], join=(j > 0))
        if (k0 // 8) % 2 == 0:
            fw.op(fw.dve, lambda e: e.tensor_copy(out=pT[:, k0 * 128:(k0 + kn) * 128], in_=psb[:, 0:kn * 128]), reads=[psb], writes=[pT], join=True)
        else:
            fw.op(fw.act, lambda e: e.copy(out=pT[:, k0 * 128:(k0 + kn) * 128], in_=psb[:, 0:kn * 128]), reads=[psb], writes=[pT], join=True)
    for kt in range(nkt):
        fw.op(fw.pe, lambda e: e.matmul(acc[:, 0:128], lhsT=v_fn(kt), rhs=pT[:, kt * 128:(kt + 1) * 128], start=(first and kt == 0), stop=(last and kt == nkt - 1)),
              reads=[C.V, pT], writes=[acc], join=not (first and kt == 0))


def emit_attn_prompt(fw, C, A, l, t, mixn):
    qt = A["qt"][t % 2]
    gt = A["gate"][t % 2]
    rt = A["rt"][t % 2]
    fw.dma(fw.sp, lambda h: h.dma_start(out=qt[:, :], in_=C.Htok[t * 128:(t + 1) * 128, 512:1536]), reads=[C.Htok], writes=[qt])
    fw.dma(fw.sp, lambda h: h.dma_start(out=gt[:, :], in_=C.Htok[t * 128:(t + 1) * 128, 3072:3096]), reads=[C.Htok], writes=[gt])
    fw.dma(fw.sp, lambda h: h.dma_start(out=rt[:, :, :], in_=C.cd["rope_p"][t * 128:(t + 1) * 128, :, :]), reads=[C.cd["rope_p"]], writes=[rt])
    qv = qt[:, :].rearrange("p (h x d) -> p h x d", h=8, x=2)
    tv = [A["ssb"][:, i * 512:(i + 1) * 512].rearrange("p (h d) -> p h d", h=8) for i in range(4)]
    emit_rope(fw, qv[:, :, 0, :], qv[:, :, 1, :], rt[:, 2, :], rt[:, 3, :], tv, [128, 8, 64], [rt], qt, A["rtmp"])
    qb = A["qb"]
    fw.op(fw.act, lambda e: e.copy(out=qb[:, :], in_=qt[:, :]), reads=[qt], writes=[qb])
    psb = C.psb[A["psbi"] % 2]
    A["psbi"] += 1
    for h in range(8):
        fw.op(fw.pe, lambda e: e.transpose(out=psb[:, h * 128:(h + 1) * 128], in_=qb[:, h * 128:(h + 1) * 128], identity=C.identb[:, :]),
              reads=[qb, C.identb], writes=[psb], join=(h > 0))
    qT = A["qT"]
    fw.op(fw.dve, lambda e: e.tensor_copy(out=qT[:, :], in_=psb[:, :]), reads=[psb], writes=[qT])
    sig = A["sig"]
    fw.op(fw.act, lambda e: e.activation(out=sig[:, :], in_=gt[:, :], func=AF.Sigmoid), reads=[gt], writes=[sig])
    sig3 = sig[:, :].rearrange("p (h b) -> p h b", b=3)
    sc, ec, st4 = A["sc"], A["ec"], A["st4"]
    for g in range(2):
        psc = C.ps[0]
        for r in range(4):
            fw.op(fw.pe, lambda e: e.matmul(psc[:, r * 64:(r + 1) * 64], lhsT=qT[:, (g * 4 + r) * 128:(g * 4 + r + 1) * 128], rhs=C.kcmpT[:, g, :], start=True, stop=True),
                  reads=[qT, C.kcmpT], writes=[psc], join=(r > 0))
        sc3 = sc[:, :].rearrange("p (r n) -> p r n", r=4)
        ec3 = ec[:, :].rearrange("p (r n) -> p r n", r=4)
        fw.op(fw.dve, lambda e: e.tensor_tensor(out=sc3, in0=psc[:, 0:256].rearrange("p (r n) -> p r n", r=4),
                                                in1=C.cmpbias[:, t, :].unsqueeze(1).to_broadcast([128, 4, 64]), op=ALU.add), reads=[psc, C.cmpbias], writes=[sc])
        fw.op(fw.dve, lambda e: e.tensor_reduce(out=st4[:, 0:4], in_=sc3, axis=AX.X, op=ALU.max), reads=[sc], writes=[st4])
        fw.op(fw.dve, lambda e: e.tensor_tensor(out=sc3, in0=sc3, in1=st4[:, 0:4].unsqueeze(2).to_broadcast([128, 4, 64]), op=ALU.subtract), reads=[sc, st4], writes=[sc])
        fw.op(fw.act, lambda e: e.activation(out=ec[:, :], in_=sc[:, :], func=AF.Exp), reads=[sc], writes=[ec])
        fw.op(fw.dve, lambda e: e.tensor_reduce(out=st4[:, 4:8], in_=ec3, axis=AX.X, op=ALU.add), reads=[ec], writes=[st4])
        fw.op(fw.dve, lambda e: e.tensor_scalar(out=st4[:, 4:8], in0=st4[:, 4:8], scalar1=1e-30, scalar2=None, op0=ALU.max), reads=[st4], writes=[st4])
        fw.op(fw.dve, lambda e: e.reciprocal(out=st4[:, 8:12], in_=st4[:, 4:8]), reads=[st4], writes=[st4])
        fw.op(fw.dve, lambda e: e.tensor_scalar(out=st4[:, 8:12], in0=st4[:, 8:12], scalar1=C.cmpvalid[:, t:t + 1], scalar2=None, op0=ALU.mult), reads=[st4, C.cmpvalid], writes=[st4])
        fw.op(fw.dve, lambda e: e.tensor_tensor(out=ec3, in0=ec3, in1=st4[:, 8:12].unsqueeze(2).to_broadcast([128, 4, 64]), op=ALU.mult), reads=[ec, st4], writes=[ec])
        imp = A["imp"]
        fw.op(fw.dve, lambda e: e.tensor_reduce(out=imp[:, 0:64], in_=ec[:, :].rearrange("p (r n) -> p n r", r=4), axis=AX.X, op=ALU.add), reads=[ec], writes=[imp])
        fw.op(fw.dve, lambda e: e.tensor_reduce(out=imp[:, 64:96], in_=imp[:, 0:64].rearrange("p (m two) -> p m two", two=2), axis=AX.X, op=ALU.add), reads=[imp], writes=[imp])
        fw.op(fw.dve, lambda e: e.tensor_tensor(out=imp[:, 64:96], in0=imp[:, 64:96], in1=C.selA[:, t, :], op=ALU.mult), reads=[imp, C.selA], writes=[imp])
        fw.op(fw.dve, lambda e: e.tensor_tensor(out=imp[:, 64:96], in0=imp[:, 64:96], in1=C.selB[:, t, :], op=ALU.add), reads=[imp, C.selB], writes=[imp])
        m8 = A["m8"]
        fw.op(fw.dve, lambda e: e.max(out=m8[:, 0:8], in_=imp[:, 64:96]), reads=[imp], writes=[m8])
        fw.op(fw.dve, lambda e: e.match_replace(out=imp[:, 96:128], in_to_replace=m8[:, 0:8], in_values=imp[:, 64:96], imm_value=-1e30), reads=[imp, m8], writes=[imp])
        fw.op(fw.dve, lambda e: e.max(out=m8[:, 8:16], in_=imp[:, 96:128]), reads=[imp], writes=[m8])
        bsel = A["bsel"]
        fw.op(fw.dve, lambda e: e.tensor_scalar(out=bsel[:, :], in0=imp[:, 64:96], scalar1=m8[:, 15:16], scalar2=None, op0=ALU.is_ge), reads=[imp, m8], writes=[bsel])
        fw.op(fw.dve, lambda e: e.tensor_scalar(out=bsel[:, :], in0=bsel[:, :], scalar1=-NEG, scalar2=NEG, op0=ALU.mult, op1=ALU.add), reads=[bsel], writes=[bsel])
        pcb = A["pcb"]
        fw.op(fw.dve, lambda e: e.tensor_tensor(out=pcb[:, :].rearrange("p (r n) -> p r n", r=4), in0=ec3,
                                                in1=sig3[:, g * 4:(g + 1) * 4, 0:1].to_broadcast([128, 4, 64]), op=ALU.mult), reads=[ec, sig], writes=[pcb])
        psb = C.psb[A["psbi"] % 2]
        A["psbi"] += 1
        for r in range(4):
            fw.op(fw.pe, lambda e: e.transpose(out=psb[0:64, r * 128:(r + 1) * 128], in_=pcb[:, r * 64:(r + 1) * 64], identity=C.identb[:, :]),
                  reads=[pcb, C.identb], writes=[psb], join=(r > 0))
        pTc = A["pTc"]
        fw.op(fw.act, lambda e: e.copy(out=pTc[0:64, :], in_=psb[0:64, 0:512]), reads=[psb], writes=[pTc])
        for r in range(4):
            h = g * 4 + r
            acc = C.pacc[h % 2]
            fw.op(fw.pe, lambda e: e.matmul(acc[:, 0:128], lhsT=C.vcmp[0:64, g * 128:(g + 1) * 128], rhs=pTc[0:64, r * 128:(r + 1) * 128], start=True, stop=False),
                  reads=[C.vcmp, pTc], writes=[acc])
            nk = (t + 1) * 128
            ssb = A["ssb"]
            for ci, c0 in enumerate(range(0, nk, 512)):
                cw = min(512, nk - c0)
                ps = C.ps[1 + (ci % 3)]
                fw.op(fw.pe, lambda e: e.matmul(ps[:, 0:cw], lhsT=qT[:, h * 128:(h + 1) * 128], rhs=C.kT[:, g, c0:c0 + cw], start=True, stop=True),
                      reads=[qT, C.kT], writes=[ps])
                nb = cw // 64
                fw.op(fw.dve, lambda e: e.tensor_tensor(out=ssb[:, c0:c0 + cw].rearrange("p (m j) -> p m j", j=64), in0=ps[:, 0:cw].rearrange("p (m j) -> p m j", j=64),
                                                        in1=bsel[:, c0 // 64:c0 // 64 + nb].unsqueeze(2).to_broadcast([128, nb, 64]), op=ALU.add),
                      reads=[ps, bsel], writes=[ssb], join=True)
            fw.op(fw.dve, lambda e: e.tensor_tensor(out=ssb[:, t * 128:(t + 1) * 128], in0=ssb[:, t * 128:(t + 1) * 128], in1=C.causal[:, :], op=ALU.add),
                  reads=[ssb, C.causal], writes=[ssb])
            kt0 = max(0, t - 4)
            nkw = (t - kt0 + 1) * 128
            swb = A["swb"]
            for ci, c0 in enumerate(range(0, nkw, 512)):
                cw = min(512, nkw - c0)
                ps = C.ps[1 + (ci % 3)]
                fw.op(fw.pe, lambda e: e.matmul(ps[:, 0:cw], lhsT=qT[:, h * 128:(h + 1) * 128], rhs=C.kT[:, 2 + g, kt0 * 128 + c0:kt0 * 128 + c0 + cw], start=True, stop=True),
                      reads=[qT, C.kT], writes=[ps])
                fw.op(fw.act, lambda e: e.copy(out=swb[:, c0:c0 + cw], in_=ps[:, 0:cw]), reads=[ps], writes=[swb], join=True)
            fw.op(fw.dve, lambda e: e.tensor_tensor(out=swb[:, nkw - 128:nkw], in0=swb[:, nkw - 128:nkw], in1=C.causal[:, :], op=ALU.add), reads=[swb, C.causal], writes=[swb])
            if t >= 4:
                fw.op(fw.dve, lambda e: e.tensor_tensor(out=swb[:, 0:128], in0=swb[:, 0:128], in1=C.acausal[:, :], op=ALU.add), reads=[swb, C.acausal], writes=[swb])
            emit_softmax_pv(fw, C, A, ssb, nk, sig3[:, h, 1:2], sig, acc, None, False, False, 0, part="sm")
            emit_softmax_pv(fw, C, A, swb, nkw, sig3[:, h, 2:3], sig, acc, None, False, True, 1, part="sm")
            emit_softmax_pv(fw, C, A, ssb, nk, sig3[:, h, 1:2], sig, acc, lambda kt: C.V[:, kt, 0, g * 128:(g + 1) * 128], False, False, 0, part="pv")
            emit_softmax_pv(fw, C, A, swb, nkw, sig3[:, h, 2:3], sig, acc, lambda kt: C.V[:, kt0 + kt, 1, g * 128:(g + 1) * 128], False, True, 1, part="pv")
            if h % 2 == 0:
                fw.op(fw.dve, lambda e: e.tensor_copy(out=mixn[:, h, :], in_=acc[:, 0:128]), reads=[acc], writes=[mixn], join=True)
            else:
                fw.op(fw.act, lambda e: e.copy(out=mixn[:, h, :], in_=acc[:, 0:128]), reads=[acc], writes=[mixn], join=True)


def attn_bufs(fw):
    A = {"psbi": 0}
    A["qt"] = [fw.sbuf("qt0", [128, 1024], F32)] * 2
    A["gate"] = [fw.sbuf(f"gate{i}", [128, 24], F32) for i in range(2)]
    A["rt"] = [fw.sbuf(f"art{i}", [128, 4, 64], F32) for i in range(2)]

    A["qb"] = fw.sbuf("qb", [128, 1024], BF16)
    A["qT"] = fw.sbuf("qT", [128, 1024], BF16)
    A["sig"] = fw.sbuf("sig", [128, 24], F32)
    A["sc"] = fw.sbuf("sc", [128, 256], F32)
    A["ec"] = fw.sbuf("ec", [128, 256], F32)
    A["st4"] = fw.sbuf("st4", [128, 12], F32)
    A["st"] = [fw.sbuf("ast0", [128, 4], F32), fw.sbuf("ast1", [128, 4], F32)]
    A["imp"] = fw.sbuf("imp", [128, 128], F32)
    A["m8"] = fw.sbuf("m8", [128, 16], F32)
    A["bsel"] = fw.sbuf("bsel", [128, 32], F32)
    A["pcb"] = fw.sbuf("pcb", [128, 256], BF16)
    A["pTc"] = fw.sbuf("pTc", [128, 512], BF16)
    A["ssb"] = fw.sbuf("ssb", [128, S], F32)
    A["rtmp"] = [A["ssb"]] * 4
    A["swb"] = fw.sbuf("swb", [128, 640], F32)
    A["eb"] = [fw.sbuf("eb", [128, S], BF16), fw.sbuf("ebw", [128, 640], BF16)]
    A["pT"] = [fw.sbuf("pT", [128, S], BF16), fw.sbuf("pTw", [128, 640], BF16)]
    return A


def emit_mix_ln(fw, C, M, l, t, P, x_src, mixg, mixn, convT, X1):
    xt = M["xres"][0]
    fw.dma(fw.sp, lambda h: h.dma_start(out=xt[0:P, :], in_=x_src[t * 128:t * 128 + P, :]), reads=[x_src], writes=[xt])
    x1 = xt
    for n4 in range(4):
        ps = C.ps[n4]
        for c in range(16):
            if c < 4:
                lt, lb = mixg[:, c, 0:P], mixg
            elif c < 12:
                lt, lb = mixn[:, c - 4, 0:P], mixn
            else:
                lt, lb = convT[:, c - 12, t * 128:t * 128 + P], convT
            fw.op(fw.pe, lambda e: e.matmul(ps[0:P, :], lhsT=lt, rhs=M["wout"][:, c, n4 * 512:(n4 + 1) * 512], start=(c == 0), stop=(c == 15)),
                  reads=[lb, M["wout"]], writes=[ps], join=(c > 0))
        fw.op(fw.dve, lambda e: e.scalar_tensor_tensor(out=x1[0:P, n4 * 512:(n4 + 1) * 512], in0=xt[0:P, n4 * 512:(n4 + 1) * 512], scalar=ALPHA, in1=ps[0:P, :], op0=ALU.mult, op1=ALU.add),
              reads=[xt, ps], writes=[x1])
    emit_ln_free(fw, C, x1, P, M["ln1g"], M["ln1b"], M["junk"], M["lnst"])
    fw.dma(fw.sp, lambda h: h.dma_start(out=X1[t * 128:t * 128 + P, :], in_=x1[0:P, :]), reads=[x1], writes=[X1], join=True)


def mix_bufs(fw, C, l, P):
    M = {}
    wout = fw.sbuf("wout", [128, 16, D], BF16)
    for c4 in range(4):
        fw.dma(fw.pool, lambda h: h.dma_start(out=wout[:, c4 * 4:(c4 + 1) * 4, :], in_=C.w_out[l, c4 * 512:(c4 + 1) * 512, :].rearrange("(c p) n -> p c n", p=128)),
               reads=[C.w_out], writes=[wout], join=True)
    M["wout"] = wout
    M["xres"] = [fw.sbuf("xres0", [128, D], F32)]
    M["x1"] = M["xres"]
    M["junk"] = None
    M["lnst"] = fw.sbuf("lnst", [128, 8], F32)
    M["ln1g"] = load_bcast(fw, "ln1g", C.ln1_g[l, :], D, C.ln1_g, P)
    M["ln1b"] = load_bcast(fw, "ln1b", C.ln1_b[l, :], D, C.ln1_b, P)
    return M


def phase_mix_prompt(fw, C, l, x_src, X1, samp=None):
    fw.push()
    for nm in ["cmpbias", "cmpvalid", "selA", "selB"]:
        setattr(C, nm, load_const(fw, C, nm))
    M = mix_bufs(fw, C, l, 128)
    G = gmlp_consts(fw, C, l, 128)
    A = attn_bufs(fw)
    M["junk"] = A["eb"][0]
    mixg = [fw.sbuf(f"mixg{i}", [128, 4, 128], BF16) for i in range(2)]
    mixn = [fw.sbuf(f"mixn{i}", [128, 8, 128], BF16) for i in range(2)]
    for t in range(NT):
        for tab, dst in ((C.peer_u, C.UB), (C.peer_v, C.VB)):
            for k in (2 * t, 2 * t + 1):
                fw.dma(fw.pool, lambda h: h.dma_start(out=dst[l * 16384 + k * 512:l * 16384 + (k + 1) * 512, :], in_=tab[l, k * 512:(k + 1) * 512, :]),
                       reads=[tab], writes=[dst], join=True)
        out_gm = C.out_gm[l, :, :] if t == NT - 1 else None
        emit_gmlp(fw, C, G, l, t, 128, C.Htok, C.HfT, mixg[t % 2], out_gm)
        emit_attn_prompt(fw, C, A, l, t, mixn[t % 2])
        emit_mix_ln(fw, C, M, l, t, 128, x_src, mixg[t % 2], mixn[t % 2], C.convT, X1)
    if samp is not None:
        xs_src, XS1, mixn_s, convTs = samp
        emit_gmlp(fw, C, G, l, 0, TS, C.HtokS, C.HfTS, mixg[0], C.out_sgm[l, :, :])
        emit_mix_ln(fw, C, M, l, 0, TS, xs_src, mixg[0], mixn_s, convTs, XS1)
    fw.pop()


INPUT_SPECS = [
    ("w_in", [DEPTH, D, IN_W]), ("gm_ln_g", [DEPTH, 512]), ("gm_ln_b", [DEPTH, 512]), ("gm_ws", [DEPTH, 4, 128, 128]), ("gm_bs", [DEPTH, 4, 128]),
    ("cmp_wk", [DEPTH, 2, 32]), ("cmp_wv", [DEPTH, 2, 32]), ("conv_dw", [DEPTH, 31, 512]), ("conv_db", [DEPTH, 512]),
    ("conv_ln_g", [DEPTH, 512]), ("conv_ln_b", [DEPTH, 512]), ("conv_pw", [DEPTH, 512, 512]), ("w_out", [DEPTH, D, D]),
    ("ln1_g", [DEPTH, D]), ("ln1_b", [DEPTH, D]), ("peer_wq", [DEPTH, D, D]), ("peer_subkeys", [DEPTH, 8, 2, 128, 128]),
    ("ln2_g", [DEPTH, D]), ("ln2_b", [DEPTH, D]), ("peer_u", [DEPTH, 16384, D]), ("peer_v", [DEPTH, 16384, D]),
]


def phase_peer(fw, C, l, jobs, NB=10):
    fw.push()
    wq = fw.sbuf("wq", [128, 16, D], BF16)
    for c4 in range(4):
        fw.dma(fw.pool, lambda h: h.dma_start(out=wq[:, c4 * 4:(c4 + 1) * 4, :], in_=C.peer_wq[l, c4 * 512:(c4 + 1) * 512, :].rearrange("(c p) n -> p c n", p=128)),
               reads=[C.peer_wq], writes=[wq], join=True)
    skr = fw.sbuf("skr", [128, 16, 128], F32)
    fw.dma(fw.sp, lambda h: h.dma_start(out=skr[:, :, :], in_=C.peer_subkeys[l, :, :, :, :].rearrange("h p k d -> k (h p) d")), reads=[C.peer_subkeys], writes=[skr])
    skT = fw.sbuf("skT", [128, 16, 128], BF16)
    for q4 in range(4):
        ps = C.ps[q4]
        for j in range(4):
            fw.op(fw.pe, lambda e: e.transpose(out=ps[:, j * 128:(j + 1) * 128], in_=skr[:, q4 * 4 + j, :], identity=C.ident[:, :]), reads=[skr, C.ident], writes=[ps], join=(j > 0))
        fw.op(fw.dve, lambda e: e.tensor_copy(out=skT[:, q4 * 4:(q4 + 1) * 4, :], in_=ps[:, :].rearrange("p (j k) -> p j k", j=4)), reads=[ps], writes=[skT], join=True)
    ln2g = load_bcast(fw, "ln2g", C.ln2_g[l, :], D, C.ln2_g, 128)
    ln2b = load_bcast(fw, "ln2b", C.ln2_b[l, :], D, C.ln2_b, 128)
    iota = load_const(fw, C, "iota16")
    x1 = fw.sbuf("px1", [128, D], F32)
    x1T = fw.sbuf("px1T", [128, 16, 128], BF16)
    qTb = fw.sbuf("pqT", [128, 16, 128], BF16)
    sb = fw.sbuf("psc", [128, 16, 128], F32)
    m = fw.sbuf("pm", [128, 16, 16], F32)
    ix = fw.sbuf("pix", [128, 16, 16], U32)
    ixf = fw.sbuf("pixf", [128, 16, 16], F32)
    tmp = fw.sbuf("ptmp", [128, 256], F32)
    cand = fw.sbuf("pcand", [128, 8, 256], F32)
    oh = fw.sbuf("poh", [128, 8, 256], F32)
    tm = fw.sbuf("ptm", [128, 8, 16], F32)
    pos = fw.sbuf("ppos", [128, 8, 16], U32)
    pa = fw.sbuf("ppa", [128, 8, 16], U32)
    paf = fw.sbuf("ppaf", [128, 2, 128], F32)
    isel = fw.sbuf("pisel", [128, 2, 128], F32)
    eid = fw.sbuf("peid", [128, 128], I32)
    gate = fw.sbuf("pgate", [128, 8, 16], F32)
    gst = fw.sbuf("pgst", [128, 16], F32)
    actp = fw.sbuf("pactp", [128, 128], F32)
    wgt = fw.sbuf("pwgt", [128, 128], F32)
    gtmp = fw.sbuf("pgtmp", [128, 128], F32)
    y = fw.sbuf("py", [128, D], F32)
    junk = fw.sbuf("pjunk", [128, D], BF16)
    lnst = fw.sbuf("plnst", [128, 8], F32)
    ring = [fw.sbuf(f"pring{i}", [128, D], BF16) for i in range(NB)]
    dgs = [fw.sbuf(f"pdg{i}", [128, 128], BF16) for i in range(4)]
    x1b = fw.sbuf("px1b", [128, D], BF16)
    u_rows = C.UB[:, :]
    v_rows = C.VB[:, :]
    gi = 0
    for (X1, P, X2, t, ridx) in [(j[0], j[2], j[3], t_, j[4] if len(j) > 4 else None) for j in jobs for t_ in range(j[1])]:
        if ridx is None:
            fw.dma(fw.sp, lambda h: h.dma_start(out=x1[0:P, :], in_=X1[t * 128:t * 128 + P, :]), reads=[X1], writes=[x1])
        else:
            fw.dma(fw.pool, lambda h: h.indirect_dma_start(out=x1[0:P, :], out_offset=None, in_=X1[:, :], in_offset=bass.IndirectOffsetOnAxis(ap=ridx[0:P, t:t + 1], axis=0)),
                   reads=[X1, ridx], writes=[x1])
        for cb in range(4):
            ps = C.ps[cb]
            for k in range(4):
                c = cb * 4 + k
                fw.op(fw.pe, lambda e: e.transpose(out=ps[:, k * 128:k * 128 + P], in_=x1[0:P, c * 128:(c + 1) * 128], identity=C.ident[0:P, 0:P]),
                      reads=[x1, C.ident], writes=[ps], join=(k > 0))
            src = ps[:, :].rearrange("p (k q) -> p k q", k=4)[:, :, 0:P]
            fw.op(fw.act if cb % 2 else fw.dve, (lambda e: e.copy(out=x1T[:, cb * 4:(cb + 1) * 4, 0:P], in_=src)) if cb % 2 else (lambda e: e.tensor_copy(out=x1T[:, cb * 4:(cb + 1) * 4, 0:P], in_=src)),
                  reads=[ps], writes=[x1T], join=True)
        for q4 in range(4):
            ps = C.ps[q4]
            for j in range(4):
                hp = q4 * 4 + j
                for c in range(16):
                    fw.op(fw.pe, lambda e: e.matmul(ps[:, j * 128:j * 128 + P], lhsT=wq[:, c, hp * 128:(hp + 1) * 128], rhs=x1T[:, c, 0:P], start=(c == 0), stop=(c == 15)),
                          reads=[wq, x1T], writes=[ps], join=not (j == 0 and c == 0))
            src = ps[:, :].rearrange("p (j q) -> p j q", j=4)[:, :, 0:P]
            fw.op(fw.act if q4 % 2 else fw.dve, (lambda e: e.copy(out=qTb[:, q4 * 4:(q4 + 1) * 4, 0:P], in_=src)) if q4 % 2 else (lambda e: e.tensor_copy(out=qTb[:, q4 * 4:(q4 + 1) * 4, 0:P], in_=src)),
                  reads=[ps], writes=[qTb], join=True)
        for q4 in range(4):
            ps = C.ps[q4]
            for j in range(4):
                hp = q4 * 4 + j
                fw.op(fw.pe, lambda e: e.matmul(ps[0:P, j * 128:(j + 1) * 128], lhsT=qTb[:, hp, 0:P], rhs=skT[:, hp, :], start=True, stop=True),
                      reads=[qTb, skT], writes=[ps], join=(j > 0))
            fw.op(fw.act if q4 % 2 else fw.dve, (lambda e: e.copy(out=sb[0:P, q4 * 4:(q4 + 1) * 4, :], in_=ps[0:P, :].rearrange("p (j k) -> p j k", j=4))) if q4 % 2 else
                  (lambda e: e.tensor_copy(out=sb[0:P, q4 * 4:(q4 + 1) * 4, :], in_=ps[0:P, :].rearrange("p (j k) -> p j k", j=4))), reads=[ps], writes=[sb], join=True)
        for hp in range(16):
            fw.op(fw.dve, lambda e: e.max(out=m[0:P, hp, 0:8], in_=sb[0:P, hp, :]), reads=[sb], writes=[m])
            fw.op(fw.dve, lambda e: e.match_replace(out=tmp[0:P, 0:128], in_to_replace=m[0:P, hp, 0:8], in_values=sb[0:P, hp, :], imm_value=-1e30), reads=[sb, m], writes=[tmp])
            fw.op(fw.dve, lambda e: e.max(out=m[0:P, hp, 8:16], in_=tmp[0:P, 0:128]), reads=[tmp], writes=[m])
            fw.op(fw.dve, lambda e: e.max_index(out=ix[0:P, hp, 0:8], in_max=m[0:P, hp, 0:8], in_values=sb[0:P, hp, :]), reads=[sb, m], writes=[ix])
            fw.op(fw.dve, lambda e: e.max_index(out=ix[0:P, hp, 8:16], in_max=m[0:P, hp, 8:16], in_values=tmp[0:P, 0:128]), reads=[tmp, m], writes=[ix])
        fw.op(fw.dve, lambda e: e.tensor_copy(out=ixf[0:P, :, :], in_=ix[0:P, :, :]), reads=[ix], writes=[ixf])
        m4 = m[0:P, :, :].rearrange("p (h two) k -> p h two k", two=2)
        i4 = ixf[0:P, :, :].rearrange("p (h two) k -> p h two k", two=2)
        c4v = cand[0:P, :, :].rearrange("p h (a b) -> p h a b", a=16)
        fw.op(fw.dve, lambda e: e.tensor_tensor(out=c4v, in0=m4[:, :, 0, :].unsqueeze(3).to_broadcast([P, 8, 16, 16]),
                                                in1=m4[:, :, 1, :].unsqueeze(2).to_broadcast([P, 8, 16, 16]), op=ALU.add), reads=[m], writes=[cand])
        for h in range(8):
            fw.op(fw.dve, lambda e: e.max(out=tm[0:P, h, 0:8], in_=cand[0:P, h, :]), reads=[cand], writes=[tm])
            fw.op(fw.dve, lambda e: e.match_replace(out=tmp[0:P, :], in_to_replace=tm[0:P, h, 0:8], in_values=cand[0:P, h, :], imm_value=-1e30), reads=[cand, tm], writes=[tmp])
            fw.op(fw.dve, lambda e: e.max(out=tm[0:P, h, 8:16], in_=tmp[0:P, :]), reads=[tmp], writes=[tm])
            fw.op(fw.dve, lambda e: e.max_index(out=pos[0:P, h, 0:8], in_max=tm[0:P, h, 0:8], in_values=cand[0:P, h, :]), reads=[cand, tm], writes=[pos])
            fw.op(fw.dve, lambda e: e.max_index(out=pos[0:P, h, 8:16], in_max=tm[0:P, h, 8:16], in_values=tmp[0:P, :]), reads=[tmp, tm], writes=[pos])
        fw.op(fw.dve, lambda e: e.tensor_single_scalar(out=pa[0:P, :, :], in_=pos[0:P, :, :], scalar=4, op=ALU.logical_shift_right), reads=[pos], writes=[pa])
        fw.op(fw.dve, lambda e: e.tensor_copy(out=paf[0:P, 0, :], in_=pa[0:P, :, :].rearrange("p h k -> p (h k)")), reads=[pa], writes=[paf])
        fw.op(fw.dve, lambda e: e.tensor_single_scalar(out=pa[0:P, :, :], in_=pos[0:P, :, :], scalar=15, op=ALU.bitwise_and), reads=[pos, paf], writes=[pa])
        fw.op(fw.dve, lambda e: e.tensor_copy(out=paf[0:P, 1, :], in_=pa[0:P, :, :].rearrange("p h k -> p (h k)")), reads=[pa], writes=[paf])
        for w_ in range(2):
            o4 = oh[0:P, :, :].rearrange("p h (k a) -> p h k a", a=16)
            sel = paf[0:P, w_, :].rearrange("p (h k) -> p h k", h=8)
            fw.op(fw.dve, lambda e: e.tensor_tensor(out=o4, in0=sel.unsqueeze(3).to_broadcast([P, 8, 16, 16]),
                                                    in1=iota[0:P, :].unsqueeze(1).unsqueeze(1).to_broadcast([P, 8, 16, 16]), op=ALU.is_equal), reads=[paf, iota], writes=[oh])
            fw.op(fw.dve, lambda e: e.tensor_tensor(out=o4, in0=o4, in1=i4[:, :, w_, :].unsqueeze(2).to_broadcast([P, 8, 16, 16]), op=ALU.mult), reads=[oh, ixf], writes=[oh])
            fw.op(fw.dve, lambda e: e.tensor_reduce(out=isel[0:P, w_, :], in_=oh[0:P, :, :].rearrange("p h (k a) -> p (h k) a", a=16), axis=AX.X, op=ALU.add), reads=[oh], writes=[isel])
        fw.op(fw.dve, lambda e: e.scalar_tensor_tensor(out=eid[0:P, :], in0=isel[0:P, 0, :], scalar=128.0, in1=isel[0:P, 1, :], op0=ALU.mult, op1=ALU.add), reads=[isel], writes=[eid])
        fw.op(fw.dve, lambda e: e.tensor_tensor(out=gate[0:P, :, :], in0=tm[0:P, :, :], in1=tm[0:P, :, 0:1].to_broadcast([P, 8, 16]), op=ALU.subtract), reads=[tm], writes=[gate])
        fw.op(fw.act, lambda e: e.activation(out=gate[0:P, :, :], in_=gate[0:P, :, :], func=AF.Exp), reads=[gate], writes=[gate])
        fw.op(fw.dve, lambda e: e.tensor_reduce(out=gst[0:P, 0:8], in_=gate[0:P, :, :], axis=AX.X, op=ALU.add), reads=[gate], writes=[gst])
        fw.op(fw.dve, lambda e: e.reciprocal(out=gst[0:P, 8:16], in_=gst[0:P, 0:8]), reads=[gst], writes=[gst])
        fw.op(fw.dve, lambda e: e.tensor_tensor(out=gate[0:P, :, :], in0=gate[0:P, :, :], in1=gst[0:P, 8:16].unsqueeze(2).to_broadcast([P, 8, 16]), op=ALU.mult), reads=[gate, gst], writes=[gate])
        fw.op(fw.dve, lambda e: e.memset(actp[0:P, :], 0.0), writes=[actp])
        fw.op(fw.act, lambda e: e.copy(out=x1b[0:P, :], in_=x1[0:P, :]), reads=[x1], writes=[x1b])
        for s_ in range(128):
            rb = ring[gi % NB]
            gi += 1
            fw.dma(fw.pool, lambda h: h.indirect_dma_start(out=rb[0:P, :], out_offset=None, in_=u_rows, in_offset=bass.IndirectOffsetOnAxis(ap=eid[0:P, s_:s_ + 1], axis=0),
                                                         element_offset=l * 16384 * D), reads=[eid, C.UB], writes=[rb])
            fw.op(fw.dve, lambda e: e.scalar_tensor_tensor(out=junk[0:P, :], in0=rb[0:P, :], scalar=1.0, in1=x1b[0:P, :], op0=ALU.mult, op1=ALU.mult, accum_out=actp[0:P, s_:s_ + 1]),
                  reads=[rb, x1b], writes=[junk, actp], join=True)
        emit_gelu(fw, fw.dve, fw.act, wgt[0:P, :], actp[0:P, :], gtmp[0:P, :], [actp], [wgt], gtmp)
        fw.op(fw.dve, lambda e: e.tensor_tensor(out=wgt[0:P, :], in0=wgt[0:P, :], in1=gate[0:P, :, :].rearrange("p h k -> p (h k)"), op=ALU.mult), reads=[wgt, gate], writes=[wgt])
        for s_ in range(128):
            rb = ring[gi % NB]
            gi += 1
            fw.dma(fw.pool, lambda h: h.indirect_dma_start(out=rb[0:P, :], out_offset=None, in_=v_rows, in_offset=bass.IndirectOffsetOnAxis(ap=eid[0:P, s_:s_ + 1], axis=0),
                                                         element_offset=l * 16384 * D), reads=[eid, C.VB], writes=[rb])
            dg = dgs[s_ % 4]
            fw.op(fw.act, lambda e: e.mul(out=dg[0:P, 0:P], in_=C.identb[0:P, 0:P], mul=wgt[0:P, s_:s_ + 1]), reads=[C.identb, wgt], writes=[dg])
            for n4 in range(4):
                fw.op(fw.pe, lambda e: e.matmul(C.ps[n4][0:P, :], lhsT=dg[0:P, 0:P], rhs=rb[0:P, n4 * 512:(n4 + 1) * 512], start=(s_ == 0), stop=(s_ == 127)),
                      reads=[dg, rb], writes=[C.ps[n4]], join=(s_ > 0))
        for n4 in range(4):
            fw.op(fw.dve, lambda e: e.scalar_tensor_tensor(out=y[0:P, n4 * 512:(n4 + 1) * 512], in0=x1[0:P, n4 * 512:(n4 + 1) * 512], scalar=ALPHA, in1=C.ps[n4][0:P, :], op0=ALU.mult, op1=ALU.add),
                  reads=[x1, C.ps[n4]], writes=[y], join=(n4 > 0))
        emit_ln_free(fw, C, y, P, ln2g, ln2b, junk, lnst)
        fw.dma(fw.sp, lambda h: h.dma_start(out=X2[t * 128:t * 128 + P, :], in_=y[0:P, :]), reads=[y], writes=[X2], join=True)
    fw.pop()


def sample_consts():
    c = {}
    rows = np.arange(64)
    tok = rows % 8
    g = rows // 32
    c["msum"] = ((g[:, None] == g[None, :]) & (tok[:, None] == tok[None, :])).astype(np.float32)
    sA = np.ones((64, 257), np.float32); sB = np.zeros((64, 257), np.float32)
    for col in (0, 255, 256):
        sA[:, col] = 0.0; sB[:, col] = 1.0e4
    c["ssA"] = sA; c["ssB"] = sB
    c["caus8"] = np.where(np.arange(8)[None, :] <= tok[:, None], 0.0, NEG).astype(np.float32)
    idx = np.arange(520)[None, :]
    ok = np.where(idx < 512, idx >= tok[:, None], (idx - 512) <= tok[:, None])
    c["wbias"] = np.where(ok, 0.0, NEG).astype(np.float32)
    return c


SCONST_SHAPES = {"msum": [64, 64], "ssA": [64, 257], "ssB": [64, 257], "caus8": [64, 8], "wbias": [64, 520]}
CONST_SHAPES.update(SCONST_SHAPES)


def emit_rows_softmax(fw, sbuf_s, n, st, gate_ap, gate_buf, R=64):
    fw.op(fw.dve, lambda e: e.tensor_reduce(out=st[0:R, 0:1], in_=sbuf_s[0:R, 0:n], axis=AX.X, op=ALU.max, negate=True), reads=[sbuf_s], writes=[st])
    fw.op(fw.act, lambda e: e.activation(out=sbuf_s[0:R, 0:n], in_=sbuf_s[0:R, 0:n], func=AF.Exp, bias=st[0:R, 0:1], scale=1.0, accum_out=st[0:R, 1:2]),
          reads=[sbuf_s, st], writes=[sbuf_s, st])
    fw.op(fw.dve, lambda e: e.reciprocal(out=st[0:R, 2:3], in_=st[0:R, 1:2]), reads=[st], writes=[st])
    if gate_ap is not None:
        fw.op(fw.dve, lambda e: e.tensor_tensor(out=st[0:R, 2:3], in0=st[0:R, 2:3], in1=gate_ap, op=ALU.mult), reads=[st, gate_buf], writes=[st])
    fw.op(fw.dve, lambda e: e.tensor_scalar(out=sbuf_s[0:R, 0:n], in0=sbuf_s[0:R, 0:n], scalar1=st[0:R, 2:3], scalar2=None, op0=ALU.mult), reads=[sbuf_s, st], writes=[sbuf_s])


def phase_nsa_sample(fw, C, l, HtokS, mixn_s):
    fw.push()
    R = 64
    cst = {nm: load_const(fw, C, nm) for nm in SCONST_SHAPES}
    pti = fw.sbuf("pti", [128, 1], I32)
    fw.dma(fw.sp, lambda h: h.dma_start(out=pti[:, :], in_=C.pt[:, :]), reads=[C.pt], writes=[pti])
    idx8 = fw.sbuf("idx8", [128, 1], I32)
    fw.op(fw.dve, lambda e: e.tensor_scalar(out=idx8[:, :], in0=pti[:, :], scalar1=8.0, scalar2=None, op0=ALU.mult), reads=[pti], writes=[idx8])
    kv = fw.sbuf("skv", [TS, 1536], F32)
    rt = fw.sbuf("srt", [TS, 4, 64], F32)
    fw.dma(fw.sp, lambda h: h.dma_start(out=kv[:, :], in_=HtokS[0:TS, 1536:3072]), reads=[HtokS], writes=[kv])
    fw.dma(fw.sp, lambda h: h.dma_start(out=rt[:, :, :], in_=C.cd["rope_s"][:, :, :]), reads=[C.cd["rope_s"]], writes=[rt])
    rtmp = [fw.sbuf(f"srtmp{i}", [TS, 512], F32) for i in range(4)]
    kview = kv[:, :].rearrange("p (a v g h d) -> p a v g h d", a=3, v=2, g=2, h=2, d=64)
    tv = [r[:, 0:384].rearrange("p (a g d) -> p a g d", a=3, g=2) for r in rtmp]
    emit_rope(fw, kview[:, :, 0, :, 0, :], kview[:, :, 0, :, 1, :], rt[:, 0, :], rt[:, 1, :], tv, [TS, 3, 2, 64], [rt], kv, rtmp)
    for a in range(6):
        fw.dma(fw.sp, lambda h: h.dma_start(out=C.out_skv[l, a, :, :], in_=kv[:, a * 256:(a + 1) * 256]), reads=[kv], writes=[C.out_skv], join=True)
    for wi in range(2):
        fw.dma(fw.sp, lambda h: h.dma_start(out=C.out_swin[l, wi, 0:504, :], in_=C.cwin[wi][l, 8:512, :]), reads=[C.cwin[wi]], writes=[C.out_swin], join=True)
        fw.dma(fw.sp, lambda h: h.dma_start(out=C.out_swin[l, wi, 504:512, :], in_=kv[:, 1024 + wi * 256:1280 + wi * 256]), reads=[kv], writes=[C.out_swin], join=True)
    kvb = fw.sbuf("skvb", [TS, 1536], BF16)
    fw.op(fw.act, lambda e: e.copy(out=kvb[:, :], in_=kv[:, :]), reads=[kv], writes=[kvb])
    psb = C.psb[0]
    for i, (a, g) in enumerate([(1, 0), (1, 1), (2, 0), (2, 1)]):
        c0 = a * 512 + g * 128
        fw.op(fw.pe, lambda e: e.transpose(out=psb[:, i * 8:(i + 1) * 8], in_=kvb[:, c0:c0 + 128], identity=C.identb[0:TS, 0:TS]), reads=[kvb, C.identb], writes=[psb], join=(i > 0))
    knT = fw.sbuf("knT", [128, 32], BF16)
    fw.op(fw.dve, lambda e: e.tensor_copy(out=knT[:, :], in_=psb[:, 0:32]), reads=[psb], writes=[knT])
    qt = fw.sbuf("sqt", [TS, 1024], F32)
    gt = fw.sbuf("sgt", [TS, 24], F32)
    fw.dma(fw.sp, lambda h: h.dma_start(out=qt[:, :], in_=HtokS[0:TS, 512:1536]), reads=[HtokS], writes=[qt])
    fw.dma(fw.sp, lambda h: h.dma_start(out=gt[:, :], in_=HtokS[0:TS, 3072:3096]), reads=[HtokS], writes=[gt])
    qv = qt[:, :].rearrange("p (h x d) -> p h x d", h=8, x=2)
    tv2 = [r[:, :].rearrange("p (h d) -> p h d", h=8) for r in rtmp]
    emit_rope(fw, qv[:, :, 0, :], qv[:, :, 1, :], rt[:, 2, :], rt[:, 3, :], tv2, [TS, 8, 64], [rt], qt, rtmp)
    qb = fw.sbuf("sqb", [TS, 1024], BF16)
    fw.op(fw.act, lambda e: e.copy(out=qb[:, :], in_=qt[:, :]), reads=[qt], writes=[qb])
    psb = C.psb[1]
    for h in range(8):
        fw.op(fw.pe, lambda e: e.transpose(out=psb[:, h * 8:(h + 1) * 8], in_=qb[:, h * 128:(h + 1) * 128], identity=C.identb[0:TS, 0:TS]), reads=[qb, C.identb], writes=[psb], join=(h > 0))
    Q = [fw.sbuf(f"sQ{g}", [128, 64], BF16) for g in range(2)]
    for g in range(2):
        fw.op(fw.dve, lambda e: e.memset(Q[g][:, :], 0.0), writes=[Q[g]])
        fw.op(fw.dve, lambda e: e.tensor_copy(out=Q[g][:, g * 32:(g + 1) * 32], in_=psb[:, g * 32:(g + 1) * 32]), reads=[psb], writes=[Q[g]])
    fw.op(fw.act, lambda e: e.activation(out=gt[:, :], in_=gt[:, :], func=AF.Sigmoid), reads=[gt], writes=[gt])
    fw.dma(fw.sp, lambda h: h.dma_start(out=C.gscr[:, :], in_=gt[:, :]), reads=[gt], writes=[C.gscr])
    g64 = fw.sbuf("g64", [64, 3], F32)
    for h in range(8):
        fw.dma(fw.sp, lambda h_: h_.dma_start(out=g64[h * 8:(h + 1) * 8, :], in_=C.gscr[:, h * 3:(h + 1) * 3]), reads=[C.gscr], writes=[g64], join=True)
    st = fw.sbuf("sst", [64, 4], F32)
    wsm = []
    for wi, wsrc in enumerate([C.cmp_wk, C.cmp_wv]):
        w_ = fw.sbuf(f"wsm{wi}", [128, 64], F32)
        fw.dma(fw.sp, lambda h: h.dma_start(out=w_[:, :], in_=wsrc[l, :, :].rearrange("g j -> (g j)").partition_broadcast(128)), reads=[wsrc], writes=[w_])
        wsm.append(w_)
    chunk = [fw.sbuf(f"chunk{i}", [128, 4096], F32) for i in range(2)]
    cacc = [fw.sbuf(f"cacc{i}", [128, 4, 256], F32) for i in range(2)]
    red = fw.sbuf("sred", [128, 256], F32)
    ci = 0
    for wi, cache in enumerate([C.c_cmp_k, C.c_cmp_v]):
        for jb8 in range(8):
            ch = chunk[ci % 2]
            ci += 1
            fw.dma(fw.pool, lambda h: h.indirect_dma_start(out=ch[:, :], out_offset=None, in_=cache[:, :], in_offset=bass.IndirectOffsetOnAxis(ap=idx8[:, 0:1], axis=0),
                                                         element_offset=(l * 1280 * 8 + jb8) * 4096), reads=[idx8, cache], writes=[ch])
            jb, j0 = jb8 // 2, (jb8 % 2) * 16
            wv_ = wsm[wi][:, :].rearrange("p (g j) -> p j g", g=2)[:, j0:j0 + 16, :].unsqueeze(3).to_broadcast([128, 16, 2, 128])
            chv4 = ch[:, :].rearrange("p (j g d) -> p j g d", j=16, g=2)
            fw.op(fw.dve, lambda e: e.tensor_tensor(out=chv4, in0=chv4, in1=wv_, op=ALU.mult), reads=[ch, wsm[wi]], writes=[ch])
            if j0 == 0:
                fw.op(fw.dve, lambda e: e.tensor_reduce(out=cacc[wi][:, jb, :], in_=ch[:, :].rearrange("p (j c) -> p c j", j=16), axis=AX.X, op=ALU.add), reads=[ch], writes=[cacc[wi]], join=True)
            else:
                fw.op(fw.dve, lambda e: e.tensor_reduce(out=red[:, :], in_=ch[:, :].rearrange("p (j c) -> p c j", j=16), axis=AX.X, op=ALU.add), reads=[ch], writes=[red])
                fw.op(fw.dve, lambda e: e.tensor_tensor(out=cacc[wi][:, jb, :], in0=cacc[wi][:, jb, :], in1=red[:, :], op=ALU.add), reads=[cacc[wi], red], writes=[cacc[wi]])
    kcT = fw.sbuf("skcT", [128, 2, 512], BF16)
    for g in range(2):
        ps = C.ps[g]
        for jb in range(4):
            fw.op(fw.pe, lambda e: e.transpose(out=ps[:, jb * 128:(jb + 1) * 128], in_=cacc[0][:, jb, g * 128:(g + 1) * 128], identity=C.ident[:, :]), reads=[cacc[0], C.ident], writes=[ps], join=(jb > 0))
        fw.op(fw.dve, lambda e: e.tensor_copy(out=kcT[:, g, :], in_=ps[:, :]), reads=[ps], writes=[kcT], join=True)
    vcb = fw.sbuf("svcb", [128, 4, 256], BF16)
    fw.op(fw.act, lambda e: e.copy(out=vcb[:, :, :], in_=cacc[1][:, :, :]), reads=[cacc[1]], writes=[vcb])
    ps = C.ps[2]
    for g in range(2):
        fw.op(fw.pe, lambda e: e.matmul(ps[0:R, :], lhsT=Q[g][:, :], rhs=kcT[:, g, :], start=(g == 0), stop=(g == 1)), reads=[Q[g], kcT], writes=[ps], join=(g > 0))
    pc = fw.sbuf("spc", [64, 512], F32)
    fw.op(fw.act, lambda e: e.copy(out=pc[:, :], in_=ps[0:R, :]), reads=[ps], writes=[pc])
    emit_rows_softmax(fw, pc, 512, st, None, None)
    ps = C.ps[3]
    fw.op(fw.pe, lambda e: e.matmul(ps[0:R, :], lhsT=cst["msum"][:, :], rhs=pc[:, :], start=True, stop=True), reads=[cst["msum"], pc], writes=[ps])
    impr = fw.sbuf("simpr", [64, 512], F32)
    fw.op(fw.act, lambda e: e.copy(out=impr[:, :], in_=ps[0:R, :]), reads=[ps], writes=[impr])
    sco = fw.sbuf("ssco", [64, 264], F32)
    sco2 = fw.sbuf("ssco2", [64, 264], F32)
    iv = impr[:, :].rearrange("r (hb two p) -> r hb two p", hb=2, two=2)
    fw.op(fw.dve, lambda e: e.memset(sco[:, 256:264], 0.0), writes=[sco])
    fw.op(fw.dve, lambda e: e.tensor_tensor(out=sco[:, 0:256].rearrange("r (hb p) -> r hb p", hb=2), in0=iv[:, :, 0, :], in1=iv[:, :, 1, :], op=ALU.add), reads=[impr], writes=[sco], join=True)
    fw.op(fw.dve, lambda e: e.tensor_tensor(out=sco[:, 0:257], in0=sco[:, 0:257], in1=cst["ssA"][:, :], op=ALU.mult), reads=[sco, cst["ssA"]], writes=[sco])
    fw.op(fw.dve, lambda e: e.tensor_tensor(out=sco[:, 0:257], in0=sco[:, 0:257], in1=cst["ssB"][:, :], op=ALU.add), reads=[sco, cst["ssB"]], writes=[sco])
    m8 = fw.sbuf("sm8", [64, 16], F32)
    fw.op(fw.dve, lambda e: e.max(out=m8[:, 0:8], in_=sco[:, 0:257]), reads=[sco], writes=[m8])
    fw.op(fw.dve, lambda e: e.match_replace(out=sco2[:, 0:257], in_to_replace=m8[:, 0:8], in_values=sco[:, 0:257], imm_value=-1e30), reads=[sco, m8], writes=[sco2])
    fw.op(fw.dve, lambda e: e.max(out=m8[:, 8:16], in_=sco2[:, 0:257]), reads=[sco2], writes=[m8])
    bsel = fw.sbuf("sbsel", [64, 257], F32)
    fw.op(fw.dve, lambda e: e.tensor_scalar(out=bsel[:, :], in0=sco[:, 0:257], scalar1=m8[:, 15:16], scalar2=None, op0=ALU.is_ge), reads=[sco, m8], writes=[bsel])
    fw.op(fw.dve, lambda e: e.tensor_scalar(out=bsel[:, :], in0=bsel[:, :], scalar1=-NEG, scalar2=NEG, op0=ALU.mult, op1=ALU.add), reads=[bsel], writes=[bsel])
    fw.op(fw.dve, lambda e: e.tensor_scalar(out=pc[:, :], in0=pc[:, :], scalar1=g64[:, 0:1], scalar2=None, op0=ALU.mult), reads=[pc, g64], writes=[pc])
    pcT = fw.sbuf("spcT", [128, 4, 64], BF16)
    ps = C.ps[0]
    for jb in range(4):
        fw.op(fw.pe, lambda e: e.transpose(out=ps[:, jb * 64:(jb + 1) * 64], in_=pc[:, jb * 128:(jb + 1) * 128], identity=C.ident[0:R, 0:R]), reads=[pc, C.ident], writes=[ps], join=(jb > 0))
    fw.op(fw.dve, lambda e: e.tensor_copy(out=pcT[:, :, :], in_=ps[:, 0:256].rearrange("p (j r) -> p j r", j=4)), reads=[ps], writes=[pcT])
    Ss = fw.sbuf("sS", [64, 16392], F32)
    vt = [fw.sbuf(f"svt{i}", [128, 16, 256], BF16) for i in range(2)]
    ktile = [fw.sbuf(f"sktile{i}", [128, 512], BF16) for i in range(4)]
    ki = 0
    for jb8 in range(8):
        ch = chunk[ci % 2]
        ci += 1
        fw.dma(fw.pool, lambda h: h.indirect_dma_start(out=ch[:, :], out_offset=None, in_=C.c_slc_k[:, :], in_offset=bass.IndirectOffsetOnAxis(ap=idx8[:, 0:1], axis=0),
                                                     element_offset=(l * 1280 * 8 + jb8) * 4096), reads=[idx8, C.c_slc_k], writes=[ch])
        chv = ch[:, :].rearrange("p (j g d) -> p j g d", j=16, g=2)
        for j4 in range(4):
            kts = []
            for g in range(2):
                pst = C.ps[(ki % 2) * 2 + g]
                for jj in range(4):
                    fw.op(fw.pe, lambda e: e.transpose(out=pst[:, jj * 128:(jj + 1) * 128], in_=chv[:, j4 * 4 + jj, g, :], identity=C.ident[:, :]), reads=[ch, C.ident], writes=[pst], join=(jj > 0))
                kt_ = ktile[(ki % 2) * 2 + g]
                if g == 0:
                    fw.op(fw.dve, lambda e: e.tensor_copy(out=kt_[:, :], in_=pst[:, :]), reads=[pst], writes=[kt_])
                else:
                    fw.op(fw.act, lambda e: e.copy(out=kt_[:, :], in_=pst[:, :]), reads=[pst], writes=[kt_])
                kts.append(kt_)
            pss = C.pacc[ki % 2]
            for g in range(2):
                fw.op(fw.pe, lambda e: e.matmul(pss[0:R, :], lhsT=Q[g][:, :], rhs=kts[g][:, :], start=(g == 0), stop=(g == 1)), reads=[Q[g], kts[g]], writes=[pss], join=(g > 0))
            jabs = jb8 * 16 + j4 * 4
            hb = 1 if jabs >= 64 else 0
            fw.op(fw.dve, lambda e: e.tensor_tensor(out=Ss[:, jabs * 128:(jabs + 4) * 128].rearrange("r (j p) -> r j p", j=4), in0=pss[0:R, :].rearrange("r (j p) -> r j p", j=4),
                                                    in1=bsel[:, hb * 128:(hb + 1) * 128].unsqueeze(1).to_broadcast([R, 4, 128]), op=ALU.add), reads=[pss, bsel], writes=[Ss], join=True)
            ki += 1
    pss = C.pacc[0]
    for g in range(2):
        fw.op(fw.pe, lambda e: e.matmul(pss[0:R, 0:8], lhsT=Q[g][:, :], rhs=knT[:, g * 8:(g + 1) * 8], start=(g == 0), stop=(g == 1)), reads=[Q[g], knT], writes=[pss], join=(g > 0))
    fw.op(fw.dve, lambda e: e.scalar_tensor_tensor(out=Ss[:, 16384:16392], in0=pss[0:R, 0:8], scalar=bsel[:, 256:257], in1=cst["caus8"][:, :], op0=ALU.add, op1=ALU.add),
          reads=[pss, bsel, cst["caus8"]], writes=[Ss], join=True)
    emit_rows_softmax(fw, Ss, 16392, st, g64[:, 1:2], g64)
    pT = fw.sbuf("spT", [128, 128, 64], BF16)
    for j8 in range(16):
        ps = C.ps[j8 % 4]
        for jj in range(8):
            j = j8 * 8 + jj
            fw.op(fw.pe, lambda e: e.transpose(out=ps[:, jj * 64:(jj + 1) * 64], in_=Ss[:, j * 128:(j + 1) * 128], identity=C.ident[0:R, 0:R]), reads=[Ss, C.ident], writes=[ps], join=(jj > 0))
        fw.op(fw.act if j8 % 2 else fw.dve, (lambda e: e.copy(out=pT[:, j8 * 8:(j8 + 1) * 8, :], in_=ps[:, :].rearrange("p (j r) -> p j r", j=8))) if j8 % 2 else
              (lambda e: e.tensor_copy(out=pT[:, j8 * 8:(j8 + 1) * 8, :], in_=ps[:, :].rearrange("p (j r) -> p j r", j=8))), reads=[ps], writes=[pT], join=True)
    ps = C.ps[0]
    fw.op(fw.pe, lambda e: e.transpose(out=ps[0:8, 0:64], in_=Ss[:, 16384:16392], identity=C.ident[0:R, 0:R]), reads=[Ss, C.ident], writes=[ps])
    pTn = fw.sbuf("spTn", [8, 64], BF16)
    fw.op(fw.dve, lambda e: e.tensor_copy(out=pTn[:, :], in_=ps[0:8, 0:64]), reads=[ps], writes=[pTn])
    wk = fw.sbuf("swk", [128, 4, 256], F32)
    wv = fw.sbuf("swv", [128, 4, 256], F32)
    fw.dma(fw.sp, lambda h: h.dma_start(out=wk[:, :, :], in_=C.cwin[0][l, :, :].rearrange("(a p) c -> p a c", p=128)), reads=[C.cwin[0]], writes=[wk])
    fw.dma(fw.sp, lambda h: h.dma_start(out=wv[:, :, :], in_=C.cwin[1][l, :, :].rearrange("(a p) c -> p a c", p=128)), reads=[C.cwin[1]], writes=[wv])
    wvb = fw.sbuf("swvb", [128, 4, 256], BF16)
    fw.op(fw.act, lambda e: e.copy(out=wvb[:, :, :], in_=wv[:, :, :]), reads=[wv], writes=[wvb])
    wkT = fw.sbuf("swkT", [128, 2, 512], BF16)
    for g in range(2):
        ps = C.ps[1 + g]
        for a in range(4):
            fw.op(fw.pe, lambda e: e.transpose(out=ps[:, a * 128:(a + 1) * 128], in_=wk[:, a, g * 128:(g + 1) * 128], identity=C.ident[:, :]), reads=[wk, C.ident], writes=[ps], join=(a > 0))
        fw.op(fw.dve, lambda e: e.tensor_copy(out=wkT[:, g, :], in_=ps[:, :]), reads=[ps], writes=[wkT], join=True)
    Sw = fw.sbuf("sSw", [64, 520], F32)
    pss = C.pacc[1]
    for g in range(2):
        fw.op(fw.pe, lambda e: e.matmul(pss[0:R, :], lhsT=Q[g][:, :], rhs=wkT[:, g, :], start=(g == 0), stop=(g == 1)), reads=[Q[g], wkT], writes=[pss], join=(g > 0))
    fw.op(fw.dve, lambda e: e.tensor_tensor(out=Sw[:, 0:512], in0=pss[0:R, :], in1=cst["wbias"][:, 0:512], op=ALU.add), reads=[pss, cst["wbias"]], writes=[Sw], join=True)
    pss = C.pacc[0]
    for g in range(2):
        fw.op(fw.pe, lambda e: e.matmul(pss[0:R, 0:8], lhsT=Q[g][:, :], rhs=knT[:, 16 + g * 8:16 + (g + 1) * 8], start=(g == 0), stop=(g == 1)), reads=[Q[g], knT], writes=[pss], join=(g > 0))
    fw.op(fw.dve, lambda e: e.tensor_tensor(out=Sw[:, 512:520], in0=pss[0:R, 0:8], in1=cst["wbias"][:, 512:520], op=ALU.add), reads=[pss, cst["wbias"]], writes=[Sw], join=True)
    emit_rows_softmax(fw, Sw, 520, st, g64[:, 2:3], g64)
    pwT = fw.sbuf("spwT", [128, 4, 64], BF16)
    ps = C.ps[3]
    for a in range(4):
        fw.op(fw.pe, lambda e: e.transpose(out=ps[:, a * 64:(a + 1) * 64], in_=Sw[:, a * 128:(a + 1) * 128], identity=C.ident[0:R, 0:R]), reads=[Sw, C.ident], writes=[ps], join=(a > 0))
    fw.op(fw.pe, lambda e: e.transpose(out=ps[0:8, 256:320], in_=Sw[:, 512:520], identity=C.ident[0:R, 0:R]), reads=[Sw, C.ident], writes=[ps], join=True)
    fw.op(fw.dve, lambda e: e.tensor_copy(out=pwT[:, :, :], in_=ps[:, 0:256].rearrange("p (a r) -> p a r", a=4)), reads=[ps], writes=[pwT])
    pwTn = fw.sbuf("spwTn", [8, 64], BF16)
    fw.op(fw.dve, lambda e: e.tensor_copy(out=pwTn[:, :], in_=ps[0:8, 256:320]), reads=[ps], writes=[pwTn])
    accs = [C.pacc[0], C.pacc[1]]
    for g in range(2):
        gs = slice(g * 32, (g + 1) * 32)
        for jb in range(4):
            fw.op(fw.pe, lambda e: e.matmul(accs[g][:, 0:32], lhsT=vcb[:, jb, g * 128:(g + 1) * 128], rhs=pcT[:, jb, gs], start=(jb == 0), stop=False), reads=[vcb, pcT], writes=[accs[g]], join=(jb > 0))
    for jb8 in range(8):
        ch = chunk[ci % 2]
        v_ = vt[ci % 2]
        ci += 1
        fw.dma(fw.pool, lambda h: h.indirect_dma_start(out=ch[:, :], out_offset=None, in_=C.c_slc_v[:, :], in_offset=bass.IndirectOffsetOnAxis(ap=idx8[:, 0:1], axis=0),
                                                     element_offset=(l * 1280 * 8 + jb8) * 4096), reads=[idx8, C.c_slc_v], writes=[ch])
        fw.op(fw.act if jb8 % 2 else fw.dve, (lambda e: e.copy(out=v_[:, :, :], in_=ch[:, :].rearrange("p (j c) -> p j c", j=16))) if jb8 % 2 else
              (lambda e: e.tensor_copy(out=v_[:, :, :], in_=ch[:, :].rearrange("p (j c) -> p j c", j=16))), reads=[ch], writes=[v_])
        for g in range(2):
            gs = slice(g * 32, (g + 1) * 32)
            for jj in range(16):
                j = jb8 * 16 + jj
                fw.op(fw.pe, lambda e: e.matmul(accs[g][:, 0:32], lhsT=v_[:, jj, g * 128:(g + 1) * 128], rhs=pT[:, j, gs], start=False, stop=False), reads=[v_, pT], writes=[accs[g]], join=True)
    for g in range(2):
        acc = accs[g]
        gs = slice(g * 32, (g + 1) * 32)
        fw.op(fw.pe, lambda e: e.matmul(acc[:, 0:32], lhsT=kvb[:, 768 + g * 128:896 + g * 128], rhs=pTn[:, gs], start=False, stop=False), reads=[kvb, pTn], writes=[acc], join=True)
        for a in range(4):
            fw.op(fw.pe, lambda e: e.matmul(acc[:, 0:32], lhsT=wvb[:, a, g * 128:(g + 1) * 128], rhs=pwT[:, a, gs], start=False, stop=False), reads=[wvb, pwT], writes=[acc], join=True)
        fw.op(fw.pe, lambda e: e.matmul(acc[:, 0:32], lhsT=kvb[:, 1280 + g * 128:1408 + g * 128], rhs=pwTn[:, gs], start=False, stop=True), reads=[kvb, pwTn], writes=[acc], join=True)
        fw.op(fw.dve, lambda e: e.tensor_copy(out=mixn_s[:, g * 4:(g + 1) * 4, 0:TS], in_=acc[:, 0:32].rearrange("p (r t) -> p r t", r=4)), reads=[acc], writes=[mixn_s], join=True)
    fw.pop()


def build(mode="full"):
    nc = bass.Bass("TRN2", target_bir_lowering=False)
    fw = FW(nc)
    C = Ctx()
    dbg = mode != "full"
    nlayers = 1 if dbg else DEPTH
    def inp(name, shape, dtype=F32):
        return fw.dram(name, shape, dtype, kind="ExternalInput")
    C.xp = inp("xp", [S, D])
    C.xs = inp("xs", [TS, D])
    C.pt = inp("pt", [128, 1], I32)
    C.c_cmp_k = inp("c_cmp_k", [DEPTH * 1280 * 8, 4096])
    C.c_cmp_v = inp("c_cmp_v", [DEPTH * 1280 * 8, 4096])
    C.c_slc_k = inp("c_slc_k", [DEPTH * 1280 * 8, 4096])
    C.c_slc_v = inp("c_slc_v", [DEPTH * 1280 * 8, 4096])
    C.cwin = [inp("c_win_k", [DEPTH, 512, 256]), inp("c_win_v", [DEPTH, 512, 256])]
    C.state_conv = inp("state_conv", [DEPTH, 30, 512])
    for nm, shp in INPUT_SPECS:
        setattr(C, nm, inp(nm, shp))
    C.cd = {nm: inp(nm, shp) for nm, shp in CONST_SHAPES.items()}
    C.out_kv = fw.dram("o_pkv", [DEPTH, 6, S, 256], F32, kind="ExternalOutput")
    C.out_conv_buf = fw.dram("o_pconv", [DEPTH, 30, 512], F32, kind="ExternalOutput")
    C.out_gm = fw.dram("o_pgm", [DEPTH, 128, 512], F32, kind="ExternalOutput")
    C.out_gm_buf = C.out_gm
    C.out_skv = fw.dram("o_skv", [DEPTH, 6, TS, 256], F32, kind="ExternalOutput")
    C.out_swin = fw.dram("o_swin", [DEPTH, 2, 512, 256], F32, kind="ExternalOutput")
    C.out_sconv = fw.dram("o_sconv", [DEPTH, 30, 512], F32, kind="ExternalOutput")
    C.out_sgm = fw.dram("o_sgm", [DEPTH, TS, 512], F32, kind="ExternalOutput")
    C.out_y = fw.dram("o_y", [S // 2, D], F32, kind="ExternalOutput")
    C.prow_d = inp("prow", [128, NT // 2], I32)
    C.prow = fw.sbuf("prow_sb", [128, NT // 2], I32)
    fw.dma(fw.sp, lambda h: h.dma_start(out=C.prow[:, :], in_=C.prow_d[:, :]), reads=[C.prow_d], writes=[C.prow])
    C.out_ys = fw.dram("o_ys", [TS, D], F32, kind="ExternalOutput")
    C.ident = load_const(fw, C, "ident")
    C.identb = fw.sbuf("identb", [128, 128], BF16)
    fw.op(fw.dve, lambda e: e.tensor_copy(out=C.identb[:, :], in_=C.ident[:, :]), reads=[C.ident], writes=[C.identb])
    C.epsc = fw.sbuf("epsc", [128, 1], F32)
    fw.op(fw.dve, lambda e: e.memset(C.epsc[:, :], LN_EPS), writes=[C.epsc])
    for nm in ["causal", "acausal", "mask4", "triu", "onesdiv"]:
        setattr(C, nm, load_const(fw, C, nm))
    C.ps = [fw.psum(f"ps{i}", [128, 512], F32) for i in range(4)]
    C.psb = [fw.psum(f"psb{i}", [128, 1024], BF16) for i in range(2)]
    C.pacc = [fw.psum(f"pacc{i}", [128, 512], F32) for i in range(2)]
    okind = "ExternalOutput" if dbg else "Internal"
    C.Htok = fw.dram("Htok", [S, NTOKC], F32)
    C.HfT = fw.dram("HfT", [1536, S], F32)
    C.HtokS = fw.dram("HtokS", [TS, NTOKC], F32)
    C.HfTS = fw.dram("HfTS", [1536, TS], F32)
    C.gscr = fw.dram("gscr", [TS, 24], F32)
    C.UB = fw.dram("UB", [DEPTH * 16384, D], BF16)
    C.VB = fw.dram("VB", [DEPTH * 16384, D], BF16)
    C.X1 = fw.dram("X1", [S, D], F32, kind=okind)
    C.XS1 = fw.dram("XS1", [TS, D], F32, kind=okind)
    X2 = [fw.dram("X2a", [S, D], F32), C.out_y] if not dbg else [C.out_y]
    XS2 = [fw.dram("XS2a", [TS, D], F32), C.out_ys] if not dbg else [C.out_ys]
    mixn_s = fw.sbuf("mixn_s", [128, 8, TS], BF16)
    convTs = fw.sbuf("convTs", [128, 4, TS], BF16)
    x_src, xs_src = C.xp, C.xs
    for l in range(nlayers):
        fw.push()
        C.xin = [fw.sbuf(f"xin{i}", [128, D], F32) for i in range(2)]
        C.wbuf = [fw.sbuf(f"wbuf{i}", [128, 16, 512], BF16) for i in range(2)]
        C.hbuf = [fw.sbuf(f"hbuf{i}", [128, 512], F32) for i in range(4)]
        C.htmp = fw.sbuf("htmp", [128, 512], F32)
        xT = fw.sbuf("xT", [128, 16, S], BF16)
        phase_proj(fw, C, l, x_src, NT, C.Htok, C.HfT, xT, "p")
        phase_proj(fw, C, l, xs_src, 0, C.HtokS, C.HfTS, xT, "s")
        fw.pop()
        phase_nsa_sample(fw, C, l, C.HtokS, mixn_s)
        fw.push()
        C.kT = fw.sbuf("kT", [128, 4, S], BF16)
        C.V = fw.sbuf("V", [128, NT, 2, 256], BF16)
        C.kcmpT = fw.sbuf("kcmpT", [128, 2, 64], BF16)
        C.vcmp = fw.sbuf("vcmp", [64, 256], BF16)
        C.convT = fw.sbuf("convT", [128, 4, S], BF16)
        fw.push()
        phase_kv_prompt(fw, C, l)
        fw.pop()
        phase_conv(fw, C, l, C.HfT, S, None, C.out_conv_buf[l, :, :], C.convT)
        phase_conv(fw, C, l, C.HfTS, TS, C.state_conv[l, :, :], C.out_sconv[l, :, :], convTs)
        phase_mix_prompt(fw, C, l, x_src, C.X1, (xs_src, C.XS1, mixn_s, convTs))
        fw.pop()
        if l == nlayers - 1:
            phase_peer(fw, C, l, [(C.X1, NT // 2 if not dbg else 1, 128, X2[l], C.prow), (C.XS1, 1, TS, XS2[l])])
        else:
            phase_peer(fw, C, l, [(C.X1, NT, 128, X2[l]), (C.XS1, 1, TS, XS2[l])])
        x_src, xs_src = X2[l], XS2[l]
    fw.finish()
    return nc


def core_inputs(inputs, c):
    b = c // 2
    m = {"xp": np.ascontiguousarray(inputs["x_prompt"][b]), "xs": np.ascontiguousarray(inputs["x_sample"][c]),
         "pt": np.ascontiguousarray(inputs["page_table"][c].reshape(128, 1)).astype(np.int32)}
    for nm, key in [("c_cmp_k", "cache_cmp_k"), ("c_cmp_v", "cache_cmp_v"), ("c_slc_k", "cache_slc_k"), ("c_slc_v", "cache_slc_v")]:
        m[nm] = np.asarray(inputs[key]).reshape(DEPTH * 1280 * 8, 4096)
    m["c_win_k"] = np.ascontiguousarray(np.asarray(inputs["cache_win_k"])[:, c].reshape(DEPTH, 512, 256))
    m["c_win_v"] = np.ascontiguousarray(np.asarray(inputs["cache_win_v"])[:, c].reshape(DEPTH, 512, 256))
    m["state_conv"] = np.ascontiguousarray(np.asarray(inputs["state_conv"])[:, c])
    hh = c % 2
    m["prow"] = ((hh * (NT // 2) + np.arange(NT // 2)[None, :]) * 128 + np.arange(128)[:, None]).astype(np.int32)
    for nm, shp in INPUT_SPECS:
        m[nm] = np.asarray(inputs[nm])
    m.update(host_consts())
    return m


_NC_CACHE = {}


def kernel(**inputs):
    n = 8
    if "nc" not in _NC_CACHE:
        _NC_CACHE["nc"] = build("full")
    nc = _NC_CACHE["nc"]
    in_maps = [core_inputs(inputs, c) for c in range(n)]
    res = run_bass_kernel_spmd(nc, in_maps, core_ids=list(range(n))).results
    B = 4
    ev = [res[2 * b] for b in range(B)]
    y_prompt = np.stack([np.concatenate([res[2 * b]["o_y"], res[2 * b + 1]["o_y"]], 0) for b in range(B)], 0).astype(np.float32)
    y_sample = np.stack([res[c]["o_ys"] for c in range(n)], 0).astype(np.float32)
    pkv = np.stack([r["o_pkv"] for r in ev], 0)
    def pk(a, rows=None):
        t = pkv[:, :, a]
        if rows is not None:
            t = t[:, :, rows:]
        t = np.transpose(t, (1, 0, 2, 3))
        return np.ascontiguousarray(t.reshape(t.shape[0], t.shape[1], t.shape[2], 2, 128)).astype(np.float32)
    p_conv = np.ascontiguousarray(np.stack([r["o_pconv"] for r in ev], 1)).astype(np.float32)
    p_gm = np.ascontiguousarray(np.stack([r["o_pgm"] for r in ev], 1)).astype(np.float32)
    skv = np.stack([res[c]["o_skv"] for c in range(n)], 0)
    def sk(a):
        t = np.transpose(skv[:, :, a], (1, 0, 2, 3))
        return np.ascontiguousarray(t.reshape(DEPTH, n, TS, 2, 128)).astype(np.float32)
    swin = np.stack([res[c]["o_swin"] for c in range(n)], 0)
    def sw(a):
        t = np.transpose(swin[:, :, a], (1, 0, 2, 3))
        return np.ascontiguousarray(t.reshape(DEPTH, n, 512, 2, 128)).astype(np.float32)
    s_conv = np.ascontiguousarray(np.stack([res[c]["o_sconv"] for c in range(n)], 1)).astype(np.float32)
    s_gm = np.ascontiguousarray(np.stack([res[c]["o_sgm"] for c in range(n)], 1)).astype(np.float32)
    return (y_prompt, y_sample, pk(0), pk(1), pk(2), pk(3), pk(4, S - 512), pk(5, S - 512), p_conv, p_gm,
            sk(0), sk(1), sk(2), sk(3), sw(0), sw(1), s_conv, s_gm)
```
